# Optimizing a Trainium2 kernel written in Bass

```python
import jax
import jax.numpy as jnp
from jax import lax
import numpy as np

D_MODEL = 2048
BATCH = 2
SEQ = 4096
DEPTH = 1
DEC_BATCH = 16
DEC_SEQ = 16
PAST_LEN = 1024

CHUNK = 64
Q_BLOCK = 128
SB_HEADS = 8
SB_HEAD_DIM = 128
SB_WIDTH = SB_HEADS * SB_HEAD_DIM
MLA_HEADS = 8
Q_LORA = 512
KV_LORA = 512
NOPE_DIM = 128
ROPE_DIM = 64
V_DIM = 128
MLA_WIDTH = MLA_HEADS * V_DIM
ROPE_THETA = 10000.0
D_FF = 5504
LN_EPS = 1e-5
RMS_EPS = 1e-6
ALPHA = (2 * DEPTH) ** 0.25
DN_BETA = (8 * DEPTH) ** -0.25
SB_SCALE = SB_HEAD_DIM ** -0.5
MLA_SCALE = (NOPE_DIM + ROPE_DIM) ** -0.5
IN_SPLITS = (SB_WIDTH, 2 * SB_WIDTH, 3 * SB_WIDTH, 3 * SB_WIDTH + Q_LORA,
             3 * SB_WIDTH + Q_LORA + KV_LORA, 3 * SB_WIDTH + Q_LORA + KV_LORA + ROPE_DIM)
IN_COLS = IN_SPLITS[-1] + 2 * D_MODEL

kernel_name = 'stickbreak_mla_macaron_deepnorm_stream_step'


def layer_norm(x, g, b):
    xf = x.astype(jnp.float32)
    mu = jnp.mean(xf, -1, keepdims=True)
    var = jnp.mean(jnp.square(xf - mu), -1, keepdims=True)
    y = (xf - mu) * lax.rsqrt(var + LN_EPS) * g.astype(jnp.float32) + b.astype(jnp.float32)
    return y.astype(x.dtype)


def rms_norm(x, g):
    xf = x.astype(jnp.float32)
    y = xf * lax.rsqrt(jnp.mean(jnp.square(xf), -1, keepdims=True) + RMS_EPS) * g.astype(jnp.float32)
    return y.astype(x.dtype)


def rope(x, pos):
    half = ROPE_DIM // 2
    inv_freq = ROPE_THETA ** (-jnp.arange(0, ROPE_DIM, 2, dtype=jnp.float32) / ROPE_DIM)
    ang = pos.astype(jnp.float32)[:, None] * inv_freq[None, :]
    ang = ang.reshape(ang.shape[0], *([1] * (x.ndim - 3)), half)
    cos, sin = jnp.cos(ang), jnp.sin(ang)
    xf = x.astype(jnp.float32)
    x1, x2 = xf[..., :half], xf[..., half:]
    return jnp.concatenate([x1 * cos - x2 * sin, x2 * cos + x1 * sin], -1).astype(x.dtype)


def swiglu(x, w_in, w_out):
    gate, up = jnp.split(x @ w_in, 2, axis=-1)
    return (jax.nn.silu(gate) * up) @ w_out


def stick_breaking(q, k, v, q_pos, k_pos):
    z = jnp.einsum('bqhd,bkhd->bhqk', q.astype(jnp.float32), k.astype(jnp.float32)) * SB_SCALE
    visible = k_pos[None, :] < q_pos[:, None]
    log_1m = jnp.where(visible, jax.nn.log_sigmoid(-z), 0.0)
    tail = lax.cumsum(log_1m, axis=3, reverse=True) - log_1m
    a = jnp.where(visible, jnp.exp(jax.nn.log_sigmoid(z) + tail), 0.0)
    return jnp.einsum('bhqk,bkhd->bqhd', a, v.astype(jnp.float32))


def mla_attend(q_nope, q_rope, k_nope, k_rope, v, q_pos, k_pos):
    s = (jnp.einsum('bqhd,bkhd->bhqk', q_nope.astype(jnp.float32), k_nope.astype(jnp.float32))
         + jnp.einsum('bqhr,bkr->bhqk', q_rope.astype(jnp.float32), k_rope.astype(jnp.float32))) * MLA_SCALE
    visible = (k_pos[None, :] // CHUNK) <= (q_pos[:, None] // CHUNK)
    s = jnp.where(visible, s, jnp.finfo(jnp.float32).min)
    p = jax.nn.softmax(s, axis=-1)
    return jnp.einsum('bhqk,bkhd->bqhd', p, v.astype(jnp.float32))


def map_query_blocks(attend, q_parts, q_pos):
    n_blocks = q_pos.shape[0] // Q_BLOCK

    def to_blocks(a):
        a = a.reshape(a.shape[0], n_blocks, Q_BLOCK, *a.shape[2:])
        return jnp.moveaxis(a, 1, 0)

    xs = tuple(to_blocks(a) for a in q_parts) + (q_pos.reshape(n_blocks, Q_BLOCK),)
    out = lax.map(lambda blk: attend(*blk), xs)
    out = jnp.moveaxis(out, 0, 1)
    return out.reshape(out.shape[0], n_blocks * Q_BLOCK, *out.shape[3:])


def token_mixing(h, pos, past, past_pos, p, blocked):
    past_k, past_v, past_ckv, past_kr = past
    B, T, _ = h.shape
    u = h @ p['w_in']
    q_sb, k_sb, v_sb, c_q, c_kv, k_r, gates = jnp.split(u, IN_SPLITS, axis=-1)
    q_sb = q_sb.reshape(B, T, SB_HEADS, SB_HEAD_DIM)
    k_sb = k_sb.reshape(B, T, SB_HEADS, SB_HEAD_DIM)
    v_sb = v_sb.reshape(B, T, SB_HEADS, SB_HEAD_DIM)
    q_mla = (rms_norm(c_q, p['g_cq']) @ p['w_uq']).reshape(B, T, MLA_HEADS, NOPE_DIM + ROPE_DIM)
    q_nope, q_rope = q_mla[..., :NOPE_DIM], rope(q_mla[..., NOPE_DIM:], pos)
    c_kv = rms_norm(c_kv, p['g_ckv'])
    k_r = rope(k_r, pos)
    k_pos = jnp.concatenate([past_pos, pos])
    k_all = jnp.concatenate([past_k, k_sb], axis=1)
    v_all = jnp.concatenate([past_v, v_sb], axis=1)
    ckv_all = jnp.concatenate([past_ckv, c_kv], axis=1)
    kr_all = jnp.concatenate([past_kr, k_r], axis=1)
    kv = (ckv_all @ p['w_ukv']).reshape(B, k_pos.shape[0], MLA_HEADS, NOPE_DIM + V_DIM)
    k_nope, v_mla = kv[..., :NOPE_DIM], kv[..., NOPE_DIM:]

    sb_fn = lambda q, qp: stick_breaking(q, k_all, v_all, qp, k_pos)
    mla_fn = lambda qn, qr, qp: mla_attend(qn, qr, k_nope, kr_all, v_mla, qp, k_pos)
    if blocked:
        o_sb = map_query_blocks(sb_fn, (q_sb,), pos)
        o_mla = map_query_blocks(mla_fn, (q_nope, q_rope), pos)
    else:
        o_sb = sb_fn(q_sb, pos)
        o_mla = mla_fn(q_nope, q_rope, pos)
    o_sb = o_sb.reshape(B, T, SB_WIDTH).astype(h.dtype)
    o_mla = o_mla.reshape(B, T, MLA_WIDTH).astype(h.dtype)
    g = jax.nn.sigmoid((gates + p['b_gate']).astype(jnp.float32))
    g_sb, g_mla = g[..., :D_MODEL], g[..., D_MODEL:]
    merged = g_sb * (o_sb @ p['w_br_sb']).astype(jnp.float32) + g_mla * (o_mla @ p['w_br_mla']).astype(jnp.float32)
    out = merged.astype(h.dtype) @ p['w_o']
    return out, (k_sb, v_sb, c_kv, k_r)


def encoder_layer(x, pos, past, past_pos, p, blocked):
    h = layer_norm(ALPHA * x + 0.5 * swiglu(x, p['ffn1_w_in'], p['ffn1_w_out']), p['ln1_g'], p['ln1_b'])
    mix, new_rows = token_mixing(h, pos, past, past_pos, p, blocked)
    h = layer_norm(ALPHA * h + mix, p['ln2_g'], p['ln2_b'])
    h = layer_norm(ALPHA * h + 0.5 * swiglu(h, p['ffn2_w_in'], p['ffn2_w_out']), p['ln3_g'], p['ln3_b'])
    return h, new_rows


def setup_inputs(seed: int = 0) -> dict:
    key = jax.random.key(seed)
    ks = jax.random.split(key, 32)
    f32 = jnp.float32
    L = DEPTH

    def nrm(k, shape, scale=1.0):
        return jax.random.normal(k, shape, f32) * scale

    return {
        'x_prompt': nrm(ks[0], (BATCH, SEQ, D_MODEL)),
        'x_sample': nrm(ks[1], (DEC_BATCH, DEC_SEQ, D_MODEL)),
        'cache_sb_k': nrm(ks[2], (L, DEC_BATCH, PAST_LEN, SB_HEADS, SB_HEAD_DIM)),
        'cache_sb_v': nrm(ks[3], (L, DEC_BATCH, PAST_LEN, SB_HEADS, SB_HEAD_DIM)),
        'cache_mla_ckv': nrm(ks[4], (L, DEC_BATCH, PAST_LEN, KV_LORA)),
        'cache_mla_krope': nrm(ks[5], (L, DEC_BATCH, PAST_LEN, ROPE_DIM)),
        'ffn1_w_in': nrm(ks[6], (L, D_MODEL, 2 * D_FF), D_MODEL ** -0.5),
        'ffn1_w_out': nrm(ks[7], (L, D_FF, D_MODEL), DN_BETA * D_FF ** -0.5),
        'ln1_g': 1.0 + nrm(ks[8], (L, D_MODEL), 0.02),
        'ln1_b': nrm(ks[9], (L, D_MODEL), 0.02),
        'w_in': nrm(ks[10], (L, D_MODEL, IN_COLS), D_MODEL ** -0.5),
        'b_gate': nrm(ks[11], (L, 2 * D_MODEL), 0.02),
        'g_cq': 1.0 + nrm(ks[12], (L, Q_LORA), 0.02),
        'w_uq': nrm(ks[13], (L, Q_LORA, MLA_HEADS * (NOPE_DIM + ROPE_DIM)), Q_LORA ** -0.5),
        'g_ckv': 1.0 + nrm(ks[14], (L, KV_LORA), 0.02),
        'w_ukv': nrm(ks[15], (L, KV_LORA, MLA_HEADS * (NOPE_DIM + V_DIM)), KV_LORA ** -0.5),
        'w_br_sb': nrm(ks[16], (L, SB_WIDTH, D_MODEL), DN_BETA * SB_WIDTH ** -0.5),
        'w_br_mla': nrm(ks[17], (L, MLA_WIDTH, D_MODEL), DN_BETA * MLA_WIDTH ** -0.5),
        'w_o': nrm(ks[18], (L, D_MODEL, D_MODEL), DN_BETA * D_MODEL ** -0.5),
        'ln2_g': 1.0 + nrm(ks[19], (L, D_MODEL), 0.02),
        'ln2_b': nrm(ks[20], (L, D_MODEL), 0.02),
        'ffn2_w_in': nrm(ks[21], (L, D_MODEL, 2 * D_FF), D_MODEL ** -0.5),
        'ffn2_w_out': nrm(ks[22], (L, D_FF, D_MODEL), DN_BETA * D_FF ** -0.5),
        'ln3_g': 1.0 + nrm(ks[23], (L, D_MODEL), 0.02),
        'ln3_b': nrm(ks[24], (L, D_MODEL), 0.02),
    }


def reference(x_prompt, x_sample, cache_sb_k, cache_sb_v, cache_mla_ckv, cache_mla_krope,
              ffn1_w_in, ffn1_w_out, ln1_g, ln1_b, w_in, b_gate, g_cq, w_uq, g_ckv, w_ukv,
              w_br_sb, w_br_mla, w_o, ln2_g, ln2_b, ffn2_w_in, ffn2_w_out, ln3_g, ln3_b):
    params = dict(ffn1_w_in=ffn1_w_in, ffn1_w_out=ffn1_w_out, ln1_g=ln1_g, ln1_b=ln1_b,
                  w_in=w_in, b_gate=b_gate, g_cq=g_cq, w_uq=w_uq, g_ckv=g_ckv, w_ukv=w_ukv,
                  w_br_sb=w_br_sb, w_br_mla=w_br_mla, w_o=w_o, ln2_g=ln2_g, ln2_b=ln2_b,
                  ffn2_w_in=ffn2_w_in, ffn2_w_out=ffn2_w_out, ln3_g=ln3_g, ln3_b=ln3_b)
    b_p, t_p = x_prompt.shape[0], x_prompt.shape[1]
    t_s = x_sample.shape[1]
    past_len = cache_sb_k.shape[2]
    pos_p = jnp.arange(t_p, dtype=jnp.int32)
    pos_s = past_len + jnp.arange(t_s, dtype=jnp.int32)
    past_pos_p = jnp.arange(0, dtype=jnp.int32)
    past_pos_s = jnp.arange(past_len, dtype=jnp.int32)
    dt = x_prompt.dtype
    no_past = (jnp.zeros((b_p, 0, SB_HEADS, SB_HEAD_DIM), dt), jnp.zeros((b_p, 0, SB_HEADS, SB_HEAD_DIM), dt),
               jnp.zeros((b_p, 0, KV_LORA), dt), jnp.zeros((b_p, 0, ROPE_DIM), dt))

    h_p, h_s = x_prompt, x_sample
    rows_p = ([], [], [], [])
    rows_s = ([], [], [], [])
    for layer in range(DEPTH):
        p = {name: w[layer] for name, w in params.items()}
        h_p, new_p = encoder_layer(h_p, pos_p, no_past, past_pos_p, p, True)
        past_s = (cache_sb_k[layer], cache_sb_v[layer], cache_mla_ckv[layer], cache_mla_krope[layer])
        h_s, new_s = encoder_layer(h_s, pos_s, past_s, past_pos_s, p, False)
        for acc, r in zip(rows_p, new_p):
            acc.append(r)
        for acc, r in zip(rows_s, new_s):
            acc.append(r)

    return (h_p, h_s,
            jnp.stack(rows_p[0]), jnp.stack(rows_p[1]), jnp.stack(rows_p[2]), jnp.stack(rows_p[3]),
            jnp.stack(rows_s[0]), jnp.stack(rows_s[1]), jnp.stack(rows_s[2]), jnp.stack(rows_s[3]))
```

```python
import numpy as np
import concourse.bass as bass
import concourse.mybir as mybir
from concourse.bass_utils import run_bass_kernel_spmd

F32 = mybir.dt.float32
BF16 = mybir.dt.bfloat16
AF = mybir.ActivationFunctionType
ALU = mybir.AluOpType

SEM_LIM = 30000
NEG = -30000.0

D = 2048
DC = 16
FF = 5504
FC = 43
NP_ = 1024
NS = 32
NT = NP_ + NS
TT = [(0, 512, 0), (512, 512, 1), (1024, 32, 2)]
TM = [(i * 128, 128) for i in range(8)] + [(1024, 32)]
H = 8
PAST = 1024
ALPHA = 2.0 ** 0.25
SB_SCALE = 128.0 ** -0.5
MLA_SCALE = 192.0 ** -0.5
LN_EPS = 1e-5
RMS_EPS = 1e-6
EX_ROWS = 4160
R_KT, R_V, R_NT, R_VM, R_RT = 0, 1024, 2048, 3072, 4096


class Buf:
    __slots__ = ("name", "lastw", "readers", "excl")

    def __init__(self, name, excl=False):
        self.name = name
        self.lastw = None
        self.readers = {}
        self.excl = excl


class Op:
    __slots__ = ("eng", "fn", "deps", "signal", "count", "key", "seq", "is_dma", "inc")

    def __init__(self, eng, fn, key, seq, is_dma, inc=None):
        self.eng = eng
        self.fn = fn
        self.key = key
        self.seq = seq
        self.is_dma = is_dma
        self.deps = {}
        self.signal = is_dma
        self.count = 0
        self.inc = inc if inc is not None else (16 if is_dma else 1)


class Prog:
    ENGS = ("pe", "act", "dve", "pool", "sp")

    def __init__(self, nc):
        self.nc = nc
        self.ops = {e: [] for e in self.ENGS}
        self.latest = {}
        self.pending = {e: {} for e in self.ENGS}
        self.seq = 0
        self.dma_counts = {}
        self.dma_inc = {}
        self.nbuf = 0

    def buf(self, name=None, excl=False):
        self.nbuf += 1
        return Buf(name or f"b{self.nbuf}", excl)

    def bufs(self, n, name="b"):
        return [self.buf(f"{name}{i}") for i in range(n)]

    def _add(self, eng, fn, reads, writes, semkey=None, inc=None, touch=()):
        is_dma = semkey is not None
        key = ("dma", semkey) if is_dma else eng
        self.seq += 1
        o = Op(eng, fn, key, self.seq, is_dma, inc)
        if is_dma:
            self.dma_inc[semkey] = o.inc
        deps = {}

        def add(d):
            if d is None:
                return
            if d.key == "pe" and eng == "pe" and not is_dma:
                return
            cur = deps.get(d.key)
            if cur is None or cur.seq < d.seq:
                deps[d.key] = d

        for b in reads:
            add(b.lastw)
            if b.excl:
                for r in b.readers.values():
                    if r.key != key:
                        add(r)
        for b in writes:
            add(b.lastw)
            for r in b.readers.values():
                add(r)
        for d in self.pending[eng].values():
            add(d)
        self.pending[eng] = {}
        o.deps = deps
        for b in reads:
            cur = b.readers.get(key)
            if cur is None or cur.seq < o.seq:
                b.readers[key] = o
        for b in writes:
            b.lastw = o
            b.readers = {}
        for b in touch:
            b.lastw = o
            b.readers = {}
        if is_dma:
            c = self.dma_counts.get(semkey, 0) + 1
            self.dma_counts[semkey] = c
            o.count = c
        self.ops[eng].append(o)
        self.latest[key] = o
        return o

    def op(self, eng, fn, reads=(), writes=()):
        return self._add(eng, fn, reads, writes)

    def dma(self, queue, out, in_, semkey, reads=(), writes=(), **kw):
        def fn(e):
            return e.dma_start(out=out, in_=in_, **kw)
        return self._add(queue, fn, reads, writes, semkey=semkey)

    def dma_batch(self, queue, pairs, semkey, reads=(), writes=()):
        n = len(pairs)
        for j, (out, in_) in enumerate(pairs):
            def fn(e, out=out, in_=in_):
                return e.dma_start(out=out, in_=in_)
            if n == 1:
                self._add(queue, fn, reads, writes, semkey=semkey)
            elif j == 0:
                self._add(queue, fn, reads, writes, semkey=semkey)
            elif j == n - 1:
                self._add(queue, fn, (), (), semkey=semkey, touch=writes)
            else:
                self._add(queue, fn, (), (), semkey=semkey)

    def barrier(self):
        snap = {k: v for k, v in self.latest.items()
                if not (isinstance(k, tuple) and str(k[1]).startswith("cc"))}
        for e in self.ENGS:
            self.pending[e] = dict(snap)

    def emit(self):
        nc = self.nc
        for e in self.ENGS:
            for o in self.ops[e]:
                for d in o.deps.values():
                    d.signal = True
        tot = {}
        for e in self.ENGS:
            c = 0
            for o in self.ops[e]:
                if not o.is_dma and o.signal:
                    c += 1
                    o.count = c
            tot[e] = c
        sems = {}

        def nsem(units):
            return max(1, (units + SEM_LIM - 1) // SEM_LIM)

        for e in self.ENGS:
            sems[e] = [nc.alloc_semaphore(f"s_{e}_{i}") for i in range(nsem(tot[e]))]
        for k, c in self.dma_counts.items():
            sems[("dma", k)] = [nc.alloc_semaphore(f"sd_{k}_{i}") for i in range(nsem(c * self.dma_inc[k]))]

        def target(o):
            units = o.count * o.inc
            idx = (units - 1) // SEM_LIM
            val = (units - 1) % SEM_LIM + 1
            return sems[o.key][idx], idx, val

        handles = {"pe": "tensor", "act": "scalar", "dve": "vector", "pool": "gpsimd", "sp": "sync"}
        final_waits = [o for k, o in self.latest.items() if o.is_dma]
        with nc.Block() as block:
            for e in self.ENGS:
                ops = self.ops[e]
                extra = final_waits if e == "sp" else []

                def body(eng, ops=ops, extra=extra):
                    waited = {}

                    def wait_for(d):
                        sem, idx, val = target(d)
                        wk = (d.key, idx)
                        if waited.get(wk, 0) < val:
                            eng.wait_ge(sem, val)
                            waited[wk] = val

                    for o in ops:
                        for d in o.deps.values():
                            wait_for(d)
                        ins = o.fn(eng)
                        if o.signal:
                            sem, idx, val = target(o)
                            ins.then_inc(sem, o.inc)
                    for d in extra:
                        wait_for(d)

                getattr(block, handles[e])(body)


class Arena:
    def __init__(self, nc):
        self.nc = nc
        b0 = nc.sbuf_base
        n = (nc.sbuf_top - b0 - 3072) // 4
        self.slab = nc.alloc_sbuf_tensor("slab", [128, n], F32)
        self.base = (b0 + 63) // 64 * 64
        self.top = (b0 + n * 4) // 64 * 64
        self.cur = self.base
        self.lim = self.top
        self.n = 0

    def size(self, shape, dtype):
        per = 4 if dtype == F32 else 2
        for s in shape[1:]:
            per *= s
        return (per + 63) // 64 * 64

    def alloc(self, shape, dtype):
        per = self.size(shape, dtype)
        off = self.cur
        self.cur += per
        assert self.cur <= self.lim, f"SBUF overflow {self.cur} > {self.lim}"
        self.n += 1
        return self.nc.alloc_sbuf_tensor_at(f"sb{self.n}", list(shape), dtype, offset=off)

    def region(self, lo, hi):
        self.cur = (lo + 63) // 64 * 64
        self.lim = hi


class Builder:
    IN_SHAPES = {
        "x": [NT, D], "w1a": [D, 2 * FF], "w2a": [FF, D], "w1b": [D, 2 * FF], "w2b": [FF, D],
        "w_in": [D, 8256], "w_uq": [512, 1536], "w_ukv": [512, 2048], "w_br_sb": [1024, D],
        "w_br_mla": [1024, D], "w_o": [D, D], "lnp": [128, 96], "bgate": [128, 32], "gcq": [128, 512],
        "gckv": [128, 512], "cos_tm": [128, 9, 32], "sin_tm": [128, 9, 32], "cos_fm": [64, NT],
        "sin_fm": [64, NT], "ident": [128, 128], "cmats": [128, 4, 128], "mb_sb": [128, 16, 512],
        "mb_mla": [128, 16, 512], "mb_new": [16, 128], "c_k": [2, PAST, 1024], "c_v": [2, PAST, 1024],
        "c_ckv": [2, PAST, 512], "c_kr": [2, PAST, 64],
    }

    class _Lazy(dict):
        def __init__(self, b):
            super().__init__()
            self.b = b

        def __missing__(self, k):
            v = self.b.din(k, Builder.IN_SHAPES[k])
            self[k] = v
            return v

    def __init__(self, stop_after=None):
        self.stop_after = stop_after
        nc = bass.Bass("TRN2", target_bir_lowering=False)
        self.nc = nc
        self.P = Prog(nc)
        self.A = Arena(nc)
        self.i = Builder._Lazy(self)
        self.o = {}
        self.build()
        self.P.emit()

    def din(self, name, shape, dtype=F32):
        return self.nc.dram_tensor(name, list(shape), dtype, kind="ExternalInput").ap()

    def dout(self, name, shape, dtype=F32):
        return self.nc.dram_tensor(name, list(shape), dtype, kind="ExternalOutput").ap()

    def mm(self, out, lhsT, rhs, start, stop, reads, writes):
        self.P.op("pe", lambda e: e.matmul(out, lhsT, rhs, start=start, stop=stop, skip_group_check=True),
                  reads, writes)

    def tr(self, out, in_, ident, reads, writes):
        self.P.op("pe", lambda e: e.transpose(out, in_, ident), reads, writes)

    def act(self, out, in_, func, reads, writes, **kw):
        self.P.op("act", lambda e: e.activation(out, in_, func, **kw), reads, writes)

    def tt(self, out, in0, in1, op, reads, writes, eng="dve"):
        self.P.op(eng, lambda e: e.tensor_tensor(out, in0, in1, op), reads, writes)

    def ts(self, out, in0, s1, s2, op0, op1, reads, writes, eng="dve"):
        if op1 is None:
            self.P.op(eng, lambda e: e.tensor_scalar(out, in0, s1, None, op0), reads, writes)
        else:
            self.P.op(eng, lambda e: e.tensor_scalar(out, in0, s1, s2, op0, op1), reads, writes)

    def stt(self, out, in0, scalar, in1, op0, op1, reads, writes):
        self.P.op("dve", lambda e: e.scalar_tensor_tensor(out, in0, scalar, in1, op0, op1), reads, writes)

    def cp(self, out, in_, reads, writes, eng="dve"):
        self.P.op(eng, lambda e: e.tensor_copy(out, in_), reads, writes)

    def memset(self, ap, val, writes, eng="dve"):
        self.P.op(eng, lambda e: e.memset(ap, val), [], writes)

    def PS(self, b):
        return self.psum[:, b * 512:(b + 1) * 512]

    def exi(self, row0, nrows):
        j = row0 // 512
        a = row0 - j * 512
        return self.ex_in[j].ap()[a:a + nrows, :], j

    def exo(self, r, row0, nrows):
        j = row0 // 512
        a = r * self.ex_rows[j] + row0 - j * 512
        return self.ex_out[j].ap()[a:a + nrows, :], j

    def build(self):
        nc, P, A, i = self.nc, self.P, self.A, self.i
        st = self.stop_after
        self.psum = nc.alloc_psum_tensor("ps", [128, 4096], F32)
        self.bk = [P.buf(f"bank{b}", True) for b in range(8)]

        self.ident = A.alloc([128, 128], F32)
        self.identb = A.alloc([128, 128], BF16)
        self.cm = A.alloc([128, 4, 128], BF16)
        self.lnp = A.alloc([128, 96], F32)
        self.lnpa = A.alloc([128, 96], F32)
        self.bg = A.alloc([128, 32], F32)
        self.epsc = A.alloc([128, 4], F32)
        self.ones32 = A.alloc([128, 128], F32)
        self.bconst = P.buf("consts")
        P.dma("sp", self.ident[:], i["ident"], "c0", writes=[self.bconst])
        P.dma("pool", self.cm[:], i["cmats"], "c1", writes=[self.bconst])
        P.dma("pool", self.identb[:], i["ident"], "c4", writes=[self.bconst])
        P.dma("sp", self.lnp[:], i["lnp"], "c2", writes=[self.bconst])
        P.dma("sp", self.bg[:], i["bgate"], "c3", writes=[self.bconst])
        self.ts(self.lnpa[:], self.lnp[:], ALPHA, None, ALU.mult, None, [self.bconst], [self.bconst])
        self.memset(self.epsc[:, 0:1], LN_EPS, [self.bconst])
        self.memset(self.epsc[:, 1:2], RMS_EPS, [self.bconst])
        self.memset(self.epsc[:, 2:3], 1.0, [self.bconst])
        self.memset(self.ones32[:, :], 1.0, [self.bconst])
        self.actb = A.alloc([128, DC, NT], BF16)
        self.b_act = P.bufs(DC, "act")
        self.R2 = A.cur
        self.resid = A.alloc([128, DC, NT], F32)
        self.b_res = P.bufs(DC, "res")
        self.R3 = A.cur
        self.TOP = A.top
        assert self.TOP - self.R3 >= 86000, (self.TOP, self.R3)

        if st is None:
            o = self.o
            o["y"] = self.dout("y", [NT, D])
            o["nk"] = self.dout("nk", [NT, 1024])
            o["nv"] = self.dout("nv", [NT, 1024])
            o["nckv"] = self.dout("nckv", [NT, 512])
            o["nkr"] = self.dout("nkr", [NT, 64])
            self.ex_rows = [512] * 8 + [64]
            self.ex_in = [nc.dram_tensor(f"ex_in{j}", [n, NP_], BF16) for j, n in enumerate(self.ex_rows)]
            self.ex_out = [nc.dram_tensor(f"ex_out{j}", [4 * n, NP_], BF16) for j, n in enumerate(self.ex_rows)]
            self.b_exc = P.bufs(9, "exc")
            self.b_exo = P.bufs(9, "exo")
            self.spill = nc.dram_tensor("spill", [128, DC, NT], F32)

        self.ffn_prefetch(i["w1a"], i["w2a"])
        self.phase_load_x()
        if st == "x":
            return self.dump_act()
        self.phase_ffn()
        if st == "ffn1":
            return self.dump_act()
        if st is None:
            self.proj_prefetch()
        self.phase_ln(0, final=False)
        if st == "ln1":
            return self.dump_act()
        P.dma("sp", self.spill.ap(), self.resid[:], "spill", reads=self.b_res)
        P.barrier()
        self.phase_proj()
        self.phase_sample_attn()
        self.phase_prompt_attn()
        self.phase_merge()
        self.ffn_prefetch(i["w1b"], i["w2b"])
        self.phase_ln(1, final=False)
        self.phase_ffn()
        self.phase_ln(2, final=True)
        self.phase_out()

    def phase_load_x(self):
        P, A, i = self.P, self.A, self.i
        A.region(self.TOP - 19072, self.TOP)
        xs = [A.alloc([128, D], F32) for _ in range(2)]
        bx = P.bufs(2, "xs")
        k = 0
        for ti, (t0, m) in enumerate(TM):
            s = ti % 2
            P.dma("sp", xs[s][:m, :], i["x"][t0:t0 + m, :], f"x{s}", writes=[bx[s]])
            for q in range(4):
                b = 4 + k % 4
                k += 1
                ps = self.PS(b)
                for j in range(4):
                    c = q * 4 + j
                    self.tr(ps[:, j * 128:j * 128 + m], xs[s][:m, c * 128:(c + 1) * 128], self.ident[:m, :m],
                            [bx[s], self.bconst], [self.bk[b]])
                src = ps.rearrange("p (j t) -> p j t", j=4)[:, :, :m]
                self.act(self.resid[:, q * 4:q * 4 + 4, t0:t0 + m], src, AF.Identity, [self.bk[b]],
                         self.b_res[q * 4:q * 4 + 4], scale=ALPHA)
                self.cp(self.actb[:, q * 4:q * 4 + 4, t0:t0 + m], src, [self.bk[b]], self.b_act[q * 4:q * 4 + 4])
        P.barrier()

    def ffn_prefetch(self, w1, w2):
        P, A = self.P, self.A
        A.region(self.R3, self.TOP - 19072)
        NPAIR, NG = 22, 11
        w1g = [A.alloc([128, DC, 256], BF16) for _ in range(2)]
        w1u = [A.alloc([128, DC, 256], BF16) for _ in range(2)]
        self.ffn_free_lo = A.cur
        aT = [A.alloc([128, 4, NT], BF16) for _ in range(2)]
        sg = A.alloc([128, NT], BF16)
        self.ffn_free_hi = A.cur
        w2s = [A.alloc([128, 4, D], BF16) for _ in range(2)]
        b_w1, b_aT, b_w2 = P.bufs(2, "w1"), P.bufs(2, "aT"), P.bufs(2, "w2")
        b_sg = P.bufs(3, "sg")

        def load_w1(p):
            s = p % 2
            w = (2 if p < 21 else 1) * 128
            c0 = p * 256
            P.dma("pool", w1g[s][:, :, :w], w1[:, c0:c0 + w].rearrange("(c p) n -> p c n", p=128),
                  f"w1g{s}", writes=[b_w1[s]])
            P.dma("pool", w1u[s][:, :, :w], w1[:, FF + c0:FF + c0 + w].rearrange("(c p) n -> p c n", p=128),
                  f"w1u{s}", writes=[b_w1[s]])

        def load_w2(g):
            s = g % 2
            nch = 4 if g < 10 else 3
            r0 = g * 512
            P.dma("pool", w2s[s][:, :nch, :], w2[r0:r0 + nch * 128, :].rearrange("(j p) n -> p j n", p=128),
                  f"w2{s}", writes=[b_w2[s]])

        load_w1(0)
        load_w1(1)
        load_w2(0)
        load_w2(1)
        self.ffn_state = (w1g, w1u, aT, w2s, sg, b_w1, b_aT, b_w2, b_sg, load_w1, load_w2)

    def phase_ffn(self):
        P = self.P
        NPAIR, NG = 22, 11
        (w1g, w1u, aT, w2s, sg, b_w1, b_aT, b_w2, b_sg, load_w1, load_w2) = self.ffn_state
        for g in range(NG):
            gs = g % 2
            nch_g = 4 if g < 10 else 3
            for pp in range(2):
                p = g * 2 + pp
                if p >= NPAIR:
                    continue
                s = p % 2
                nch = 2 if p < 21 else 1
                for jj in range(nch):
                    ja = pp * 2 + jj
                    for (b0, wt) in ((0, w1g[s]), (3, w1u[s])):
                        for c in range(DC):
                            for (t0, n, bk) in TT:
                                self.mm(self.PS(b0 + bk)[:, :n], wt[:, c, jj * 128:(jj + 1) * 128],
                                        self.actb[:, c, t0:t0 + n], c == 0, c == DC - 1,
                                        [b_w1[s], self.b_act[c]], [self.bk[b0 + bk]])
                    for (t0, n, bk) in TT:
                        self.act(sg[:, t0:t0 + n], self.PS(bk)[:, :n], AF.Silu, [self.bk[bk]], [b_sg[bk]])
                    for (t0, n, bk) in TT:
                        self.tt(aT[gs][:, ja, t0:t0 + n], sg[:, t0:t0 + n], self.PS(3 + bk)[:, :n],
                                ALU.mult, [b_sg[bk], self.bk[3 + bk]], [b_aT[gs]])
                if p + 2 < NPAIR:
                    load_w1(p + 2)
            for ci in range(DC):
                b0 = 0 if ci % 2 == 0 else 3
                for jj in range(nch_g):
                    for (t0, n, bk) in TT:
                        self.mm(self.PS(b0 + bk)[:, :n], w2s[gs][:, jj, ci * 128:(ci + 1) * 128],
                                aT[gs][:, jj, t0:t0 + n], jj == 0, jj == nch_g - 1,
                                [b_w2[gs], b_aT[gs]], [self.bk[b0 + bk]])
                for (t0, n, bk) in TT:
                    self.stt(self.resid[:, ci, t0:t0 + n], self.PS(b0 + bk)[:, :n], 0.5,
                             self.resid[:, ci, t0:t0 + n], ALU.mult, ALU.add,
                             [self.bk[b0 + bk], self.b_res[ci]], [self.b_res[ci]])
            if g + 2 < NG:
                load_w2(g + 2)
        P.barrier()

    def phase_ln(self, idx, final):
        P, A = self.P, self.A
        A.region(self.ffn_free_lo, self.ffn_free_hi)
        xb = [A.alloc([128, NT], BF16) for _ in range(2)]
        xq = [A.alloc([128, NT], BF16) for _ in range(2)]
        mt = A.alloc([128, NT], F32)
        rs = A.alloc([128, NT], F32)
        A.region(self.TOP - 19072, self.TOP)
        tmp = [A.alloc([128, NT], F32) for _ in range(2)]
        b_xb, b_xq = P.bufs(2, "xb"), P.bufs(2, "xq")
        b_mt, b_rs = P.buf("mt"), P.buf("rs")
        b_tmp = P.bufs(2, "tmp")
        ones = self.cm[:, 2, :]
        for c in range(DC):
            s = c % 2
            self.act(xb[s][:], self.resid[:, c, :], AF.Copy, [self.b_res[c]], [b_xb[s]])
            self.tt(xq[s][:], self.resid[:, c, :], self.resid[:, c, :], ALU.mult, [self.b_res[c]], [b_xq[s]])
            for (t0, n, bk) in TT:
                self.mm(self.PS(bk)[:, :n], ones, xb[s][:, t0:t0 + n], c == 0, c == DC - 1,
                        [b_xb[s], self.bconst], [self.bk[bk]])
            for (t0, n, bk) in TT:
                self.mm(self.PS(3 + bk)[:, :n], ones, xq[s][:, t0:t0 + n], c == 0, c == DC - 1,
                        [b_xq[s], self.bconst], [self.bk[3 + bk]])
        for (t0, n, bk) in TT:
            sl = slice(t0, t0 + n)
            pa = self.PS(bk)[:, :n]
            pb = self.PS(3 + bk)[:, :n]
            self.ts(mt[:, sl], pa, 1.0 / D, None, ALU.mult, None, [self.bk[bk]], [b_mt])
            self.tt(tmp[0][:, sl], mt[:, sl], mt[:, sl], ALU.mult, [b_mt], [b_tmp[0]])
            self.stt(rs[:, sl], pb, 1.0 / D, tmp[0][:, sl], ALU.mult, ALU.subtract,
                     [self.bk[3 + bk], b_tmp[0]], [b_rs])
            self.act(rs[:, sl], rs[:, sl], AF.Ln, [b_rs, self.bconst], [b_rs], bias=self.epsc[:, 0:1])
            self.act(rs[:, sl], rs[:, sl], AF.Exp, [b_rs], [b_rs], scale=-0.5)
            self.stt(mt[:, sl], mt[:, sl], -1.0, rs[:, sl], ALU.mult, ALU.mult, [b_mt, b_rs], [b_mt])
        g = self.lnp[:, idx * 32:idx * 32 + 16]
        b = self.lnp[:, idx * 32 + 16:idx * 32 + 32]
        ga = self.lnpa[:, idx * 32:idx * 32 + 16]
        ba = self.lnpa[:, idx * 32 + 16:idx * 32 + 32]
        for c in range(DC):
            s = c % 2
            self.tt(tmp[s][:], self.resid[:, c, :], rs[:], ALU.mult, [self.b_res[c], b_rs], [b_tmp[s]])
            self.tt(tmp[s][:], tmp[s][:], mt[:], ALU.add, [b_tmp[s], b_mt], [b_tmp[s]])
            self.act(self.actb[:, c, :], tmp[s][:], AF.Identity, [b_tmp[s], self.bconst], [self.b_act[c]],
                     scale=g[:, c:c + 1], bias=b[:, c:c + 1])
            sc, bi = (g, b) if final else (ga, ba)
            self.act(self.resid[:, c, :], tmp[s][:], AF.Identity, [b_tmp[s], self.bconst], [self.b_res[c]],
                     scale=sc[:, c:c + 1], bias=bi[:, c:c + 1])
        P.barrier()

    PROJ_BLOCKS = [("k", 1024, 512, 0), ("k", 1536, 512, 1), ("v", 2048, 512, 0), ("v", 2560, 512, 1),
                   ("cq", 3072, 512, 0), ("ckv", 3584, 512, 0), ("kr", 4096, 64, 0)]

    def proj_prefetch(self):
        P, A, i = self.P, self.A, self.i
        A.region(self.R3, self.R3 + 32768)
        self.wblk = [A.alloc([128, DC, 512], BF16) for _ in range(2)]
        self.b_wblk = P.bufs(2, "wblk")
        self.load_blk(0)
        self.load_blk(1)

    def load_blk(self, bi):
        kind, c0, nc_, _ = self.PROJ_BLOCKS[bi]
        s = bi % 2
        self.P.dma("pool", self.wblk[s][:, :, :nc_],
                   self.i["w_in"][:, c0:c0 + nc_].rearrange("(c p) n -> p c n", p=128),
                   f"wblk{s}", writes=[self.b_wblk[s]])

    def phase_proj(self):
        P, A, i, o = self.P, self.A, self.i, self.o
        w_in = i["w_in"]
        A.region(self.TOP - 34000, self.TOP)
        self.cqnT = A.alloc([128, 4, NT], BF16)
        self.KTs = A.alloc([128, 8, NS], BF16)
        self.NTs = A.alloc([128, 8, NS], BF16)
        self.krT = A.alloc([64, NT], BF16)
        self.Vs_new = A.alloc([32, 1024], BF16)
        self.VMs_new = A.alloc([32, 1024], BF16)
        self.wukv = A.alloc([128, 4, 2048], BF16)
        self.b_cqnT, self.b_KTs, self.b_NTs = P.buf("cqnT"), P.buf("KTs"), P.buf("NTs")
        self.b_krT, self.b_Vsn, self.b_VMsn, self.b_wukv = P.buf("krT"), P.buf("Vsn"), P.buf("VMsn"), P.buf("wukv")
        P.dma("pool", self.wukv[:], i["w_ukv"].rearrange("(c p) n -> p c n", p=128), "wukv", writes=[self.b_wukv])
        A.region(self.R2, self.TOP - 34000)
        wblk = self.wblk
        stg32 = [A.alloc([128, 512], F32) for _ in range(2)]
        stgb = [A.alloc([128, 512], BF16) for _ in range(2)]
        nrm = [A.alloc([128, 512], F32) for _ in range(2)]
        junk = A.alloc([128, 512], F32)
        ckvnT = A.alloc([128, 4, NT], BF16)
        KTl = [A.alloc([128, NT], BF16) for _ in range(2)]
        gcq = A.alloc([128, 512], F32)
        gckv = A.alloc([128, 512], F32)
        cos = A.alloc([128, 9, 32], F32)
        sin = A.alloc([128, 9, 32], F32)
        kro = [A.alloc([128, 64], F32) for _ in range(2)]
        rt = [A.alloc([128, 32], F32) for _ in range(4)]
        ssq = A.alloc([128, 4], F32)
        b_wblk = self.b_wblk
        b_stg32, b_stgb, b_nrm = P.bufs(2, "stg32"), P.bufs(2, "stgb"), P.bufs(2, "nrm")
        b_junk, b_ckvnT, b_KTl = P.buf("junk"), P.buf("ckvnT"), P.bufs(2, "KTl")
        b_tab, b_kro, b_rt, b_ssq = P.buf("tab"), P.bufs(2, "kro"), P.buf("rt"), P.buf("ssq")
        P.dma("sp", gcq[:], i["gcq"], "t0", writes=[b_tab])
        P.dma("sp", gckv[:], i["gckv"], "t1", writes=[b_tab])
        P.dma("sp", cos[:], i["cos_tm"], "t2", writes=[b_tab])
        P.dma("sp", sin[:], i["sin_tm"], "t3", writes=[b_tab])
        blocks = self.PROJ_BLOCKS
        cnt = {"s32": 0, "sb": 0, "nrm": 0, "bank": 0, "ktl": 0, "kro": 0}

        load_blk = self.load_blk

        def rmsnorm(ps, m, bkb, gtile):
            self.act(junk[:m, :], ps[:m, :], AF.Square, [bkb], [b_junk])
            self.P.op("dve", lambda e: e.reduce_sum(ssq[:m, 0:1], junk[:m, :], mybir.AxisListType.X),
                      [b_junk], [b_ssq])
            self.act(ssq[:m, 1:2], ssq[:m, 0:1], AF.Ln, [b_ssq, self.bconst], [b_ssq], scale=1.0 / 512,
                     bias=self.epsc[:m, 1:2])
            self.act(ssq[:m, 2:3], ssq[:m, 1:2], AF.Exp, [b_ssq], [b_ssq], scale=-0.5)
            s = cnt["nrm"] % 2
            cnt["nrm"] += 1
            self.stt(nrm[s][:m, :], ps[:m, :], ssq[:m, 2:3], gtile[:m, :], ALU.mult, ALU.mult,
                     [bkb, b_ssq, b_tab], [b_nrm[s]])
            return s

        def transposes_to(dst, b_dst, src, b_src, m, t0, nchunk, bank):
            ps = self.PS(bank)
            for k in range(nchunk):
                self.tr(ps[:, k * 128:k * 128 + m], src[:m, k * 128:(k + 1) * 128], self.ident[:m, :m],
                        [b_src, self.bconst], [self.bk[bank]])
            srcv = ps.rearrange("p (j t) -> p j t", j=4)[:, :nchunk, :m]
            self.cp(dst[:, 0:nchunk, t0:t0 + m], srcv, [self.bk[bank]], [b_dst])

        for bi, (kind, c0, nc_, half) in enumerate(blocks):
            s = bi % 2
            for ti, (t0, m) in enumerate(TM):
                bank = 6 + cnt["bank"] % 2
                cnt["bank"] += 1
                ps = self.PS(bank)
                bkb = self.bk[bank]
                for c in range(DC):
                    self.mm(ps[:m, :nc_], self.actb[:, c, t0:t0 + m], wblk[s][:, c, :nc_], c == 0, c == DC - 1,
                            [self.b_act[c], b_wblk[s]], [bkb])
                if kind in ("k", "v"):
                    s2 = cnt["s32"] % 2
                    cnt["s32"] += 1
                    self.act(stg32[s2][:m, :], ps[:m, :], AF.Identity, [bkb], [b_stg32[s2]])
                    dst = o["nk"] if kind == "k" else o["nv"]
                    P.dma("sp", dst[t0:t0 + m, half * 512:(half + 1) * 512], stg32[s2][:m, :], f"o32_{s2}",
                          reads=[b_stg32[s2]])
                    if kind == "v":
                        if m == 128:
                            s3 = cnt["sb"] % 2
                            cnt["sb"] += 1
                            self.cp(stgb[s3][:m, :], ps[:m, :], [bkb], [b_stgb[s3]])
                            ea, ej = self.exi(R_V + t0, m)
                            P.dma("sp", ea[:, half * 512:(half + 1) * 512], stgb[s3][:m, :],
                                  f"ob_{s3}", reads=[b_stgb[s3], self.b_exc[ej]])
                        else:
                            self.cp(self.Vs_new[:m, half * 512:(half + 1) * 512], ps[:m, :], [bkb], [self.b_Vsn])
                elif kind == "cq":
                    sn = rmsnorm(ps, m, bkb, gcq)
                    bank2 = 6 + cnt["bank"] % 2
                    cnt["bank"] += 1
                    transposes_to(self.cqnT, self.b_cqnT, nrm[sn], b_nrm[sn], m, t0, 4, bank2)
                elif kind == "ckv":
                    sn = rmsnorm(ps, m, bkb, gckv)
                    P.dma("sp", o["nckv"][t0:t0 + m, :], nrm[sn][:m, :], f"on_{sn}", reads=[b_nrm[sn]])
                    bank2 = 6 + cnt["bank"] % 2
                    cnt["bank"] += 1
                    transposes_to(ckvnT, b_ckvnT, nrm[sn], b_nrm[sn], m, t0, 4, bank2)
                else:
                    sk = cnt["kro"] % 2
                    cnt["kro"] += 1
                    x1, x2 = ps[:m, 0:32], ps[:m, 32:64]
                    cs, sn_ = cos[:m, ti, :], sin[:m, ti, :]
                    self.tt(rt[0][:m, :], x1, cs, ALU.mult, [bkb, b_tab], [b_rt])
                    self.tt(rt[1][:m, :], x2, sn_, ALU.mult, [bkb, b_tab], [b_rt])
                    self.tt(rt[2][:m, :], x2, cs, ALU.mult, [bkb, b_tab], [b_rt])
                    self.tt(rt[3][:m, :], x1, sn_, ALU.mult, [bkb, b_tab], [b_rt])
                    self.tt(kro[sk][:m, 0:32], rt[0][:m, :], rt[1][:m, :], ALU.subtract, [b_rt], [b_kro[sk]])
                    self.tt(kro[sk][:m, 32:64], rt[2][:m, :], rt[3][:m, :], ALU.add, [b_rt], [b_kro[sk]])
                    P.dma("sp", o["nkr"][t0:t0 + m, :], kro[sk][:m, :], f"okr_{sk}", reads=[b_kro[sk]])
                    bank2 = 6 + cnt["bank"] % 2
                    cnt["bank"] += 1
                    ps2 = self.PS(bank2)
                    self.tr(ps2[:64, :m], kro[sk][:m, 0:64], self.ident[:m, :m], [b_kro[sk], self.bconst],
                            [self.bk[bank2]])
                    self.cp(self.krT[:, t0:t0 + m], ps2[:64, :m], [self.bk[bank2]], [self.b_krT])
            if kind == "k":
                for hh in range(4):
                    h = half * 4 + hh
                    b0 = 0 if hh % 2 == 0 else 3
                    sl_ = cnt["ktl"] % 2
                    cnt["ktl"] += 1
                    for c in range(DC):
                        for (t0, n, bk) in TT:
                            self.mm(self.PS(b0 + bk)[:, :n], wblk[s][:, c, hh * 128:(hh + 1) * 128],
                                    self.actb[:, c, t0:t0 + n], c == 0, c == DC - 1,
                                    [b_wblk[s], self.b_act[c]], [self.bk[b0 + bk]])
                    for (t0, n, bk) in TT[:2]:
                        self.act(KTl[sl_][:, t0:t0 + n], self.PS(b0 + bk)[:, :n], AF.Identity,
                                 [self.bk[b0 + bk]], [b_KTl[sl_]])
                    self.cp(self.KTs[:, h, :], self.PS(b0 + 2)[:, :NS], [self.bk[b0 + 2]], [self.b_KTs])
                    ea, ej = self.exi(R_KT + h * 128, 128)
                    P.dma("sp", ea, KTl[sl_][:, 0:NP_], f"okt_{sl_}", reads=[b_KTl[sl_], self.b_exc[ej]])
            if bi + 2 < len(blocks):
                load_blk(bi + 2)
            if bi == 3:
                self.issue_cc([0, 1, 2, 3])
        ea, ej = self.exi(R_RT, 64)
        P.dma("sp", ea, self.krT[:, 0:NP_], "okrt", reads=[self.b_krT, self.b_exc[ej]])
        for h in range(H):
            b0 = 0 if h % 2 == 0 else 3
            sl_ = cnt["ktl"] % 2
            cnt["ktl"] += 1
            for kc in range(4):
                for (t0, n, bk) in TT:
                    self.mm(self.PS(b0 + bk)[:, :n], self.wukv[:, kc, h * 256:h * 256 + 128],
                            ckvnT[:, kc, t0:t0 + n], kc == 0, kc == 3, [self.b_wukv, b_ckvnT],
                            [self.bk[b0 + bk]])
            for (t0, n, bk) in TT[:2]:
                self.act(KTl[sl_][:, t0:t0 + n], self.PS(b0 + bk)[:, :n], AF.Identity, [self.bk[b0 + bk]],
                         [b_KTl[sl_]])
            self.cp(self.NTs[:, h, :], self.PS(b0 + 2)[:, :NS], [self.bk[b0 + 2]], [self.b_NTs])
            ea, ej = self.exi(R_NT + h * 128, 128)
            P.dma("sp", ea, KTl[sl_][:, 0:NP_], f"okt_{sl_}", reads=[b_KTl[sl_], self.b_exc[ej]])
        for ti, (t0, m) in enumerate(TM):
            for half in range(2):
                bank = 6 + cnt["bank"] % 2
                cnt["bank"] += 1
                ps = self.PS(bank)
                for hh in range(4):
                    hc = (half * 4 + hh) * 256 + 128
                    for kc in range(4):
                        self.mm(ps[:m, hh * 128:(hh + 1) * 128], ckvnT[:, kc, t0:t0 + m],
                                self.wukv[:, kc, hc:hc + 128], kc == 0, kc == 3,
                                [b_ckvnT, self.b_wukv], [self.bk[bank]])
                if m == 128:
                    s3 = cnt["sb"] % 2
                    cnt["sb"] += 1
                    self.cp(stgb[s3][:m, :], ps[:m, :], [self.bk[bank]], [b_stgb[s3]])
                    ea, ej = self.exi(R_VM + t0, m)
                    P.dma("sp", ea[:, half * 512:(half + 1) * 512], stgb[s3][:m, :],
                          f"ob_{s3}", reads=[b_stgb[s3], self.b_exc[ej]])
                else:
                    self.cp(self.VMs_new[:m, half * 512:(half + 1) * 512], ps[:m, :], [self.bk[bank]],
                            [self.b_VMsn])
        P.barrier()

    def issue_cc(self, js):
        P = self.P
        for j in js:
            def fn(e, j=j):
                return e.collective_compute("AllGather", ALU.bypass, replica_groups=[[0, 1, 2, 3], [4, 5, 6, 7]],
                                            ins=[self.ex_in[j].ap().opt()], outs=[self.ex_out[j].ap().opt()])
            P._add("pool", fn, [], [self.b_exc[j], self.b_exo[j]], semkey=f"cc{j}", inc=1)

    def attn_setup(self, alt=None):
        P, A = self.P, self.A
        w = {}
        w["e32"] = [A.alloc([128, 512], F32) for _ in range(2)]
        w["sp"] = [A.alloc([128, 512], BF16) for _ in range(3)]
        w["S"] = [A.alloc([128, 512], BF16) for _ in range(3)]
        w["A"] = [A.alloc([128, 512], BF16) for _ in range(3)]
        if alt is not None:
            save = (A.cur, A.lim)
            A.region(*alt)
        w["rec"] = A.alloc([128, 512], F32)
        w["acc"] = [A.alloc([128, 512], F32) for _ in range(2)]
        if alt is not None:
            self.alt_cur = A.cur
            A.cur, A.lim = save
        w["b_acc"] = P.bufs(2, "acc")
        w["b_e32"] = P.bufs(2, "e32")
        for k in ("sp", "S", "A"):
            w["b_" + k] = P.bufs(3, k)
        w["b_rec"] = P.buf("rec")
        w["cnt"] = 0
        return w

    def sb_problem(self, w, N, groups, tiles, obank, evac):
        negtri, negones = self.cm[:, 0, :], self.cm[:, 1, :]
        S, bS = w["S"], w["b_S"]
        for j in range(3):
            self.memset(S[j][:, :N], 0.0, [bS[j]])
        O = self.PS(obank)
        nt = len(tiles)
        base = w["cnt"]
        w["cnt"] += nt
        wbank = lambda k: (0, 1, 2, 5)[(base + k) % 4]

        def stA(k):
            t = tiles[k]
            nk, wb = t["nk"], wbank(k)
            cl = t.get("c_lo", 0)
            W = self.PS(wb)
            for gi, (c0, ncg, q, bq) in enumerate(groups):
                lo = max(c0, cl)
                self.mm(W[:nk, lo:c0 + ncg], t["kt"][gi], q[:, lo - c0:], gi == 0, False, t["reads"] + [bq],
                        [self.bk[wb]])
            if t["mask"] is not None:
                self.mm(W[:nk, cl:N], self.identb[:nk, :nk], t["mask"][:, cl:N], False, False,
                        [self.bconst, self.b_mask], [self.bk[wb]])
            s2 = (base + k) % 2
            self.act(w["e32"][s2][:nk, cl:N], W[:nk, cl:N], AF.Exp, [self.bk[wb]], [w["b_e32"][s2]])

        def stA2(k):
            t = tiles[k]
            nk, cl = t["nk"], t.get("c_lo", 0)
            s2, s3 = (base + k) % 2, (base + k) % 3
            self.act(w["sp"][s3][:nk, cl:N], w["e32"][s2][:nk, cl:N], AF.Ln, [w["b_e32"][s2], self.bconst],
                     [w["b_sp"][s3]], bias=self.epsc[:nk, 2:3])

        def stB(k):
            t = tiles[k]
            nk, wb = t["nk"], wbank(k)
            cl = t.get("c_lo", 0)
            W = self.PS(wb)
            s3 = (base + k) % 3
            first = k == 0
            self.mm(W[:nk, cl:N], negtri[:nk, :nk], w["sp"][s3][:nk, cl:N], False, first,
                    [self.bconst, w["b_sp"][s3]], [self.bk[wb]])
            if not first:
                self.mm(W[:nk, cl:N], negones[:, :nk], S[k % 3][:, cl:N], False, True,
                        [self.bconst, bS[k % 3]], [self.bk[wb]])
            self.act(w["A"][s3][:nk, cl:N], W[:nk, cl:N], AF.Exp, [self.bk[wb]], [w["b_A"][s3]])
            if k < nt - 1:
                nx = (k + 1) % 3
                if nk < 128:
                    self.tt(S[nx][:nk, cl:N], S[k % 3][:nk, cl:N], w["sp"][s3][:nk, cl:N], ALU.add,
                            [bS[k % 3], w["b_sp"][s3]], [bS[nx]])
                else:
                    self.tt(S[nx][:, cl:N], S[k % 3][:, cl:N], w["sp"][s3][:, cl:N], ALU.add,
                            [bS[k % 3], w["b_sp"][s3]], [bS[nx]])

        def stC(k):
            t = tiles[k]
            nk = t["nk"]
            s3 = (base + k) % 3
            cl = t.get("c_lo", 0)
            for gi, (c0, ncg, q, bq) in enumerate(groups):
                lo = max(c0, cl)
                self.mm(O[:, lo:c0 + ncg], t["v"][gi], w["A"][s3][:nk, lo:c0 + ncg], k == 0 and gi == 0,
                        k == nt - 1, t["reads"] + [w["b_A"][s3]], [self.bk[obank]])

        for it in range(nt + 3):
            if it < nt:
                stA(it)
            if 1 <= it <= nt:
                stA2(it - 1)
            if 2 <= it <= nt + 1:
                stB(it - 2)
            if 3 <= it:
                stC(it - 3)
        evac(O, obank)

    def mla_problem(self, w, N, groups, tiles, obank, dbank, evac):
        ones = self.cm[:, 2, :]
        O, Dn = self.PS(obank), self.PS(dbank)
        nt = len(tiles)
        base = w["cnt"]
        w["cnt"] += nt

        def stA(k):
            t = tiles[k]
            nk, zb = t["nk"], (base + k) % 3
            Z = self.PS(zb)
            s3 = (base + k) % 3
            cl = t.get("c_lo", 0)
            for gi, (c0, ncg, qn, qr, bq) in enumerate(groups):
                lo = max(c0, cl)
                self.mm(Z[:nk, lo:c0 + ncg], t["nt"][gi], qn[:, lo - c0:], gi == 0, False, t["reads"] + [bq],
                        [self.bk[zb]])
                self.mm(Z[:nk, lo:c0 + ncg], t["rt"], qr[:, lo - c0:], False,
                        t["mask"] is None and gi == len(groups) - 1, t["reads"] + [bq], [self.bk[zb]])
            if t["mask"] is not None:
                self.mm(Z[:nk, cl:N], self.identb[:nk, :nk], t["mask"][:, cl:N], False, True,
                        [self.bconst, self.b_mask], [self.bk[zb]])
            self.act(w["A"][s3][:nk, cl:N], Z[:nk, cl:N], AF.Exp, [self.bk[zb]], [w["b_A"][s3]])

        def stC(k):
            t = tiles[k]
            nk = t["nk"]
            s3 = (base + k) % 3
            first, last = k == 0, k == nt - 1
            cl = t.get("c_lo", 0)
            for gi, (c0, ncg, qn, qr, bq) in enumerate(groups):
                lo = max(c0, cl)
                self.mm(O[:, lo:c0 + ncg], t["vm"][gi], w["A"][s3][:nk, lo:c0 + ncg], first and gi == 0, last,
                        t["reads"] + [w["b_A"][s3]], [self.bk[obank]])
            a2 = k % 2
            self.tt(w["acc"][a2][:nk, cl:N], w["acc"][a2][:nk, cl:N], w["A"][s3][:nk, cl:N], ALU.add,
                    [w["b_acc"][a2], w["b_A"][s3]], [w["b_acc"][a2]])

        self.memset(w["acc"][0][:, :N], 0.0, [w["b_acc"][0]])
        self.memset(w["acc"][1][:, :N], 0.0, [w["b_acc"][1]])
        for it in range(nt + 1):
            if it < nt:
                stA(it)
            if it >= 1:
                stC(it - 1)
        self.mm(Dn[:, :N], self.ones32[:, :], w["acc"][0][:, :N], True, False, [self.bconst, w["b_acc"][0]],
                [self.bk[dbank]])
        self.mm(Dn[:, :N], self.ones32[:, :], w["acc"][1][:, :N], False, True, [self.bconst, w["b_acc"][1]],
                [self.bk[dbank]])
        self.P.op("dve", lambda e: e.reciprocal(w["rec"][:, :N], Dn[:, :N]), [self.bk[dbank]], [w["b_rec"]])
        evac(O, obank, w["rec"], w["b_rec"])

    def phase_sample_attn(self):
        P, A, i = self.P, self.A, self.i
        w_in = i["w_in"]
        A.region(self.R3, self.R3 + 34000)
        self.oT_sb = A.alloc([128, H, NT], BF16)
        self.oT_mla = A.alloc([128, H, NT], BF16)
        self.b_oT_sb, self.b_oT_mla = P.buf("oTsb"), P.buf("oTmla")
        A.region(self.R2, self.R3)
        w = self.attn_setup()
        self.aw = w
        Qss = A.alloc([128, H, NS], BF16)
        Qns = A.alloc([128, H, NS], BF16)
        Qrs = A.alloc([64, H, NS], BF16)
        Qrt = [A.alloc([64, NS], F32) for _ in range(2)]
        cosf = A.alloc([64, NS], F32)
        sinf = A.alloc([64, NS], F32)
        self.cosf, self.sinf = cosf, sinf
        self.b_tabf = P.buf("tabf")
        P.dma("sp", cosf[:], i["cos_fm"][:, NP_:NT], "t0", writes=[self.b_tabf])
        P.dma("sp", sinf[:], i["sin_fm"][:, NP_:NT], "t1", writes=[self.b_tabf])
        mbn = A.alloc([16, 128], BF16)
        mbn32 = A.alloc([16, 128], F32)
        b_mbn32 = P.buf("mbn32")
        self.b_mask = P.buf("mask")
        P.dma("sp", mbn32[:], i["mb_new"], "mb", writes=[b_mbn32])
        self.cp(mbn[:], mbn32[:], [b_mbn32], [self.b_mask])
        wq = None
        wuq = [A.alloc([128, 4, 256], BF16) for _ in range(2)]

        self.wq, self.wuq = wq, wuq
        self.b_wq, self.b_wuq = P.bufs(2, "wq"), P.bufs(2, "wuq")
        b_Qs = P.buf("Qs_s")
        b_Qrt = P.buf("Qrt")
        Vn1 = A.alloc([16, 1024], BF16)
        VMn1 = A.alloc([16, 1024], BF16)
        b_Vn1 = P.buf("Vn1")
        P.dma("sp", Vn1[:], self.Vs_new[16:32, :], "vn1", reads=[self.b_Vsn], writes=[b_Vn1])
        P.dma("sp", VMn1[:], self.VMs_new[16:32, :], "vmn1", reads=[self.b_VMsn], writes=[b_Vn1])
        self.wcount = 0
        wq2 = [A.alloc([128, DC, 256], BF16) for _ in range(2)]
        b_wq2 = P.bufs(2, "wq2")
        uqv = i["w_uq"].rearrange("(c p) n -> p c n", p=128)

        def load_pair(hp):
            sl2 = hp % 2
            P.dma("pool", wq2[sl2][:], w_in[:, hp * 256:(hp + 1) * 256].rearrange("(c p) n -> p c n", p=128),
                  f"wq2{sl2}", writes=[b_wq2[sl2]])

        def load_wuq(h):
            sl2 = h % 2
            b0 = h * 192
            P.dma("pool", wuq[sl2][:, :, 0:192], uqv[:, :, b0:b0 + 192], f"wuqa{sl2}", writes=[self.b_wuq[sl2]])
            P.dma("pool", wuq[sl2][:, :, 192:224], uqv[:, :, b0 + 160:b0 + 192], f"wuqb{sl2}",
                  writes=[self.b_wuq[sl2]])
            P.dma("pool", wuq[sl2][:, :, 224:256], uqv[:, :, b0 + 128:b0 + 160], f"wuqc{sl2}",
                  writes=[self.b_wuq[sl2]])
            return sl2

        load_pair(0)
        load_pair(1)
        for h in range(H):
            s = load_wuq(h)
            sp2, off = (h // 2) % 2, (h % 2) * 128
            bA, bB = (6, 7) if h % 2 == 0 else (3, 4)
            ps = self.PS(bA)
            for c in range(DC):
                self.mm(ps[:, :NS], wq2[sp2][:, c, off:off + 128], self.actb[:, c, NP_:NT], c == 0, c == DC - 1,
                        [b_wq2[sp2], self.b_act[c]], [self.bk[bA]])
            if h % 2 == 1 and h // 2 + 2 < 4:
                load_pair(h // 2 + 2)
            self.act(Qss[:, h, :], ps[:, :NS], AF.Identity, [self.bk[bA]], [b_Qs], scale=SB_SCALE)
            ps = self.PS(bB)
            for kc in range(4):
                self.mm(ps[:, :NS], wuq[s][:, kc, 0:128], self.cqnT[:, kc, NP_:NT], kc == 0, kc == 3,
                        [self.b_wuq[s], self.b_cqnT], [self.bk[bB]])
            self.act(Qns[:, h, :], ps[:, :NS], AF.Identity, [self.bk[bB]], [b_Qs], scale=MLA_SCALE)
            ps = self.PS(bA)
            for kc in range(4):
                self.mm(ps[:64, :NS], wuq[s][:, kc, 128:192], self.cqnT[:, kc, NP_:NT], kc == 0, kc == 3,
                        [self.b_wuq[s], self.b_cqnT], [self.bk[bA]])
            for kc in range(4):
                self.mm(ps[:64, 64:64 + NS], wuq[s][:, kc, 192:256], self.cqnT[:, kc, NP_:NT], kc == 0, kc == 3,
                        [self.b_wuq[s], self.b_cqnT], [self.bk[bA]])
            self.tt(Qrt[0][:, :], ps[:64, :NS], cosf[:, :], ALU.mult, [self.bk[bA], self.b_tabf], [b_Qrt])
            self.tt(Qrt[1][:, :], ps[:64, 64:64 + NS], sinf[:, :], ALU.mult, [self.bk[bA], self.b_tabf], [b_Qrt])
            self.tt(Qrt[0][:, :], Qrt[0][:, :], Qrt[1][:, :], ALU.add, [b_Qrt], [b_Qrt])
            self.act(Qrs[:, h, :], Qrt[0][:, :], AF.Identity, [b_Qrt], [b_Qs], scale=MLA_SCALE)
        r2cur = A.cur
        A.region(self.R3 + 34000, self.TOP - 34000)
        KTc = A.alloc([128, H, PAST], BF16)
        Vc = A.alloc([128, 8, 1024], BF16)
        A.region(r2cur, self.R3)
        ckvTc = A.alloc([128, 4, PAST], BF16)
        krTc = A.alloc([64, PAST], BF16)
        kst = [A.alloc([128, 1024], F32) for _ in range(2)]
        krst = A.alloc([128, 8, 64], F32)
        b_KTc, b_Vc, b_ckvTc, b_krTc = P.buf("KTc"), P.buf("Vc"), P.buf("ckvTc"), P.buf("krTc")
        b_kst, b_krst = P.bufs(2, "kst"), P.buf("krst")
        kcnt = 0
        for sq in range(2):
            qsl = slice(sq * 16, sq * 16 + 16)
            for kt in range(8):
                s = kcnt % 2
                kcnt += 1
                P.dma("sp", kst[s][:, :], i["c_k"][sq, kt * 128:(kt + 1) * 128, :], f"kst{s}", writes=[b_kst[s]])
                for hg in range(2):
                    bank = 6 + hg
                    ps = self.PS(bank)
                    for j in range(4):
                        h = hg * 4 + j
                        self.tr(ps[:, j * 128:(j + 1) * 128], kst[s][:, h * 128:(h + 1) * 128], self.ident[:, :],
                                [b_kst[s], self.bconst], [self.bk[bank]])
                    self.cp(KTc[:, hg * 4:hg * 4 + 4, kt * 128:(kt + 1) * 128],
                            ps.rearrange("p (j t) -> p j t", j=4), [self.bk[bank]], [b_KTc])
            P.dma("pool", Vc[:], i["c_v"][sq].rearrange("(k p) c -> p k c", p=128), "vc", writes=[b_Vc])
            groups = [(h * 16, 16, Qss[:, h, qsl], b_Qs) for h in range(H)]
            vnew = self.Vs_new if sq == 0 else Vn1
            b_vnew = self.b_Vsn if sq == 0 else b_Vn1
            tiles = [dict(nk=16, kt=[self.KTs[:, h, qsl] for h in range(H)],
                          v=[vnew[:16, h * 128:(h + 1) * 128] for h in range(H)], mask=mbn[:, :],
                          reads=[self.b_KTs, b_vnew])]
            for kt in range(7, -1, -1):
                tiles.append(dict(nk=128, kt=[KTc[:, h, kt * 128:(kt + 1) * 128] for h in range(H)],
                                  v=[Vc[:, kt, h * 128:(h + 1) * 128] for h in range(H)], mask=None,
                                  reads=[b_KTc, b_Vc]))
            c0 = NP_ + sq * 16

            def evac_sb(O, obank, c0=c0):
                self.cp(self.oT_sb[:, :, c0:c0 + 16], O[:, :128].rearrange("p (h q) -> p h q", q=16),
                        [self.bk[obank]], [self.b_oT_sb])
            self.sb_problem(w, 128, groups, tiles, 3, evac_sb)
            NTc, VMc = KTc, Vc
            for kt in range(8):
                s = kcnt % 2
                kcnt += 1
                P.dma("sp", kst[s][:, :512], i["c_ckv"][sq, kt * 128:(kt + 1) * 128, :], f"kst{s}",
                      writes=[b_kst[s]])
                bank = 6 + kt % 2
                ps = self.PS(bank)
                for j in range(4):
                    self.tr(ps[:, j * 128:(j + 1) * 128], kst[s][:, j * 128:(j + 1) * 128], self.ident[:, :],
                            [b_kst[s], self.bconst], [self.bk[bank]])
                self.cp(ckvTc[:, :, kt * 128:(kt + 1) * 128], ps.rearrange("p (j t) -> p j t", j=4),
                        [self.bk[bank]], [b_ckvTc])
            P.dma("sp", krst[:], i["c_kr"][sq].rearrange("(k p) c -> p k c", p=128), "krst", writes=[b_krst])
            for kg in range(2):
                bank = 6 + kg
                ps = self.PS(bank)
                for j in range(4):
                    kt = kg * 4 + j
                    self.tr(ps[:64, j * 128:(j + 1) * 128], krst[:, kt, :], self.ident[:, :],
                            [b_krst, self.bconst], [self.bk[bank]])
                self.cp(krTc[:, kg * 512:(kg + 1) * 512], ps[:64, :], [self.bk[bank]], [b_krTc])
            for h in range(H):
                for half in range(2):
                    bank = 6 + (h * 2 + half) % 2
                    ps = self.PS(bank)
                    for kc in range(4):
                        self.mm(ps[:, :], self.wukv[:, kc, h * 256:h * 256 + 128],
                                ckvTc[:, kc, half * 512:(half + 1) * 512], kc == 0, kc == 3,
                                [self.b_wukv, b_ckvTc], [self.bk[bank]])
                    self.act(NTc[:, h, half * 512:(half + 1) * 512], ps[:, :], AF.Identity, [self.bk[bank]],
                             [b_KTc])
            for kt in range(8):
                for half in range(2):
                    bank = 6 + (kt * 2 + half) % 2
                    ps = self.PS(bank)
                    for hh in range(4):
                        hc = (half * 4 + hh) * 256 + 128
                        for kc in range(4):
                            self.mm(ps[:, hh * 128:(hh + 1) * 128], ckvTc[:, kc, kt * 128:(kt + 1) * 128],
                                    self.wukv[:, kc, hc:hc + 128], kc == 0, kc == 3,
                                    [b_ckvTc, self.b_wukv], [self.bk[bank]])
                    self.cp(VMc[:, kt, half * 512:(half + 1) * 512], ps[:, :], [self.bk[bank]], [b_Vc])
            groups = [(h * 16, 16, Qns[:, h, qsl], Qrs[:, h, qsl], b_Qs) for h in range(H)]
            vmnew = self.VMs_new if sq == 0 else VMn1
            b_vmnew = self.b_VMsn if sq == 0 else b_Vn1
            tiles = [dict(nk=16, nt=[self.NTs[:, h, qsl] for h in range(H)], rt=self.krT[:, c0:c0 + 16],
                          vm=[vmnew[:16, h * 128:(h + 1) * 128] for h in range(H)], mask=None,
                          reads=[self.b_NTs, self.b_krT, b_vmnew])]
            for kt in range(7, -1, -1):
                tiles.append(dict(nk=128, nt=[NTc[:, h, kt * 128:(kt + 1) * 128] for h in range(H)],
                                  rt=krTc[:, kt * 128:(kt + 1) * 128],
                                  vm=[VMc[:, kt, h * 128:(h + 1) * 128] for h in range(H)], mask=None,
                                  reads=[b_KTc, b_krTc, b_Vc]))

            def evac_mla(O, obank, rec, b_rec, c0=c0):
                self.tt(self.oT_mla[:, :, c0:c0 + 16], O[:, :128].rearrange("p (h q) -> p h q", q=16),
                        rec[:, :128].rearrange("p (h q) -> p h q", q=16), ALU.mult,
                        [self.bk[obank], b_rec], [self.b_oT_mla])
            self.mla_problem(w, 128, groups, tiles, 3, 5, evac_mla)
        P.barrier()

    def load_wq(self, h):
        P, i = self.P, self.i
        s = self.wcount % 2
        self.wcount += 1
        P.dma("pool", self.wq[s][:], i["w_in"][:, h * 128:(h + 1) * 128].rearrange("(c p) n -> p c n", p=128),
              f"wq{s}", writes=[self.b_wq[s]])
        uq = i["w_uq"].rearrange("(c p) n -> p c n", p=128)
        b0 = h * 192
        P.dma("pool", self.wuq[s][:, :, 0:192], uq[:, :, b0:b0 + 192], f"wuqa{s}", writes=[self.b_wuq[s]])
        P.dma("pool", self.wuq[s][:, :, 192:224], uq[:, :, b0 + 160:b0 + 192], f"wuqb{s}", writes=[self.b_wuq[s]])
        P.dma("pool", self.wuq[s][:, :, 224:256], uq[:, :, b0 + 128:b0 + 160], f"wuqc{s}", writes=[self.b_wuq[s]])
        return s

    def phase_prompt_attn(self):
        P, A, i = self.P, self.A, self.i
        A.region(self.R2, self.R3)
        w = self.attn_setup(alt=(self.TOP - 34000 + 8448, self.TOP))
        mask = A.alloc([128, 16, 512], BF16)
        self.b_mask = P.buf("mask2")
        Kg = [A.alloc([128, 4096], BF16) for _ in range(2)]
        Vg = [A.alloc([128, 32, 128], BF16) for _ in range(2)]
        A.region(self.R3 + 34000, self.TOP - 34000)
        Qs = [A.alloc([128, NP_], BF16) for _ in range(2)]
        Rg = A.alloc([64, 4096], BF16)
        Qr = [A.alloc([64, NP_], BF16) for _ in range(2)]
        Qrt = [A.alloc([64, 512], F32) for _ in range(2)]
        wq = [A.alloc([128, DC, 128], BF16) for _ in range(2)]
        wuq = [A.alloc([128, 4, 256], BF16) for _ in range(2)]
        self.wq, self.wuq = wq, wuq
        self.b_wq, self.b_wuq = P.bufs(2, "wq2"), P.bufs(2, "wuq2")
        b_Kg, b_Vg, b_Rg = P.bufs(2, "Kg"), P.bufs(2, "Vg"), P.buf("Rg")
        b_Qs, b_Qr, b_Qrt = P.bufs(2, "Qs"), P.bufs(2, "Qr"), P.buf("Qrt2")
        A.region(self.alt_cur, self.TOP)
        cosf = A.alloc([64, NP_], F32)
        sinf = A.alloc([64, NP_], F32)
        b_tabf = P.buf("tabf2")
        P.dma("sp", cosf[:], i["cos_fm"][:, 0:NP_], "t0", writes=[b_tabf])
        P.dma("sp", sinf[:], i["sin_fm"][:, 0:NP_], "t1", writes=[b_tabf])

        def load_kv(slot, row0, h):
            pairs, rds = [], []
            for r in range(4):
                ea, ej = self.exo(r, row0 + h * 128, 128)
                pairs.append((Kg[slot][:, r * 1024:(r + 1) * 1024], ea))
                rds.append(self.b_exo[ej])
            P.dma_batch("sp", pairs, f"kg{slot}", reads=rds, writes=[b_Kg[slot]])
            vrow = R_V if row0 == R_KT else R_VM
            pairs, rds = [], []
            for r in range(4):
                for hf in range(2):
                    ea, ej = self.exo(r, vrow + hf * 512, 512)
                    pairs.append((Vg[slot][:, r * 8 + hf * 4:r * 8 + hf * 4 + 4, :],
                                  ea[:, h * 128:(h + 1) * 128].rearrange("(m p) d -> p m d", p=128)))
                    rds.append(self.b_exo[ej])
            P.dma_batch("sp", pairs, f"vg{slot}", reads=rds, writes=[b_Vg[slot]])

        def key_tiles(M, slot, mla):
            tl = []
            for kt in range(16 * M + 15, -1, -1):
                r, m = kt % 4, kt // 4
                col = r * 1024 + m * 128
                mk = mask[:, kt - 16 * M, :] if kt >= 16 * M else None
                d = dict(nk=128, mask=mk, reads=[b_Kg[slot], b_Vg[slot]] + ([b_Rg] if mla else []))
                if kt >= 16 * M:
                    d["c_lo"] = 128 * max(0, (kt - 16 * M - 3 + 3) // 4)
                if mla:
                    d["nt"] = [Kg[slot][:, col:col + 128]]
                    d["rt"] = Rg[:, col:col + 128]
                    d["vm"] = [Vg[slot][:, r * 8 + m, :]]
                else:
                    d["kt"] = [Kg[slot][:, col:col + 128]]
                    d["v"] = [Vg[slot][:, r * 8 + m, :]]
                tl.append(d)
            return tl

        self.wcount = 0
        pcount = 0
        P.dma("pool", mask[:], i["mb_sb"], "mb2", writes=[self.b_mask])
        wslots = [self.load_wq(0), self.load_wq(1)]
        self.issue_cc([4, 5, 6, 7, 8])

        def qproj_sb(h):
            slot, s = h % 2, wslots[h]
            for half in range(2):
                bank = 6 + half
                ps = self.PS(bank)
                for c in range(DC):
                    self.mm(ps[:, :], wq[s][:, c, :], self.actb[:, c, half * 512:(half + 1) * 512], c == 0,
                            c == DC - 1, [self.b_wq[s], self.b_act[c]], [self.bk[bank]])
                self.act(Qs[slot][:, half * 512:(half + 1) * 512], ps[:, :], AF.Identity, [self.bk[bank]],
                         [b_Qs[slot]], scale=SB_SCALE)
            if h + 2 < H:
                wslots.append(self.load_wq(h + 2))

        load_kv(0, R_KT, 0)
        qproj_sb(0)
        for h in range(H):
            slot = h % 2
            for M in range(2):
                groups = [(0, 512, Qs[slot][:, M * 512:(M + 1) * 512], b_Qs[slot])]
                obank = 3 + pcount % 2
                pcount += 1

                def evac(O, ob, h=h, M=M):
                    self.act(self.oT_sb[:, h, M * 512:(M + 1) * 512], O[:, :], AF.Identity, [self.bk[ob]],
                             [self.b_oT_sb])
                self.sb_problem(w, 512, groups, key_tiles(M, slot, False), obank, evac)
                if M == 0 and h + 1 < H:
                    load_kv((h + 1) % 2, R_KT, h + 1)
                    qproj_sb(h + 1)
        P.dma("pool", mask[:], i["mb_mla"], "mb2", writes=[self.b_mask])
        for r in range(4):
            ea, ej = self.exo(r, R_RT, 64)
            P.dma("sp", Rg[:, r * 1024:(r + 1) * 1024], ea, "rg", reads=[self.b_exo[ej]],
                  writes=[b_Rg])
        mslots = [self.load_wq(0), self.load_wq(1)]

        def qproj_mla(h):
            slot, s = h % 2, mslots[h]
            for half in range(2):
                tsl = slice(half * 512, (half + 1) * 512)
                bank = 6 + half
                ps = self.PS(bank)
                for kc in range(4):
                    self.mm(ps[:, :], wuq[s][:, kc, 0:128], self.cqnT[:, kc, tsl], kc == 0, kc == 3,
                            [self.b_wuq[s], self.b_cqnT], [self.bk[bank]])
                self.act(Qs[slot][:, tsl], ps[:, :], AF.Identity, [self.bk[bank]], [b_Qs[slot]], scale=MLA_SCALE)
            for half in range(2):
                tsl = slice(half * 512, (half + 1) * 512)
                for part, bank in ((0, 6), (1, 7)):
                    ps = self.PS(bank)
                    for kc in range(4):
                        self.mm(ps[:64, :], wuq[s][:, kc, 128 + part * 64:192 + part * 64], self.cqnT[:, kc, tsl],
                                kc == 0, kc == 3, [self.b_wuq[s], self.b_cqnT], [self.bk[bank]])
                self.tt(Qrt[0][:, :], self.PS(6)[:64, :], cosf[:, tsl], ALU.mult, [self.bk[6], b_tabf], [b_Qrt])
                self.tt(Qrt[1][:, :], self.PS(7)[:64, :], sinf[:, tsl], ALU.mult, [self.bk[7], b_tabf], [b_Qrt])
                self.tt(Qrt[0][:, :], Qrt[0][:, :], Qrt[1][:, :], ALU.add, [b_Qrt], [b_Qrt])
                self.act(Qr[slot][:, tsl], Qrt[0][:, :], AF.Identity, [b_Qrt], [b_Qr[slot]], scale=MLA_SCALE)
            if h + 2 < H:
                mslots.append(self.load_wq(h + 2))

        load_kv(0, R_NT, 0)
        qproj_mla(0)
        for h in range(H):
            slot = h % 2
            for M in range(2):
                groups = [(0, 512, Qs[slot][:, M * 512:(M + 1) * 512], Qr[slot][:, M * 512:(M + 1) * 512],
                           b_Qs[slot])]
                obank = 3 + pcount % 2
                dbank = 5
                pcount += 1

                def evac(O, ob, rec, b_rec, h=h, M=M):
                    self.tt(self.oT_mla[:, h, M * 512:(M + 1) * 512], O[:, :], rec[:, :], ALU.mult,
                            [self.bk[ob], b_rec], [self.b_oT_mla])
                tl = key_tiles(M, slot, True)
                for t in tl:
                    t["reads"] = t["reads"] + [b_Qr[slot]]
                self.mla_problem(w, 512, groups, tl, obank, dbank, evac)
                if M == 0 and h + 1 < H:
                    load_kv((h + 1) % 2, R_NT, h + 1)
                    qproj_mla(h + 1)
        P.barrier()

    def phase_merge(self):
        P, A, i = self.P, self.A, self.i
        w_in = i["w_in"]
        A.region(self.R3 + 34000, self.R3 + 34000 + 34000)
        mg = A.alloc([128, DC, NT], BF16)
        b_mg = P.bufs(DC, "mg")
        A.region(self.R2, self.R3)
        wg = [A.alloc([128, DC, 256], BF16) for _ in range(2)]
        wb = [A.alloc([128, 16, 128], BF16) for _ in range(2)]
        gs = A.alloc([128, NT], F32)
        gm = A.alloc([128, NT], F32)
        t1 = A.alloc([128, NT], F32)
        t2 = A.alloc([128, NT], F32)
        b_wg, b_wb = P.bufs(2, "wg"), P.bufs(2, "wb")
        b_gs, b_gm, b_t1, b_t2 = P.bufs(3, "gs"), P.bufs(3, "gm"), P.bufs(3, "t1"), P.bufs(3, "t2")

        def load(ci):
            s = ci % 2
            P.dma("pool", wg[s][:, :, 0:128],
                  w_in[:, 4160 + ci * 128:4160 + (ci + 1) * 128].rearrange("(c p) n -> p c n", p=128),
                  f"wga{s}", writes=[b_wg[s]])
            P.dma("pool", wg[s][:, :, 128:256],
                  w_in[:, 6208 + ci * 128:6208 + (ci + 1) * 128].rearrange("(c p) n -> p c n", p=128),
                  f"wgb{s}", writes=[b_wg[s]])
            P.dma("pool", wb[s][:, 0:8, :],
                  i["w_br_sb"][:, ci * 128:(ci + 1) * 128].rearrange("(c p) n -> p c n", p=128),
                  f"wba{s}", writes=[b_wb[s]])
            P.dma("pool", wb[s][:, 8:16, :],
                  i["w_br_mla"][:, ci * 128:(ci + 1) * 128].rearrange("(c p) n -> p c n", p=128),
                  f"wbb{s}", writes=[b_wb[s]])

        load(0)
        load(1)
        for ci in range(DC):
            s = ci % 2
            for (b0, off, gt, b_g, bidx) in ((0, 0, gs, b_gs, ci), (3, 128, gm, b_gm, 16 + ci)):
                for c in range(DC):
                    for (t0, n, bk) in TT:
                        self.mm(self.PS(b0 + bk)[:, :n], wg[s][:, c, off:off + 128], self.actb[:, c, t0:t0 + n],
                                c == 0, c == DC - 1, [b_wg[s], self.b_act[c]], [self.bk[b0 + bk]])
                for (t0, n, bk) in TT:
                    self.act(gt[:, t0:t0 + n], self.PS(b0 + bk)[:, :n], AF.Sigmoid, [self.bk[b0 + bk], self.bconst],
                             [b_g[bk]], bias=self.bg[:, bidx:bidx + 1])
            for (b0, r0, oT, b_oT) in ((0, 0, self.oT_sb, self.b_oT_sb), (3, 8, self.oT_mla, self.b_oT_mla)):
                for h in range(H):
                    for (t0, n, bk) in TT:
                        self.mm(self.PS(b0 + bk)[:, :n], wb[s][:, r0 + h, :], oT[:, h, t0:t0 + n], h == 0, h == H - 1,
                                [b_wb[s], b_oT], [self.bk[b0 + bk]])
            for (t0, n, bk) in TT:
                self.tt(t1[:, t0:t0 + n], gs[:, t0:t0 + n], self.PS(bk)[:, :n], ALU.mult,
                        [b_gs[bk], self.bk[bk]], [b_t1[bk]])
                self.tt(t2[:, t0:t0 + n], gm[:, t0:t0 + n], self.PS(3 + bk)[:, :n], ALU.mult,
                        [b_gm[bk], self.bk[3 + bk]], [b_t2[bk]])
                self.tt(mg[:, ci, t0:t0 + n], t1[:, t0:t0 + n], t2[:, t0:t0 + n], ALU.add,
                        [b_t1[bk], b_t2[bk]], [b_mg[ci]])
            if ci + 2 < DC:
                load(ci + 2)
        P.barrier()
        P.dma("sp", self.resid[:], self.spill.ap(), "spill", writes=self.b_res)
        A.region(self.R3, self.R3 + 34000)
        wo = [A.alloc([128, DC, 512], BF16) for _ in range(2)]
        b_wo = P.bufs(2, "wo")

        def load_wo(q):
            s = q % 2
            P.dma("pool", wo[s][:], i["w_o"][:, q * 512:(q + 1) * 512].rearrange("(c p) n -> p c n", p=128),
                  f"wo{s}", writes=[b_wo[s]])
        load_wo(0)
        load_wo(1)
        for q in range(4):
            s = q % 2
            for j in range(4):
                ci = q * 4 + j
                b0 = 0 if ci % 2 == 0 else 3
                for c in range(DC):
                    for (t0, n, bk) in TT:
                        self.mm(self.PS(b0 + bk)[:, :n], wo[s][:, c, j * 128:(j + 1) * 128], mg[:, c, t0:t0 + n],
                                c == 0, c == DC - 1, [b_wo[s], b_mg[c]], [self.bk[b0 + bk]])
                for (t0, n, bk) in TT:
                    self.tt(self.resid[:, ci, t0:t0 + n], self.resid[:, ci, t0:t0 + n], self.PS(b0 + bk)[:, :n],
                            ALU.add, [self.b_res[ci], self.bk[b0 + bk]], [self.b_res[ci]])
            if q + 2 < 4:
                load_wo(q + 2)
        P.barrier()

    def phase_out(self):
        P, A, o = self.P, self.A, self.o
        A.region(self.R3, self.TOP)
        ys = [A.alloc([128, D], F32) for _ in range(2)]
        b_ys = P.bufs(2, "ys")
        k = 0
        for ti, (t0, m) in enumerate(TM):
            s = ti % 2
            for q in range(4):
                b = 4 + k % 4
                k += 1
                ps = self.PS(b)
                for j in range(4):
                    c = q * 4 + j
                    self.tr(ps[:m, j * 128:(j + 1) * 128], self.resid[:, c, t0:t0 + m], self.ident[:, :],
                            [self.b_res[c], self.bconst], [self.bk[b]])
                if k % 2 == 0:
                    self.act(ys[s][:m, q * 512:(q + 1) * 512], ps[:m, :], AF.Identity, [self.bk[b]], [b_ys[s]])
                else:
                    self.cp(ys[s][:m, q * 512:(q + 1) * 512], ps[:m, :], [self.bk[b]], [b_ys[s]])
            P.dma("sp", o["y"][t0:t0 + m, :], ys[s][:m, :], f"y{s}", reads=[b_ys[s]])

    def dump_act(self):
        dbg = self.nc.dram_tensor("dbg", [128, DC, NT], F32, kind="ExternalOutput").ap()
        self.P.dma("sp", dbg, self.resid[:], "dbg", reads=self.b_res)


def build_nc(stop_after=None, **kw):
    b = Builder(stop_after=stop_after, **kw)
    b.nc._used_inputs = list(b.i.keys())
    return b.nc


def host_inputs(inp):
    f = np.float32
    xp = np.asarray(inp["x_prompt"], f)
    xs = np.asarray(inp["x_sample"], f)
    common = {
        "w1a": np.ascontiguousarray(inp["ffn1_w_in"][0]), "w2a": np.ascontiguousarray(inp["ffn1_w_out"][0]),
        "w1b": np.ascontiguousarray(inp["ffn2_w_in"][0]), "w2b": np.ascontiguousarray(inp["ffn2_w_out"][0]),
        "w_in": np.ascontiguousarray(inp["w_in"][0]), "w_uq": np.ascontiguousarray(inp["w_uq"][0]),
        "w_ukv": np.ascontiguousarray(inp["w_ukv"][0]), "w_br_sb": np.ascontiguousarray(inp["w_br_sb"][0]),
        "w_br_mla": np.ascontiguousarray(inp["w_br_mla"][0]), "w_o": np.ascontiguousarray(inp["w_o"][0]),
    }
    lnp = np.concatenate([np.asarray(inp[k][0], f).reshape(16, 128).T for k in
                          ("ln1_g", "ln1_b", "ln2_g", "ln2_b", "ln3_g", "ln3_b")], axis=1)
    common["lnp"] = np.ascontiguousarray(lnp)
    common["bgate"] = np.ascontiguousarray(np.asarray(inp["b_gate"][0], f).reshape(32, 128).T)
    common["gcq"] = np.ascontiguousarray(np.broadcast_to(np.asarray(inp["g_cq"][0], f), (128, 512)))
    common["gckv"] = np.ascontiguousarray(np.broadcast_to(np.asarray(inp["g_ckv"][0], f), (128, 512)))
    common["ident"] = np.eye(128, dtype=f)
    cm = np.zeros((128, 4, 128), f)
    jj, ss = np.meshgrid(np.arange(128), np.arange(128), indexing="ij")
    cm[:, 0, :] = -(jj >= ss).astype(f)
    cm[:, 1, :] = -1.0
    cm[:, 2, :] = 1.0
    common["cmats"] = cm
    kk, tq = np.meshgrid(np.arange(16), np.arange(16), indexing="ij")
    common["mb_new"] = np.ascontiguousarray(np.tile(np.where(kk < tq, 0.0, NEG).astype(f), (1, 8)))
    maps = []
    inv_freq = (10000.0 ** (-np.arange(0, 64, 2, dtype=np.float32) / 64)).astype(f)
    for c in range(8):
        b, r = c // 4, c % 4
        tiles = [4 * m + r for m in range(8)]
        rows = np.concatenate([np.arange(g * 128, (g + 1) * 128) for g in tiles])
        x = np.concatenate([xp[b, rows], xs[2 * c], xs[2 * c + 1]], 0)
        pos = np.concatenate([rows, PAST + np.arange(16), PAST + np.arange(16)]).astype(f)
        ang = pos[:, None] * inv_freq[None, :]
        cos, sin = np.cos(ang).astype(f), np.sin(ang).astype(f)
        cpad = np.zeros((9 * 128, 32), f)
        spad = np.zeros((9 * 128, 32), f)
        cpad[:NT] = cos
        spad[:NT] = sin
        m = dict(common)
        m["x"] = np.ascontiguousarray(x)
        m["cos_tm"] = np.ascontiguousarray(cpad.reshape(9, 128, 32).transpose(1, 0, 2))
        m["sin_tm"] = np.ascontiguousarray(spad.reshape(9, 128, 32).transpose(1, 0, 2))
        m["cos_fm"] = np.ascontiguousarray(np.concatenate([cos.T, cos.T], 0))
        m["sin_fm"] = np.ascontiguousarray(np.concatenate([-sin.T, sin.T], 0))
        k_, q_ = np.meshgrid(np.arange(128), np.arange(128), indexing="ij")
        msb = np.zeros((128, 16, 512), f)
        mml = np.zeros((128, 16, 512), f)
        for o in range(16):
            for i4 in range(4):
                qt = 4 * i4 + r
                sl = slice(i4 * 128, (i4 + 1) * 128)
                if o > qt:
                    msb[:, o, sl] = NEG
                    mml[:, o, sl] = NEG
                elif o == qt:
                    msb[:, o, sl] = np.where(k_ < q_, 0.0, NEG)
                    mml[:, o, sl] = np.where((k_ // 64) <= (q_ // 64), 0.0, NEG)
        m["mb_sb"] = msb
        m["mb_mla"] = mml
        m["c_k"] = np.ascontiguousarray(np.asarray(inp["cache_sb_k"][0, 2 * c:2 * c + 2], f).reshape(2, PAST, 1024))
        m["c_v"] = np.ascontiguousarray(np.asarray(inp["cache_sb_v"][0, 2 * c:2 * c + 2], f).reshape(2, PAST, 1024))
        m["c_ckv"] = np.ascontiguousarray(np.asarray(inp["cache_mla_ckv"][0, 2 * c:2 * c + 2], f))
        m["c_kr"] = np.ascontiguousarray(np.asarray(inp["cache_mla_krope"][0, 2 * c:2 * c + 2], f))
        maps.append(m)
    return maps


_NC = None


def kernel(**inputs):
    global _NC
    maps = host_inputs(inputs)
    if _NC is None:
        _NC = build_nc()
    used = _NC._used_inputs
    maps = [{k: m[k] for k in used} for m in maps]
    res = run_bass_kernel_spmd(_NC, maps, core_ids=list(range(8)))
    return assemble(res.results)


def assemble(results):
    f = np.float32
    y_p = np.zeros((2, 4096, D), f)
    y_s = np.zeros((16, 16, D), f)
    nk_p = np.zeros((1, 2, 4096, 8, 128), f)
    nv_p = np.zeros((1, 2, 4096, 8, 128), f)
    nc_p = np.zeros((1, 2, 4096, 512), f)
    nr_p = np.zeros((1, 2, 4096, 64), f)
    nk_s = np.zeros((1, 16, 16, 8, 128), f)
    nv_s = np.zeros((1, 16, 16, 8, 128), f)
    nc_s = np.zeros((1, 16, 16, 512), f)
    nr_s = np.zeros((1, 16, 16, 64), f)
    for c in range(8):
        b, r = c // 4, c % 4
        R = results[c]
        for m in range(8):
            g = 4 * m + r
            dst = slice(g * 128, (g + 1) * 128)
            src = slice(m * 128, (m + 1) * 128)
            y_p[b, dst] = R["y"][src]
            nk_p[0, b, dst] = R["nk"][src].reshape(128, 8, 128)
            nv_p[0, b, dst] = R["nv"][src].reshape(128, 8, 128)
            nc_p[0, b, dst] = R["nckv"][src]
            nr_p[0, b, dst] = R["nkr"][src]
        for s in range(2):
            src = slice(1024 + 16 * s, 1024 + 16 * s + 16)
            y_s[2 * c + s] = R["y"][src]
            nk_s[0, 2 * c + s] = R["nk"][src].reshape(16, 8, 128)
            nv_s[0, 2 * c + s] = R["nv"][src].reshape(16, 8, 128)
            nc_s[0, 2 * c + s] = R["nckv"][src]
            nr_s[0, 2 * c + s] = R["nkr"][src]
    return (y_p, y_s, nk_p, nv_p, nc_p, nr_p, nk_s, nv_s, nc_s, nr_s)
```

```python
import numpy as np
import concourse.bass as bass
import concourse.mybir as mybir
from concourse.bass_utils import run_bass_kernel_spmd

F32 = mybir.dt.float32
BF16 = mybir.dt.bfloat16
AF = mybir.ActivationFunctionType
ALU = mybir.AluOpType

SEM_LIM = 30000
NEG = -30000.0

D = 2048
DC = 16
FF = 5504
FC = 43
NP_ = 1024
NS = 32
NT = NP_ + NS
TT = [(0, 512, 0), (512, 512, 1), (1024, 32, 2)]
TM = [(i * 128, 128) for i in range(8)] + [(1024, 32)]
H = 8
PAST = 1024
ALPHA = 2.0 ** 0.25
SB_SCALE = 128.0 ** -0.5
MLA_SCALE = 192.0 ** -0.5
LN_EPS = 1e-5
RMS_EPS = 1e-6
EX_ROWS = 4160
R_KT, R_V, R_NT, R_VM, R_RT = 0, 1024, 2048, 3072, 4096


class Buf:
    __slots__ = ("name", "lastw", "readers", "excl")

    def __init__(self, name, excl=False):
        self.name = name
        self.lastw = None
        self.readers = {}
        self.excl = excl


class Op:
    __slots__ = ("eng", "fn", "deps", "signal", "count", "key", "seq", "is_dma", "inc")

    def __init__(self, eng, fn, key, seq, is_dma, inc=None):
        self.eng = eng
        self.fn = fn
        self.key = key
        self.seq = seq
        self.is_dma = is_dma
        self.deps = {}
        self.signal = is_dma
        self.count = 0
        self.inc = inc if inc is not None else (16 if is_dma else 1)


class Prog:
    ENGS = ("pe", "act", "dve", "pool", "sp")

    def __init__(self, nc):
        self.nc = nc
        self.ops = {e: [] for e in self.ENGS}
        self.latest = {}
        self.pending = {e: {} for e in self.ENGS}
        self.seq = 0
        self.dma_counts = {}
        self.dma_inc = {}
        self.nbuf = 0

    def buf(self, name=None, excl=False):
        self.nbuf += 1
        return Buf(name or f"b{self.nbuf}", excl)

    def bufs(self, n, name="b"):
        return [self.buf(f"{name}{i}") for i in range(n)]

    def _add(self, eng, fn, reads, writes, semkey=None, inc=None, touch=()):
        is_dma = semkey is not None
        key = ("dma", semkey) if is_dma else eng
        self.seq += 1
        o = Op(eng, fn, key, self.seq, is_dma, inc)
        if is_dma:
            self.dma_inc[semkey] = o.inc
        deps = {}

        def add(d):
            if d is None:
                return
            if d.key == "pe" and eng == "pe" and not is_dma:
                return
            cur = deps.get(d.key)
            if cur is None or cur.seq < d.seq:
                deps[d.key] = d

        for b in reads:
            add(b.lastw)
            if b.excl:
                for r in b.readers.values():
                    if r.key != key:
                        add(r)
        for b in writes:
            add(b.lastw)
            for r in b.readers.values():
                add(r)
        for d in self.pending[eng].values():
            add(d)
        self.pending[eng] = {}
        o.deps = deps
        for b in reads:
            cur = b.readers.get(key)
            if cur is None or cur.seq < o.seq:
                b.readers[key] = o
        for b in writes:
            b.lastw = o
            b.readers = {}
        for b in touch:
            b.lastw = o
            b.readers = {}
        if is_dma:
            c = self.dma_counts.get(semkey, 0) + 1
            self.dma_counts[semkey] = c
            o.count = c
        self.ops[eng].append(o)
        self.latest[key] = o
        return o

    def op(self, eng, fn, reads=(), writes=()):
        return self._add(eng, fn, reads, writes)

    def dma(self, queue, out, in_, semkey, reads=(), writes=(), **kw):
        def fn(e):
            return e.dma_start(out=out, in_=in_, **kw)
        return self._add(queue, fn, reads, writes, semkey=semkey)

    def dma_batch(self, queue, pairs, semkey, reads=(), writes=()):
        n = len(pairs)
        for j, (out, in_) in enumerate(pairs):
            def fn(e, out=out, in_=in_):
                return e.dma_start(out=out, in_=in_)
            if n == 1:
                self._add(queue, fn, reads, writes, semkey=semkey)
            elif j == 0:
                self._add(queue, fn, reads, writes, semkey=semkey)
            elif j == n - 1:
                self._add(queue, fn, (), (), semkey=semkey, touch=writes)
            else:
                self._add(queue, fn, (), (), semkey=semkey)

    def barrier(self):
        snap = {k: v for k, v in self.latest.items()
                if not (isinstance(k, tuple) and str(k[1]).startswith("cc"))}
        for e in self.ENGS:
            self.pending[e] = dict(snap)

    def emit(self):
        nc = self.nc
        for e in self.ENGS:
            for o in self.ops[e]:
                for d in o.deps.values():
                    d.signal = True
        tot = {}
        for e in self.ENGS:
            c = 0
            for o in self.ops[e]:
                if not o.is_dma and o.signal:
                    c += 1
                    o.count = c
            tot[e] = c
        sems = {}

        def nsem(units):
            return max(1, (units + SEM_LIM - 1) // SEM_LIM)

        for e in self.ENGS:
            sems[e] = [nc.alloc_semaphore(f"s_{e}_{i}") for i in range(nsem(tot[e]))]
        for k, c in self.dma_counts.items():
            sems[("dma", k)] = [nc.alloc_semaphore(f"sd_{k}_{i}") for i in range(nsem(c * self.dma_inc[k]))]

        def target(o):
            units = o.count * o.inc
            idx = (units - 1) // SEM_LIM
            val = (units - 1) % SEM_LIM + 1
            return sems[o.key][idx], idx, val

        handles = {"pe": "tensor", "act": "scalar", "dve": "vector", "pool": "gpsimd", "sp": "sync"}
        final_waits = [o for k, o in self.latest.items() if o.is_dma]
        with nc.Block() as block:
            for e in self.ENGS:
                ops = self.ops[e]
                extra = final_waits if e == "sp" else []

                def body(eng, ops=ops, extra=extra):
                    waited = {}

                    def wait_for(d):
                        sem, idx, val = target(d)
                        wk = (d.key, idx)
                        if waited.get(wk, 0) < val:
                            eng.wait_ge(sem, val)
                            waited[wk] = val

                    for o in ops:
                        for d in o.deps.values():
                            wait_for(d)
                        ins = o.fn(eng)
                        if o.signal:
                            sem, idx, val = target(o)
                            ins.then_inc(sem, o.inc)
                    for d in extra:
                        wait_for(d)

                getattr(block, handles[e])(body)


class Arena:
    def __init__(self, nc):
        self.nc = nc
        b0 = nc.sbuf_base
        n = (nc.sbuf_top - b0 - 3072) // 4
        self.slab = nc.alloc_sbuf_tensor("slab", [128, n], F32)
        self.base = (b0 + 63) // 64 * 64
        self.top = (b0 + n * 4) // 64 * 64
        self.cur = self.base
        self.lim = self.top
        self.n = 0

    def size(self, shape, dtype):
        per = 4 if dtype == F32 else 2
        for s in shape[1:]:
            per *= s
        return (per + 63) // 64 * 64

    def alloc(self, shape, dtype):
        per = self.size(shape, dtype)
        off = self.cur
        self.cur += per
        assert self.cur <= self.lim, f"SBUF overflow {self.cur} > {self.lim}"
        self.n += 1
        return self.nc.alloc_sbuf_tensor_at(f"sb{self.n}", list(shape), dtype, offset=off)

    def region(self, lo, hi):
        self.cur = (lo + 63) // 64 * 64
        self.lim = hi


class Builder:
    IN_SHAPES = {
        "x": [NT, D], "w1a": [D, 2 * FF], "w2a": [FF, D], "w1b": [D, 2 * FF], "w2b": [FF, D],
        "w_in": [D, 8256], "w_uq": [512, 1536], "w_ukv": [512, 2048], "w_br_sb": [1024, D],
        "w_br_mla": [1024, D], "w_o": [D, D], "lnp": [128, 96], "bgate": [128, 32], "gcq": [128, 512],
        "gckv": [128, 512], "cos_tm": [128, 9, 32], "sin_tm": [128, 9, 32], "cos_fm": [64, NT],
        "sin_fm": [64, NT], "ident": [128, 128], "cmats": [128, 4, 128], "mb_sb": [128, 16, 512],
        "mb_mla": [128, 16, 512], "mb_new": [16, 128], "c_k": [2, PAST, 1024], "c_v": [2, PAST, 1024],
        "c_ckv": [2, PAST, 512], "c_kr": [2, PAST, 64],
    }

    class _Lazy(dict):
        def __init__(self, b):
            super().__init__()
            self.b = b

        def __missing__(self, k):
            v = self.b.din(k, Builder.IN_SHAPES[k])
            self[k] = v
            return v

    def __init__(self, stop_after=None):
        self.stop_after = stop_after
        nc = bass.Bass("TRN2", target_bir_lowering=False)
        self.nc = nc
        self.P = Prog(nc)
        self.A = Arena(nc)
        self.i = Builder._Lazy(self)
        self.o = {}
        self.build()
        self.P.emit()

    def din(self, name, shape, dtype=F32):
        return self.nc.dram_tensor(name, list(shape), dtype, kind="ExternalInput").ap()

    def dout(self, name, shape, dtype=F32):
        return self.nc.dram_tensor(name, list(shape), dtype, kind="ExternalOutput").ap()

    def mm(self, out, lhsT, rhs, start, stop, reads, writes):
        self.P.op("pe", lambda e: e.matmul(out, lhsT, rhs, start=start, stop=stop, skip_group_check=True),
                  reads, writes)

    def tr(self, out, in_, ident, reads, writes):
        self.P.op("pe", lambda e: e.transpose(out, in_, ident), reads, writes)

    def act(self, out, in_, func, reads, writes, **kw):
        self.P.op("act", lambda e: e.activation(out, in_, func, **kw), reads, writes)

    def tt(self, out, in0, in1, op, reads, writes, eng="dve"):
        self.P.op(eng, lambda e: e.tensor_tensor(out, in0, in1, op), reads, writes)

    def ts(self, out, in0, s1, s2, op0, op1, reads, writes, eng="dve"):
        if op1 is None:
            self.P.op(eng, lambda e: e.tensor_scalar(out, in0, s1, None, op0), reads, writes)
        else:
            self.P.op(eng, lambda e: e.tensor_scalar(out, in0, s1, s2, op0, op1), reads, writes)

    def stt(self, out, in0, scalar, in1, op0, op1, reads, writes):
        self.P.op("dve", lambda e: e.scalar_tensor_tensor(out, in0, scalar, in1, op0, op1), reads, writes)

    def cp(self, out, in_, reads, writes, eng="dve"):
        self.P.op(eng, lambda e: e.tensor_copy(out, in_), reads, writes)

    def memset(self, ap, val, writes, eng="dve"):
        self.P.op(eng, lambda e: e.memset(ap, val), [], writes)

    def PS(self, b):
        return self.psum[:, b * 512:(b + 1) * 512]

    def exi(self, row0, nrows):
        j = row0 // 512
        a = row0 - j * 512
        return self.ex_in[j].ap()[a:a + nrows, :], j

    def exo(self, r, row0, nrows):
        j = row0 // 512
        a = r * self.ex_rows[j] + row0 - j * 512
        return self.ex_out[j].ap()[a:a + nrows, :], j

    def build(self):
        nc, P, A, i = self.nc, self.P, self.A, self.i
        st = self.stop_after
        self.psum = nc.alloc_psum_tensor("ps", [128, 4096], F32)
        self.bk = [P.buf(f"bank{b}", True) for b in range(8)]

        self.ident = A.alloc([128, 128], F32)
        self.identb = A.alloc([128, 128], BF16)
        self.cm = A.alloc([128, 4, 128], BF16)
        self.lnp = A.alloc([128, 96], F32)
        self.lnpa = A.alloc([128, 96], F32)
        self.bg = A.alloc([128, 32], F32)
        self.epsc = A.alloc([128, 4], F32)
        self.ones32 = A.alloc([128, 128], F32)
        self.bconst = P.buf("consts")
        P.dma("sp", self.ident[:], i["ident"], "c0", writes=[self.bconst])
        P.dma("pool", self.cm[:], i["cmats"], "c1", writes=[self.bconst])
        P.dma("pool", self.identb[:], i["ident"], "c4", writes=[self.bconst])
        P.dma("sp", self.lnp[:], i["lnp"], "c2", writes=[self.bconst])
        P.dma("sp", self.bg[:], i["bgate"], "c3", writes=[self.bconst])
        self.ts(self.lnpa[:], self.lnp[:], ALPHA, None, ALU.mult, None, [self.bconst], [self.bconst])
        self.memset(self.epsc[:, 0:1], LN_EPS, [self.bconst])
        self.memset(self.epsc[:, 1:2], RMS_EPS, [self.bconst])
        self.memset(self.epsc[:, 2:3], 1.0, [self.bconst])
        self.memset(self.ones32[:, :], 1.0, [self.bconst])
        self.actb = A.alloc([128, DC, NT], BF16)
        self.b_act = P.bufs(DC, "act")
        self.R2 = A.cur
        self.resid = A.alloc([128, DC, NT], F32)
        self.b_res = P.bufs(DC, "res")
        self.R3 = A.cur
        self.TOP = A.top
        assert self.TOP - self.R3 >= 86000, (self.TOP, self.R3)

        if st is None:
            o = self.o
            o["y"] = self.dout("y", [NT, D])
            o["nk"] = self.dout("nk", [NT, 1024])
            o["nv"] = self.dout("nv", [NT, 1024])
            o["nckv"] = self.dout("nckv", [NT, 512])
            o["nkr"] = self.dout("nkr", [NT, 64])
            self.ex_rows = [512] * 8 + [64]
            self.ex_in = [nc.dram_tensor(f"ex_in{j}", [n, NP_], BF16) for j, n in enumerate(self.ex_rows)]
            self.ex_out = [nc.dram_tensor(f"ex_out{j}", [4 * n, NP_], BF16) for j, n in enumerate(self.ex_rows)]
            self.b_exc = P.bufs(9, "exc")
            self.b_exo = P.bufs(9, "exo")
            self.spill = nc.dram_tensor("spill", [128, DC, NT], F32)

        self.ffn_prefetch(i["w1a"], i["w2a"])
        self.phase_load_x()
        if st == "x":
            return self.dump_act()
        self.phase_ffn()
        if st == "ffn1":
            return self.dump_act()
        if st is None:
            self.proj_prefetch()
        self.phase_ln(0, final=False)
        if st == "ln1":
            return self.dump_act()
        P.dma("sp", self.spill.ap(), self.resid[:], "spill", reads=self.b_res)
        P.barrier()
        self.phase_proj()
        self.phase_sample_attn()
        self.phase_prompt_attn()
        self.phase_merge()
        self.ffn_prefetch(i["w1b"], i["w2b"])
        self.phase_ln(1, final=False)
        self.phase_ffn()
        self.phase_ln(2, final=True)
        self.phase_out()

    def phase_load_x(self):
        P, A, i = self.P, self.A, self.i
        A.region(self.TOP - 19072, self.TOP)
        xs = [A.alloc([128, D], F32) for _ in range(2)]
        bx = P.bufs(2, "xs")
        k = 0
        for ti, (t0, m) in enumerate(TM):
            s = ti % 2
            P.dma("sp", xs[s][:m, :], i["x"][t0:t0 + m, :], f"x{s}", writes=[bx[s]])
            for q in range(4):
                b = 4 + k % 4
                k += 1
                ps = self.PS(b)
                for j in range(4):
                    c = q * 4 + j
                    self.tr(ps[:, j * 128:j * 128 + m], xs[s][:m, c * 128:(c + 1) * 128], self.ident[:m, :m],
                            [bx[s], self.bconst], [self.bk[b]])
                src = ps.rearrange("p (j t) -> p j t", j=4)[:, :, :m]
                self.act(self.resid[:, q * 4:q * 4 + 4, t0:t0 + m], src, AF.Identity, [self.bk[b]],
                         self.b_res[q * 4:q * 4 + 4], scale=ALPHA)
                self.cp(self.actb[:, q * 4:q * 4 + 4, t0:t0 + m], src, [self.bk[b]], self.b_act[q * 4:q * 4 + 4])
        P.barrier()

    def ffn_prefetch(self, w1, w2):
        P, A = self.P, self.A
        A.region(self.R3, self.TOP - 19072)
        NPAIR, NG = 22, 11
        w1g = [A.alloc([128, DC, 256], BF16) for _ in range(2)]
        w1u = [A.alloc([128, DC, 256], BF16) for _ in range(2)]
        self.ffn_free_lo = A.cur
        aT = [A.alloc([128, 4, NT], BF16) for _ in range(2)]
        sg = A.alloc([128, NT], BF16)
        self.ffn_free_hi = A.cur
        w2s = [A.alloc([128, 4, D], BF16) for _ in range(2)]
        b_w1, b_aT, b_w2 = P.bufs(2, "w1"), P.bufs(2, "aT"), P.bufs(2, "w2")
        b_sg = P.bufs(3, "sg")

        def load_w1(p):
            s = p % 2
            w = (2 if p < 21 else 1) * 128
            c0 = p * 256
            P.dma("pool", w1g[s][:, :, :w], w1[:, c0:c0 + w].rearrange("(c p) n -> p c n", p=128),
                  f"w1g{s}", writes=[b_w1[s]])
            P.dma("pool", w1u[s][:, :, :w], w1[:, FF + c0:FF + c0 + w].rearrange("(c p) n -> p c n", p=128),
                  f"w1u{s}", writes=[b_w1[s]])

        def load_w2(g):
            s = g % 2
            nch = 4 if g < 10 else 3
            r0 = g * 512
            P.dma("pool", w2s[s][:, :nch, :], w2[r0:r0 + nch * 128, :].rearrange("(j p) n -> p j n", p=128),
                  f"w2{s}", writes=[b_w2[s]])

        load_w1(0)
        load_w1(1)
        load_w2(0)
        load_w2(1)
        self.ffn_state = (w1g, w1u, aT, w2s, sg, b_w1, b_aT, b_w2, b_sg, load_w1, load_w2)

    def phase_ffn(self):
        P = self.P
        NPAIR, NG = 22, 11
        (w1g, w1u, aT, w2s, sg, b_w1, b_aT, b_w2, b_sg, load_w1, load_w2) = self.ffn_state
        for g in range(NG):
            gs = g % 2
            nch_g = 4 if g < 10 else 3
            for pp in range(2):
                p = g * 2 + pp
                if p >= NPAIR:
                    continue
                s = p % 2
                nch = 2 if p < 21 else 1
                for jj in range(nch):
                    ja = pp * 2 + jj
                    for (b0, wt) in ((0, w1g[s]), (3, w1u[s])):
                        for c in range(DC):
                            for (t0, n, bk) in TT:
                                self.mm(self.PS(b0 + bk)[:, :n], wt[:, c, jj * 128:(jj + 1) * 128],
                                        self.actb[:, c, t0:t0 + n], c == 0, c == DC - 1,
                                        [b_w1[s], self.b_act[c]], [self.bk[b0 + bk]])
                    for (t0, n, bk) in TT:
                        self.act(sg[:, t0:t0 + n], self.PS(bk)[:, :n], AF.Silu, [self.bk[bk]], [b_sg[bk]])
                    for (t0, n, bk) in TT:
                        self.tt(aT[gs][:, ja, t0:t0 + n], sg[:, t0:t0 + n], self.PS(3 + bk)[:, :n],
                                ALU.mult, [b_sg[bk], self.bk[3 + bk]], [b_aT[gs]])
                if p + 2 < NPAIR:
                    load_w1(p + 2)
            for ci in range(DC):
                b0 = 0 if ci % 2 == 0 else 3
                for jj in range(nch_g):
                    for (t0, n, bk) in TT:
                        self.mm(self.PS(b0 + bk)[:, :n], w2s[gs][:, jj, ci * 128:(ci + 1) * 128],
                                aT[gs][:, jj, t0:t0 + n], jj == 0, jj == nch_g - 1,
                                [b_w2[gs], b_aT[gs]], [self.bk[b0 + bk]])
                for (t0, n, bk) in TT:
                    self.stt(self.resid[:, ci, t0:t0 + n], self.PS(b0 + bk)[:, :n], 0.5,
                             self.resid[:, ci, t0:t0 + n], ALU.mult, ALU.add,
                             [self.bk[b0 + bk], self.b_res[ci]], [self.b_res[ci]])
            if g + 2 < NG:
                load_w2(g + 2)
        P.barrier()

    def phase_ln(self, idx, final):
        P, A = self.P, self.A
        A.region(self.ffn_free_lo, self.ffn_free_hi)
        xb = [A.alloc([128, NT], BF16) for _ in range(2)]
        xq = [A.alloc([128, NT], BF16) for _ in range(2)]
        mt = A.alloc([128, NT], F32)
        rs = A.alloc([128, NT], F32)
        A.region(self.TOP - 19072, self.TOP)
        tmp = [A.alloc([128, NT], F32) for _ in range(2)]
        b_xb, b_xq = P.bufs(2, "xb"), P.bufs(2, "xq")
        b_mt, b_rs = P.buf("mt"), P.buf("rs")
        b_tmp = P.bufs(2, "tmp")
        ones = self.cm[:, 2, :]
        for c in range(DC):
            s = c % 2
            self.act(xb[s][:], self.resid[:, c, :], AF.Copy, [self.b_res[c]], [b_xb[s]])
            self.tt(xq[s][:], self.resid[:, c, :], self.resid[:, c, :], ALU.mult, [self.b_res[c]], [b_xq[s]])
            for (t0, n, bk) in TT:
                self.mm(self.PS(bk)[:, :n], ones, xb[s][:, t0:t0 + n], c == 0, c == DC - 1,
                        [b_xb[s], self.bconst], [self.bk[bk]])
            for (t0, n, bk) in TT:
                self.mm(self.PS(3 + bk)[:, :n], ones, xq[s][:, t0:t0 + n], c == 0, c == DC - 1,
                        [b_xq[s], self.bconst], [self.bk[3 + bk]])
        for (t0, n, bk) in TT:
            sl = slice(t0, t0 + n)
            pa = self.PS(bk)[:, :n]
            pb = self.PS(3 + bk)[:, :n]
            self.ts(mt[:, sl], pa, 1.0 / D, None, ALU.mult, None, [self.bk[bk]], [b_mt])
            self.tt(tmp[0][:, sl], mt[:, sl], mt[:, sl], ALU.mult, [b_mt], [b_tmp[0]])
            self.stt(rs[:, sl], pb, 1.0 / D, tmp[0][:, sl], ALU.mult, ALU.subtract,
                     [self.bk[3 + bk], b_tmp[0]], [b_rs])
            self.act(rs[:, sl], rs[:, sl], AF.Ln, [b_rs, self.bconst], [b_rs], bias=self.epsc[:, 0:1])
            self.act(rs[:, sl], rs[:, sl], AF.Exp, [b_rs], [b_rs], scale=-0.5)
            self.stt(mt[:, sl], mt[:, sl], -1.0, rs[:, sl], ALU.mult, ALU.mult, [b_mt, b_rs], [b_mt])
        g = self.lnp[:, idx * 32:idx * 32 + 16]
        b = self.lnp[:, idx * 32 + 16:idx * 32 + 32]
        ga = self.lnpa[:, idx * 32:idx * 32 + 16]
        ba = self.lnpa[:, idx * 32 + 16:idx * 32 + 32]
        for c in range(DC):
            s = c % 2
            self.tt(tmp[s][:], self.resid[:, c, :], rs[:], ALU.mult, [self.b_res[c], b_rs], [b_tmp[s]])
            self.tt(tmp[s][:], tmp[s][:], mt[:], ALU.add, [b_tmp[s], b_mt], [b_tmp[s]])
            self.act(self.actb[:, c, :], tmp[s][:], AF.Identity, [b_tmp[s], self.bconst], [self.b_act[c]],
                     scale=g[:, c:c + 1], bias=b[:, c:c + 1])
            sc, bi = (g, b) if final else (ga, ba)
            self.act(self.resid[:, c, :], tmp[s][:], AF.Identity, [b_tmp[s], self.bconst], [self.b_res[c]],
                     scale=sc[:, c:c + 1], bias=bi[:, c:c + 1])
        P.barrier()

    PROJ_BLOCKS = [("k", 1024, 512, 0), ("k", 1536, 512, 1), ("v", 2048, 512, 0), ("v", 2560, 512, 1),
                   ("cq", 3072, 512, 0), ("ckv", 3584, 512, 0), ("kr", 4096, 64, 0)]

    def proj_prefetch(self):
        P, A, i = self.P, self.A, self.i
        A.region(self.R3, self.R3 + 32768)
        self.wblk = [A.alloc([128, DC, 512], BF16) for _ in range(2)]
        self.b_wblk = P.bufs(2, "wblk")
        self.load_blk(0)
        self.load_blk(1)

    def load_blk(self, bi):
        kind, c0, nc_, _ = self.PROJ_BLOCKS[bi]
        s = bi % 2
        self.P.dma("pool", self.wblk[s][:, :, :nc_],
                   self.i["w_in"][:, c0:c0 + nc_].rearrange("(c p) n -> p c n", p=128),
                   f"wblk{s}", writes=[self.b_wblk[s]])

    def phase_proj(self):
        P, A, i, o = self.P, self.A, self.i, self.o
        w_in = i["w_in"]
        A.region(self.TOP - 34000, self.TOP)
        self.cqnT = A.alloc([128, 4, NT], BF16)
        self.KTs = A.alloc([128, 8, NS], BF16)
        self.NTs = A.alloc([128, 8, NS], BF16)
        self.krT = A.alloc([64, NT], BF16)
        self.Vs_new = A.alloc([32, 1024], BF16)
        self.VMs_new = A.alloc([32, 1024], BF16)
        self.wukv = A.alloc([128, 4, 2048], BF16)
        self.b_cqnT, self.b_KTs, self.b_NTs = P.buf("cqnT"), P.buf("KTs"), P.buf("NTs")
        self.b_krT, self.b_Vsn, self.b_VMsn, self.b_wukv = P.buf("krT"), P.buf("Vsn"), P.buf("VMsn"), P.buf("wukv")
        P.dma("pool", self.wukv[:], i["w_ukv"].rearrange("(c p) n -> p c n", p=128), "wukv", writes=[self.b_wukv])
        A.region(self.R2, self.TOP - 34000)
        wblk = self.wblk
        stg32 = [A.alloc([128, 512], F32) for _ in range(4)]
        stgb = [A.alloc([128, 512], BF16) for _ in range(4)]
        nrm = [A.alloc([128, 512], F32) for _ in range(2)]
        junk = A.alloc([128, 512], F32)
        ckvnT = A.alloc([128, 4, NT], BF16)
        KTl = [A.alloc([128, NT], BF16) for _ in range(3)]
        gcq = A.alloc([128, 512], F32)
        gckv = A.alloc([128, 512], F32)
        cos = A.alloc([128, 9, 32], F32)
        sin = A.alloc([128, 9, 32], F32)
        kro = [A.alloc([128, 64], F32) for _ in range(2)]
        rt = [A.alloc([128, 32], F32) for _ in range(4)]
        ssq = A.alloc([128, 4], F32)
        b_wblk = self.b_wblk
        b_stg32, b_stgb, b_nrm = P.bufs(4, "stg32"), P.bufs(4, "stgb"), P.bufs(2, "nrm")
        b_junk, b_ckvnT, b_KTl = P.buf("junk"), P.buf("ckvnT"), P.bufs(3, "KTl")
        b_tab, b_kro, b_rt, b_ssq = P.buf("tab"), P.bufs(2, "kro"), P.buf("rt"), P.buf("ssq")
        P.dma("sp", gcq[:], i["gcq"], "t0", writes=[b_tab])
        P.dma("sp", gckv[:], i["gckv"], "t1", writes=[b_tab])
        P.dma("sp", cos[:], i["cos_tm"], "t2", writes=[b_tab])
        P.dma("sp", sin[:], i["sin_tm"], "t3", writes=[b_tab])
        blocks = self.PROJ_BLOCKS
        cnt = {"s32": 0, "sb": 0, "nrm": 0, "bank": 0, "ktl": 0, "kro": 0}

        load_blk = self.load_blk

        def rmsnorm(ps, m, bkb, gtile):
            self.act(junk[:m, :], ps[:m, :], AF.Square, [bkb], [b_junk])
            self.P.op("dve", lambda e: e.reduce_sum(ssq[:m, 0:1], junk[:m, :], mybir.AxisListType.X),
                      [b_junk], [b_ssq])
            self.act(ssq[:m, 1:2], ssq[:m, 0:1], AF.Ln, [b_ssq, self.bconst], [b_ssq], scale=1.0 / 512,
                     bias=self.epsc[:m, 1:2])
            self.act(ssq[:m, 2:3], ssq[:m, 1:2], AF.Exp, [b_ssq], [b_ssq], scale=-0.5)
            s = cnt["nrm"] % 2
            cnt["nrm"] += 1
            self.stt(nrm[s][:m, :], ps[:m, :], ssq[:m, 2:3], gtile[:m, :], ALU.mult, ALU.mult,
                     [bkb, b_ssq, b_tab], [b_nrm[s]])
            return s

        def transposes_to(dst, b_dst, src, b_src, m, t0, nchunk, bank):
            ps = self.PS(bank)
            for k in range(nchunk):
                self.tr(ps[:, k * 128:k * 128 + m], src[:m, k * 128:(k + 1) * 128], self.ident[:m, :m],
                        [b_src, self.bconst], [self.bk[bank]])
            srcv = ps.rearrange("p (j t) -> p j t", j=4)[:, :nchunk, :m]
            self.cp(dst[:, 0:nchunk, t0:t0 + m], srcv, [self.bk[bank]], [b_dst])

        for bi, (kind, c0, nc_, half) in enumerate(blocks):
            s = bi % 2
            for ti, (t0, m) in enumerate(TM):
                bank = 6 + cnt["bank"] % 2
                cnt["bank"] += 1
                ps = self.PS(bank)
                bkb = self.bk[bank]
                for c in range(DC):
                    self.mm(ps[:m, :nc_], self.actb[:, c, t0:t0 + m], wblk[s][:, c, :nc_], c == 0, c == DC - 1,
                            [self.b_act[c], b_wblk[s]], [bkb])
                if kind in ("k", "v"):
                    s2 = cnt["s32"] % 4
                    cnt["s32"] += 1
                    self.act(stg32[s2][:m, :], ps[:m, :], AF.Identity, [bkb], [b_stg32[s2]])
                    dst = o["nk"] if kind == "k" else o["nv"]
                    P.dma("sp", dst[t0:t0 + m, half * 512:(half + 1) * 512], stg32[s2][:m, :], f"o32_{s2}",
                          reads=[b_stg32[s2]])
                    if kind == "v":
                        if m == 128:
                            s3 = cnt["sb"] % 4
                            cnt["sb"] += 1
                            self.cp(stgb[s3][:m, :], ps[:m, :], [bkb], [b_stgb[s3]])
                            ea, ej = self.exi(R_V + t0, m)
                            P.dma("sp", ea[:, half * 512:(half + 1) * 512], stgb[s3][:m, :],
                                  f"ob_{s3}", reads=[b_stgb[s3], self.b_exc[ej]])
                        else:
                            self.cp(self.Vs_new[:m, half * 512:(half + 1) * 512], ps[:m, :], [bkb], [self.b_Vsn])
                elif kind == "cq":
                    sn = rmsnorm(ps, m, bkb, gcq)
                    bank2 = 6 + cnt["bank"] % 2
                    cnt["bank"] += 1
                    transposes_to(self.cqnT, self.b_cqnT, nrm[sn], b_nrm[sn], m, t0, 4, bank2)
                elif kind == "ckv":
                    sn = rmsnorm(ps, m, bkb, gckv)
                    P.dma("sp", o["nckv"][t0:t0 + m, :], nrm[sn][:m, :], f"on_{sn}", reads=[b_nrm[sn]])
                    bank2 = 6 + cnt["bank"] % 2
                    cnt["bank"] += 1
                    transposes_to(ckvnT, b_ckvnT, nrm[sn], b_nrm[sn], m, t0, 4, bank2)
                else:
                    sk = cnt["kro"] % 2
                    cnt["kro"] += 1
                    x1, x2 = ps[:m, 0:32], ps[:m, 32:64]
                    cs, sn_ = cos[:m, ti, :], sin[:m, ti, :]
                    self.tt(rt[0][:m, :], x1, cs, ALU.mult, [bkb, b_tab], [b_rt])
                    self.tt(rt[1][:m, :], x2, sn_, ALU.mult, [bkb, b_tab], [b_rt])
                    self.tt(rt[2][:m, :], x2, cs, ALU.mult, [bkb, b_tab], [b_rt])
                    self.tt(rt[3][:m, :], x1, sn_, ALU.mult, [bkb, b_tab], [b_rt])
                    self.tt(kro[sk][:m, 0:32], rt[0][:m, :], rt[1][:m, :], ALU.subtract, [b_rt], [b_kro[sk]])
                    self.tt(kro[sk][:m, 32:64], rt[2][:m, :], rt[3][:m, :], ALU.add, [b_rt], [b_kro[sk]])
                    P.dma("sp", o["nkr"][t0:t0 + m, :], kro[sk][:m, :], f"okr_{sk}", reads=[b_kro[sk]])
                    bank2 = 6 + cnt["bank"] % 2
                    cnt["bank"] += 1
                    ps2 = self.PS(bank2)
                    self.tr(ps2[:64, :m], kro[sk][:m, 0:64], self.ident[:m, :m], [b_kro[sk], self.bconst],
                            [self.bk[bank2]])
                    self.cp(self.krT[:, t0:t0 + m], ps2[:64, :m], [self.bk[bank2]], [self.b_krT])
            if kind == "k":
                for hh in range(4):
                    h = half * 4 + hh
                    b0 = 0 if hh % 2 == 0 else 3
                    sl_ = cnt["ktl"] % 3
                    cnt["ktl"] += 1
                    for c in range(DC):
                        for (t0, n, bk) in TT:
                            self.mm(self.PS(b0 + bk)[:, :n], wblk[s][:, c, hh * 128:(hh + 1) * 128],
                                    self.actb[:, c, t0:t0 + n], c == 0, c == DC - 1,
                                    [b_wblk[s], self.b_act[c]], [self.bk[b0 + bk]])
                    for (t0, n, bk) in TT[:2]:
                        self.act(KTl[sl_][:, t0:t0 + n], self.PS(b0 + bk)[:, :n], AF.Identity,
                                 [self.bk[b0 + bk]], [b_KTl[sl_]])
                    self.cp(self.KTs[:, h, :], self.PS(b0 + 2)[:, :NS], [self.bk[b0 + 2]], [self.b_KTs])
                    ea, ej = self.exi(R_KT + h * 128, 128)
                    P.dma("sp", ea, KTl[sl_][:, 0:NP_], f"okt_{sl_}", reads=[b_KTl[sl_], self.b_exc[ej]])
            if bi + 2 < len(blocks):
                load_blk(bi + 2)
            if bi == 3:
                self.issue_cc([0, 1, 2, 3])
        ea, ej = self.exi(R_RT, 64)
        P.dma("sp", ea, self.krT[:, 0:NP_], "okrt", reads=[self.b_krT, self.b_exc[ej]])
        for h in range(H):
            b0 = 0 if h % 2 == 0 else 3
            sl_ = cnt["ktl"] % 3
            cnt["ktl"] += 1
            for kc in range(4):
                for (t0, n, bk) in TT:
                    self.mm(self.PS(b0 + bk)[:, :n], self.wukv[:, kc, h * 256:h * 256 + 128],
                            ckvnT[:, kc, t0:t0 + n], kc == 0, kc == 3, [self.b_wukv, b_ckvnT],
                            [self.bk[b0 + bk]])
            for (t0, n, bk) in TT[:2]:
                self.act(KTl[sl_][:, t0:t0 + n], self.PS(b0 + bk)[:, :n], AF.Identity, [self.bk[b0 + bk]],
                         [b_KTl[sl_]])
            self.cp(self.NTs[:, h, :], self.PS(b0 + 2)[:, :NS], [self.bk[b0 + 2]], [self.b_NTs])
            ea, ej = self.exi(R_NT + h * 128, 128)
            P.dma("sp", ea, KTl[sl_][:, 0:NP_], f"okt_{sl_}", reads=[b_KTl[sl_], self.b_exc[ej]])
        for ti, (t0, m) in enumerate(TM):
            for half in range(2):
                bank = 6 + cnt["bank"] % 2
                cnt["bank"] += 1
                ps = self.PS(bank)
                for hh in range(4):
                    hc = (half * 4 + hh) * 256 + 128
                    for kc in range(4):
                        self.mm(ps[:m, hh * 128:(hh + 1) * 128], ckvnT[:, kc, t0:t0 + m],
                                self.wukv[:, kc, hc:hc + 128], kc == 0, kc == 3,
                                [b_ckvnT, self.b_wukv], [self.bk[bank]])
                if m == 128:
                    s3 = cnt["sb"] % 4
                    cnt["sb"] += 1
                    self.cp(stgb[s3][:m, :], ps[:m, :], [self.bk[bank]], [b_stgb[s3]])
                    ea, ej = self.exi(R_VM + t0, m)
                    P.dma("sp", ea[:, half * 512:(half + 1) * 512], stgb[s3][:m, :],
                          f"ob_{s3}", reads=[b_stgb[s3], self.b_exc[ej]])
                else:
                    self.cp(self.VMs_new[:m, half * 512:(half + 1) * 512], ps[:m, :], [self.bk[bank]],
                            [self.b_VMsn])
        P.barrier()

    def issue_cc(self, js):
        P = self.P
        for j in js:
            def fn(e, j=j):
                return e.collective_compute("AllGather", ALU.bypass, replica_groups=[[0, 1, 2, 3], [4, 5, 6, 7]],
                                            ins=[self.ex_in[j].ap().opt()], outs=[self.ex_out[j].ap().opt()])
            P._add("pool", fn, [], [self.b_exc[j], self.b_exo[j]], semkey=f"cc{j}", inc=1)

    def attn_setup(self, alt=None):
        P, A = self.P, self.A
        w = {}
        w["e32"] = [A.alloc([128, 512], F32) for _ in range(2)]
        w["sp"] = [A.alloc([128, 512], BF16) for _ in range(3)]
        w["S"] = [A.alloc([128, 512], BF16) for _ in range(3)]
        w["A"] = [A.alloc([128, 512], BF16) for _ in range(3)]
        if alt is not None:
            save = (A.cur, A.lim)
            A.region(*alt)
        w["rec"] = A.alloc([128, 512], F32)
        w["acc"] = [A.alloc([128, 512], F32) for _ in range(2)]
        if alt is not None:
            self.alt_cur = A.cur
            A.cur, A.lim = save
        w["b_acc"] = P.bufs(2, "acc")
        w["b_e32"] = P.bufs(2, "e32")
        for k in ("sp", "S", "A"):
            w["b_" + k] = P.bufs(3, k)
        w["b_rec"] = P.buf("rec")
        w["cnt"] = 0
        return w

    def sb_problem(self, w, N, groups, tiles, obank, evac):
        negtri, negones = self.cm[:, 0, :], self.cm[:, 1, :]
        S, bS = w["S"], w["b_S"]
        for j in range(3):
            self.memset(S[j][:, :N], 0.0, [bS[j]])
        O = self.PS(obank)
        nt = len(tiles)
        base = w["cnt"]
        w["cnt"] += nt
        wbank = lambda k: (0, 1, 2, 5)[(base + k) % 4]

        def stA(k):
            t = tiles[k]
            nk, wb = t["nk"], wbank(k)
            cl = t.get("c_lo", 0)
            W = self.PS(wb)
            for gi, (c0, ncg, q, bq) in enumerate(groups):
                lo = max(c0, cl)
                self.mm(W[:nk, lo:c0 + ncg], t["kt"][gi], q[:, lo - c0:], gi == 0, False, t["reads"] + [bq],
                        [self.bk[wb]])
            if t["mask"] is not None:
                self.mm(W[:nk, cl:N], self.identb[:nk, :nk], t["mask"][:, cl:N], False, False,
                        [self.bconst, self.b_mask], [self.bk[wb]])
            s2 = (base + k) % 2
            self.act(w["e32"][s2][:nk, cl:N], W[:nk, cl:N], AF.Exp, [self.bk[wb]], [w["b_e32"][s2]])

        def stA2(k):
            t = tiles[k]
            nk, cl = t["nk"], t.get("c_lo", 0)
            s2, s3 = (base + k) % 2, (base + k) % 3
            self.act(w["sp"][s3][:nk, cl:N], w["e32"][s2][:nk, cl:N], AF.Ln, [w["b_e32"][s2], self.bconst],
                     [w["b_sp"][s3]], bias=self.epsc[:nk, 2:3])

        def stB(k):
            t = tiles[k]
            nk, wb = t["nk"], wbank(k)
            cl = t.get("c_lo", 0)
            W = self.PS(wb)
            s3 = (base + k) % 3
            first = k == 0
            self.mm(W[:nk, cl:N], negtri[:nk, :nk], w["sp"][s3][:nk, cl:N], False, first,
                    [self.bconst, w["b_sp"][s3]], [self.bk[wb]])
            if not first:
                self.mm(W[:nk, cl:N], negones[:, :nk], S[k % 3][:, cl:N], False, True,
                        [self.bconst, bS[k % 3]], [self.bk[wb]])
            self.act(w["A"][s3][:nk, cl:N], W[:nk, cl:N], AF.Exp, [self.bk[wb]], [w["b_A"][s3]])
            if k < nt - 1:
                nx = (k + 1) % 3
                if nk < 128:
                    self.tt(S[nx][:nk, cl:N], S[k % 3][:nk, cl:N], w["sp"][s3][:nk, cl:N], ALU.add,
                            [bS[k % 3], w["b_sp"][s3]], [bS[nx]])
                else:
                    self.tt(S[nx][:, cl:N], S[k % 3][:, cl:N], w["sp"][s3][:, cl:N], ALU.add,
                            [bS[k % 3], w["b_sp"][s3]], [bS[nx]])

        def stC(k):
            t = tiles[k]
            nk = t["nk"]
            s3 = (base + k) % 3
            cl = t.get("c_lo", 0)
            for gi, (c0, ncg, q, bq) in enumerate(groups):
                lo = max(c0, cl)
                self.mm(O[:, lo:c0 + ncg], t["v"][gi], w["A"][s3][:nk, lo:c0 + ncg], k == 0 and gi == 0,
                        k == nt - 1, t["reads"] + [w["b_A"][s3]], [self.bk[obank]])

        for it in range(nt + 3):
            if it < nt:
                stA(it)
            if 1 <= it <= nt:
                stA2(it - 1)
            if 2 <= it <= nt + 1:
                stB(it - 2)
            if 3 <= it:
                stC(it - 3)
        evac(O, obank)

    def mla_problem(self, w, N, groups, tiles, obank, dbank, evac):
        ones = self.cm[:, 2, :]
        O, Dn = self.PS(obank), self.PS(dbank)
        nt = len(tiles)
        base = w["cnt"]
        w["cnt"] += nt

        def stA(k):
            t = tiles[k]
            nk, zb = t["nk"], (base + k) % 3
            Z = self.PS(zb)
            s3 = (base + k) % 3
            cl = t.get("c_lo", 0)
            for gi, (c0, ncg, qn, qr, bq) in enumerate(groups):
                lo = max(c0, cl)
                self.mm(Z[:nk, lo:c0 + ncg], t["nt"][gi], qn[:, lo - c0:], gi == 0, False, t["reads"] + [bq],
                        [self.bk[zb]])
                self.mm(Z[:nk, lo:c0 + ncg], t["rt"], qr[:, lo - c0:], False,
                        t["mask"] is None and gi == len(groups) - 1, t["reads"] + [bq], [self.bk[zb]])
            if t["mask"] is not None:
                self.mm(Z[:nk, cl:N], self.identb[:nk, :nk], t["mask"][:, cl:N], False, True,
                        [self.bconst, self.b_mask], [self.bk[zb]])
            self.act(w["A"][s3][:nk, cl:N], Z[:nk, cl:N], AF.Exp, [self.bk[zb]], [w["b_A"][s3]])

        def stC(k):
            t = tiles[k]
            nk = t["nk"]
            s3 = (base + k) % 3
            first, last = k == 0, k == nt - 1
            cl = t.get("c_lo", 0)
            for gi, (c0, ncg, qn, qr, bq) in enumerate(groups):
                lo = max(c0, cl)
                self.mm(O[:, lo:c0 + ncg], t["vm"][gi], w["A"][s3][:nk, lo:c0 + ncg], first and gi == 0, last,
                        t["reads"] + [w["b_A"][s3]], [self.bk[obank]])
            a2 = k % 2
            self.tt(w["acc"][a2][:nk, cl:N], w["acc"][a2][:nk, cl:N], w["A"][s3][:nk, cl:N], ALU.add,
                    [w["b_acc"][a2], w["b_A"][s3]], [w["b_acc"][a2]])

        self.memset(w["acc"][0][:, :N], 0.0, [w["b_acc"][0]])
        self.memset(w["acc"][1][:, :N], 0.0, [w["b_acc"][1]])
        for it in range(nt + 1):
            if it < nt:
                stA(it)
            if it >= 1:
                stC(it - 1)
        self.mm(Dn[:, :N], self.ones32[:, :], w["acc"][0][:, :N], True, False, [self.bconst, w["b_acc"][0]],
                [self.bk[dbank]])
        self.mm(Dn[:, :N], self.ones32[:, :], w["acc"][1][:, :N], False, True, [self.bconst, w["b_acc"][1]],
                [self.bk[dbank]])
        self.P.op("dve", lambda e: e.reciprocal(w["rec"][:, :N], Dn[:, :N]), [self.bk[dbank]], [w["b_rec"]])
        evac(O, obank, w["rec"], w["b_rec"])

    def phase_sample_attn(self):
        P, A, i = self.P, self.A, self.i
        w_in = i["w_in"]
        A.region(self.R3, self.R3 + 34000)
        self.oT_sb = A.alloc([128, H, NT], BF16)
        self.oT_mla = A.alloc([128, H, NT], BF16)
        self.b_oT_sb, self.b_oT_mla = P.buf("oTsb"), P.buf("oTmla")
        A.region(self.R2, self.R3)
        w = self.attn_setup()
        self.aw = w
        Qss = A.alloc([128, H, NS], BF16)
        Qns = A.alloc([128, H, NS], BF16)
        Qrs = A.alloc([64, H, NS], BF16)
        Qrt = [A.alloc([64, NS], F32) for _ in range(2)]
        cosf = A.alloc([64, NS], F32)
        sinf = A.alloc([64, NS], F32)
        self.cosf, self.sinf = cosf, sinf
        self.b_tabf = P.buf("tabf")
        P.dma("sp", cosf[:], i["cos_fm"][:, NP_:NT], "t0", writes=[self.b_tabf])
        P.dma("sp", sinf[:], i["sin_fm"][:, NP_:NT], "t1", writes=[self.b_tabf])
        mbn = A.alloc([16, 128], BF16)
        mbn32 = A.alloc([16, 128], F32)
        b_mbn32 = P.buf("mbn32")
        self.b_mask = P.buf("mask")
        P.dma("sp", mbn32[:], i["mb_new"], "mb", writes=[b_mbn32])
        self.cp(mbn[:], mbn32[:], [b_mbn32], [self.b_mask])
        wq = None
        wuq = [A.alloc([128, 4, 256], BF16) for _ in range(2)]

        self.wq, self.wuq = wq, wuq
        self.b_wq, self.b_wuq = P.bufs(2, "wq"), P.bufs(2, "wuq")
        b_Qs = P.buf("Qs_s")
        b_Qrt = P.buf("Qrt")
        Vn1 = A.alloc([16, 1024], BF16)
        VMn1 = A.alloc([16, 1024], BF16)
        b_Vn1 = P.buf("Vn1")
        P.dma("sp", Vn1[:], self.Vs_new[16:32, :], "vn1", reads=[self.b_Vsn], writes=[b_Vn1])
        P.dma("sp", VMn1[:], self.VMs_new[16:32, :], "vmn1", reads=[self.b_VMsn], writes=[b_Vn1])
        self.wcount = 0
        wq2 = [A.alloc([128, DC, 256], BF16) for _ in range(2)]
        b_wq2 = P.bufs(2, "wq2")
        uqv = i["w_uq"].rearrange("(c p) n -> p c n", p=128)

        def load_pair(hp):
            sl2 = hp % 2
            P.dma("pool", wq2[sl2][:], w_in[:, hp * 256:(hp + 1) * 256].rearrange("(c p) n -> p c n", p=128),
                  f"wq2{sl2}", writes=[b_wq2[sl2]])

        def load_wuq(h):
            sl2 = h % 2
            b0 = h * 192
            P.dma("pool", wuq[sl2][:, :, 0:192], uqv[:, :, b0:b0 + 192], f"wuqa{sl2}", writes=[self.b_wuq[sl2]])
            P.dma("pool", wuq[sl2][:, :, 192:224], uqv[:, :, b0 + 160:b0 + 192], f"wuqb{sl2}",
                  writes=[self.b_wuq[sl2]])
            P.dma("pool", wuq[sl2][:, :, 224:256], uqv[:, :, b0 + 128:b0 + 160], f"wuqc{sl2}",
                  writes=[self.b_wuq[sl2]])
            return sl2

        load_pair(0)
        load_pair(1)
        for h in range(H):
            s = load_wuq(h)
            sp2, off = (h // 2) % 2, (h % 2) * 128
            bA, bB = (6, 7) if h % 2 == 0 else (3, 4)
            ps = self.PS(bA)
            for c in range(DC):
                self.mm(ps[:, :NS], wq2[sp2][:, c, off:off + 128], self.actb[:, c, NP_:NT], c == 0, c == DC - 1,
                        [b_wq2[sp2], self.b_act[c]], [self.bk[bA]])
            if h % 2 == 1 and h // 2 + 2 < 4:
                load_pair(h // 2 + 2)
            self.act(Qss[:, h, :], ps[:, :NS], AF.Identity, [self.bk[bA]], [b_Qs], scale=SB_SCALE)
            ps = self.PS(bB)
            for kc in range(4):
                self.mm(ps[:, :NS], wuq[s][:, kc, 0:128], self.cqnT[:, kc, NP_:NT], kc == 0, kc == 3,
                        [self.b_wuq[s], self.b_cqnT], [self.bk[bB]])
            self.act(Qns[:, h, :], ps[:, :NS], AF.Identity, [self.bk[bB]], [b_Qs], scale=MLA_SCALE)
            ps = self.PS(bA)
            for kc in range(4):
                self.mm(ps[:64, :NS], wuq[s][:, kc, 128:192], self.cqnT[:, kc, NP_:NT], kc == 0, kc == 3,
                        [self.b_wuq[s], self.b_cqnT], [self.bk[bA]])
            for kc in range(4):
                self.mm(ps[:64, 64:64 + NS], wuq[s][:, kc, 192:256], self.cqnT[:, kc, NP_:NT], kc == 0, kc == 3,
                        [self.b_wuq[s], self.b_cqnT], [self.bk[bA]])
            self.tt(Qrt[0][:, :], ps[:64, :NS], cosf[:, :], ALU.mult, [self.bk[bA], self.b_tabf], [b_Qrt])
            self.tt(Qrt[1][:, :], ps[:64, 64:64 + NS], sinf[:, :], ALU.mult, [self.bk[bA], self.b_tabf], [b_Qrt])
            self.tt(Qrt[0][:, :], Qrt[0][:, :], Qrt[1][:, :], ALU.add, [b_Qrt], [b_Qrt])
            self.act(Qrs[:, h, :], Qrt[0][:, :], AF.Identity, [b_Qrt], [b_Qs], scale=MLA_SCALE)
        r2cur = A.cur
        A.region(self.R3 + 34000, self.TOP - 34000)
        KTc = A.alloc([128, H, PAST], BF16)
        Vc = A.alloc([128, 8, 1024], BF16)
        A.region(r2cur, self.R3)
        ckvTc = A.alloc([128, 4, PAST], BF16)
        krTc = A.alloc([64, PAST], BF16)
        kst = [A.alloc([128, 1024], F32) for _ in range(2)]
        krst = A.alloc([128, 8, 64], F32)
        b_KTc, b_Vc, b_ckvTc, b_krTc = P.buf("KTc"), P.buf("Vc"), P.buf("ckvTc"), P.buf("krTc")
        b_kst, b_krst = P.bufs(2, "kst"), P.buf("krst")
        kcnt = 0
        for sq in range(2):
            qsl = slice(sq * 16, sq * 16 + 16)
            for kt in range(8):
                s = kcnt % 2
                kcnt += 1
                P.dma("sp", kst[s][:, :], i["c_k"][sq, kt * 128:(kt + 1) * 128, :], f"kst{s}", writes=[b_kst[s]])
                for hg in range(2):
                    bank = 6 + hg
                    ps = self.PS(bank)
                    for j in range(4):
                        h = hg * 4 + j
                        self.tr(ps[:, j * 128:(j + 1) * 128], kst[s][:, h * 128:(h + 1) * 128], self.ident[:, :],
                                [b_kst[s], self.bconst], [self.bk[bank]])
                    self.cp(KTc[:, hg * 4:hg * 4 + 4, kt * 128:(kt + 1) * 128],
                            ps.rearrange("p (j t) -> p j t", j=4), [self.bk[bank]], [b_KTc])
            P.dma("pool", Vc[:], i["c_v"][sq].rearrange("(k p) c -> p k c", p=128), "vc", writes=[b_Vc])
            groups = [(h * 16, 16, Qss[:, h, qsl], b_Qs) for h in range(H)]
            vnew = self.Vs_new if sq == 0 else Vn1
            b_vnew = self.b_Vsn if sq == 0 else b_Vn1
            tiles = [dict(nk=16, kt=[self.KTs[:, h, qsl] for h in range(H)],
                          v=[vnew[:16, h * 128:(h + 1) * 128] for h in range(H)], mask=mbn[:, :],
                          reads=[self.b_KTs, b_vnew])]
            for kt in range(7, -1, -1):
                tiles.append(dict(nk=128, kt=[KTc[:, h, kt * 128:(kt + 1) * 128] for h in range(H)],
                                  v=[Vc[:, kt, h * 128:(h + 1) * 128] for h in range(H)], mask=None,
                                  reads=[b_KTc, b_Vc]))
            c0 = NP_ + sq * 16

            def evac_sb(O, obank, c0=c0):
                self.cp(self.oT_sb[:, :, c0:c0 + 16], O[:, :128].rearrange("p (h q) -> p h q", q=16),
                        [self.bk[obank]], [self.b_oT_sb])
            self.sb_problem(w, 128, groups, tiles, 3, evac_sb)
            NTc, VMc = KTc, Vc
            for kt in range(8):
                s = kcnt % 2
                kcnt += 1
                P.dma("sp", kst[s][:, :512], i["c_ckv"][sq, kt * 128:(kt + 1) * 128, :], f"kst{s}",
                      writes=[b_kst[s]])
                bank = 6 + kt % 2
                ps = self.PS(bank)
                for j in range(4):
                    self.tr(ps[:, j * 128:(j + 1) * 128], kst[s][:, j * 128:(j + 1) * 128], self.ident[:, :],
                            [b_kst[s], self.bconst], [self.bk[bank]])
                self.cp(ckvTc[:, :, kt * 128:(kt + 1) * 128], ps.rearrange("p (j t) -> p j t", j=4),
                        [self.bk[bank]], [b_ckvTc])
            P.dma("sp", krst[:], i["c_kr"][sq].rearrange("(k p) c -> p k c", p=128), "krst", writes=[b_krst])
            for kg in range(2):
                bank = 6 + kg
                ps = self.PS(bank)
                for j in range(4):
                    kt = kg * 4 + j
                    self.tr(ps[:64, j * 128:(j + 1) * 128], krst[:, kt, :], self.ident[:, :],
                            [b_krst, self.bconst], [self.bk[bank]])
                self.cp(krTc[:, kg * 512:(kg + 1) * 512], ps[:64, :], [self.bk[bank]], [b_krTc])
            for h in range(H):
                for half in range(2):
                    bank = 6 + (h * 2 + half) % 2
                    ps = self.PS(bank)
                    for kc in range(4):
                        self.mm(ps[:, :], self.wukv[:, kc, h * 256:h * 256 + 128],
                                ckvTc[:, kc, half * 512:(half + 1) * 512], kc == 0, kc == 3,
                                [self.b_wukv, b_ckvTc], [self.bk[bank]])
                    self.act(NTc[:, h, half * 512:(half + 1) * 512], ps[:, :], AF.Identity, [self.bk[bank]],
                             [b_KTc])
            for kt in range(8):
                for half in range(2):
                    bank = 6 + (kt * 2 + half) % 2
                    ps = self.PS(bank)
                    for hh in range(4):
                        hc = (half * 4 + hh) * 256 + 128
                        for kc in range(4):
                            self.mm(ps[:, hh * 128:(hh + 1) * 128], ckvTc[:, kc, kt * 128:(kt + 1) * 128],
                                    self.wukv[:, kc, hc:hc + 128], kc == 0, kc == 3,
                                    [b_ckvTc, self.b_wukv], [self.bk[bank]])
                    self.cp(VMc[:, kt, half * 512:(half + 1) * 512], ps[:, :], [self.bk[bank]], [b_Vc])
            groups = [(h * 16, 16, Qns[:, h, qsl], Qrs[:, h, qsl], b_Qs) for h in range(H)]
            vmnew = self.VMs_new if sq == 0 else VMn1
            b_vmnew = self.b_VMsn if sq == 0 else b_Vn1
            tiles = [dict(nk=16, nt=[self.NTs[:, h, qsl] for h in range(H)], rt=self.krT[:, c0:c0 + 16],
                          vm=[vmnew[:16, h * 128:(h + 1) * 128] for h in range(H)], mask=None,
                          reads=[self.b_NTs, self.b_krT, b_vmnew])]
            for kt in range(7, -1, -1):
                tiles.append(dict(nk=128, nt=[NTc[:, h, kt * 128:(kt + 1) * 128] for h in range(H)],
                                  rt=krTc[:, kt * 128:(kt + 1) * 128],
                                  vm=[VMc[:, kt, h * 128:(h + 1) * 128] for h in range(H)], mask=None,
                                  reads=[b_KTc, b_krTc, b_Vc]))

            def evac_mla(O, obank, rec, b_rec, c0=c0):
                self.tt(self.oT_mla[:, :, c0:c0 + 16], O[:, :128].rearrange("p (h q) -> p h q", q=16),
                        rec[:, :128].rearrange("p (h q) -> p h q", q=16), ALU.mult,
                        [self.bk[obank], b_rec], [self.b_oT_mla])
            self.mla_problem(w, 128, groups, tiles, 3, 5, evac_mla)
        P.barrier()

    def load_wq(self, h):
        P, i = self.P, self.i
        s = self.wcount % 2
        self.wcount += 1
        P.dma("pool", self.wq[s][:], i["w_in"][:, h * 128:(h + 1) * 128].rearrange("(c p) n -> p c n", p=128),
              f"wq{s}", writes=[self.b_wq[s]])
        uq = i["w_uq"].rearrange("(c p) n -> p c n", p=128)
        b0 = h * 192
        P.dma("pool", self.wuq[s][:, :, 0:192], uq[:, :, b0:b0 + 192], f"wuqa{s}", writes=[self.b_wuq[s]])
        P.dma("pool", self.wuq[s][:, :, 192:224], uq[:, :, b0 + 160:b0 + 192], f"wuqb{s}", writes=[self.b_wuq[s]])
        P.dma("pool", self.wuq[s][:, :, 224:256], uq[:, :, b0 + 128:b0 + 160], f"wuqc{s}", writes=[self.b_wuq[s]])
        return s

    def phase_prompt_attn(self):
        P, A, i = self.P, self.A, self.i
        A.region(self.R2, self.R3)
        w = self.attn_setup(alt=(self.TOP - 34000 + 8448, self.TOP))
        mask = A.alloc([128, 16, 512], BF16)
        self.b_mask = P.buf("mask2")
        Kg = [A.alloc([128, 4096], BF16) for _ in range(2)]
        Vg = [A.alloc([128, 32, 128], BF16) for _ in range(2)]
        A.region(self.R3 + 34000, self.TOP - 34000)
        Qs = [A.alloc([128, NP_], BF16) for _ in range(2)]
        Rg = A.alloc([64, 4096], BF16)
        Qr = [A.alloc([64, NP_], BF16) for _ in range(2)]
        Qrt = [A.alloc([64, 512], F32) for _ in range(2)]
        wq = [A.alloc([128, DC, 128], BF16) for _ in range(2)]
        wuq = [A.alloc([128, 4, 256], BF16) for _ in range(2)]
        self.wq, self.wuq = wq, wuq
        self.b_wq, self.b_wuq = P.bufs(2, "wq2"), P.bufs(2, "wuq2")
        b_Kg, b_Vg, b_Rg = P.bufs(2, "Kg"), P.bufs(2, "Vg"), P.buf("Rg")
        b_Qs, b_Qr, b_Qrt = P.bufs(2, "Qs"), P.bufs(2, "Qr"), P.buf("Qrt2")
        A.region(self.alt_cur, self.TOP)
        cosf = A.alloc([64, NP_], F32)
        sinf = A.alloc([64, NP_], F32)
        b_tabf = P.buf("tabf2")
        P.dma("sp", cosf[:], i["cos_fm"][:, 0:NP_], "t0", writes=[b_tabf])
        P.dma("sp", sinf[:], i["sin_fm"][:, 0:NP_], "t1", writes=[b_tabf])

        def load_kv(slot, row0, h):
            pairs, rds = [], []
            for r in range(4):
                ea, ej = self.exo(r, row0 + h * 128, 128)
                pairs.append((Kg[slot][:, r * 1024:(r + 1) * 1024], ea))
                rds.append(self.b_exo[ej])
            P.dma_batch("sp", pairs, f"kg{slot}", reads=rds, writes=[b_Kg[slot]])
            vrow = R_V if row0 == R_KT else R_VM
            pairs, rds = [], []
            for r in range(4):
                for hf in range(2):
                    ea, ej = self.exo(r, vrow + hf * 512, 512)
                    pairs.append((Vg[slot][:, r * 8 + hf * 4:r * 8 + hf * 4 + 4, :],
                                  ea[:, h * 128:(h + 1) * 128].rearrange("(m p) d -> p m d", p=128)))
                    rds.append(self.b_exo[ej])
            P.dma_batch("sp", pairs, f"vg{slot}", reads=rds, writes=[b_Vg[slot]])

        def key_tiles(M, slot, mla):
            tl = []
            for kt in range(16 * M + 15, -1, -1):
                r, m = kt % 4, kt // 4
                col = r * 1024 + m * 128
                mk = mask[:, kt - 16 * M, :] if kt >= 16 * M else None
                d = dict(nk=128, mask=mk, reads=[b_Kg[slot], b_Vg[slot]] + ([b_Rg] if mla else []))
                if kt >= 16 * M:
                    d["c_lo"] = 128 * max(0, (kt - 16 * M - 3 + 3) // 4)
                if mla:
                    d["nt"] = [Kg[slot][:, col:col + 128]]
                    d["rt"] = Rg[:, col:col + 128]
                    d["vm"] = [Vg[slot][:, r * 8 + m, :]]
                else:
                    d["kt"] = [Kg[slot][:, col:col + 128]]
                    d["v"] = [Vg[slot][:, r * 8 + m, :]]
                tl.append(d)
            return tl

        self.wcount = 0
        pcount = 0
        P.dma("pool", mask[:], i["mb_sb"], "mb2", writes=[self.b_mask])
        wslots = [self.load_wq(0), self.load_wq(1)]
        self.issue_cc([4, 5, 6, 7, 8])

        def qproj_sb(h):
            slot, s = h % 2, wslots[h]
            for half in range(2):
                bank = 6 + half
                ps = self.PS(bank)
                for c in range(DC):
                    self.mm(ps[:, :], wq[s][:, c, :], self.actb[:, c, half * 512:(half + 1) * 512], c == 0,
                            c == DC - 1, [self.b_wq[s], self.b_act[c]], [self.bk[bank]])
                self.act(Qs[slot][:, half * 512:(half + 1) * 512], ps[:, :], AF.Identity, [self.bk[bank]],
                         [b_Qs[slot]], scale=SB_SCALE)
            if h + 2 < H:
                wslots.append(self.load_wq(h + 2))

        load_kv(0, R_KT, 0)
        qproj_sb(0)
        for h in range(H):
            slot = h % 2
            for M in range(2):
                groups = [(0, 512, Qs[slot][:, M * 512:(M + 1) * 512], b_Qs[slot])]
                obank = 3 + pcount % 2
                pcount += 1

                def evac(O, ob, h=h, M=M):
                    self.act(self.oT_sb[:, h, M * 512:(M + 1) * 512], O[:, :], AF.Identity, [self.bk[ob]],
                             [self.b_oT_sb])
                self.sb_problem(w, 512, groups, key_tiles(M, slot, False), obank, evac)
                if M == 0 and h + 1 < H:
                    load_kv((h + 1) % 2, R_KT, h + 1)
                    qproj_sb(h + 1)
        P.dma("pool", mask[:], i["mb_mla"], "mb2", writes=[self.b_mask])
        for r in range(4):
            ea, ej = self.exo(r, R_RT, 64)
            P.dma("sp", Rg[:, r * 1024:(r + 1) * 1024], ea, "rg", reads=[self.b_exo[ej]],
                  writes=[b_Rg])
        mslots = [self.load_wq(0), self.load_wq(1)]

        def qproj_mla(h):
            slot, s = h % 2, mslots[h]
            for half in range(2):
                tsl = slice(half * 512, (half + 1) * 512)
                bank = 6 + half
                ps = self.PS(bank)
                for kc in range(4):
                    self.mm(ps[:, :], wuq[s][:, kc, 0:128], self.cqnT[:, kc, tsl], kc == 0, kc == 3,
                            [self.b_wuq[s], self.b_cqnT], [self.bk[bank]])
                self.act(Qs[slot][:, tsl], ps[:, :], AF.Identity, [self.bk[bank]], [b_Qs[slot]], scale=MLA_SCALE)
            for half in range(2):
                tsl = slice(half * 512, (half + 1) * 512)
                for part, bank in ((0, 6), (1, 7)):
                    ps = self.PS(bank)
                    for kc in range(4):
                        self.mm(ps[:64, :], wuq[s][:, kc, 128 + part * 64:192 + part * 64], self.cqnT[:, kc, tsl],
                                kc == 0, kc == 3, [self.b_wuq[s], self.b_cqnT], [self.bk[bank]])
                self.tt(Qrt[0][:, :], self.PS(6)[:64, :], cosf[:, tsl], ALU.mult, [self.bk[6], b_tabf], [b_Qrt])
                self.tt(Qrt[1][:, :], self.PS(7)[:64, :], sinf[:, tsl], ALU.mult, [self.bk[7], b_tabf], [b_Qrt])
                self.tt(Qrt[0][:, :], Qrt[0][:, :], Qrt[1][:, :], ALU.add, [b_Qrt], [b_Qrt])
                self.act(Qr[slot][:, tsl], Qrt[0][:, :], AF.Identity, [b_Qrt], [b_Qr[slot]], scale=MLA_SCALE)
            if h + 2 < H:
                mslots.append(self.load_wq(h + 2))

        load_kv(0, R_NT, 0)
        qproj_mla(0)
        for h in range(H):
            slot = h % 2
            for M in range(2):
                groups = [(0, 512, Qs[slot][:, M * 512:(M + 1) * 512], Qr[slot][:, M * 512:(M + 1) * 512],
                           b_Qs[slot])]
                obank = 3 + pcount % 2
                dbank = 5
                pcount += 1

                def evac(O, ob, rec, b_rec, h=h, M=M):
                    self.tt(self.oT_mla[:, h, M * 512:(M + 1) * 512], O[:, :], rec[:, :], ALU.mult,
                            [self.bk[ob], b_rec], [self.b_oT_mla])
                tl = key_tiles(M, slot, True)
                for t in tl:
                    t["reads"] = t["reads"] + [b_Qr[slot]]
                self.mla_problem(w, 512, groups, tl, obank, dbank, evac)
                if M == 0 and h + 1 < H:
                    load_kv((h + 1) % 2, R_NT, h + 1)
                    qproj_mla(h + 1)
        P.barrier()

    def phase_merge(self):
        P, A, i = self.P, self.A, self.i
        w_in = i["w_in"]
        A.region(self.R3 + 34000, self.R3 + 34000 + 34000)
        mg = A.alloc([128, DC, NT], BF16)
        b_mg = P.bufs(DC, "mg")
        A.region(self.R2, self.R3)
        wg = [A.alloc([128, DC, 256], BF16) for _ in range(2)]
        wb = [A.alloc([128, 16, 128], BF16) for _ in range(2)]
        gs = A.alloc([128, NT], F32)
        gm = A.alloc([128, NT], F32)
        t1 = A.alloc([128, NT], F32)
        t2 = A.alloc([128, NT], F32)
        b_wg, b_wb = P.bufs(2, "wg"), P.bufs(2, "wb")
        b_gs, b_gm, b_t1, b_t2 = P.bufs(3, "gs"), P.bufs(3, "gm"), P.bufs(3, "t1"), P.bufs(3, "t2")

        def load(ci):
            s = ci % 2
            P.dma("pool", wg[s][:, :, 0:128],
                  w_in[:, 4160 + ci * 128:4160 + (ci + 1) * 128].rearrange("(c p) n -> p c n", p=128),
                  f"wga{s}", writes=[b_wg[s]])
            P.dma("pool", wg[s][:, :, 128:256],
                  w_in[:, 6208 + ci * 128:6208 + (ci + 1) * 128].rearrange("(c p) n -> p c n", p=128),
                  f"wgb{s}", writes=[b_wg[s]])
            P.dma("pool", wb[s][:, 0:8, :],
                  i["w_br_sb"][:, ci * 128:(ci + 1) * 128].rearrange("(c p) n -> p c n", p=128),
                  f"wba{s}", writes=[b_wb[s]])
            P.dma("pool", wb[s][:, 8:16, :],
                  i["w_br_mla"][:, ci * 128:(ci + 1) * 128].rearrange("(c p) n -> p c n", p=128),
                  f"wbb{s}", writes=[b_wb[s]])

        load(0)
        load(1)
        for ci in range(DC):
            s = ci % 2
            for (b0, off, gt, b_g, bidx) in ((0, 0, gs, b_gs, ci), (3, 128, gm, b_gm, 16 + ci)):
                for c in range(DC):
                    for (t0, n, bk) in TT:
                        self.mm(self.PS(b0 + bk)[:, :n], wg[s][:, c, off:off + 128], self.actb[:, c, t0:t0 + n],
                                c == 0, c == DC - 1, [b_wg[s], self.b_act[c]], [self.bk[b0 + bk]])
                for (t0, n, bk) in TT:
                    self.act(gt[:, t0:t0 + n], self.PS(b0 + bk)[:, :n], AF.Sigmoid, [self.bk[b0 + bk], self.bconst],
                             [b_g[bk]], bias=self.bg[:, bidx:bidx + 1])
            for (b0, r0, oT, b_oT) in ((0, 0, self.oT_sb, self.b_oT_sb), (3, 8, self.oT_mla, self.b_oT_mla)):
                for h in range(H):
                    for (t0, n, bk) in TT:
                        self.mm(self.PS(b0 + bk)[:, :n], wb[s][:, r0 + h, :], oT[:, h, t0:t0 + n], h == 0, h == H - 1,
                                [b_wb[s], b_oT], [self.bk[b0 + bk]])
            for (t0, n, bk) in TT:
                self.tt(t1[:, t0:t0 + n], gs[:, t0:t0 + n], self.PS(bk)[:, :n], ALU.mult,
                        [b_gs[bk], self.bk[bk]], [b_t1[bk]])
                self.tt(t2[:, t0:t0 + n], gm[:, t0:t0 + n], self.PS(3 + bk)[:, :n], ALU.mult,
                        [b_gm[bk], self.bk[3 + bk]], [b_t2[bk]])
                self.tt(mg[:, ci, t0:t0 + n], t1[:, t0:t0 + n], t2[:, t0:t0 + n], ALU.add,
                        [b_t1[bk], b_t2[bk]], [b_mg[ci]])
            if ci + 2 < DC:
                load(ci + 2)
        P.barrier()
        P.dma("sp", self.resid[:], self.spill.ap(), "spill", writes=self.b_res)
        A.region(self.R3, self.R3 + 34000)
        wo = [A.alloc([128, DC, 512], BF16) for _ in range(2)]
        b_wo = P.bufs(2, "wo")

        def load_wo(q):
            s = q % 2
            P.dma("pool", wo[s][:], i["w_o"][:, q * 512:(q + 1) * 512].rearrange("(c p) n -> p c n", p=128),
                  f"wo{s}", writes=[b_wo[s]])
        load_wo(0)
        load_wo(1)
        for q in range(4):
            s = q % 2
            for j in range(4):
                ci = q * 4 + j
                b0 = 0 if ci % 2 == 0 else 3
                for c in range(DC):
                    for (t0, n, bk) in TT:
                        self.mm(self.PS(b0 + bk)[:, :n], wo[s][:, c, j * 128:(j + 1) * 128], mg[:, c, t0:t0 + n],
                                c == 0, c == DC - 1, [b_wo[s], b_mg[c]], [self.bk[b0 + bk]])
                for (t0, n, bk) in TT:
                    self.tt(self.resid[:, ci, t0:t0 + n], self.resid[:, ci, t0:t0 + n], self.PS(b0 + bk)[:, :n],
                            ALU.add, [self.b_res[ci], self.bk[b0 + bk]], [self.b_res[ci]])
            if q + 2 < 4:
                load_wo(q + 2)
        P.barrier()

    def phase_out(self):
        P, A, o = self.P, self.A, self.o
        A.region(self.R3, self.TOP)
        ys = [A.alloc([128, D], F32) for _ in range(2)]
        b_ys = P.bufs(2, "ys")
        k = 0
        for ti, (t0, m) in enumerate(TM):
            s = ti % 2
            for q in range(4):
                b = 4 + k % 4
                k += 1
                ps = self.PS(b)
                for j in range(4):
                    c = q * 4 + j
                    self.tr(ps[:m, j * 128:(j + 1) * 128], self.resid[:, c, t0:t0 + m], self.ident[:, :],
                            [self.b_res[c], self.bconst], [self.bk[b]])
                if k % 2 == 0:
                    self.act(ys[s][:m, q * 512:(q + 1) * 512], ps[:m, :], AF.Identity, [self.bk[b]], [b_ys[s]])
                else:
                    self.cp(ys[s][:m, q * 512:(q + 1) * 512], ps[:m, :], [self.bk[b]], [b_ys[s]])
            P.dma("sp", o["y"][t0:t0 + m, :], ys[s][:m, :], f"y{s}", reads=[b_ys[s]])

    def dump_act(self):
        dbg = self.nc.dram_tensor("dbg", [128, DC, NT], F32, kind="ExternalOutput").ap()
        self.P.dma("sp", dbg, self.resid[:], "dbg", reads=self.b_res)


def build_nc(stop_after=None, **kw):
    b = Builder(stop_after=stop_after, **kw)
    b.nc._used_inputs = list(b.i.keys())
    return b.nc


def host_inputs(inp):
    f = np.float32
    xp = np.asarray(inp["x_prompt"], f)
    xs = np.asarray(inp["x_sample"], f)
    common = {
        "w1a": np.ascontiguousarray(inp["ffn1_w_in"][0]), "w2a": np.ascontiguousarray(inp["ffn1_w_out"][0]),
        "w1b": np.ascontiguousarray(inp["ffn2_w_in"][0]), "w2b": np.ascontiguousarray(inp["ffn2_w_out"][0]),
        "w_in": np.ascontiguousarray(inp["w_in"][0]), "w_uq": np.ascontiguousarray(inp["w_uq"][0]),
        "w_ukv": np.ascontiguousarray(inp["w_ukv"][0]), "w_br_sb": np.ascontiguousarray(inp["w_br_sb"][0]),
        "w_br_mla": np.ascontiguousarray(inp["w_br_mla"][0]), "w_o": np.ascontiguousarray(inp["w_o"][0]),
    }
    lnp = np.concatenate([np.asarray(inp[k][0], f).reshape(16, 128).T for k in
                          ("ln1_g", "ln1_b", "ln2_g", "ln2_b", "ln3_g", "ln3_b")], axis=1)
    common["lnp"] = np.ascontiguousarray(lnp)
    common["bgate"] = np.ascontiguousarray(np.asarray(inp["b_gate"][0], f).reshape(32, 128).T)
    common["gcq"] = np.ascontiguousarray(np.broadcast_to(np.asarray(inp["g_cq"][0], f), (128, 512)))
    common["gckv"] = np.ascontiguousarray(np.broadcast_to(np.asarray(inp["g_ckv"][0], f), (128, 512)))
    common["ident"] = np.eye(128, dtype=f)
    cm = np.zeros((128, 4, 128), f)
    jj, ss = np.meshgrid(np.arange(128), np.arange(128), indexing="ij")
    cm[:, 0, :] = -(jj >= ss).astype(f)
    cm[:, 1, :] = -1.0
    cm[:, 2, :] = 1.0
    common["cmats"] = cm
    kk, tq = np.meshgrid(np.arange(16), np.arange(16), indexing="ij")
    common["mb_new"] = np.ascontiguousarray(np.tile(np.where(kk < tq, 0.0, NEG).astype(f), (1, 8)))
    maps = []
    inv_freq = (10000.0 ** (-np.arange(0, 64, 2, dtype=np.float32) / 64)).astype(f)
    for c in range(8):
        b, r = c // 4, c % 4
        tiles = [4 * m + r for m in range(8)]
        rows = np.concatenate([np.arange(g * 128, (g + 1) * 128) for g in tiles])
        x = np.concatenate([xp[b, rows], xs[2 * c], xs[2 * c + 1]], 0)
        pos = np.concatenate([rows, PAST + np.arange(16), PAST + np.arange(16)]).astype(f)
        ang = pos[:, None] * inv_freq[None, :]
        cos, sin = np.cos(ang).astype(f), np.sin(ang).astype(f)
        cpad = np.zeros((9 * 128, 32), f)
        spad = np.zeros((9 * 128, 32), f)
        cpad[:NT] = cos
        spad[:NT] = sin
        m = dict(common)
        m["x"] = np.ascontiguousarray(x)
        m["cos_tm"] = np.ascontiguousarray(cpad.reshape(9, 128, 32).transpose(1, 0, 2))
        m["sin_tm"] = np.ascontiguousarray(spad.reshape(9, 128, 32).transpose(1, 0, 2))
        m["cos_fm"] = np.ascontiguousarray(np.concatenate([cos.T, cos.T], 0))
        m["sin_fm"] = np.ascontiguousarray(np.concatenate([-sin.T, sin.T], 0))
        k_, q_ = np.meshgrid(np.arange(128), np.arange(128), indexing="ij")
        msb = np.zeros((128, 16, 512), f)
        mml = np.zeros((128, 16, 512), f)
        for o in range(16):
            for i4 in range(4):
                qt = 4 * i4 + r
                sl = slice(i4 * 128, (i4 + 1) * 128)
                if o > qt:
                    msb[:, o, sl] = NEG
                    mml[:, o, sl] = NEG
                elif o == qt:
                    msb[:, o, sl] = np.where(k_ < q_, 0.0, NEG)
                    mml[:, o, sl] = np.where((k_ // 64) <= (q_ // 64), 0.0, NEG)
        m["mb_sb"] = msb
        m["mb_mla"] = mml
        m["c_k"] = np.ascontiguousarray(np.asarray(inp["cache_sb_k"][0, 2 * c:2 * c + 2], f).reshape(2, PAST, 1024))
        m["c_v"] = np.ascontiguousarray(np.asarray(inp["cache_sb_v"][0, 2 * c:2 * c + 2], f).reshape(2, PAST, 1024))
        m["c_ckv"] = np.ascontiguousarray(np.asarray(inp["cache_mla_ckv"][0, 2 * c:2 * c + 2], f))
        m["c_kr"] = np.ascontiguousarray(np.asarray(inp["cache_mla_krope"][0, 2 * c:2 * c + 2], f))
        maps.append(m)
    return maps


_NC = None


def kernel(**inputs):
    global _NC
    maps = host_inputs(inputs)
    if _NC is None:
        _NC = build_nc()
    used = _NC._used_inputs
    maps = [{k: m[k] for k in used} for m in maps]
    res = run_bass_kernel_spmd(_NC, maps, core_ids=list(range(8)))
    return assemble(res.results)


def assemble(results):
    f = np.float32
    y_p = np.zeros((2, 4096, D), f)
    y_s = np.zeros((16, 16, D), f)
    nk_p = np.zeros((1, 2, 4096, 8, 128), f)
    nv_p = np.zeros((1, 2, 4096, 8, 128), f)
    nc_p = np.zeros((1, 2, 4096, 512), f)
    nr_p = np.zeros((1, 2, 4096, 64), f)
    nk_s = np.zeros((1, 16, 16, 8, 128), f)
    nv_s = np.zeros((1, 16, 16, 8, 128), f)
    nc_s = np.zeros((1, 16, 16, 512), f)
    nr_s = np.zeros((1, 16, 16, 64), f)
    for c in range(8):
        b, r = c // 4, c % 4
        R = results[c]
        for m in range(8):
            g = 4 * m + r
            dst = slice(g * 128, (g + 1) * 128)
            src = slice(m * 128, (m + 1) * 128)
            y_p[b, dst] = R["y"][src]
            nk_p[0, b, dst] = R["nk"][src].reshape(128, 8, 128)
            nv_p[0, b, dst] = R["nv"][src].reshape(128, 8, 128)
            nc_p[0, b, dst] = R["nckv"][src]
            nr_p[0, b, dst] = R["nkr"][src]
        for s in range(2):
            src = slice(1024 + 16 * s, 1024 + 16 * s + 16)
            y_s[2 * c + s] = R["y"][src]
            nk_s[0, 2 * c + s] = R["nk"][src].reshape(16, 8, 128)
            nv_s[0, 2 * c + s] = R["nv"][src].reshape(16, 8, 128)
            nc_s[0, 2 * c + s] = R["nckv"][src]
            nr_s[0, 2 * c + s] = R["nkr"][src]
    return (y_p, y_s, nk_p, nv_p, nc_p, nr_p, nk_s, nv_s, nc_s, nr_s)
```

```python
import numpy as np
import concourse.bass as bass
import concourse.mybir as mybir
from concourse.bass_utils import run_bass_kernel_spmd

F32 = mybir.dt.float32
BF16 = mybir.dt.bfloat16
AF = mybir.ActivationFunctionType
ALU = mybir.AluOpType

SEM_LIM = 30000
NEG = -30000.0

D = 2048
DC = 16
FF = 5504
FC = 43
NP_ = 1024
NS = 32
NT = NP_ + NS
TT = [(0, 352, 0), (352, 352, 1), (704, 352, 2)]
TM = [(i * 128, 128) for i in range(8)] + [(1024, 32)]
H = 8
PAST = 1024
ALPHA = 2.0 ** 0.25
SB_SCALE = 128.0 ** -0.5
MLA_SCALE = 192.0 ** -0.5
LN_EPS = 1e-5
RMS_EPS = 1e-6
EX_ROWS = 4160
R_KT, R_V, R_NT, R_VM, R_RT = 0, 1024, 2048, 3072, 4096


class Buf:
    __slots__ = ("name", "lastw", "readers", "excl")

    def __init__(self, name, excl=False):
        self.name = name
        self.lastw = None
        self.readers = {}
        self.excl = excl


class Op:
    __slots__ = ("eng", "fn", "deps", "signal", "count", "key", "seq", "is_dma", "inc")

    def __init__(self, eng, fn, key, seq, is_dma, inc=None):
        self.eng = eng
        self.fn = fn
        self.key = key
        self.seq = seq
        self.is_dma = is_dma
        self.deps = {}
        self.signal = is_dma
        self.count = 0
        self.inc = inc if inc is not None else (16 if is_dma else 1)


class Prog:
    ENGS = ("pe", "act", "dve", "pool", "sp")

    def __init__(self, nc):
        self.nc = nc
        self.ops = {e: [] for e in self.ENGS}
        self.latest = {}
        self.pending = {e: {} for e in self.ENGS}
        self.seq = 0
        self.dma_counts = {}
        self.dma_inc = {}
        self.nbuf = 0

    def buf(self, name=None, excl=False):
        self.nbuf += 1
        return Buf(name or f"b{self.nbuf}", excl)

    def bufs(self, n, name="b"):
        return [self.buf(f"{name}{i}") for i in range(n)]

    def _add(self, eng, fn, reads, writes, semkey=None, inc=None, touch=()):
        is_dma = semkey is not None
        key = ("dma", semkey) if is_dma else eng
        self.seq += 1
        o = Op(eng, fn, key, self.seq, is_dma, inc)
        if is_dma:
            self.dma_inc[semkey] = o.inc
        deps = {}

        def add(d):
            if d is None:
                return
            if d.key == "pe" and eng == "pe" and not is_dma:
                return
            cur = deps.get(d.key)
            if cur is None or cur.seq < d.seq:
                deps[d.key] = d

        for b in reads:
            add(b.lastw)
            if b.excl:
                for r in b.readers.values():
                    if r.key != key:
                        add(r)
        for b in writes:
            add(b.lastw)
            for r in b.readers.values():
                add(r)
        for d in self.pending[eng].values():
            add(d)
        self.pending[eng] = {}
        o.deps = deps
        for b in reads:
            cur = b.readers.get(key)
            if cur is None or cur.seq < o.seq:
                b.readers[key] = o
        for b in writes:
            b.lastw = o
            b.readers = {}
        for b in touch:
            b.lastw = o
            b.readers = {}
        if is_dma:
            c = self.dma_counts.get(semkey, 0) + 1
            self.dma_counts[semkey] = c
            o.count = c
        self.ops[eng].append(o)
        self.latest[key] = o
        return o

    def op(self, eng, fn, reads=(), writes=()):
        return self._add(eng, fn, reads, writes)

    def dma(self, queue, out, in_, semkey, reads=(), writes=(), **kw):
        def fn(e):
            return e.dma_start(out=out, in_=in_, **kw)
        return self._add(queue, fn, reads, writes, semkey=semkey)

    def dma_batch(self, queue, pairs, semkey, reads=(), writes=()):
        n = len(pairs)
        for j, (out, in_) in enumerate(pairs):
            def fn(e, out=out, in_=in_):
                return e.dma_start(out=out, in_=in_)
            if n == 1:
                self._add(queue, fn, reads, writes, semkey=semkey)
            elif j == 0:
                self._add(queue, fn, reads, writes, semkey=semkey)
            elif j == n - 1:
                self._add(queue, fn, (), (), semkey=semkey, touch=writes)
            else:
                self._add(queue, fn, (), (), semkey=semkey)

    def barrier(self):
        snap = {k: v for k, v in self.latest.items()
                if not (isinstance(k, tuple) and str(k[1]).startswith("cc"))}
        for e in self.ENGS:
            self.pending[e] = dict(snap)

    def emit(self):
        nc = self.nc
        for e in self.ENGS:
            for o in self.ops[e]:
                for d in o.deps.values():
                    d.signal = True
        tot = {}
        for e in self.ENGS:
            c = 0
            for o in self.ops[e]:
                if not o.is_dma and o.signal:
                    c += 1
                    o.count = c
            tot[e] = c
        sems = {}

        def nsem(units):
            return max(1, (units + SEM_LIM - 1) // SEM_LIM)

        for e in self.ENGS:
            sems[e] = [nc.alloc_semaphore(f"s_{e}_{i}") for i in range(nsem(tot[e]))]
        for k, c in self.dma_counts.items():
            sems[("dma", k)] = [nc.alloc_semaphore(f"sd_{k}_{i}") for i in range(nsem(c * self.dma_inc[k]))]

        def target(o):
            units = o.count * o.inc
            idx = (units - 1) // SEM_LIM
            val = (units - 1) % SEM_LIM + 1
            return sems[o.key][idx], idx, val

        handles = {"pe": "tensor", "act": "scalar", "dve": "vector", "pool": "gpsimd", "sp": "sync"}
        final_waits = [o for k, o in self.latest.items() if o.is_dma]
        with nc.Block() as block:
            for e in self.ENGS:
                ops = self.ops[e]
                extra = final_waits if e == "sp" else []

                def body(eng, ops=ops, extra=extra):
                    waited = {}

                    def wait_for(d):
                        sem, idx, val = target(d)
                        wk = (d.key, idx)
                        if waited.get(wk, 0) < val:
                            eng.wait_ge(sem, val)
                            waited[wk] = val

                    for o in ops:
                        for d in o.deps.values():
                            wait_for(d)
                        ins = o.fn(eng)
                        if o.signal:
                            sem, idx, val = target(o)
                            ins.then_inc(sem, o.inc)
                    for d in extra:
                        wait_for(d)

                getattr(block, handles[e])(body)


class Arena:
    def __init__(self, nc):
        self.nc = nc
        b0 = nc.sbuf_base
        n = (nc.sbuf_top - b0 - 3072) // 4
        self.slab = nc.alloc_sbuf_tensor("slab", [128, n], F32)
        self.base = (b0 + 63) // 64 * 64
        self.top = (b0 + n * 4) // 64 * 64
        self.cur = self.base
        self.lim = self.top
        self.n = 0

    def size(self, shape, dtype):
        per = 4 if dtype == F32 else 2
        for s in shape[1:]:
            per *= s
        return (per + 63) // 64 * 64

    def alloc(self, shape, dtype):
        per = self.size(shape, dtype)
        off = self.cur
        self.cur += per
        assert self.cur <= self.lim, f"SBUF overflow {self.cur} > {self.lim}"
        self.n += 1
        return self.nc.alloc_sbuf_tensor_at(f"sb{self.n}", list(shape), dtype, offset=off)

    def region(self, lo, hi):
        self.cur = (lo + 63) // 64 * 64
        self.lim = hi


class Builder:
    IN_SHAPES = {
        "x": [NT, D], "w1a": [D, 2 * FF], "w2a": [FF, D], "w1b": [D, 2 * FF], "w2b": [FF, D],
        "w_in": [D, 8256], "w_uq": [512, 1536], "w_ukv": [512, 2048], "w_br_sb": [1024, D],
        "w_br_mla": [1024, D], "w_o": [D, D], "lnp": [128, 96], "bgate": [128, 32], "gcq": [128, 512],
        "gckv": [128, 512], "cos_tm": [128, 9, 32], "sin_tm": [128, 9, 32], "cos_fm": [64, NT],
        "sin_fm": [64, NT], "ident": [128, 128], "cmats": [128, 4, 128], "mb_sb": [128, 16, 512],
        "mb_mla": [128, 16, 512], "mb_new": [16, 128], "c_k": [2, PAST, 1024], "c_v": [2, PAST, 1024],
        "c_ckv": [2, PAST, 512], "c_kr": [2, PAST, 64],
    }

    class _Lazy(dict):
        def __init__(self, b):
            super().__init__()
            self.b = b

        def __missing__(self, k):
            v = self.b.din(k, Builder.IN_SHAPES[k])
            self[k] = v
            return v

    def __init__(self, stop_after=None):
        self.stop_after = stop_after
        nc = bass.Bass("TRN2", target_bir_lowering=False)
        self.nc = nc
        self.P = Prog(nc)
        self.A = Arena(nc)
        self.i = Builder._Lazy(self)
        self.o = {}
        self.build()
        self.P.emit()

    def din(self, name, shape, dtype=F32):
        return self.nc.dram_tensor(name, list(shape), dtype, kind="ExternalInput").ap()

    def dout(self, name, shape, dtype=F32):
        return self.nc.dram_tensor(name, list(shape), dtype, kind="ExternalOutput").ap()

    def mm(self, out, lhsT, rhs, start, stop, reads, writes):
        self.P.op("pe", lambda e: e.matmul(out, lhsT, rhs, start=start, stop=stop, skip_group_check=True),
                  reads, writes)

    def tr(self, out, in_, ident, reads, writes):
        self.P.op("pe", lambda e: e.transpose(out, in_, ident), reads, writes)

    def act(self, out, in_, func, reads, writes, **kw):
        self.P.op("act", lambda e: e.activation(out, in_, func, **kw), reads, writes)

    def tt(self, out, in0, in1, op, reads, writes, eng="dve"):
        self.P.op(eng, lambda e: e.tensor_tensor(out, in0, in1, op), reads, writes)

    def ts(self, out, in0, s1, s2, op0, op1, reads, writes, eng="dve"):
        if op1 is None:
            self.P.op(eng, lambda e: e.tensor_scalar(out, in0, s1, None, op0), reads, writes)
        else:
            self.P.op(eng, lambda e: e.tensor_scalar(out, in0, s1, s2, op0, op1), reads, writes)

    def stt(self, out, in0, scalar, in1, op0, op1, reads, writes):
        self.P.op("dve", lambda e: e.scalar_tensor_tensor(out, in0, scalar, in1, op0, op1), reads, writes)

    def cp(self, out, in_, reads, writes, eng="dve"):
        self.P.op(eng, lambda e: e.tensor_copy(out, in_), reads, writes)

    def memset(self, ap, val, writes, eng="dve"):
        self.P.op(eng, lambda e: e.memset(ap, val), [], writes)

    def PS(self, b):
        return self.psum[:, b * 512:(b + 1) * 512]

    def exi(self, row0, nrows):
        j = row0 // 512
        a = row0 - j * 512
        return self.ex_in[j].ap()[a:a + nrows, :], j

    def exo(self, r, row0, nrows):
        j = row0 // 512
        a = r * self.ex_rows[j] + row0 - j * 512
        return self.ex_out[j].ap()[a:a + nrows, :], j

    def build(self):
        nc, P, A, i = self.nc, self.P, self.A, self.i
        st = self.stop_after
        self.psum = nc.alloc_psum_tensor("ps", [128, 4096], F32)
        self.bk = [P.buf(f"bank{b}", True) for b in range(8)]

        self.ident = A.alloc([128, 128], F32)
        self.identb = A.alloc([128, 128], BF16)
        self.cm = A.alloc([128, 4, 128], BF16)
        self.lnp = A.alloc([128, 96], F32)
        self.lnpa = A.alloc([128, 96], F32)
        self.bg = A.alloc([128, 32], F32)
        self.epsc = A.alloc([128, 4], F32)
        self.ones32 = A.alloc([128, 128], F32)
        self.bconst = P.buf("consts")
        P.dma("sp", self.ident[:], i["ident"], "c0", writes=[self.bconst])
        P.dma("pool", self.cm[:], i["cmats"], "c1", writes=[self.bconst])
        P.dma("pool", self.identb[:], i["ident"], "c4", writes=[self.bconst])
        P.dma("sp", self.lnp[:], i["lnp"], "c2", writes=[self.bconst])
        P.dma("sp", self.bg[:], i["bgate"], "c3", writes=[self.bconst])
        self.ts(self.lnpa[:], self.lnp[:], ALPHA, None, ALU.mult, None, [self.bconst], [self.bconst])
        self.memset(self.epsc[:, 0:1], LN_EPS, [self.bconst])
        self.memset(self.epsc[:, 1:2], RMS_EPS, [self.bconst])
        self.memset(self.epsc[:, 2:3], 1.0, [self.bconst])
        self.memset(self.ones32[:, :], 1.0, [self.bconst])
        self.actb = A.alloc([128, DC, NT], BF16)
        self.b_act = P.bufs(DC, "act")
        self.R2 = A.cur
        self.resid = A.alloc([128, DC, NT], F32)
        self.b_res = P.bufs(DC, "res")
        self.R3 = A.cur
        self.TOP = A.top
        assert self.TOP - self.R3 >= 86000, (self.TOP, self.R3)

        if st is None:
            o = self.o
            o["y"] = self.dout("y", [NT, D])
            o["nk"] = self.dout("nk", [NT, 1024])
            o["nv"] = self.dout("nv", [NT, 1024])
            o["nckv"] = self.dout("nckv", [NT, 512])
            o["nkr"] = self.dout("nkr", [NT, 64])
            self.ex_rows = [512] * 8 + [64]
            self.ex_in = [nc.dram_tensor(f"ex_in{j}", [n, NP_], BF16) for j, n in enumerate(self.ex_rows)]
            self.ex_out = [nc.dram_tensor(f"ex_out{j}", [4 * n, NP_], BF16) for j, n in enumerate(self.ex_rows)]
            self.b_exc = P.bufs(9, "exc")
            self.b_exo = P.bufs(9, "exo")
            self.spill = nc.dram_tensor("spill", [128, DC, NT], F32)

        self.ffn_prefetch(i["w1a"], i["w2a"])
        self.phase_load_x()
        if st == "x":
            return self.dump_act()
        self.phase_ffn()
        if st == "ffn1":
            return self.dump_act()
        if st is None:
            self.proj_prefetch()
        self.phase_ln(0, final=False)
        if st == "ln1":
            return self.dump_act()
        P.dma("sp", self.spill.ap(), self.resid[:], "spill", reads=self.b_res)
        P.barrier()
        self.phase_proj()
        self.phase_sample_attn()
        self.phase_prompt_attn()
        self.phase_merge()
        self.ffn_prefetch(i["w1b"], i["w2b"])
        self.phase_ln(1, final=False)
        self.phase_ffn()
        self.phase_ln(2, final=True)
        self.phase_out()

    def phase_load_x(self):
        P, A, i = self.P, self.A, self.i
        A.region(self.TOP - 19072, self.TOP)
        xs = [A.alloc([128, D], F32) for _ in range(2)]
        bx = P.bufs(2, "xs")
        k = 0
        for ti, (t0, m) in enumerate(TM):
            s = ti % 2
            P.dma("sp", xs[s][:m, :], i["x"][t0:t0 + m, :], f"x{s}", writes=[bx[s]])
            for q in range(4):
                b = 4 + k % 4
                k += 1
                ps = self.PS(b)
                for j in range(4):
                    c = q * 4 + j
                    self.tr(ps[:, j * 128:j * 128 + m], xs[s][:m, c * 128:(c + 1) * 128], self.ident[:m, :m],
                            [bx[s], self.bconst], [self.bk[b]])
                src = ps.rearrange("p (j t) -> p j t", j=4)[:, :, :m]
                self.act(self.resid[:, q * 4:q * 4 + 4, t0:t0 + m], src, AF.Identity, [self.bk[b]],
                         self.b_res[q * 4:q * 4 + 4], scale=ALPHA)
                self.cp(self.actb[:, q * 4:q * 4 + 4, t0:t0 + m], src, [self.bk[b]], self.b_act[q * 4:q * 4 + 4])
        P.barrier()

    def ffn_prefetch(self, w1, w2):
        P, A = self.P, self.A
        A.region(self.R3, self.TOP - 19072)
        NPAIR, NG = 22, 11
        w1g = [A.alloc([128, DC, 256], BF16) for _ in range(2)]
        w1u = [A.alloc([128, DC, 256], BF16) for _ in range(2)]
        self.ffn_free_lo = A.cur
        aT = [A.alloc([128, 4, NT], BF16) for _ in range(2)]
        sg = A.alloc([128, NT], BF16)
        self.ffn_free_hi = A.cur
        w2s = [A.alloc([128, 4, D], BF16) for _ in range(2)]
        b_w1, b_aT, b_w2 = P.bufs(2, "w1"), P.bufs(2, "aT"), P.bufs(2, "w2")
        b_sg = P.bufs(3, "sg")

        def load_w1(p):
            s = p % 2
            w = (2 if p < 21 else 1) * 128
            c0 = p * 256
            P.dma("pool", w1g[s][:, :, :w], w1[:, c0:c0 + w].rearrange("(c p) n -> p c n", p=128),
                  f"w1g{s}", writes=[b_w1[s]])
            P.dma("pool", w1u[s][:, :, :w], w1[:, FF + c0:FF + c0 + w].rearrange("(c p) n -> p c n", p=128),
                  f"w1u{s}", writes=[b_w1[s]])

        def load_w2(g):
            s = g % 2
            nch = 4 if g < 10 else 3
            r0 = g * 512
            P.dma("pool", w2s[s][:, :nch, :], w2[r0:r0 + nch * 128, :].rearrange("(j p) n -> p j n", p=128),
                  f"w2{s}", writes=[b_w2[s]])

        load_w1(0)
        load_w1(1)
        load_w2(0)
        load_w2(1)
        self.ffn_state = (w1g, w1u, aT, w2s, sg, b_w1, b_aT, b_w2, b_sg, load_w1, load_w2)

    def phase_ffn(self):
        P = self.P
        NPAIR, NG = 22, 11
        (w1g, w1u, aT, w2s, sg, b_w1, b_aT, b_w2, b_sg, load_w1, load_w2) = self.ffn_state
        for g in range(NG):
            gs = g % 2
            nch_g = 4 if g < 10 else 3
            for pp in range(2):
                p = g * 2 + pp
                if p >= NPAIR:
                    continue
                s = p % 2
                nch = 2 if p < 21 else 1
                for jj in range(nch):
                    ja = pp * 2 + jj
                    for (b0, wt) in ((0, w1g[s]), (3, w1u[s])):
                        for c in range(DC):
                            for (t0, n, bk) in TT:
                                self.mm(self.PS(b0 + bk)[:, :n], wt[:, c, jj * 128:(jj + 1) * 128],
                                        self.actb[:, c, t0:t0 + n], c == 0, c == DC - 1,
                                        [b_w1[s], self.b_act[c]], [self.bk[b0 + bk]])
                    for (t0, n, bk) in TT:
                        self.act(sg[:, t0:t0 + n], self.PS(bk)[:, :n], AF.Silu, [self.bk[bk]], [b_sg[bk]])
                    for (t0, n, bk) in TT:
                        self.tt(aT[gs][:, ja, t0:t0 + n], sg[:, t0:t0 + n], self.PS(3 + bk)[:, :n],
                                ALU.mult, [b_sg[bk], self.bk[3 + bk]], [b_aT[gs]])
                if p + 2 < NPAIR:
                    load_w1(p + 2)
            for ci in range(DC):
                b0 = 0 if ci % 2 == 0 else 3
                for jj in range(nch_g):
                    for (t0, n, bk) in TT:
                        self.mm(self.PS(b0 + bk)[:, :n], w2s[gs][:, jj, ci * 128:(ci + 1) * 128],
                                aT[gs][:, jj, t0:t0 + n], jj == 0, jj == nch_g - 1,
                                [b_w2[gs], b_aT[gs]], [self.bk[b0 + bk]])
                for (t0, n, bk) in TT:
                    self.stt(self.resid[:, ci, t0:t0 + n], self.PS(b0 + bk)[:, :n], 0.5,
                             self.resid[:, ci, t0:t0 + n], ALU.mult, ALU.add,
                             [self.bk[b0 + bk], self.b_res[ci]], [self.b_res[ci]])
            if g + 2 < NG:
                load_w2(g + 2)
        P.barrier()

    def phase_ln(self, idx, final):
        P, A = self.P, self.A
        A.region(self.ffn_free_lo, self.ffn_free_hi)
        xb = [A.alloc([128, NT], BF16) for _ in range(2)]
        xq = [A.alloc([128, NT], BF16) for _ in range(2)]
        mt = A.alloc([128, NT], F32)
        rs = A.alloc([128, NT], F32)
        A.region(self.TOP - 19072, self.TOP)
        tmp = [A.alloc([128, NT], F32) for _ in range(2)]
        b_xb, b_xq = P.bufs(2, "xb"), P.bufs(2, "xq")
        b_mt, b_rs = P.buf("mt"), P.buf("rs")
        b_tmp = P.bufs(2, "tmp")
        ones = self.cm[:, 2, :]
        for c in range(DC):
            s = c % 2
            self.act(xb[s][:], self.resid[:, c, :], AF.Copy, [self.b_res[c]], [b_xb[s]])
            self.tt(xq[s][:], self.resid[:, c, :], self.resid[:, c, :], ALU.mult, [self.b_res[c]], [b_xq[s]])
            for (t0, n, bk) in TT:
                self.mm(self.PS(bk)[:, :n], ones, xb[s][:, t0:t0 + n], c == 0, c == DC - 1,
                        [b_xb[s], self.bconst], [self.bk[bk]])
            for (t0, n, bk) in TT:
                self.mm(self.PS(3 + bk)[:, :n], ones, xq[s][:, t0:t0 + n], c == 0, c == DC - 1,
                        [b_xq[s], self.bconst], [self.bk[3 + bk]])
        for (t0, n, bk) in TT:
            sl = slice(t0, t0 + n)
            pa = self.PS(bk)[:, :n]
            pb = self.PS(3 + bk)[:, :n]
            self.ts(mt[:, sl], pa, 1.0 / D, None, ALU.mult, None, [self.bk[bk]], [b_mt])
            self.tt(tmp[0][:, sl], mt[:, sl], mt[:, sl], ALU.mult, [b_mt], [b_tmp[0]])
            self.stt(rs[:, sl], pb, 1.0 / D, tmp[0][:, sl], ALU.mult, ALU.subtract,
                     [self.bk[3 + bk], b_tmp[0]], [b_rs])
            self.act(rs[:, sl], rs[:, sl], AF.Ln, [b_rs, self.bconst], [b_rs], bias=self.epsc[:, 0:1])
            self.act(rs[:, sl], rs[:, sl], AF.Exp, [b_rs], [b_rs], scale=-0.5)
            self.stt(mt[:, sl], mt[:, sl], -1.0, rs[:, sl], ALU.mult, ALU.mult, [b_mt, b_rs], [b_mt])
        g = self.lnp[:, idx * 32:idx * 32 + 16]
        b = self.lnp[:, idx * 32 + 16:idx * 32 + 32]
        ga = self.lnpa[:, idx * 32:idx * 32 + 16]
        ba = self.lnpa[:, idx * 32 + 16:idx * 32 + 32]
        for c in range(DC):
            s = c % 2
            self.tt(tmp[s][:], self.resid[:, c, :], rs[:], ALU.mult, [self.b_res[c], b_rs], [b_tmp[s]])
            self.tt(tmp[s][:], tmp[s][:], mt[:], ALU.add, [b_tmp[s], b_mt], [b_tmp[s]])
            self.act(self.actb[:, c, :], tmp[s][:], AF.Identity, [b_tmp[s], self.bconst], [self.b_act[c]],
                     scale=g[:, c:c + 1], bias=b[:, c:c + 1])
            sc, bi = (g, b) if final else (ga, ba)
            self.act(self.resid[:, c, :], tmp[s][:], AF.Identity, [b_tmp[s], self.bconst], [self.b_res[c]],
                     scale=sc[:, c:c + 1], bias=bi[:, c:c + 1])
        P.barrier()

    PROJ_BLOCKS = [("k", 1024, 512, 0), ("k", 1536, 512, 1), ("v", 2048, 512, 0), ("v", 2560, 512, 1),
                   ("cq", 3072, 512, 0), ("ckv", 3584, 512, 0), ("kr", 4096, 64, 0)]

    def proj_prefetch(self):
        P, A, i = self.P, self.A, self.i
        A.region(self.R3, self.R3 + 32768)
        self.wblk = [A.alloc([128, DC, 512], BF16) for _ in range(2)]
        self.b_wblk = P.bufs(2, "wblk")
        self.load_blk(0)
        self.load_blk(1)

    def load_blk(self, bi):
        kind, c0, nc_, _ = self.PROJ_BLOCKS[bi]
        s = bi % 2
        self.P.dma("pool", self.wblk[s][:, :, :nc_],
                   self.i["w_in"][:, c0:c0 + nc_].rearrange("(c p) n -> p c n", p=128),
                   f"wblk{s}", writes=[self.b_wblk[s]])

    def phase_proj(self):
        P, A, i, o = self.P, self.A, self.i, self.o
        w_in = i["w_in"]
        A.region(self.TOP - 34000, self.TOP)
        self.cqnT = A.alloc([128, 4, NT], BF16)
        self.KTs = A.alloc([128, 8, NS], BF16)
        self.NTs = A.alloc([128, 8, NS], BF16)
        self.krT = A.alloc([64, NT], BF16)
        self.Vs_new = A.alloc([32, 1024], BF16)
        self.VMs_new = A.alloc([32, 1024], BF16)
        self.wukv = A.alloc([128, 4, 2048], BF16)
        self.b_cqnT, self.b_KTs, self.b_NTs = P.buf("cqnT"), P.buf("KTs"), P.buf("NTs")
        self.b_krT, self.b_Vsn, self.b_VMsn, self.b_wukv = P.buf("krT"), P.buf("Vsn"), P.buf("VMsn"), P.buf("wukv")
        P.dma("pool", self.wukv[:], i["w_ukv"].rearrange("(c p) n -> p c n", p=128), "wukv", writes=[self.b_wukv])
        A.region(self.R2, self.TOP - 34000)
        wblk = self.wblk
        stg32 = [A.alloc([128, 512], F32) for _ in range(2)]
        stgb = [A.alloc([128, 512], BF16) for _ in range(2)]
        nrm = [A.alloc([128, 512], F32) for _ in range(2)]
        junk = A.alloc([128, 512], F32)
        ckvnT = A.alloc([128, 4, NT], BF16)
        KTl = [A.alloc([128, NT], BF16) for _ in range(2)]
        gcq = A.alloc([128, 512], F32)
        gckv = A.alloc([128, 512], F32)
        cos = A.alloc([128, 9, 32], F32)
        sin = A.alloc([128, 9, 32], F32)
        kro = [A.alloc([128, 64], F32) for _ in range(2)]
        rt = [A.alloc([128, 32], F32) for _ in range(4)]
        ssq = A.alloc([128, 4], F32)
        b_wblk = self.b_wblk
        b_stg32, b_stgb, b_nrm = P.bufs(2, "stg32"), P.bufs(2, "stgb"), P.bufs(2, "nrm")
        b_junk, b_ckvnT, b_KTl = P.buf("junk"), P.buf("ckvnT"), P.bufs(2, "KTl")
        b_tab, b_kro, b_rt, b_ssq = P.buf("tab"), P.bufs(2, "kro"), P.buf("rt"), P.buf("ssq")
        P.dma("sp", gcq[:], i["gcq"], "t0", writes=[b_tab])
        P.dma("sp", gckv[:], i["gckv"], "t1", writes=[b_tab])
        P.dma("sp", cos[:], i["cos_tm"], "t2", writes=[b_tab])
        P.dma("sp", sin[:], i["sin_tm"], "t3", writes=[b_tab])
        blocks = self.PROJ_BLOCKS
        cnt = {"s32": 0, "sb": 0, "nrm": 0, "bank": 0, "ktl": 0, "kro": 0}

        load_blk = self.load_blk

        def rmsnorm(ps, m, bkb, gtile):
            self.act(junk[:m, :], ps[:m, :], AF.Square, [bkb], [b_junk])
            self.P.op("dve", lambda e: e.reduce_sum(ssq[:m, 0:1], junk[:m, :], mybir.AxisListType.X),
                      [b_junk], [b_ssq])
            self.act(ssq[:m, 1:2], ssq[:m, 0:1], AF.Ln, [b_ssq, self.bconst], [b_ssq], scale=1.0 / 512,
                     bias=self.epsc[:m, 1:2])
            self.act(ssq[:m, 2:3], ssq[:m, 1:2], AF.Exp, [b_ssq], [b_ssq], scale=-0.5)
            s = cnt["nrm"] % 2
            cnt["nrm"] += 1
            self.stt(nrm[s][:m, :], ps[:m, :], ssq[:m, 2:3], gtile[:m, :], ALU.mult, ALU.mult,
                     [bkb, b_ssq, b_tab], [b_nrm[s]])
            return s

        def transposes_to(dst, b_dst, src, b_src, m, t0, nchunk, bank):
            ps = self.PS(bank)
            for k in range(nchunk):
                self.tr(ps[:, k * 128:k * 128 + m], src[:m, k * 128:(k + 1) * 128], self.ident[:m, :m],
                        [b_src, self.bconst], [self.bk[bank]])
            srcv = ps.rearrange("p (j t) -> p j t", j=4)[:, :nchunk, :m]
            self.cp(dst[:, 0:nchunk, t0:t0 + m], srcv, [self.bk[bank]], [b_dst])

        for bi, (kind, c0, nc_, half) in enumerate(blocks):
            s = bi % 2
            for ti, (t0, m) in enumerate(TM):
                bank = 6 + cnt["bank"] % 2
                cnt["bank"] += 1
                ps = self.PS(bank)
                bkb = self.bk[bank]
                for c in range(DC):
                    self.mm(ps[:m, :nc_], self.actb[:, c, t0:t0 + m], wblk[s][:, c, :nc_], c == 0, c == DC - 1,
                            [self.b_act[c], b_wblk[s]], [bkb])
                if kind in ("k", "v"):
                    s2 = cnt["s32"] % 2
                    cnt["s32"] += 1
                    self.act(stg32[s2][:m, :], ps[:m, :], AF.Identity, [bkb], [b_stg32[s2]])
                    dst = o["nk"] if kind == "k" else o["nv"]
                    P.dma("sp", dst[t0:t0 + m, half * 512:(half + 1) * 512], stg32[s2][:m, :], f"o32_{s2}",
                          reads=[b_stg32[s2]])
                    if kind == "v":
                        if m == 128:
                            s3 = cnt["sb"] % 2
                            cnt["sb"] += 1
                            self.cp(stgb[s3][:m, :], ps[:m, :], [bkb], [b_stgb[s3]])
                            ea, ej = self.exi(R_V + t0, m)
                            P.dma("sp", ea[:, half * 512:(half + 1) * 512], stgb[s3][:m, :],
                                  f"ob_{s3}", reads=[b_stgb[s3], self.b_exc[ej]])
                        else:
                            self.cp(self.Vs_new[:m, half * 512:(half + 1) * 512], ps[:m, :], [bkb], [self.b_Vsn])
                elif kind == "cq":
                    sn = rmsnorm(ps, m, bkb, gcq)
                    bank2 = 6 + cnt["bank"] % 2
                    cnt["bank"] += 1
                    transposes_to(self.cqnT, self.b_cqnT, nrm[sn], b_nrm[sn], m, t0, 4, bank2)
                elif kind == "ckv":
                    sn = rmsnorm(ps, m, bkb, gckv)
                    P.dma("sp", o["nckv"][t0:t0 + m, :], nrm[sn][:m, :], f"on_{sn}", reads=[b_nrm[sn]])
                    bank2 = 6 + cnt["bank"] % 2
                    cnt["bank"] += 1
                    transposes_to(ckvnT, b_ckvnT, nrm[sn], b_nrm[sn], m, t0, 4, bank2)
                else:
                    sk = cnt["kro"] % 2
                    cnt["kro"] += 1
                    x1, x2 = ps[:m, 0:32], ps[:m, 32:64]
                    cs, sn_ = cos[:m, ti, :], sin[:m, ti, :]
                    self.tt(rt[0][:m, :], x1, cs, ALU.mult, [bkb, b_tab], [b_rt])
                    self.tt(rt[1][:m, :], x2, sn_, ALU.mult, [bkb, b_tab], [b_rt])
                    self.tt(rt[2][:m, :], x2, cs, ALU.mult, [bkb, b_tab], [b_rt])
                    self.tt(rt[3][:m, :], x1, sn_, ALU.mult, [bkb, b_tab], [b_rt])
                    self.tt(kro[sk][:m, 0:32], rt[0][:m, :], rt[1][:m, :], ALU.subtract, [b_rt], [b_kro[sk]])
                    self.tt(kro[sk][:m, 32:64], rt[2][:m, :], rt[3][:m, :], ALU.add, [b_rt], [b_kro[sk]])
                    P.dma("sp", o["nkr"][t0:t0 + m, :], kro[sk][:m, :], f"okr_{sk}", reads=[b_kro[sk]])
                    bank2 = 6 + cnt["bank"] % 2
                    cnt["bank"] += 1
                    ps2 = self.PS(bank2)
                    self.tr(ps2[:64, :m], kro[sk][:m, 0:64], self.ident[:m, :m], [b_kro[sk], self.bconst],
                            [self.bk[bank2]])
                    self.cp(self.krT[:, t0:t0 + m], ps2[:64, :m], [self.bk[bank2]], [self.b_krT])
            if kind == "k":
                for hh in range(4):
                    h = half * 4 + hh
                    b0 = 0 if hh % 2 == 0 else 3
                    sl_ = cnt["ktl"] % 2
                    cnt["ktl"] += 1
                    for c in range(DC):
                        for (t0, n, bk) in TT:
                            self.mm(self.PS(b0 + bk)[:, :n], wblk[s][:, c, hh * 128:(hh + 1) * 128],
                                    self.actb[:, c, t0:t0 + n], c == 0, c == DC - 1,
                                    [b_wblk[s], self.b_act[c]], [self.bk[b0 + bk]])
                    for (t0, n, bk) in TT:
                        self.act(KTl[sl_][:, t0:t0 + n], self.PS(b0 + bk)[:, :n], AF.Identity,
                                 [self.bk[b0 + bk]], [b_KTl[sl_]])
                    self.cp(self.KTs[:, h, :], KTl[sl_][:, NP_:NT], [b_KTl[sl_]], [self.b_KTs])
                    ea, ej = self.exi(R_KT + h * 128, 128)
                    P.dma("sp", ea, KTl[sl_][:, 0:NP_], f"okt_{sl_}", reads=[b_KTl[sl_], self.b_exc[ej]])
            if bi + 2 < len(blocks):
                load_blk(bi + 2)
            if bi == 3:
                self.issue_cc([0, 1, 2, 3])
        ea, ej = self.exi(R_RT, 64)
        P.dma("sp", ea, self.krT[:, 0:NP_], "okrt", reads=[self.b_krT, self.b_exc[ej]])
        for h in range(H):
            b0 = 0 if h % 2 == 0 else 3
            sl_ = cnt["ktl"] % 2
            cnt["ktl"] += 1
            for kc in range(4):
                for (t0, n, bk) in TT:
                    self.mm(self.PS(b0 + bk)[:, :n], self.wukv[:, kc, h * 256:h * 256 + 128],
                            ckvnT[:, kc, t0:t0 + n], kc == 0, kc == 3, [self.b_wukv, b_ckvnT],
                            [self.bk[b0 + bk]])
            for (t0, n, bk) in TT:
                self.act(KTl[sl_][:, t0:t0 + n], self.PS(b0 + bk)[:, :n], AF.Identity, [self.bk[b0 + bk]],
                         [b_KTl[sl_]])
            self.cp(self.NTs[:, h, :], KTl[sl_][:, NP_:NT], [b_KTl[sl_]], [self.b_NTs])
            ea, ej = self.exi(R_NT + h * 128, 128)
            P.dma("sp", ea, KTl[sl_][:, 0:NP_], f"okt_{sl_}", reads=[b_KTl[sl_], self.b_exc[ej]])
        for ti, (t0, m) in enumerate(TM):
            for half in range(2):
                bank = 6 + cnt["bank"] % 2
                cnt["bank"] += 1
                ps = self.PS(bank)
                for hh in range(4):
                    hc = (half * 4 + hh) * 256 + 128
                    for kc in range(4):
                        self.mm(ps[:m, hh * 128:(hh + 1) * 128], ckvnT[:, kc, t0:t0 + m],
                                self.wukv[:, kc, hc:hc + 128], kc == 0, kc == 3,
                                [b_ckvnT, self.b_wukv], [self.bk[bank]])
                if m == 128:
                    s3 = cnt["sb"] % 2
                    cnt["sb"] += 1
                    self.cp(stgb[s3][:m, :], ps[:m, :], [self.bk[bank]], [b_stgb[s3]])
                    ea, ej = self.exi(R_VM + t0, m)
                    P.dma("sp", ea[:, half * 512:(half + 1) * 512], stgb[s3][:m, :],
                          f"ob_{s3}", reads=[b_stgb[s3], self.b_exc[ej]])
                else:
                    self.cp(self.VMs_new[:m, half * 512:(half + 1) * 512], ps[:m, :], [self.bk[bank]],
                            [self.b_VMsn])
        P.barrier()

    def issue_cc(self, js):
        P = self.P
        for j in js:
            def fn(e, j=j):
                return e.collective_compute("AllGather", ALU.bypass, replica_groups=[[0, 1, 2, 3], [4, 5, 6, 7]],
                                            ins=[self.ex_in[j].ap().opt()], outs=[self.ex_out[j].ap().opt()])
            P._add("pool", fn, [], [self.b_exc[j], self.b_exo[j]], semkey=f"cc{j}", inc=1)

    def attn_setup(self, alt=None):
        P, A = self.P, self.A
        w = {}
        w["e32"] = [A.alloc([128, 512], F32) for _ in range(2)]
        w["sp"] = [A.alloc([128, 512], BF16) for _ in range(3)]
        w["S"] = [A.alloc([128, 512], BF16) for _ in range(3)]
        w["A"] = [A.alloc([128, 512], BF16) for _ in range(3)]
        if alt is not None:
            save = (A.cur, A.lim)
            A.region(*alt)
        w["rec"] = A.alloc([128, 512], F32)
        w["acc"] = [A.alloc([128, 512], F32) for _ in range(2)]
        if alt is not None:
            self.alt_cur = A.cur
            A.cur, A.lim = save
        w["b_acc"] = P.bufs(2, "acc")
        w["b_e32"] = P.bufs(2, "e32")
        for k in ("sp", "S", "A"):
            w["b_" + k] = P.bufs(3, k)
        w["b_rec"] = P.buf("rec")
        w["cnt"] = 0
        return w

    def sb_problem(self, w, N, groups, tiles, obank, evac):
        negtri, negones = self.cm[:, 0, :], self.cm[:, 1, :]
        S, bS = w["S"], w["b_S"]
        for j in range(3):
            self.memset(S[j][:, :N], 0.0, [bS[j]])
        O = self.PS(obank)
        nt = len(tiles)
        base = w["cnt"]
        w["cnt"] += nt
        wbank = lambda k: (0, 1, 2, 5)[(base + k) % 4]

        def stA(k):
            t = tiles[k]
            nk, wb = t["nk"], wbank(k)
            cl = t.get("c_lo", 0)
            W = self.PS(wb)
            for gi, (c0, ncg, q, bq) in enumerate(groups):
                lo = max(c0, cl)
                self.mm(W[:nk, lo:c0 + ncg], t["kt"][gi], q[:, lo - c0:], gi == 0, False, t["reads"] + [bq],
                        [self.bk[wb]])
            if t["mask"] is not None:
                self.mm(W[:nk, cl:N], self.identb[:nk, :nk], t["mask"][:, cl:N], False, False,
                        [self.bconst, self.b_mask], [self.bk[wb]])
            s2 = (base + k) % 2
            self.act(w["e32"][s2][:nk, cl:N], W[:nk, cl:N], AF.Exp, [self.bk[wb]], [w["b_e32"][s2]])

        def stA2(k):
            t = tiles[k]
            nk, cl = t["nk"], t.get("c_lo", 0)
            s2, s3 = (base + k) % 2, (base + k) % 3
            self.act(w["sp"][s3][:nk, cl:N], w["e32"][s2][:nk, cl:N], AF.Ln, [w["b_e32"][s2], self.bconst],
                     [w["b_sp"][s3]], bias=self.epsc[:nk, 2:3])

        def stB(k):
            t = tiles[k]
            nk, wb = t["nk"], wbank(k)
            cl = t.get("c_lo", 0)
            W = self.PS(wb)
            s3 = (base + k) % 3
            first = k == 0
            self.mm(W[:nk, cl:N], negtri[:nk, :nk], w["sp"][s3][:nk, cl:N], False, first,
                    [self.bconst, w["b_sp"][s3]], [self.bk[wb]])
            if not first:
                self.mm(W[:nk, cl:N], negones[:, :nk], S[k % 3][:, cl:N], False, True,
                        [self.bconst, bS[k % 3]], [self.bk[wb]])
            self.act(w["A"][s3][:nk, cl:N], W[:nk, cl:N], AF.Exp, [self.bk[wb]], [w["b_A"][s3]])
            if k < nt - 1:
                nx = (k + 1) % 3
                if nk < 128:
                    self.tt(S[nx][:nk, cl:N], S[k % 3][:nk, cl:N], w["sp"][s3][:nk, cl:N], ALU.add,
                            [bS[k % 3], w["b_sp"][s3]], [bS[nx]])
                else:
                    self.tt(S[nx][:, cl:N], S[k % 3][:, cl:N], w["sp"][s3][:, cl:N], ALU.add,
                            [bS[k % 3], w["b_sp"][s3]], [bS[nx]])

        def stC(k):
            t = tiles[k]
            nk = t["nk"]
            s3 = (base + k) % 3
            cl = t.get("c_lo", 0)
            for gi, (c0, ncg, q, bq) in enumerate(groups):
                lo = max(c0, cl)
                self.mm(O[:, lo:c0 + ncg], t["v"][gi], w["A"][s3][:nk, lo:c0 + ncg], k == 0 and gi == 0,
                        k == nt - 1, t["reads"] + [w["b_A"][s3]], [self.bk[obank]])

        for it in range(nt + 3):
            if it < nt:
                stA(it)
            if 1 <= it <= nt:
                stA2(it - 1)
            if 2 <= it <= nt + 1:
                stB(it - 2)
            if 3 <= it:
                stC(it - 3)
        evac(O, obank)

    def mla_problem(self, w, N, groups, tiles, obank, dbank, evac):
        ones = self.cm[:, 2, :]
        O, Dn = self.PS(obank), self.PS(dbank)
        nt = len(tiles)
        base = w["cnt"]
        w["cnt"] += nt

        def stA(k):
            t = tiles[k]
            nk, zb = t["nk"], (base + k) % 3
            Z = self.PS(zb)
            s3 = (base + k) % 3
            cl = t.get("c_lo", 0)
            for gi, (c0, ncg, qn, qr, bq) in enumerate(groups):
                lo = max(c0, cl)
                self.mm(Z[:nk, lo:c0 + ncg], t["nt"][gi], qn[:, lo - c0:], gi == 0, False, t["reads"] + [bq],
                        [self.bk[zb]])
                self.mm(Z[:nk, lo:c0 + ncg], t["rt"], qr[:, lo - c0:], False,
                        t["mask"] is None and gi == len(groups) - 1, t["reads"] + [bq], [self.bk[zb]])
            if t["mask"] is not None:
                self.mm(Z[:nk, cl:N], self.identb[:nk, :nk], t["mask"][:, cl:N], False, True,
                        [self.bconst, self.b_mask], [self.bk[zb]])
            self.act(w["A"][s3][:nk, cl:N], Z[:nk, cl:N], AF.Exp, [self.bk[zb]], [w["b_A"][s3]])

        def stC(k):
            t = tiles[k]
            nk = t["nk"]
            s3 = (base + k) % 3
            first, last = k == 0, k == nt - 1
            cl = t.get("c_lo", 0)
            for gi, (c0, ncg, qn, qr, bq) in enumerate(groups):
                lo = max(c0, cl)
                self.mm(O[:, lo:c0 + ncg], t["vm"][gi], w["A"][s3][:nk, lo:c0 + ncg], first and gi == 0, last,
                        t["reads"] + [w["b_A"][s3]], [self.bk[obank]])
            a2 = k % 2
            self.tt(w["acc"][a2][:nk, cl:N], w["acc"][a2][:nk, cl:N], w["A"][s3][:nk, cl:N], ALU.add,
                    [w["b_acc"][a2], w["b_A"][s3]], [w["b_acc"][a2]])

        self.memset(w["acc"][0][:, :N], 0.0, [w["b_acc"][0]])
        self.memset(w["acc"][1][:, :N], 0.0, [w["b_acc"][1]])
        for it in range(nt + 1):
            if it < nt:
                stA(it)
            if it >= 1:
                stC(it - 1)
        self.mm(Dn[:, :N], self.ones32[:, :], w["acc"][0][:, :N], True, False, [self.bconst, w["b_acc"][0]],
                [self.bk[dbank]])
        self.mm(Dn[:, :N], self.ones32[:, :], w["acc"][1][:, :N], False, True, [self.bconst, w["b_acc"][1]],
                [self.bk[dbank]])
        self.P.op("dve", lambda e: e.reciprocal(w["rec"][:, :N], Dn[:, :N]), [self.bk[dbank]], [w["b_rec"]])
        evac(O, obank, w["rec"], w["b_rec"])

    def phase_sample_attn(self):
        P, A, i = self.P, self.A, self.i
        w_in = i["w_in"]
        A.region(self.R3, self.R3 + 34000)
        self.oT_sb = A.alloc([128, H, NT], BF16)
        self.oT_mla = A.alloc([128, H, NT], BF16)
        self.b_oT_sb, self.b_oT_mla = P.buf("oTsb"), P.buf("oTmla")
        A.region(self.R2, self.R3)
        w = self.attn_setup()
        self.aw = w
        Qss = A.alloc([128, H, NS], BF16)
        Qns = A.alloc([128, H, NS], BF16)
        Qrs = A.alloc([64, H, NS], BF16)
        Qrt = [A.alloc([64, NS], F32) for _ in range(2)]
        cosf = A.alloc([64, NS], F32)
        sinf = A.alloc([64, NS], F32)
        self.cosf, self.sinf = cosf, sinf
        self.b_tabf = P.buf("tabf")
        P.dma("sp", cosf[:], i["cos_fm"][:, NP_:NT], "t0", writes=[self.b_tabf])
        P.dma("sp", sinf[:], i["sin_fm"][:, NP_:NT], "t1", writes=[self.b_tabf])
        mbn = A.alloc([16, 128], BF16)
        mbn32 = A.alloc([16, 128], F32)
        b_mbn32 = P.buf("mbn32")
        self.b_mask = P.buf("mask")
        P.dma("sp", mbn32[:], i["mb_new"], "mb", writes=[b_mbn32])
        self.cp(mbn[:], mbn32[:], [b_mbn32], [self.b_mask])
        wq = None
        wuq = [A.alloc([128, 4, 256], BF16) for _ in range(2)]

        self.wq, self.wuq = wq, wuq
        self.b_wq, self.b_wuq = P.bufs(2, "wq"), P.bufs(2, "wuq")
        b_Qs = P.buf("Qs_s")
        b_Qrt = P.buf("Qrt")
        Vn1 = A.alloc([16, 1024], BF16)
        VMn1 = A.alloc([16, 1024], BF16)
        b_Vn1 = P.buf("Vn1")
        P.dma("sp", Vn1[:], self.Vs_new[16:32, :], "vn1", reads=[self.b_Vsn], writes=[b_Vn1])
        P.dma("sp", VMn1[:], self.VMs_new[16:32, :], "vmn1", reads=[self.b_VMsn], writes=[b_Vn1])
        self.wcount = 0
        wq2 = [A.alloc([128, DC, 256], BF16) for _ in range(2)]
        b_wq2 = P.bufs(2, "wq2")
        uqv = i["w_uq"].rearrange("(c p) n -> p c n", p=128)

        def load_pair(hp):
            sl2 = hp % 2
            P.dma("pool", wq2[sl2][:], w_in[:, hp * 256:(hp + 1) * 256].rearrange("(c p) n -> p c n", p=128),
                  f"wq2{sl2}", writes=[b_wq2[sl2]])

        def load_wuq(h):
            sl2 = h % 2
            b0 = h * 192
            P.dma("pool", wuq[sl2][:, :, 0:192], uqv[:, :, b0:b0 + 192], f"wuqa{sl2}", writes=[self.b_wuq[sl2]])
            P.dma("pool", wuq[sl2][:, :, 192:224], uqv[:, :, b0 + 160:b0 + 192], f"wuqb{sl2}",
                  writes=[self.b_wuq[sl2]])
            P.dma("pool", wuq[sl2][:, :, 224:256], uqv[:, :, b0 + 128:b0 + 160], f"wuqc{sl2}",
                  writes=[self.b_wuq[sl2]])
            return sl2

        load_pair(0)
        load_pair(1)
        for h in range(H):
            s = load_wuq(h)
            sp2, off = (h // 2) % 2, (h % 2) * 128
            bA, bB = (6, 7) if h % 2 == 0 else (3, 4)
            ps = self.PS(bA)
            for c in range(DC):
                self.mm(ps[:, :NS], wq2[sp2][:, c, off:off + 128], self.actb[:, c, NP_:NT], c == 0, c == DC - 1,
                        [b_wq2[sp2], self.b_act[c]], [self.bk[bA]])
            if h % 2 == 1 and h // 2 + 2 < 4:
                load_pair(h // 2 + 2)
            self.act(Qss[:, h, :], ps[:, :NS], AF.Identity, [self.bk[bA]], [b_Qs], scale=SB_SCALE)
            ps = self.PS(bB)
            for kc in range(4):
                self.mm(ps[:, :NS], wuq[s][:, kc, 0:128], self.cqnT[:, kc, NP_:NT], kc == 0, kc == 3,
                        [self.b_wuq[s], self.b_cqnT], [self.bk[bB]])
            self.act(Qns[:, h, :], ps[:, :NS], AF.Identity, [self.bk[bB]], [b_Qs], scale=MLA_SCALE)
            ps = self.PS(bA)
            for kc in range(4):
                self.mm(ps[:64, :NS], wuq[s][:, kc, 128:192], self.cqnT[:, kc, NP_:NT], kc == 0, kc == 3,
                        [self.b_wuq[s], self.b_cqnT], [self.bk[bA]])
            for kc in range(4):
                self.mm(ps[:64, 64:64 + NS], wuq[s][:, kc, 192:256], self.cqnT[:, kc, NP_:NT], kc == 0, kc == 3,
                        [self.b_wuq[s], self.b_cqnT], [self.bk[bA]])
            self.tt(Qrt[0][:, :], ps[:64, :NS], cosf[:, :], ALU.mult, [self.bk[bA], self.b_tabf], [b_Qrt])
            self.tt(Qrt[1][:, :], ps[:64, 64:64 + NS], sinf[:, :], ALU.mult, [self.bk[bA], self.b_tabf], [b_Qrt])
            self.tt(Qrt[0][:, :], Qrt[0][:, :], Qrt[1][:, :], ALU.add, [b_Qrt], [b_Qrt])
            self.act(Qrs[:, h, :], Qrt[0][:, :], AF.Identity, [b_Qrt], [b_Qs], scale=MLA_SCALE)
        r2cur = A.cur
        A.region(self.R3 + 34000, self.TOP - 34000)
        KTc = A.alloc([128, H, PAST], BF16)
        Vc = A.alloc([128, 8, 1024], BF16)
        A.region(r2cur, self.R3)
        ckvTc = A.alloc([128, 4, PAST], BF16)
        krTc = A.alloc([64, PAST], BF16)
        kst = [A.alloc([128, 1024], F32) for _ in range(2)]
        krst = A.alloc([128, 8, 64], F32)
        b_KTc, b_Vc, b_ckvTc, b_krTc = P.buf("KTc"), P.buf("Vc"), P.buf("ckvTc"), P.buf("krTc")
        b_kst, b_krst = P.bufs(2, "kst"), P.buf("krst")
        kcnt = 0
        for sq in range(2):
            qsl = slice(sq * 16, sq * 16 + 16)
            for kt in range(8):
                s = kcnt % 2
                kcnt += 1
                P.dma("sp", kst[s][:, :], i["c_k"][sq, kt * 128:(kt + 1) * 128, :], f"kst{s}", writes=[b_kst[s]])
                for hg in range(2):
                    bank = 6 + hg
                    ps = self.PS(bank)
                    for j in range(4):
                        h = hg * 4 + j
                        self.tr(ps[:, j * 128:(j + 1) * 128], kst[s][:, h * 128:(h + 1) * 128], self.ident[:, :],
                                [b_kst[s], self.bconst], [self.bk[bank]])
                    self.cp(KTc[:, hg * 4:hg * 4 + 4, kt * 128:(kt + 1) * 128],
                            ps.rearrange("p (j t) -> p j t", j=4), [self.bk[bank]], [b_KTc])
            P.dma("pool", Vc[:], i["c_v"][sq].rearrange("(k p) c -> p k c", p=128), "vc", writes=[b_Vc])
            groups = [(h * 16, 16, Qss[:, h, qsl], b_Qs) for h in range(H)]
            vnew = self.Vs_new if sq == 0 else Vn1
            b_vnew = self.b_Vsn if sq == 0 else b_Vn1
            tiles = [dict(nk=16, kt=[self.KTs[:, h, qsl] for h in range(H)],
                          v=[vnew[:16, h * 128:(h + 1) * 128] for h in range(H)], mask=mbn[:, :],
                          reads=[self.b_KTs, b_vnew])]
            for kt in range(7, -1, -1):
                tiles.append(dict(nk=128, kt=[KTc[:, h, kt * 128:(kt + 1) * 128] for h in range(H)],
                                  v=[Vc[:, kt, h * 128:(h + 1) * 128] for h in range(H)], mask=None,
                                  reads=[b_KTc, b_Vc]))
            c0 = NP_ + sq * 16

            def evac_sb(O, obank, c0=c0):
                self.cp(self.oT_sb[:, :, c0:c0 + 16], O[:, :128].rearrange("p (h q) -> p h q", q=16),
                        [self.bk[obank]], [self.b_oT_sb])
            self.sb_problem(w, 128, groups, tiles, 3, evac_sb)
            NTc, VMc = KTc, Vc
            for kt in range(8):
                s = kcnt % 2
                kcnt += 1
                P.dma("sp", kst[s][:, :512], i["c_ckv"][sq, kt * 128:(kt + 1) * 128, :], f"kst{s}",
                      writes=[b_kst[s]])
                bank = 6 + kt % 2
                ps = self.PS(bank)
                for j in range(4):
                    self.tr(ps[:, j * 128:(j + 1) * 128], kst[s][:, j * 128:(j + 1) * 128], self.ident[:, :],
                            [b_kst[s], self.bconst], [self.bk[bank]])
                self.cp(ckvTc[:, :, kt * 128:(kt + 1) * 128], ps.rearrange("p (j t) -> p j t", j=4),
                        [self.bk[bank]], [b_ckvTc])
            P.dma("sp", krst[:], i["c_kr"][sq].rearrange("(k p) c -> p k c", p=128), "krst", writes=[b_krst])
            for kg in range(2):
                bank = 6 + kg
                ps = self.PS(bank)
                for j in range(4):
                    kt = kg * 4 + j
                    self.tr(ps[:64, j * 128:(j + 1) * 128], krst[:, kt, :], self.ident[:, :],
                            [b_krst, self.bconst], [self.bk[bank]])
                self.cp(krTc[:, kg * 512:(kg + 1) * 512], ps[:64, :], [self.bk[bank]], [b_krTc])
            for h in range(H):
                for half in range(2):
                    bank = 6 + (h * 2 + half) % 2
                    ps = self.PS(bank)
                    for kc in range(4):
                        self.mm(ps[:, :], self.wukv[:, kc, h * 256:h * 256 + 128],
                                ckvTc[:, kc, half * 512:(half + 1) * 512], kc == 0, kc == 3,
                                [self.b_wukv, b_ckvTc], [self.bk[bank]])
                    self.act(NTc[:, h, half * 512:(half + 1) * 512], ps[:, :], AF.Identity, [self.bk[bank]],
                             [b_KTc])
            for kt in range(8):
                for half in range(2):
                    bank = 6 + (kt * 2 + half) % 2
                    ps = self.PS(bank)
                    for hh in range(4):
                        hc = (half * 4 + hh) * 256 + 128
                        for kc in range(4):
                            self.mm(ps[:, hh * 128:(hh + 1) * 128], ckvTc[:, kc, kt * 128:(kt + 1) * 128],
                                    self.wukv[:, kc, hc:hc + 128], kc == 0, kc == 3,
                                    [b_ckvTc, self.b_wukv], [self.bk[bank]])
                    self.cp(VMc[:, kt, half * 512:(half + 1) * 512], ps[:, :], [self.bk[bank]], [b_Vc])
            groups = [(h * 16, 16, Qns[:, h, qsl], Qrs[:, h, qsl], b_Qs) for h in range(H)]
            vmnew = self.VMs_new if sq == 0 else VMn1
            b_vmnew = self.b_VMsn if sq == 0 else b_Vn1
            tiles = [dict(nk=16, nt=[self.NTs[:, h, qsl] for h in range(H)], rt=self.krT[:, c0:c0 + 16],
                          vm=[vmnew[:16, h * 128:(h + 1) * 128] for h in range(H)], mask=None,
                          reads=[self.b_NTs, self.b_krT, b_vmnew])]
            for kt in range(7, -1, -1):
                tiles.append(dict(nk=128, nt=[NTc[:, h, kt * 128:(kt + 1) * 128] for h in range(H)],
                                  rt=krTc[:, kt * 128:(kt + 1) * 128],
                                  vm=[VMc[:, kt, h * 128:(h + 1) * 128] for h in range(H)], mask=None,
                                  reads=[b_KTc, b_krTc, b_Vc]))

            def evac_mla(O, obank, rec, b_rec, c0=c0):
                self.tt(self.oT_mla[:, :, c0:c0 + 16], O[:, :128].rearrange("p (h q) -> p h q", q=16),
                        rec[:, :128].rearrange("p (h q) -> p h q", q=16), ALU.mult,
                        [self.bk[obank], b_rec], [self.b_oT_mla])
            self.mla_problem(w, 128, groups, tiles, 3, 5, evac_mla)
        P.barrier()

    def load_wq(self, h):
        P, i = self.P, self.i
        s = self.wcount % 2
        self.wcount += 1
        P.dma("pool", self.wq[s][:], i["w_in"][:, h * 128:(h + 1) * 128].rearrange("(c p) n -> p c n", p=128),
              f"wq{s}", writes=[self.b_wq[s]])
        uq = i["w_uq"].rearrange("(c p) n -> p c n", p=128)
        b0 = h * 192
        P.dma("pool", self.wuq[s][:, :, 0:192], uq[:, :, b0:b0 + 192], f"wuqa{s}", writes=[self.b_wuq[s]])
        P.dma("pool", self.wuq[s][:, :, 192:224], uq[:, :, b0 + 160:b0 + 192], f"wuqb{s}", writes=[self.b_wuq[s]])
        P.dma("pool", self.wuq[s][:, :, 224:256], uq[:, :, b0 + 128:b0 + 160], f"wuqc{s}", writes=[self.b_wuq[s]])
        return s

    def phase_prompt_attn(self):
        P, A, i = self.P, self.A, self.i
        A.region(self.R2, self.R3)
        w = self.attn_setup(alt=(self.TOP - 34000 + 8448, self.TOP))
        mask = A.alloc([128, 16, 512], BF16)
        self.b_mask = P.buf("mask2")
        Kg = [A.alloc([128, 4096], BF16) for _ in range(2)]
        Vg = [A.alloc([128, 32, 128], BF16) for _ in range(2)]
        A.region(self.R3 + 34000, self.TOP - 34000)
        Qs = [A.alloc([128, NP_], BF16) for _ in range(2)]
        Rg = A.alloc([64, 4096], BF16)
        Qr = [A.alloc([64, NP_], BF16) for _ in range(2)]
        Qrt = [A.alloc([64, 512], F32) for _ in range(2)]
        wq = [A.alloc([128, DC, 128], BF16) for _ in range(2)]
        wuq = [A.alloc([128, 4, 256], BF16) for _ in range(2)]
        self.wq, self.wuq = wq, wuq
        self.b_wq, self.b_wuq = P.bufs(2, "wq2"), P.bufs(2, "wuq2")
        b_Kg, b_Vg, b_Rg = P.bufs(2, "Kg"), P.bufs(2, "Vg"), P.buf("Rg")
        b_Qs, b_Qr, b_Qrt = P.bufs(2, "Qs"), P.bufs(2, "Qr"), P.buf("Qrt2")
        A.region(self.alt_cur, self.TOP)
        cosf = A.alloc([64, NP_], F32)
        sinf = A.alloc([64, NP_], F32)
        b_tabf = P.buf("tabf2")
        P.dma("sp", cosf[:], i["cos_fm"][:, 0:NP_], "t0", writes=[b_tabf])
        P.dma("sp", sinf[:], i["sin_fm"][:, 0:NP_], "t1", writes=[b_tabf])

        def load_kv(slot, row0, h):
            pairs, rds = [], []
            for r in range(4):
                ea, ej = self.exo(r, row0 + h * 128, 128)
                pairs.append((Kg[slot][:, r * 1024:(r + 1) * 1024], ea))
                rds.append(self.b_exo[ej])
            P.dma_batch("sp", pairs, f"kg{slot}", reads=rds, writes=[b_Kg[slot]])
            vrow = R_V if row0 == R_KT else R_VM
            pairs, rds = [], []
            for r in range(4):
                for hf in range(2):
                    ea, ej = self.exo(r, vrow + hf * 512, 512)
                    pairs.append((Vg[slot][:, r * 8 + hf * 4:r * 8 + hf * 4 + 4, :],
                                  ea[:, h * 128:(h + 1) * 128].rearrange("(m p) d -> p m d", p=128)))
                    rds.append(self.b_exo[ej])
            P.dma_batch("sp", pairs, f"vg{slot}", reads=rds, writes=[b_Vg[slot]])

        def key_tiles(M, slot, mla):
            tl = []
            for kt in range(16 * M + 15, -1, -1):
                r, m = kt % 4, kt // 4
                col = r * 1024 + m * 128
                mk = mask[:, kt - 16 * M, :] if kt >= 16 * M else None
                d = dict(nk=128, mask=mk, reads=[b_Kg[slot], b_Vg[slot]] + ([b_Rg] if mla else []))
                if kt >= 16 * M:
                    d["c_lo"] = 128 * max(0, (kt - 16 * M - 3 + 3) // 4)
                if mla:
                    d["nt"] = [Kg[slot][:, col:col + 128]]
                    d["rt"] = Rg[:, col:col + 128]
                    d["vm"] = [Vg[slot][:, r * 8 + m, :]]
                else:
                    d["kt"] = [Kg[slot][:, col:col + 128]]
                    d["v"] = [Vg[slot][:, r * 8 + m, :]]
                tl.append(d)
            return tl

        self.wcount = 0
        pcount = 0
        P.dma("pool", mask[:], i["mb_sb"], "mb2", writes=[self.b_mask])
        wslots = [self.load_wq(0), self.load_wq(1)]
        self.issue_cc([4, 5, 6, 7, 8])

        def qproj_sb(h):
            slot, s = h % 2, wslots[h]
            for half in range(2):
                bank = 6 + half
                ps = self.PS(bank)
                for c in range(DC):
                    self.mm(ps[:, :], wq[s][:, c, :], self.actb[:, c, half * 512:(half + 1) * 512], c == 0,
                            c == DC - 1, [self.b_wq[s], self.b_act[c]], [self.bk[bank]])
                self.act(Qs[slot][:, half * 512:(half + 1) * 512], ps[:, :], AF.Identity, [self.bk[bank]],
                         [b_Qs[slot]], scale=SB_SCALE)
            if h + 2 < H:
                wslots.append(self.load_wq(h + 2))

        load_kv(0, R_KT, 0)
        qproj_sb(0)
        for h in range(H):
            slot = h % 2
            for M in range(2):
                groups = [(0, 512, Qs[slot][:, M * 512:(M + 1) * 512], b_Qs[slot])]
                obank = 3 + pcount % 2
                pcount += 1

                def evac(O, ob, h=h, M=M):
                    self.act(self.oT_sb[:, h, M * 512:(M + 1) * 512], O[:, :], AF.Identity, [self.bk[ob]],
                             [self.b_oT_sb])
                self.sb_problem(w, 512, groups, key_tiles(M, slot, False), obank, evac)
                if M == 0 and h + 1 < H:
                    load_kv((h + 1) % 2, R_KT, h + 1)
                    qproj_sb(h + 1)
        P.dma("pool", mask[:], i["mb_mla"], "mb2", writes=[self.b_mask])
        for r in range(4):
            ea, ej = self.exo(r, R_RT, 64)
            P.dma("sp", Rg[:, r * 1024:(r + 1) * 1024], ea, "rg", reads=[self.b_exo[ej]],
                  writes=[b_Rg])
        mslots = [self.load_wq(0), self.load_wq(1)]

        def qproj_mla(h):
            slot, s = h % 2, mslots[h]
            for half in range(2):
                tsl = slice(half * 512, (half + 1) * 512)
                bank = 6 + half
                ps = self.PS(bank)
                for kc in range(4):
                    self.mm(ps[:, :], wuq[s][:, kc, 0:128], self.cqnT[:, kc, tsl], kc == 0, kc == 3,
                            [self.b_wuq[s], self.b_cqnT], [self.bk[bank]])
                self.act(Qs[slot][:, tsl], ps[:, :], AF.Identity, [self.bk[bank]], [b_Qs[slot]], scale=MLA_SCALE)
            for half in range(2):
                tsl = slice(half * 512, (half + 1) * 512)
                for part, bank in ((0, 6), (1, 7)):
                    ps = self.PS(bank)
                    for kc in range(4):
                        self.mm(ps[:64, :], wuq[s][:, kc, 128 + part * 64:192 + part * 64], self.cqnT[:, kc, tsl],
                                kc == 0, kc == 3, [self.b_wuq[s], self.b_cqnT], [self.bk[bank]])
                self.tt(Qrt[0][:, :], self.PS(6)[:64, :], cosf[:, tsl], ALU.mult, [self.bk[6], b_tabf], [b_Qrt])
                self.tt(Qrt[1][:, :], self.PS(7)[:64, :], sinf[:, tsl], ALU.mult, [self.bk[7], b_tabf], [b_Qrt])
                self.tt(Qrt[0][:, :], Qrt[0][:, :], Qrt[1][:, :], ALU.add, [b_Qrt], [b_Qrt])
                self.act(Qr[slot][:, tsl], Qrt[0][:, :], AF.Identity, [b_Qrt], [b_Qr[slot]], scale=MLA_SCALE)
            if h + 2 < H:
                mslots.append(self.load_wq(h + 2))

        load_kv(0, R_NT, 0)
        qproj_mla(0)
        for h in range(H):
            slot = h % 2
            for M in range(2):
                groups = [(0, 512, Qs[slot][:, M * 512:(M + 1) * 512], Qr[slot][:, M * 512:(M + 1) * 512],
                           b_Qs[slot])]
                obank = 3 + pcount % 2
                dbank = 5
                pcount += 1

                def evac(O, ob, rec, b_rec, h=h, M=M):
                    self.tt(self.oT_mla[:, h, M * 512:(M + 1) * 512], O[:, :], rec[:, :], ALU.mult,
                            [self.bk[ob], b_rec], [self.b_oT_mla])
                tl = key_tiles(M, slot, True)
                for t in tl:
                    t["reads"] = t["reads"] + [b_Qr[slot]]
                self.mla_problem(w, 512, groups, tl, obank, dbank, evac)
                if M == 0 and h + 1 < H:
                    load_kv((h + 1) % 2, R_NT, h + 1)
                    qproj_mla(h + 1)
        P.barrier()

    def phase_merge(self):
        P, A, i = self.P, self.A, self.i
        w_in = i["w_in"]
        A.region(self.R3 + 34000, self.R3 + 34000 + 34000)
        mg = A.alloc([128, DC, NT], BF16)
        b_mg = P.bufs(DC, "mg")
        A.region(self.R2, self.R3)
        wg = [A.alloc([128, DC, 256], BF16) for _ in range(2)]
        wb = [A.alloc([128, 16, 128], BF16) for _ in range(2)]
        gs = A.alloc([128, NT], F32)
        gm = A.alloc([128, NT], F32)
        t1 = A.alloc([128, NT], F32)
        t2 = A.alloc([128, NT], F32)
        b_wg, b_wb = P.bufs(2, "wg"), P.bufs(2, "wb")
        b_gs, b_gm, b_t1, b_t2 = P.bufs(3, "gs"), P.bufs(3, "gm"), P.bufs(3, "t1"), P.bufs(3, "t2")

        def load(ci):
            s = ci % 2
            P.dma("pool", wg[s][:, :, 0:128],
                  w_in[:, 4160 + ci * 128:4160 + (ci + 1) * 128].rearrange("(c p) n -> p c n", p=128),
                  f"wga{s}", writes=[b_wg[s]])
            P.dma("pool", wg[s][:, :, 128:256],
                  w_in[:, 6208 + ci * 128:6208 + (ci + 1) * 128].rearrange("(c p) n -> p c n", p=128),
                  f"wgb{s}", writes=[b_wg[s]])
            P.dma("pool", wb[s][:, 0:8, :],
                  i["w_br_sb"][:, ci * 128:(ci + 1) * 128].rearrange("(c p) n -> p c n", p=128),
                  f"wba{s}", writes=[b_wb[s]])
            P.dma("pool", wb[s][:, 8:16, :],
                  i["w_br_mla"][:, ci * 128:(ci + 1) * 128].rearrange("(c p) n -> p c n", p=128),
                  f"wbb{s}", writes=[b_wb[s]])

        load(0)
        load(1)
        for ci in range(DC):
            s = ci % 2
            for (b0, off, gt, b_g, bidx) in ((0, 0, gs, b_gs, ci), (3, 128, gm, b_gm, 16 + ci)):
                for c in range(DC):
                    for (t0, n, bk) in TT:
                        self.mm(self.PS(b0 + bk)[:, :n], wg[s][:, c, off:off + 128], self.actb[:, c, t0:t0 + n],
                                c == 0, c == DC - 1, [b_wg[s], self.b_act[c]], [self.bk[b0 + bk]])
                for (t0, n, bk) in TT:
                    self.act(gt[:, t0:t0 + n], self.PS(b0 + bk)[:, :n], AF.Sigmoid, [self.bk[b0 + bk], self.bconst],
                             [b_g[bk]], bias=self.bg[:, bidx:bidx + 1])
            for (b0, r0, oT, b_oT) in ((0, 0, self.oT_sb, self.b_oT_sb), (3, 8, self.oT_mla, self.b_oT_mla)):
                for h in range(H):
                    for (t0, n, bk) in TT:
                        self.mm(self.PS(b0 + bk)[:, :n], wb[s][:, r0 + h, :], oT[:, h, t0:t0 + n], h == 0, h == H - 1,
                                [b_wb[s], b_oT], [self.bk[b0 + bk]])
            for (t0, n, bk) in TT:
                self.tt(t1[:, t0:t0 + n], gs[:, t0:t0 + n], self.PS(bk)[:, :n], ALU.mult,
                        [b_gs[bk], self.bk[bk]], [b_t1[bk]])
                self.tt(t2[:, t0:t0 + n], gm[:, t0:t0 + n], self.PS(3 + bk)[:, :n], ALU.mult,
                        [b_gm[bk], self.bk[3 + bk]], [b_t2[bk]])
                self.tt(mg[:, ci, t0:t0 + n], t1[:, t0:t0 + n], t2[:, t0:t0 + n], ALU.add,
                        [b_t1[bk], b_t2[bk]], [b_mg[ci]])
            if ci + 2 < DC:
                load(ci + 2)
        P.barrier()
        P.dma("sp", self.resid[:], self.spill.ap(), "spill", writes=self.b_res)
        A.region(self.R3, self.R3 + 34000)
        wo = [A.alloc([128, DC, 512], BF16) for _ in range(2)]
        b_wo = P.bufs(2, "wo")

        def load_wo(q):
            s = q % 2
            P.dma("pool", wo[s][:], i["w_o"][:, q * 512:(q + 1) * 512].rearrange("(c p) n -> p c n", p=128),
                  f"wo{s}", writes=[b_wo[s]])
        load_wo(0)
        load_wo(1)
        for q in range(4):
            s = q % 2
            for j in range(4):
                ci = q * 4 + j
                b0 = 0 if ci % 2 == 0 else 3
                for c in range(DC):
                    for (t0, n, bk) in TT:
                        self.mm(self.PS(b0 + bk)[:, :n], wo[s][:, c, j * 128:(j + 1) * 128], mg[:, c, t0:t0 + n],
                                c == 0, c == DC - 1, [b_wo[s], b_mg[c]], [self.bk[b0 + bk]])
                for (t0, n, bk) in TT:
                    self.tt(self.resid[:, ci, t0:t0 + n], self.resid[:, ci, t0:t0 + n], self.PS(b0 + bk)[:, :n],
                            ALU.add, [self.b_res[ci], self.bk[b0 + bk]], [self.b_res[ci]])
            if q + 2 < 4:
                load_wo(q + 2)
        P.barrier()

    def phase_out(self):
        P, A, o = self.P, self.A, self.o
        A.region(self.R3, self.TOP)
        ys = [A.alloc([128, D], F32) for _ in range(2)]
        b_ys = P.bufs(2, "ys")
        k = 0
        for ti, (t0, m) in enumerate(TM):
            s = ti % 2
            for q in range(4):
                b = 4 + k % 4
                k += 1
                ps = self.PS(b)
                for j in range(4):
                    c = q * 4 + j
                    self.tr(ps[:m, j * 128:(j + 1) * 128], self.resid[:, c, t0:t0 + m], self.ident[:, :],
                            [self.b_res[c], self.bconst], [self.bk[b]])
                if k % 2 == 0:
                    self.act(ys[s][:m, q * 512:(q + 1) * 512], ps[:m, :], AF.Identity, [self.bk[b]], [b_ys[s]])
                else:
                    self.cp(ys[s][:m, q * 512:(q + 1) * 512], ps[:m, :], [self.bk[b]], [b_ys[s]])
            P.dma("sp", o["y"][t0:t0 + m, :], ys[s][:m, :], f"y{s}", reads=[b_ys[s]])

    def dump_act(self):
        dbg = self.nc.dram_tensor("dbg", [128, DC, NT], F32, kind="ExternalOutput").ap()
        self.P.dma("sp", dbg, self.resid[:], "dbg", reads=self.b_res)


def build_nc(stop_after=None, **kw):
    b = Builder(stop_after=stop_after, **kw)
    b.nc._used_inputs = list(b.i.keys())
    return b.nc


def host_inputs(inp):
    f = np.float32
    xp = np.asarray(inp["x_prompt"], f)
    xs = np.asarray(inp["x_sample"], f)
    common = {
        "w1a": np.ascontiguousarray(inp["ffn1_w_in"][0]), "w2a": np.ascontiguousarray(inp["ffn1_w_out"][0]),
        "w1b": np.ascontiguousarray(inp["ffn2_w_in"][0]), "w2b": np.ascontiguousarray(inp["ffn2_w_out"][0]),
        "w_in": np.ascontiguousarray(inp["w_in"][0]), "w_uq": np.ascontiguousarray(inp["w_uq"][0]),
        "w_ukv": np.ascontiguousarray(inp["w_ukv"][0]), "w_br_sb": np.ascontiguousarray(inp["w_br_sb"][0]),
        "w_br_mla": np.ascontiguousarray(inp["w_br_mla"][0]), "w_o": np.ascontiguousarray(inp["w_o"][0]),
    }
    lnp = np.concatenate([np.asarray(inp[k][0], f).reshape(16, 128).T for k in
                          ("ln1_g", "ln1_b", "ln2_g", "ln2_b", "ln3_g", "ln3_b")], axis=1)
    common["lnp"] = np.ascontiguousarray(lnp)
    common["bgate"] = np.ascontiguousarray(np.asarray(inp["b_gate"][0], f).reshape(32, 128).T)
    common["gcq"] = np.ascontiguousarray(np.broadcast_to(np.asarray(inp["g_cq"][0], f), (128, 512)))
    common["gckv"] = np.ascontiguousarray(np.broadcast_to(np.asarray(inp["g_ckv"][0], f), (128, 512)))
    common["ident"] = np.eye(128, dtype=f)
    cm = np.zeros((128, 4, 128), f)
    jj, ss = np.meshgrid(np.arange(128), np.arange(128), indexing="ij")
    cm[:, 0, :] = -(jj >= ss).astype(f)
    cm[:, 1, :] = -1.0
    cm[:, 2, :] = 1.0
    common["cmats"] = cm
    kk, tq = np.meshgrid(np.arange(16), np.arange(16), indexing="ij")
    common["mb_new"] = np.ascontiguousarray(np.tile(np.where(kk < tq, 0.0, NEG).astype(f), (1, 8)))
    maps = []
    inv_freq = (10000.0 ** (-np.arange(0, 64, 2, dtype=np.float32) / 64)).astype(f)
    for c in range(8):
        b, r = c // 4, c % 4
        tiles = [4 * m + r for m in range(8)]
        rows = np.concatenate([np.arange(g * 128, (g + 1) * 128) for g in tiles])
        x = np.concatenate([xp[b, rows], xs[2 * c], xs[2 * c + 1]], 0)
        pos = np.concatenate([rows, PAST + np.arange(16), PAST + np.arange(16)]).astype(f)
        ang = pos[:, None] * inv_freq[None, :]
        cos, sin = np.cos(ang).astype(f), np.sin(ang).astype(f)
        cpad = np.zeros((9 * 128, 32), f)
        spad = np.zeros((9 * 128, 32), f)
        cpad[:NT] = cos
        spad[:NT] = sin
        m = dict(common)
        m["x"] = np.ascontiguousarray(x)
        m["cos_tm"] = np.ascontiguousarray(cpad.reshape(9, 128, 32).transpose(1, 0, 2))
        m["sin_tm"] = np.ascontiguousarray(spad.reshape(9, 128, 32).transpose(1, 0, 2))
        m["cos_fm"] = np.ascontiguousarray(np.concatenate([cos.T, cos.T], 0))
        m["sin_fm"] = np.ascontiguousarray(np.concatenate([-sin.T, sin.T], 0))
        k_, q_ = np.meshgrid(np.arange(128), np.arange(128), indexing="ij")
        msb = np.zeros((128, 16, 512), f)
        mml = np.zeros((128, 16, 512), f)
        for o in range(16):
            for i4 in range(4):
                qt = 4 * i4 + r
                sl = slice(i4 * 128, (i4 + 1) * 128)
                if o > qt:
                    msb[:, o, sl] = NEG
                    mml[:, o, sl] = NEG
                elif o == qt:
                    msb[:, o, sl] = np.where(k_ < q_, 0.0, NEG)
                    mml[:, o, sl] = np.where((k_ // 64) <= (q_ // 64), 0.0, NEG)
        m["mb_sb"] = msb
        m["mb_mla"] = mml
        m["c_k"] = np.ascontiguousarray(np.asarray(inp["cache_sb_k"][0, 2 * c:2 * c + 2], f).reshape(2, PAST, 1024))
        m["c_v"] = np.ascontiguousarray(np.asarray(inp["cache_sb_v"][0, 2 * c:2 * c + 2], f).reshape(2, PAST, 1024))
        m["c_ckv"] = np.ascontiguousarray(np.asarray(inp["cache_mla_ckv"][0, 2 * c:2 * c + 2], f))
        m["c_kr"] = np.ascontiguousarray(np.asarray(inp["cache_mla_krope"][0, 2 * c:2 * c + 2], f))
        maps.append(m)
    return maps


_NC = None


def kernel(**inputs):
    global _NC
    maps = host_inputs(inputs)
    if _NC is None:
        _NC = build_nc()
    used = _NC._used_inputs
    maps = [{k: m[k] for k in used} for m in maps]
    res = run_bass_kernel_spmd(_NC, maps, core_ids=list(range(8)))
    return assemble(res.results)


def assemble(results):
    f = np.float32
    y_p = np.zeros((2, 4096, D), f)
    y_s = np.zeros((16, 16, D), f)
    nk_p = np.zeros((1, 2, 4096, 8, 128), f)
    nv_p = np.zeros((1, 2, 4096, 8, 128), f)
    nc_p = np.zeros((1, 2, 4096, 512), f)
    nr_p = np.zeros((1, 2, 4096, 64), f)
    nk_s = np.zeros((1, 16, 16, 8, 128), f)
    nv_s = np.zeros((1, 16, 16, 8, 128), f)
    nc_s = np.zeros((1, 16, 16, 512), f)
    nr_s = np.zeros((1, 16, 16, 64), f)
    for c in range(8):
        b, r = c // 4, c % 4
        R = results[c]
        for m in range(8):
            g = 4 * m + r
            dst = slice(g * 128, (g + 1) * 128)
            src = slice(m * 128, (m + 1) * 128)
            y_p[b, dst] = R["y"][src]
            nk_p[0, b, dst] = R["nk"][src].reshape(128, 8, 128)
            nv_p[0, b, dst] = R["nv"][src].reshape(128, 8, 128)
            nc_p[0, b, dst] = R["nckv"][src]
            nr_p[0, b, dst] = R["nkr"][src]
        for s in range(2):
            src = slice(1024 + 16 * s, 1024 + 16 * s + 16)
            y_s[2 * c + s] = R["y"][src]
            nk_s[0, 2 * c + s] = R["nk"][src].reshape(16, 8, 128)
            nv_s[0, 2 * c + s] = R["nv"][src].reshape(16, 8, 128)
            nc_s[0, 2 * c + s] = R["nckv"][src]
            nr_s[0, 2 * c + s] = R["nkr"][src]
    return (y_p, y_s, nk_p, nv_p, nc_p, nr_p, nk_s, nv_s, nc_s, nr_s)
```

```python
import numpy as np
import concourse.bass as bass
import concourse.mybir as mybir
from concourse.bass_utils import run_bass_kernel_spmd

F32 = mybir.dt.float32
BF16 = mybir.dt.bfloat16
AF = mybir.ActivationFunctionType
ALU = mybir.AluOpType

SEM_LIM = 30000
NEG = -30000.0

D = 2048
DC = 16
FF = 5504
FC = 43
NP_ = 1024
NS = 32
NT = NP_ + NS
TT = [(0, 352, 0), (352, 352, 1), (704, 352, 2)]
TM = [(i * 128, 128) for i in range(8)] + [(1024, 32)]
H = 8
PAST = 1024
ALPHA = 2.0 ** 0.25
SB_SCALE = 128.0 ** -0.5
MLA_SCALE = 192.0 ** -0.5
LN_EPS = 1e-5
RMS_EPS = 1e-6
EX_ROWS = 4160
R_KT, R_V, R_NT, R_VM, R_RT = 0, 1024, 2048, 3072, 4096


class Buf:
    __slots__ = ("name", "lastw", "readers", "excl")

    def __init__(self, name, excl=False):
        self.name = name
        self.lastw = None
        self.readers = {}
        self.excl = excl


class Op:
    __slots__ = ("eng", "fn", "deps", "signal", "count", "key", "seq", "is_dma", "inc")

    def __init__(self, eng, fn, key, seq, is_dma, inc=None):
        self.eng = eng
        self.fn = fn
        self.key = key
        self.seq = seq
        self.is_dma = is_dma
        self.deps = {}
        self.signal = is_dma
        self.count = 0
        self.inc = inc if inc is not None else (16 if is_dma else 1)


class Prog:
    ENGS = ("pe", "act", "dve", "pool", "sp")

    def __init__(self, nc):
        self.nc = nc
        self.ops = {e: [] for e in self.ENGS}
        self.latest = {}
        self.pending = {e: {} for e in self.ENGS}
        self.seq = 0
        self.dma_counts = {}
        self.dma_inc = {}
        self.nbuf = 0

    def buf(self, name=None, excl=False):
        self.nbuf += 1
        return Buf(name or f"b{self.nbuf}", excl)

    def bufs(self, n, name="b"):
        return [self.buf(f"{name}{i}") for i in range(n)]

    def _add(self, eng, fn, reads, writes, semkey=None, inc=None, touch=()):
        is_dma = semkey is not None
        key = ("dma", semkey) if is_dma else eng
        self.seq += 1
        o = Op(eng, fn, key, self.seq, is_dma, inc)
        if is_dma:
            self.dma_inc[semkey] = o.inc
        deps = {}

        def add(d):
            if d is None:
                return
            if d.key == "pe" and eng == "pe" and not is_dma:
                return
            cur = deps.get(d.key)
            if cur is None or cur.seq < d.seq:
                deps[d.key] = d

        for b in reads:
            add(b.lastw)
            if b.excl:
                for r in b.readers.values():
                    if r.key != key:
                        add(r)
        for b in writes:
            add(b.lastw)
            for r in b.readers.values():
                add(r)
        for d in self.pending[eng].values():
            add(d)
        self.pending[eng] = {}
        o.deps = deps
        for b in reads:
            cur = b.readers.get(key)
            if cur is None or cur.seq < o.seq:
                b.readers[key] = o
        for b in writes:
            b.lastw = o
            b.readers = {}
        for b in touch:
            b.lastw = o
            b.readers = {}
        if is_dma:
            c = self.dma_counts.get(semkey, 0) + 1
            self.dma_counts[semkey] = c
            o.count = c
        self.ops[eng].append(o)
        self.latest[key] = o
        return o

    def op(self, eng, fn, reads=(), writes=()):
        return self._add(eng, fn, reads, writes)

    def dma(self, queue, out, in_, semkey, reads=(), writes=(), **kw):
        def fn(e):
            return e.dma_start(out=out, in_=in_, **kw)
        return self._add(queue, fn, reads, writes, semkey=semkey)

    def dma_batch(self, queue, pairs, semkey, reads=(), writes=()):
        n = len(pairs)
        for j, (out, in_) in enumerate(pairs):
            def fn(e, out=out, in_=in_):
                return e.dma_start(out=out, in_=in_)
            if n == 1:
                self._add(queue, fn, reads, writes, semkey=semkey)
            elif j == 0:
                self._add(queue, fn, reads, writes, semkey=semkey)
            elif j == n - 1:
                self._add(queue, fn, (), (), semkey=semkey, touch=writes)
            else:
                self._add(queue, fn, (), (), semkey=semkey)

    def barrier(self):
        snap = {k: v for k, v in self.latest.items()
                if not (isinstance(k, tuple) and str(k[1]).startswith("cc"))}
        for e in self.ENGS:
            self.pending[e] = dict(snap)

    def emit(self):
        nc = self.nc
        for e in self.ENGS:
            for o in self.ops[e]:
                for d in o.deps.values():
                    d.signal = True
        tot = {}
        for e in self.ENGS:
            c = 0
            for o in self.ops[e]:
                if not o.is_dma and o.signal:
                    c += 1
                    o.count = c
            tot[e] = c
        sems = {}

        def nsem(units):
            return max(1, (units + SEM_LIM - 1) // SEM_LIM)

        for e in self.ENGS:
            sems[e] = [nc.alloc_semaphore(f"s_{e}_{i}") for i in range(nsem(tot[e]))]
        for k, c in self.dma_counts.items():
            sems[("dma", k)] = [nc.alloc_semaphore(f"sd_{k}_{i}") for i in range(nsem(c * self.dma_inc[k]))]

        def target(o):
            units = o.count * o.inc
            idx = (units - 1) // SEM_LIM
            val = (units - 1) % SEM_LIM + 1
            return sems[o.key][idx], idx, val

        handles = {"pe": "tensor", "act": "scalar", "dve": "vector", "pool": "gpsimd", "sp": "sync"}
        final_waits = [o for k, o in self.latest.items() if o.is_dma]
        with nc.Block() as block:
            for e in self.ENGS:
                ops = self.ops[e]
                extra = final_waits if e == "sp" else []

                def body(eng, ops=ops, extra=extra):
                    waited = {}

                    def wait_for(d):
                        sem, idx, val = target(d)
                        wk = (d.key, idx)
                        if waited.get(wk, 0) < val:
                            eng.wait_ge(sem, val)
                            waited[wk] = val

                    for o in ops:
                        for d in o.deps.values():
                            wait_for(d)
                        ins = o.fn(eng)
                        if o.signal:
                            sem, idx, val = target(o)
                            ins.then_inc(sem, o.inc)
                    for d in extra:
                        wait_for(d)

                getattr(block, handles[e])(body)


class Arena:
    def __init__(self, nc):
        self.nc = nc
        b0 = nc.sbuf_base
        n = (nc.sbuf_top - b0 - 3072) // 4
        self.slab = nc.alloc_sbuf_tensor("slab", [128, n], F32)
        self.base = (b0 + 63) // 64 * 64
        self.top = (b0 + n * 4) // 64 * 64
        self.cur = self.base
        self.lim = self.top
        self.n = 0

    def size(self, shape, dtype):
        per = 4 if dtype == F32 else 2
        for s in shape[1:]:
            per *= s
        return (per + 63) // 64 * 64

    def alloc(self, shape, dtype):
        per = self.size(shape, dtype)
        off = self.cur
        self.cur += per
        assert self.cur <= self.lim, f"SBUF overflow {self.cur} > {self.lim}"
        self.n += 1
        return self.nc.alloc_sbuf_tensor_at(f"sb{self.n}", list(shape), dtype, offset=off)

    def region(self, lo, hi):
        self.cur = (lo + 63) // 64 * 64
        self.lim = hi


class Builder:
    IN_SHAPES = {
        "x": [NT, D], "w1a": [D, 2 * FF], "w2a": [FF, D], "w1b": [D, 2 * FF], "w2b": [FF, D],
        "w_in": [D, 8256], "w_uq": [512, 1536], "w_ukv": [512, 2048], "w_br_sb": [1024, D],
        "w_br_mla": [1024, D], "w_o": [D, D], "lnp": [128, 96], "bgate": [128, 32], "gcq": [128, 512],
        "gckv": [128, 512], "cos_tm": [128, 9, 32], "sin_tm": [128, 9, 32], "cos_fm": [64, NT],
        "sin_fm": [64, NT], "ident": [128, 128], "cmats": [128, 4, 128], "mb_sb": [128, 16, 512],
        "mb_mla": [128, 16, 512], "mb_new": [16, 128], "c_k": [2, PAST, 1024], "c_v": [2, PAST, 1024],
        "c_ckv": [2, PAST, 512], "c_kr": [2, PAST, 64],
    }

    class _Lazy(dict):
        def __init__(self, b):
            super().__init__()
            self.b = b

        def __missing__(self, k):
            v = self.b.din(k, Builder.IN_SHAPES[k])
            self[k] = v
            return v

    def __init__(self, stop_after=None):
        self.stop_after = stop_after
        nc = bass.Bass("TRN2", target_bir_lowering=False)
        self.nc = nc
        self.P = Prog(nc)
        self.A = Arena(nc)
        self.i = Builder._Lazy(self)
        self.o = {}
        self.build()
        self.P.emit()

    def din(self, name, shape, dtype=F32):
        return self.nc.dram_tensor(name, list(shape), dtype, kind="ExternalInput").ap()

    def dout(self, name, shape, dtype=F32):
        return self.nc.dram_tensor(name, list(shape), dtype, kind="ExternalOutput").ap()

    def mm(self, out, lhsT, rhs, start, stop, reads, writes):
        self.P.op("pe", lambda e: e.matmul(out, lhsT, rhs, start=start, stop=stop, skip_group_check=True),
                  reads, writes)

    def tr(self, out, in_, ident, reads, writes):
        self.P.op("pe", lambda e: e.transpose(out, in_, ident), reads, writes)

    def act(self, out, in_, func, reads, writes, **kw):
        self.P.op("act", lambda e: e.activation(out, in_, func, **kw), reads, writes)

    def tt(self, out, in0, in1, op, reads, writes, eng="dve"):
        self.P.op(eng, lambda e: e.tensor_tensor(out, in0, in1, op), reads, writes)

    def ts(self, out, in0, s1, s2, op0, op1, reads, writes, eng="dve"):
        if op1 is None:
            self.P.op(eng, lambda e: e.tensor_scalar(out, in0, s1, None, op0), reads, writes)
        else:
            self.P.op(eng, lambda e: e.tensor_scalar(out, in0, s1, s2, op0, op1), reads, writes)

    def stt(self, out, in0, scalar, in1, op0, op1, reads, writes):
        self.P.op("dve", lambda e: e.scalar_tensor_tensor(out, in0, scalar, in1, op0, op1), reads, writes)

    def cp(self, out, in_, reads, writes, eng="dve"):
        self.P.op(eng, lambda e: e.tensor_copy(out, in_), reads, writes)

    def memset(self, ap, val, writes, eng="dve"):
        self.P.op(eng, lambda e: e.memset(ap, val), [], writes)

    def PS(self, b):
        return self.psum[:, b * 512:(b + 1) * 512]

    def exi(self, row0, nrows):
        j = row0 // 512
        a = row0 - j * 512
        return self.ex_in[j].ap()[a:a + nrows, :], j

    def exo(self, r, row0, nrows):
        j = row0 // 512
        a = r * self.ex_rows[j] + row0 - j * 512
        return self.ex_out[j].ap()[a:a + nrows, :], j

    def build(self):
        nc, P, A, i = self.nc, self.P, self.A, self.i
        st = self.stop_after
        self.psum = nc.alloc_psum_tensor("ps", [128, 4096], F32)
        self.bk = [P.buf(f"bank{b}", True) for b in range(8)]

        self.ident = A.alloc([128, 128], F32)
        self.identb = A.alloc([128, 128], BF16)
        self.cm = A.alloc([128, 4, 128], BF16)
        self.lnp = A.alloc([128, 96], F32)
        self.lnpa = A.alloc([128, 96], F32)
        self.bg = A.alloc([128, 32], F32)
        self.epsc = A.alloc([128, 4], F32)
        self.ones32 = A.alloc([128, 128], F32)
        self.bconst = P.buf("consts")
        P.dma("sp", self.ident[:], i["ident"], "c0", writes=[self.bconst])
        P.dma("pool", self.cm[:], i["cmats"], "c1", writes=[self.bconst])
        P.dma("pool", self.identb[:], i["ident"], "c4", writes=[self.bconst])
        P.dma("sp", self.lnp[:], i["lnp"], "c2", writes=[self.bconst])
        P.dma("sp", self.bg[:], i["bgate"], "c3", writes=[self.bconst])
        self.ts(self.lnpa[:], self.lnp[:], ALPHA, None, ALU.mult, None, [self.bconst], [self.bconst])
        self.memset(self.epsc[:, 0:1], LN_EPS, [self.bconst])
        self.memset(self.epsc[:, 1:2], RMS_EPS, [self.bconst])
        self.memset(self.epsc[:, 2:3], 1.0, [self.bconst])
        self.memset(self.ones32[:, :], 1.0, [self.bconst])
        self.actb = A.alloc([128, DC, NT], BF16)
        self.b_act = P.bufs(DC, "act")
        self.R2 = A.cur
        self.resid = A.alloc([128, DC, NT], F32)
        self.b_res = P.bufs(DC, "res")
        self.R3 = A.cur
        self.TOP = A.top
        assert self.TOP - self.R3 >= 86000, (self.TOP, self.R3)

        if st is None:
            o = self.o
            o["y"] = self.dout("y", [NT, D])
            o["nk"] = self.dout("nk", [NT, 1024])
            o["nv"] = self.dout("nv", [NT, 1024])
            o["nckv"] = self.dout("nckv", [NT, 512])
            o["nkr"] = self.dout("nkr", [NT, 64])
            self.ex_rows = [512] * 8 + [64]
            self.ex_in = [nc.dram_tensor(f"ex_in{j}", [n, NP_], BF16) for j, n in enumerate(self.ex_rows)]
            self.ex_out = [nc.dram_tensor(f"ex_out{j}", [4 * n, NP_], BF16) for j, n in enumerate(self.ex_rows)]
            self.b_exc = P.bufs(9, "exc")
            self.b_exo = P.bufs(9, "exo")
            self.spill = nc.dram_tensor("spill", [128, DC, NT], F32)

        self.ffn_prefetch(i["w1a"], i["w2a"])
        self.phase_load_x()
        if st == "x":
            return self.dump_act()
        self.phase_ffn()
        if st == "ffn1":
            return self.dump_act()
        if st is None:
            self.proj_prefetch()
        self.phase_ln(0, final=False)
        if st == "ln1":
            return self.dump_act()
        P.dma("sp", self.spill.ap(), self.resid[:], "spill", reads=self.b_res)
        P.barrier()
        self.phase_proj()
        self.phase_sample_attn()
        self.phase_prompt_attn()
        self.phase_merge()
        self.ffn_prefetch(i["w1b"], i["w2b"])
        self.phase_ln(1, final=False)
        self.phase_ffn()
        self.phase_ln(2, final=True)
        self.phase_out()

    def phase_load_x(self):
        P, A, i = self.P, self.A, self.i
        A.region(self.TOP - 19072, self.TOP)
        xs = [A.alloc([128, D], F32) for _ in range(2)]
        bx = P.bufs(2, "xs")
        k = 0
        for ti, (t0, m) in enumerate(TM):
            s = ti % 2
            P.dma("sp", xs[s][:m, :], i["x"][t0:t0 + m, :], f"x{s}", writes=[bx[s]])
            for q in range(4):
                b = 4 + k % 4
                k += 1
                ps = self.PS(b)
                for j in range(4):
                    c = q * 4 + j
                    self.tr(ps[:, j * 128:j * 128 + m], xs[s][:m, c * 128:(c + 1) * 128], self.ident[:m, :m],
                            [bx[s], self.bconst], [self.bk[b]])
                src = ps.rearrange("p (j t) -> p j t", j=4)[:, :, :m]
                self.act(self.resid[:, q * 4:q * 4 + 4, t0:t0 + m], src, AF.Identity, [self.bk[b]],
                         self.b_res[q * 4:q * 4 + 4], scale=ALPHA)
                self.cp(self.actb[:, q * 4:q * 4 + 4, t0:t0 + m], src, [self.bk[b]], self.b_act[q * 4:q * 4 + 4])
        P.barrier()

    def ffn_prefetch(self, w1, w2):
        P, A = self.P, self.A
        A.region(self.R3, self.TOP - 19072)
        NPAIR, NG = 22, 11
        w1g = [A.alloc([128, DC, 256], BF16) for _ in range(2)]
        w1u = [A.alloc([128, DC, 256], BF16) for _ in range(2)]
        self.ffn_free_lo = A.cur
        aT = [A.alloc([128, 4, NT], BF16) for _ in range(2)]
        sg = A.alloc([128, NT], BF16)
        self.ffn_free_hi = A.cur
        w2s = [A.alloc([128, 4, D], BF16) for _ in range(2)]
        b_w1, b_aT, b_w2 = P.bufs(2, "w1"), P.bufs(2, "aT"), P.bufs(2, "w2")
        b_sg = P.bufs(3, "sg")

        def load_w1(p):
            s = p % 2
            w = (2 if p < 21 else 1) * 128
            c0 = p * 256
            P.dma("pool", w1g[s][:, :, :w], w1[:, c0:c0 + w].rearrange("(c p) n -> p c n", p=128),
                  f"w1g{s}", writes=[b_w1[s]])
            P.dma("pool", w1u[s][:, :, :w], w1[:, FF + c0:FF + c0 + w].rearrange("(c p) n -> p c n", p=128),
                  f"w1u{s}", writes=[b_w1[s]])

        def load_w2(g):
            s = g % 2
            nch = 4 if g < 10 else 3
            r0 = g * 512
            P.dma("pool", w2s[s][:, :nch, :], w2[r0:r0 + nch * 128, :].rearrange("(j p) n -> p j n", p=128),
                  f"w2{s}", writes=[b_w2[s]])

        load_w1(0)
        load_w1(1)
        load_w2(0)
        load_w2(1)
        self.ffn_state = (w1g, w1u, aT, w2s, sg, b_w1, b_aT, b_w2, b_sg, load_w1, load_w2)

    def phase_ffn(self):
        P = self.P
        NPAIR, NG = 22, 11
        (w1g, w1u, aT, w2s, sg, b_w1, b_aT, b_w2, b_sg, load_w1, load_w2) = self.ffn_state
        for g in range(NG):
            gs = g % 2
            nch_g = 4 if g < 10 else 3
            for pp in range(2):
                p = g * 2 + pp
                if p >= NPAIR:
                    continue
                s = p % 2
                nch = 2 if p < 21 else 1
                for jj in range(nch):
                    ja = pp * 2 + jj
                    for (b0, wt) in ((0, w1g[s]), (3, w1u[s])):
                        for c in range(DC):
                            for (t0, n, bk) in TT:
                                self.mm(self.PS(b0 + bk)[:, :n], wt[:, c, jj * 128:(jj + 1) * 128],
                                        self.actb[:, c, t0:t0 + n], c == 0, c == DC - 1,
                                        [b_w1[s], self.b_act[c]], [self.bk[b0 + bk]])
                    for (t0, n, bk) in TT:
                        self.act(sg[:, t0:t0 + n], self.PS(bk)[:, :n], AF.Silu, [self.bk[bk]], [b_sg[bk]])
                    for (t0, n, bk) in TT:
                        self.tt(aT[gs][:, ja, t0:t0 + n], sg[:, t0:t0 + n], self.PS(3 + bk)[:, :n],
                                ALU.mult, [b_sg[bk], self.bk[3 + bk]], [b_aT[gs]])
                if p + 2 < NPAIR:
                    load_w1(p + 2)
            for ci in range(DC):
                b0 = 0 if ci % 2 == 0 else 3
                for jj in range(nch_g):
                    for (t0, n, bk) in TT:
                        self.mm(self.PS(b0 + bk)[:, :n], w2s[gs][:, jj, ci * 128:(ci + 1) * 128],
                                aT[gs][:, jj, t0:t0 + n], jj == 0, jj == nch_g - 1,
                                [b_w2[gs], b_aT[gs]], [self.bk[b0 + bk]])
                for (t0, n, bk) in TT:
                    self.stt(self.resid[:, ci, t0:t0 + n], self.PS(b0 + bk)[:, :n], 0.5,
                             self.resid[:, ci, t0:t0 + n], ALU.mult, ALU.add,
                             [self.bk[b0 + bk], self.b_res[ci]], [self.b_res[ci]])
            if g + 2 < NG:
                load_w2(g + 2)
        P.barrier()

    def phase_ln(self, idx, final):
        P, A = self.P, self.A
        A.region(self.ffn_free_lo, self.ffn_free_hi)
        xb = [A.alloc([128, NT], BF16) for _ in range(2)]
        xq = [A.alloc([128, NT], BF16) for _ in range(2)]
        mt = A.alloc([128, NT], F32)
        rs = A.alloc([128, NT], F32)
        A.region(self.TOP - 19072, self.TOP)
        tmp = [A.alloc([128, NT], F32) for _ in range(2)]
        b_xb, b_xq = P.bufs(2, "xb"), P.bufs(2, "xq")
        b_mt, b_rs = P.buf("mt"), P.buf("rs")
        b_tmp = P.bufs(2, "tmp")
        ones = self.cm[:, 2, :]
        for c in range(DC):
            s = c % 2
            self.act(xb[s][:], self.resid[:, c, :], AF.Copy, [self.b_res[c]], [b_xb[s]])
            self.tt(xq[s][:], self.resid[:, c, :], self.resid[:, c, :], ALU.mult, [self.b_res[c]], [b_xq[s]])
            for (t0, n, bk) in TT:
                self.mm(self.PS(bk)[:, :n], ones, xb[s][:, t0:t0 + n], c == 0, c == DC - 1,
                        [b_xb[s], self.bconst], [self.bk[bk]])
            for (t0, n, bk) in TT:
                self.mm(self.PS(3 + bk)[:, :n], ones, xq[s][:, t0:t0 + n], c == 0, c == DC - 1,
                        [b_xq[s], self.bconst], [self.bk[3 + bk]])
        for (t0, n, bk) in TT:
            sl = slice(t0, t0 + n)
            pa = self.PS(bk)[:, :n]
            pb = self.PS(3 + bk)[:, :n]
            self.ts(mt[:, sl], pa, 1.0 / D, None, ALU.mult, None, [self.bk[bk]], [b_mt])
            self.tt(tmp[0][:, sl], mt[:, sl], mt[:, sl], ALU.mult, [b_mt], [b_tmp[0]])
            self.stt(rs[:, sl], pb, 1.0 / D, tmp[0][:, sl], ALU.mult, ALU.subtract,
                     [self.bk[3 + bk], b_tmp[0]], [b_rs])
            self.act(rs[:, sl], rs[:, sl], AF.Ln, [b_rs, self.bconst], [b_rs], bias=self.epsc[:, 0:1])
            self.act(rs[:, sl], rs[:, sl], AF.Exp, [b_rs], [b_rs], scale=-0.5)
            self.stt(mt[:, sl], mt[:, sl], -1.0, rs[:, sl], ALU.mult, ALU.mult, [b_mt, b_rs], [b_mt])
        g = self.lnp[:, idx * 32:idx * 32 + 16]
        b = self.lnp[:, idx * 32 + 16:idx * 32 + 32]
        ga = self.lnpa[:, idx * 32:idx * 32 + 16]
        ba = self.lnpa[:, idx * 32 + 16:idx * 32 + 32]
        for c in range(DC):
            s = c % 2
            self.tt(tmp[s][:], self.resid[:, c, :], rs[:], ALU.mult, [self.b_res[c], b_rs], [b_tmp[s]])
            self.tt(tmp[s][:], tmp[s][:], mt[:], ALU.add, [b_tmp[s], b_mt], [b_tmp[s]])
            self.act(self.actb[:, c, :], tmp[s][:], AF.Identity, [b_tmp[s], self.bconst], [self.b_act[c]],
                     scale=g[:, c:c + 1], bias=b[:, c:c + 1])
            sc, bi = (g, b) if final else (ga, ba)
            self.act(self.resid[:, c, :], tmp[s][:], AF.Identity, [b_tmp[s], self.bconst], [self.b_res[c]],
                     scale=sc[:, c:c + 1], bias=bi[:, c:c + 1])
        P.barrier()

    PROJ_BLOCKS = [("k", 1024, 512, 0), ("k", 1536, 512, 1), ("v", 2048, 512, 0), ("v", 2560, 512, 1),
                   ("cq", 3072, 512, 0), ("ckv", 3584, 512, 0), ("kr", 4096, 64, 0)]

    def proj_prefetch(self):
        P, A, i = self.P, self.A, self.i
        A.region(self.R3, self.R3 + 32768)
        self.wblk = [A.alloc([128, DC, 512], BF16) for _ in range(2)]
        self.b_wblk = P.bufs(2, "wblk")
        self.load_blk(0)
        self.load_blk(1)

    def load_blk(self, bi):
        kind, c0, nc_, _ = self.PROJ_BLOCKS[bi]
        s = bi % 2
        self.P.dma("pool", self.wblk[s][:, :, :nc_],
                   self.i["w_in"][:, c0:c0 + nc_].rearrange("(c p) n -> p c n", p=128),
                   f"wblk{s}", writes=[self.b_wblk[s]])

    def phase_proj(self):
        P, A, i, o = self.P, self.A, self.i, self.o
        w_in = i["w_in"]
        A.region(self.TOP - 34000, self.TOP)
        self.cqnT = A.alloc([128, 4, NT], BF16)
        self.KTs = A.alloc([128, 8, NS], BF16)
        self.NTs = A.alloc([128, 8, NS], BF16)
        self.krT = A.alloc([64, NT], BF16)
        self.Vs_new = A.alloc([32, 1024], BF16)
        self.VMs_new = A.alloc([32, 1024], BF16)
        self.wukv = A.alloc([128, 4, 2048], BF16)
        self.b_cqnT, self.b_KTs, self.b_NTs = P.buf("cqnT"), P.buf("KTs"), P.buf("NTs")
        self.b_krT, self.b_Vsn, self.b_VMsn, self.b_wukv = P.buf("krT"), P.buf("Vsn"), P.buf("VMsn"), P.buf("wukv")
        P.dma("pool", self.wukv[:], i["w_ukv"].rearrange("(c p) n -> p c n", p=128), "wukv", writes=[self.b_wukv])
        A.region(self.R2, self.TOP - 34000)
        wblk = self.wblk
        stg32 = [A.alloc([128, 512], F32) for _ in range(2)]
        stgb = [A.alloc([128, 512], BF16) for _ in range(2)]
        nrm = [A.alloc([128, 512], F32) for _ in range(2)]
        junk = A.alloc([128, 512], F32)
        ckvnT = A.alloc([128, 4, NT], BF16)
        KTl = [A.alloc([128, NT], BF16) for _ in range(2)]
        gcq = A.alloc([128, 512], F32)
        gckv = A.alloc([128, 512], F32)
        cos = A.alloc([128, 9, 32], F32)
        sin = A.alloc([128, 9, 32], F32)
        kro = [A.alloc([128, 64], F32) for _ in range(2)]
        rt = [A.alloc([128, 32], F32) for _ in range(4)]
        ssq = A.alloc([128, 4], F32)
        b_wblk = self.b_wblk
        b_stg32, b_stgb, b_nrm = P.bufs(2, "stg32"), P.bufs(2, "stgb"), P.bufs(2, "nrm")
        b_junk, b_ckvnT, b_KTl = P.buf("junk"), P.buf("ckvnT"), P.bufs(2, "KTl")
        b_tab, b_kro, b_rt, b_ssq = P.buf("tab"), P.bufs(2, "kro"), P.buf("rt"), P.buf("ssq")
        P.dma("sp", gcq[:], i["gcq"], "t0", writes=[b_tab])
        P.dma("sp", gckv[:], i["gckv"], "t1", writes=[b_tab])
        P.dma("sp", cos[:], i["cos_tm"], "t2", writes=[b_tab])
        P.dma("sp", sin[:], i["sin_tm"], "t3", writes=[b_tab])
        blocks = self.PROJ_BLOCKS
        cnt = {"s32": 0, "sb": 0, "nrm": 0, "bank": 0, "ktl": 0, "kro": 0}

        load_blk = self.load_blk

        def rmsnorm(ps, m, bkb, gtile):
            self.act(junk[:m, :], ps[:m, :], AF.Square, [bkb], [b_junk])
            self.P.op("dve", lambda e: e.reduce_sum(ssq[:m, 0:1], junk[:m, :], mybir.AxisListType.X),
                      [b_junk], [b_ssq])
            self.act(ssq[:m, 1:2], ssq[:m, 0:1], AF.Ln, [b_ssq, self.bconst], [b_ssq], scale=1.0 / 512,
                     bias=self.epsc[:m, 1:2])
            self.act(ssq[:m, 2:3], ssq[:m, 1:2], AF.Exp, [b_ssq], [b_ssq], scale=-0.5)
            s = cnt["nrm"] % 2
            cnt["nrm"] += 1
            self.stt(nrm[s][:m, :], ps[:m, :], ssq[:m, 2:3], gtile[:m, :], ALU.mult, ALU.mult,
                     [bkb, b_ssq, b_tab], [b_nrm[s]])
            return s

        def transposes_to(dst, b_dst, src, b_src, m, t0, nchunk, bank):
            ps = self.PS(bank)
            for k in range(nchunk):
                self.tr(ps[:, k * 128:k * 128 + m], src[:m, k * 128:(k + 1) * 128], self.ident[:m, :m],
                        [b_src, self.bconst], [self.bk[bank]])
            srcv = ps.rearrange("p (j t) -> p j t", j=4)[:, :nchunk, :m]
            self.cp(dst[:, 0:nchunk, t0:t0 + m], srcv, [self.bk[bank]], [b_dst])

        for bi, (kind, c0, nc_, half) in enumerate(blocks):
            s = bi % 2
            for ti, (t0, m) in enumerate(TM):
                bank = 6 + cnt["bank"] % 2
                cnt["bank"] += 1
                ps = self.PS(bank)
                bkb = self.bk[bank]
                for c in range(DC):
                    self.mm(ps[:m, :nc_], self.actb[:, c, t0:t0 + m], wblk[s][:, c, :nc_], c == 0, c == DC - 1,
                            [self.b_act[c], b_wblk[s]], [bkb])
                if kind in ("k", "v"):
                    s2 = cnt["s32"] % 2
                    cnt["s32"] += 1
                    self.act(stg32[s2][:m, :], ps[:m, :], AF.Identity, [bkb], [b_stg32[s2]])
                    dst = o["nk"] if kind == "k" else o["nv"]
                    P.dma("sp", dst[t0:t0 + m, half * 512:(half + 1) * 512], stg32[s2][:m, :], f"o32_{s2}",
                          reads=[b_stg32[s2]])
                    if kind == "v":
                        if m == 128:
                            s3 = cnt["sb"] % 2
                            cnt["sb"] += 1
                            self.cp(stgb[s3][:m, :], ps[:m, :], [bkb], [b_stgb[s3]])
                            ea, ej = self.exi(R_V + t0, m)
                            P.dma("sp", ea[:, half * 512:(half + 1) * 512], stgb[s3][:m, :],
                                  f"ob_{s3}", reads=[b_stgb[s3], self.b_exc[ej]])
                        else:
                            self.cp(self.Vs_new[:m, half * 512:(half + 1) * 512], ps[:m, :], [bkb], [self.b_Vsn])
                elif kind == "cq":
                    sn = rmsnorm(ps, m, bkb, gcq)
                    bank2 = 6 + cnt["bank"] % 2
                    cnt["bank"] += 1
                    transposes_to(self.cqnT, self.b_cqnT, nrm[sn], b_nrm[sn], m, t0, 4, bank2)
                elif kind == "ckv":
                    sn = rmsnorm(ps, m, bkb, gckv)
                    P.dma("sp", o["nckv"][t0:t0 + m, :], nrm[sn][:m, :], f"on_{sn}", reads=[b_nrm[sn]])
                    bank2 = 6 + cnt["bank"] % 2
                    cnt["bank"] += 1
                    transposes_to(ckvnT, b_ckvnT, nrm[sn], b_nrm[sn], m, t0, 4, bank2)
                else:
                    sk = cnt["kro"] % 2
                    cnt["kro"] += 1
                    x1, x2 = ps[:m, 0:32], ps[:m, 32:64]
                    cs, sn_ = cos[:m, ti, :], sin[:m, ti, :]
                    self.tt(rt[0][:m, :], x1, cs, ALU.mult, [bkb, b_tab], [b_rt])
                    self.tt(rt[1][:m, :], x2, sn_, ALU.mult, [bkb, b_tab], [b_rt])
                    self.tt(rt[2][:m, :], x2, cs, ALU.mult, [bkb, b_tab], [b_rt])
                    self.tt(rt[3][:m, :], x1, sn_, ALU.mult, [bkb, b_tab], [b_rt])
                    self.tt(kro[sk][:m, 0:32], rt[0][:m, :], rt[1][:m, :], ALU.subtract, [b_rt], [b_kro[sk]])
                    self.tt(kro[sk][:m, 32:64], rt[2][:m, :], rt[3][:m, :], ALU.add, [b_rt], [b_kro[sk]])
                    P.dma("sp", o["nkr"][t0:t0 + m, :], kro[sk][:m, :], f"okr_{sk}", reads=[b_kro[sk]])
                    bank2 = 6 + cnt["bank"] % 2
                    cnt["bank"] += 1
                    ps2 = self.PS(bank2)
                    self.tr(ps2[:64, :m], kro[sk][:m, 0:64], self.ident[:m, :m], [b_kro[sk], self.bconst],
                            [self.bk[bank2]])
                    self.cp(self.krT[:, t0:t0 + m], ps2[:64, :m], [self.bk[bank2]], [self.b_krT])
            if kind == "k":
                for hh in range(4):
                    h = half * 4 + hh
                    b0 = 0 if hh % 2 == 0 else 3
                    sl_ = cnt["ktl"] % 2
                    cnt["ktl"] += 1
                    for c in range(DC):
                        for (t0, n, bk) in TT:
                            self.mm(self.PS(b0 + bk)[:, :n], wblk[s][:, c, hh * 128:(hh + 1) * 128],
                                    self.actb[:, c, t0:t0 + n], c == 0, c == DC - 1,
                                    [b_wblk[s], self.b_act[c]], [self.bk[b0 + bk]])
                    for (t0, n, bk) in TT:
                        self.act(KTl[sl_][:, t0:t0 + n], self.PS(b0 + bk)[:, :n], AF.Identity,
                                 [self.bk[b0 + bk]], [b_KTl[sl_]])
                    self.cp(self.KTs[:, h, :], KTl[sl_][:, NP_:NT], [b_KTl[sl_]], [self.b_KTs])
                    ea, ej = self.exi(R_KT + h * 128, 128)
                    P.dma("sp", ea, KTl[sl_][:, 0:NP_], f"okt_{sl_}", reads=[b_KTl[sl_], self.b_exc[ej]])
            if bi + 2 < len(blocks):
                load_blk(bi + 2)
            if bi == 3:
                self.issue_cc([0, 1, 2, 3])
        ea, ej = self.exi(R_RT, 64)
        P.dma("sp", ea, self.krT[:, 0:NP_], "okrt", reads=[self.b_krT, self.b_exc[ej]])
        for h in range(H):
            b0 = 0 if h % 2 == 0 else 3
            sl_ = cnt["ktl"] % 2
            cnt["ktl"] += 1
            for kc in range(4):
                for (t0, n, bk) in TT:
                    self.mm(self.PS(b0 + bk)[:, :n], self.wukv[:, kc, h * 256:h * 256 + 128],
                            ckvnT[:, kc, t0:t0 + n], kc == 0, kc == 3, [self.b_wukv, b_ckvnT],
                            [self.bk[b0 + bk]])
            for (t0, n, bk) in TT:
                self.act(KTl[sl_][:, t0:t0 + n], self.PS(b0 + bk)[:, :n], AF.Identity, [self.bk[b0 + bk]],
                         [b_KTl[sl_]])
            self.cp(self.NTs[:, h, :], KTl[sl_][:, NP_:NT], [b_KTl[sl_]], [self.b_NTs])
            ea, ej = self.exi(R_NT + h * 128, 128)
            P.dma("sp", ea, KTl[sl_][:, 0:NP_], f"okt_{sl_}", reads=[b_KTl[sl_], self.b_exc[ej]])
        for ti, (t0, m) in enumerate(TM):
            for half in range(2):
                bank = 6 + cnt["bank"] % 2
                cnt["bank"] += 1
                ps = self.PS(bank)
                for hh in range(4):
                    hc = (half * 4 + hh) * 256 + 128
                    for kc in range(4):
                        self.mm(ps[:m, hh * 128:(hh + 1) * 128], ckvnT[:, kc, t0:t0 + m],
                                self.wukv[:, kc, hc:hc + 128], kc == 0, kc == 3,
                                [b_ckvnT, self.b_wukv], [self.bk[bank]])
                if m == 128:
                    s3 = cnt["sb"] % 2
                    cnt["sb"] += 1
                    self.cp(stgb[s3][:m, :], ps[:m, :], [self.bk[bank]], [b_stgb[s3]])
                    ea, ej = self.exi(R_VM + t0, m)
                    P.dma("sp", ea[:, half * 512:(half + 1) * 512], stgb[s3][:m, :],
                          f"ob_{s3}", reads=[b_stgb[s3], self.b_exc[ej]])
                else:
                    self.cp(self.VMs_new[:m, half * 512:(half + 1) * 512], ps[:m, :], [self.bk[bank]],
                            [self.b_VMsn])
        P.barrier()

    def issue_cc(self, js):
        P = self.P
        for j in js:
            def fn(e, j=j):
                return e.collective_compute("AllGather", ALU.bypass, replica_groups=[[0, 1, 2, 3], [4, 5, 6, 7]],
                                            ins=[self.ex_in[j].ap().opt()], outs=[self.ex_out[j].ap().opt()])
            P._add("pool", fn, [], [self.b_exc[j], self.b_exo[j]], semkey=f"cc{j}", inc=1)

    def attn_setup(self, alt=None):
        P, A = self.P, self.A
        w = {}
        w["e32"] = [A.alloc([128, 512], F32) for _ in range(2)]
        w["sp"] = [A.alloc([128, 512], BF16) for _ in range(3)]
        w["S"] = [A.alloc([128, 512], BF16) for _ in range(3)]
        w["A"] = [A.alloc([128, 512], BF16) for _ in range(3)]
        if alt is not None:
            save = (A.cur, A.lim)
            A.region(*alt)
        w["rec"] = A.alloc([128, 512], F32)
        w["acc"] = [A.alloc([128, 512], F32) for _ in range(2)]
        if alt is not None:
            self.alt_cur = A.cur
            A.cur, A.lim = save
        w["b_acc"] = P.bufs(2, "acc")
        w["b_e32"] = P.bufs(2, "e32")
        for k in ("sp", "S", "A"):
            w["b_" + k] = P.bufs(3, k)
        w["b_rec"] = P.buf("rec")
        w["cnt"] = 0
        return w

    def sb_problem(self, w, N, groups, tiles, obank, evac):
        negtri, negones = self.cm[:, 0, :], self.cm[:, 1, :]
        S, bS = w["S"], w["b_S"]
        for j in range(3):
            self.memset(S[j][:, :N], 0.0, [bS[j]])
        O = self.PS(obank)
        nt = len(tiles)
        base = w["cnt"]
        w["cnt"] += nt
        wbank = lambda k: (0, 1, 2, 5)[(base + k) % 4]

        def stA(k):
            t = tiles[k]
            nk, wb = t["nk"], wbank(k)
            cl = t.get("c_lo", 0)
            W = self.PS(wb)
            for gi, (c0, ncg, q, bq) in enumerate(groups):
                lo = max(c0, cl)
                self.mm(W[:nk, lo:c0 + ncg], t["kt"][gi], q[:, lo - c0:], gi == 0, False, t["reads"] + [bq],
                        [self.bk[wb]])
            if t["mask"] is not None:
                self.mm(W[:nk, cl:N], self.identb[:nk, :nk], t["mask"][:, cl:N], False, False,
                        [self.bconst, self.b_mask], [self.bk[wb]])
            s2 = (base + k) % 2
            self.act(w["e32"][s2][:nk, cl:N], W[:nk, cl:N], AF.Exp, [self.bk[wb]], [w["b_e32"][s2]])

        def stA2(k):
            t = tiles[k]
            nk, cl = t["nk"], t.get("c_lo", 0)
            s2, s3 = (base + k) % 2, (base + k) % 3
            self.act(w["sp"][s3][:nk, cl:N], w["e32"][s2][:nk, cl:N], AF.Ln, [w["b_e32"][s2], self.bconst],
                     [w["b_sp"][s3]], bias=self.epsc[:nk, 2:3])

        def stB(k):
            t = tiles[k]
            nk, wb = t["nk"], wbank(k)
            cl = t.get("c_lo", 0)
            W = self.PS(wb)
            s3 = (base + k) % 3
            first = k == 0
            self.mm(W[:nk, cl:N], negtri[:nk, :nk], w["sp"][s3][:nk, cl:N], False, first,
                    [self.bconst, w["b_sp"][s3]], [self.bk[wb]])
            if not first:
                self.mm(W[:nk, cl:N], negones[:, :nk], S[k % 3][:, cl:N], False, True,
                        [self.bconst, bS[k % 3]], [self.bk[wb]])
            self.act(w["A"][s3][:nk, cl:N], W[:nk, cl:N], AF.Exp, [self.bk[wb]], [w["b_A"][s3]])
            if k < nt - 1:
                nx = (k + 1) % 3
                if nk < 128:
                    self.tt(S[nx][:nk, cl:N], S[k % 3][:nk, cl:N], w["sp"][s3][:nk, cl:N], ALU.add,
                            [bS[k % 3], w["b_sp"][s3]], [bS[nx]])
                else:
                    self.tt(S[nx][:, cl:N], S[k % 3][:, cl:N], w["sp"][s3][:, cl:N], ALU.add,
                            [bS[k % 3], w["b_sp"][s3]], [bS[nx]])

        def stC(k):
            t = tiles[k]
            nk = t["nk"]
            s3 = (base + k) % 3
            cl = t.get("c_lo", 0)
            for gi, (c0, ncg, q, bq) in enumerate(groups):
                lo = max(c0, cl)
                self.mm(O[:, lo:c0 + ncg], t["v"][gi], w["A"][s3][:nk, lo:c0 + ncg], k == 0 and gi == 0,
                        k == nt - 1, t["reads"] + [w["b_A"][s3]], [self.bk[obank]])

        for it in range(nt + 3):
            if it < nt:
                stA(it)
            if 1 <= it <= nt:
                stA2(it - 1)
            if 2 <= it <= nt + 1:
                stB(it - 2)
            if 3 <= it:
                stC(it - 3)
        evac(O, obank)

    def mla_problem(self, w, N, groups, tiles, obank, dbank, evac):
        ones = self.cm[:, 2, :]
        O, Dn = self.PS(obank), self.PS(dbank)
        nt = len(tiles)
        base = w["cnt"]
        w["cnt"] += nt

        def stA(k):
            t = tiles[k]
            nk, zb = t["nk"], (base + k) % 3
            Z = self.PS(zb)
            s3 = (base + k) % 3
            cl = t.get("c_lo", 0)
            for gi, (c0, ncg, qn, qr, bq) in enumerate(groups):
                lo = max(c0, cl)
                self.mm(Z[:nk, lo:c0 + ncg], t["nt"][gi], qn[:, lo - c0:], gi == 0, False, t["reads"] + [bq],
                        [self.bk[zb]])
                self.mm(Z[:nk, lo:c0 + ncg], t["rt"], qr[:, lo - c0:], False,
                        t["mask"] is None and gi == len(groups) - 1, t["reads"] + [bq], [self.bk[zb]])
            if t["mask"] is not None:
                self.mm(Z[:nk, cl:N], self.identb[:nk, :nk], t["mask"][:, cl:N], False, True,
                        [self.bconst, self.b_mask], [self.bk[zb]])
            self.act(w["A"][s3][:nk, cl:N], Z[:nk, cl:N], AF.Exp, [self.bk[zb]], [w["b_A"][s3]])

        def stC(k):
            t = tiles[k]
            nk = t["nk"]
            s3 = (base + k) % 3
            first, last = k == 0, k == nt - 1
            cl = t.get("c_lo", 0)
            for gi, (c0, ncg, qn, qr, bq) in enumerate(groups):
                lo = max(c0, cl)
                self.mm(O[:, lo:c0 + ncg], t["vm"][gi], w["A"][s3][:nk, lo:c0 + ncg], first and gi == 0, last,
                        t["reads"] + [w["b_A"][s3]], [self.bk[obank]])
            a2 = k % 2
            self.tt(w["acc"][a2][:nk, cl:N], w["acc"][a2][:nk, cl:N], w["A"][s3][:nk, cl:N], ALU.add,
                    [w["b_acc"][a2], w["b_A"][s3]], [w["b_acc"][a2]])

        self.memset(w["acc"][0][:, :N], 0.0, [w["b_acc"][0]])
        self.memset(w["acc"][1][:, :N], 0.0, [w["b_acc"][1]])
        for it in range(nt + 1):
            if it < nt:
                stA(it)
            if it >= 1:
                stC(it - 1)
        self.mm(Dn[:, :N], self.ones32[:, :], w["acc"][0][:, :N], True, False, [self.bconst, w["b_acc"][0]],
                [self.bk[dbank]])
        self.mm(Dn[:, :N], self.ones32[:, :], w["acc"][1][:, :N], False, True, [self.bconst, w["b_acc"][1]],
                [self.bk[dbank]])
        self.P.op("dve", lambda e: e.reciprocal(w["rec"][:, :N], Dn[:, :N]), [self.bk[dbank]], [w["b_rec"]])
        evac(O, obank, w["rec"], w["b_rec"])

    def phase_sample_attn(self):
        P, A, i = self.P, self.A, self.i
        w_in = i["w_in"]
        A.region(self.R3, self.R3 + 34000)
        self.oT_sb = A.alloc([128, H, NT], BF16)
        self.oT_mla = A.alloc([128, H, NT], BF16)
        self.b_oT_sb, self.b_oT_mla = P.buf("oTsb"), P.buf("oTmla")
        A.region(self.R2, self.R3)
        w = self.attn_setup()
        self.aw = w
        Qss = A.alloc([128, H, NS], BF16)
        Qns = A.alloc([128, H, NS], BF16)
        Qrs = A.alloc([64, H, NS], BF16)
        Qrt = [A.alloc([64, NS], F32) for _ in range(2)]
        cosf = A.alloc([64, NS], F32)
        sinf = A.alloc([64, NS], F32)
        self.cosf, self.sinf = cosf, sinf
        self.b_tabf = P.buf("tabf")
        P.dma("sp", cosf[:], i["cos_fm"][:, NP_:NT], "t0", writes=[self.b_tabf])
        P.dma("sp", sinf[:], i["sin_fm"][:, NP_:NT], "t1", writes=[self.b_tabf])
        mbn = A.alloc([16, 128], BF16)
        mbn32 = A.alloc([16, 128], F32)
        b_mbn32 = P.buf("mbn32")
        self.b_mask = P.buf("mask")
        P.dma("sp", mbn32[:], i["mb_new"], "mb", writes=[b_mbn32])
        self.cp(mbn[:], mbn32[:], [b_mbn32], [self.b_mask])
        wq = None
        wuq = [A.alloc([128, 4, 256], BF16) for _ in range(2)]

        self.wq, self.wuq = wq, wuq
        self.b_wq, self.b_wuq = P.bufs(2, "wq"), P.bufs(2, "wuq")
        b_Qs = P.buf("Qs_s")
        b_Qrt = P.buf("Qrt")
        Vn1 = A.alloc([16, 1024], BF16)
        VMn1 = A.alloc([16, 1024], BF16)
        b_Vn1 = P.buf("Vn1")
        P.dma("sp", Vn1[:], self.Vs_new[16:32, :], "vn1", reads=[self.b_Vsn], writes=[b_Vn1])
        P.dma("sp", VMn1[:], self.VMs_new[16:32, :], "vmn1", reads=[self.b_VMsn], writes=[b_Vn1])
        self.wcount = 0
        wq2 = [A.alloc([128, DC, 256], BF16) for _ in range(2)]
        b_wq2 = P.bufs(2, "wq2")
        uqv = i["w_uq"].rearrange("(c p) n -> p c n", p=128)

        def load_pair(hp):
            sl2 = hp % 2
            P.dma("pool", wq2[sl2][:], w_in[:, hp * 256:(hp + 1) * 256].rearrange("(c p) n -> p c n", p=128),
                  f"wq2{sl2}", writes=[b_wq2[sl2]])

        def load_wuq(h):
            sl2 = h % 2
            b0 = h * 192
            P.dma("pool", wuq[sl2][:, :, 0:192], uqv[:, :, b0:b0 + 192], f"wuqa{sl2}", writes=[self.b_wuq[sl2]])
            P.dma("pool", wuq[sl2][:, :, 192:224], uqv[:, :, b0 + 160:b0 + 192], f"wuqb{sl2}",
                  writes=[self.b_wuq[sl2]])
            P.dma("pool", wuq[sl2][:, :, 224:256], uqv[:, :, b0 + 128:b0 + 160], f"wuqc{sl2}",
                  writes=[self.b_wuq[sl2]])
            return sl2

        load_pair(0)
        load_pair(1)
        for h in range(H):
            s = load_wuq(h)
            sp2, off = (h // 2) % 2, (h % 2) * 128
            bA, bB = (6, 7) if h % 2 == 0 else (3, 4)
            ps = self.PS(bA)
            for c in range(DC):
                self.mm(ps[:, :NS], wq2[sp2][:, c, off:off + 128], self.actb[:, c, NP_:NT], c == 0, c == DC - 1,
                        [b_wq2[sp2], self.b_act[c]], [self.bk[bA]])
            if h % 2 == 1 and h // 2 + 2 < 4:
                load_pair(h // 2 + 2)
            self.act(Qss[:, h, :], ps[:, :NS], AF.Identity, [self.bk[bA]], [b_Qs], scale=SB_SCALE)
            ps = self.PS(bB)
            for kc in range(4):
                self.mm(ps[:, :NS], wuq[s][:, kc, 0:128], self.cqnT[:, kc, NP_:NT], kc == 0, kc == 3,
                        [self.b_wuq[s], self.b_cqnT], [self.bk[bB]])
            self.act(Qns[:, h, :], ps[:, :NS], AF.Identity, [self.bk[bB]], [b_Qs], scale=MLA_SCALE)
            ps = self.PS(bA)
            for kc in range(4):
                self.mm(ps[:64, :NS], wuq[s][:, kc, 128:192], self.cqnT[:, kc, NP_:NT], kc == 0, kc == 3,
                        [self.b_wuq[s], self.b_cqnT], [self.bk[bA]])
            for kc in range(4):
                self.mm(ps[:64, 64:64 + NS], wuq[s][:, kc, 192:256], self.cqnT[:, kc, NP_:NT], kc == 0, kc == 3,
                        [self.b_wuq[s], self.b_cqnT], [self.bk[bA]])
            self.tt(Qrt[0][:, :], ps[:64, :NS], cosf[:, :], ALU.mult, [self.bk[bA], self.b_tabf], [b_Qrt])
            self.tt(Qrt[1][:, :], ps[:64, 64:64 + NS], sinf[:, :], ALU.mult, [self.bk[bA], self.b_tabf], [b_Qrt])
            self.tt(Qrt[0][:, :], Qrt[0][:, :], Qrt[1][:, :], ALU.add, [b_Qrt], [b_Qrt])
            self.act(Qrs[:, h, :], Qrt[0][:, :], AF.Identity, [b_Qrt], [b_Qs], scale=MLA_SCALE)
        r2cur = A.cur
        A.region(self.R3 + 34000, self.TOP - 34000)
        KTc = A.alloc([128, H, PAST], BF16)
        Vc = A.alloc([128, 8, 1024], BF16)
        A.region(r2cur, self.R3)
        ckvTc = A.alloc([128, 4, PAST], BF16)
        krTc = A.alloc([64, PAST], BF16)
        kst = [A.alloc([128, 1024], F32) for _ in range(2)]
        krst = A.alloc([128, 8, 64], F32)
        b_KTc, b_Vc, b_ckvTc, b_krTc = P.buf("KTc"), P.buf("Vc"), P.buf("ckvTc"), P.buf("krTc")
        b_kst, b_krst = P.bufs(2, "kst"), P.buf("krst")
        kcnt = 0
        for sq in range(2):
            qsl = slice(sq * 16, sq * 16 + 16)
            for kt in range(8):
                s = kcnt % 2
                kcnt += 1
                P.dma("sp", kst[s][:, :], i["c_k"][sq, kt * 128:(kt + 1) * 128, :], f"kst{s}", writes=[b_kst[s]])
                for hg in range(2):
                    bank = 6 + hg
                    ps = self.PS(bank)
                    for j in range(4):
                        h = hg * 4 + j
                        self.tr(ps[:, j * 128:(j + 1) * 128], kst[s][:, h * 128:(h + 1) * 128], self.ident[:, :],
                                [b_kst[s], self.bconst], [self.bk[bank]])
                    self.cp(KTc[:, hg * 4:hg * 4 + 4, kt * 128:(kt + 1) * 128],
                            ps.rearrange("p (j t) -> p j t", j=4), [self.bk[bank]], [b_KTc])
            P.dma("pool", Vc[:], i["c_v"][sq].rearrange("(k p) c -> p k c", p=128), "vc", writes=[b_Vc])
            groups = [(h * 16, 16, Qss[:, h, qsl], b_Qs) for h in range(H)]
            vnew = self.Vs_new if sq == 0 else Vn1
            b_vnew = self.b_Vsn if sq == 0 else b_Vn1
            tiles = [dict(nk=16, kt=[self.KTs[:, h, qsl] for h in range(H)],
                          v=[vnew[:16, h * 128:(h + 1) * 128] for h in range(H)], mask=mbn[:, :],
                          reads=[self.b_KTs, b_vnew])]
            for kt in range(7, -1, -1):
                tiles.append(dict(nk=128, kt=[KTc[:, h, kt * 128:(kt + 1) * 128] for h in range(H)],
                                  v=[Vc[:, kt, h * 128:(h + 1) * 128] for h in range(H)], mask=None,
                                  reads=[b_KTc, b_Vc]))
            c0 = NP_ + sq * 16

            def evac_sb(O, obank, c0=c0):
                self.cp(self.oT_sb[:, :, c0:c0 + 16], O[:, :128].rearrange("p (h q) -> p h q", q=16),
                        [self.bk[obank]], [self.b_oT_sb])
            self.sb_problem(w, 128, groups, tiles, 3, evac_sb)
            NTc, VMc = KTc, Vc
            for kt in range(8):
                s = kcnt % 2
                kcnt += 1
                P.dma("sp", kst[s][:, :512], i["c_ckv"][sq, kt * 128:(kt + 1) * 128, :], f"kst{s}",
                      writes=[b_kst[s]])
                bank = 6 + kt % 2
                ps = self.PS(bank)
                for j in range(4):
                    self.tr(ps[:, j * 128:(j + 1) * 128], kst[s][:, j * 128:(j + 1) * 128], self.ident[:, :],
                            [b_kst[s], self.bconst], [self.bk[bank]])
                self.cp(ckvTc[:, :, kt * 128:(kt + 1) * 128], ps.rearrange("p (j t) -> p j t", j=4),
                        [self.bk[bank]], [b_ckvTc])
            P.dma("sp", krst[:], i["c_kr"][sq].rearrange("(k p) c -> p k c", p=128), "krst", writes=[b_krst])
            for kg in range(2):
                bank = 6 + kg
                ps = self.PS(bank)
                for j in range(4):
                    kt = kg * 4 + j
                    self.tr(ps[:64, j * 128:(j + 1) * 128], krst[:, kt, :], self.ident[:, :],
                            [b_krst, self.bconst], [self.bk[bank]])
                self.cp(krTc[:, kg * 512:(kg + 1) * 512], ps[:64, :], [self.bk[bank]], [b_krTc])
            for h in range(H):
                for half in range(2):
                    bank = 6 + (h * 2 + half) % 2
                    ps = self.PS(bank)
                    for kc in range(4):
                        self.mm(ps[:, :], self.wukv[:, kc, h * 256:h * 256 + 128],
                                ckvTc[:, kc, half * 512:(half + 1) * 512], kc == 0, kc == 3,
                                [self.b_wukv, b_ckvTc], [self.bk[bank]])
                    self.act(NTc[:, h, half * 512:(half + 1) * 512], ps[:, :], AF.Identity, [self.bk[bank]],
                             [b_KTc])
            for kt in range(8):
                for half in range(2):
                    bank = 6 + (kt * 2 + half) % 2
                    ps = self.PS(bank)
                    for hh in range(4):
                        hc = (half * 4 + hh) * 256 + 128
                        for kc in range(4):
                            self.mm(ps[:, hh * 128:(hh + 1) * 128], ckvTc[:, kc, kt * 128:(kt + 1) * 128],
                                    self.wukv[:, kc, hc:hc + 128], kc == 0, kc == 3,
                                    [b_ckvTc, self.b_wukv], [self.bk[bank]])
                    self.cp(VMc[:, kt, half * 512:(half + 1) * 512], ps[:, :], [self.bk[bank]], [b_Vc])
            groups = [(h * 16, 16, Qns[:, h, qsl], Qrs[:, h, qsl], b_Qs) for h in range(H)]
            vmnew = self.VMs_new if sq == 0 else VMn1
            b_vmnew = self.b_VMsn if sq == 0 else b_Vn1
            tiles = [dict(nk=16, nt=[self.NTs[:, h, qsl] for h in range(H)], rt=self.krT[:, c0:c0 + 16],
                          vm=[vmnew[:16, h * 128:(h + 1) * 128] for h in range(H)], mask=None,
                          reads=[self.b_NTs, self.b_krT, b_vmnew])]
            for kt in range(7, -1, -1):
                tiles.append(dict(nk=128, nt=[NTc[:, h, kt * 128:(kt + 1) * 128] for h in range(H)],
                                  rt=krTc[:, kt * 128:(kt + 1) * 128],
                                  vm=[VMc[:, kt, h * 128:(h + 1) * 128] for h in range(H)], mask=None,
                                  reads=[b_KTc, b_krTc, b_Vc]))

            def evac_mla(O, obank, rec, b_rec, c0=c0):
                self.tt(self.oT_mla[:, :, c0:c0 + 16], O[:, :128].rearrange("p (h q) -> p h q", q=16),
                        rec[:, :128].rearrange("p (h q) -> p h q", q=16), ALU.mult,
                        [self.bk[obank], b_rec], [self.b_oT_mla])
            self.mla_problem(w, 128, groups, tiles, 3, 5, evac_mla)
        P.barrier()

    def load_wq(self, h):
        P, i = self.P, self.i
        s = self.wcount % 2
        self.wcount += 1
        P.dma("pool", self.wq[s][:], i["w_in"][:, h * 128:(h + 1) * 128].rearrange("(c p) n -> p c n", p=128),
              f"wq{s}", writes=[self.b_wq[s]])
        uq = i["w_uq"].rearrange("(c p) n -> p c n", p=128)
        b0 = h * 192
        P.dma("pool", self.wuq[s][:, :, 0:192], uq[:, :, b0:b0 + 192], f"wuqa{s}", writes=[self.b_wuq[s]])
        P.dma("pool", self.wuq[s][:, :, 192:224], uq[:, :, b0 + 160:b0 + 192], f"wuqb{s}", writes=[self.b_wuq[s]])
        P.dma("pool", self.wuq[s][:, :, 224:256], uq[:, :, b0 + 128:b0 + 160], f"wuqc{s}", writes=[self.b_wuq[s]])
        return s

    def phase_prompt_attn(self):
        P, A, i = self.P, self.A, self.i
        A.region(self.R2, self.R3)
        w = self.attn_setup(alt=(self.TOP - 34000 + 8448, self.TOP))
        mask = A.alloc([128, 16, 512], BF16)
        self.b_mask = P.buf("mask2")
        Kg = [A.alloc([128, 4096], BF16) for _ in range(2)]
        Vg = [A.alloc([128, 32, 128], BF16) for _ in range(2)]
        A.region(self.R3 + 34000, self.TOP - 34000)
        Qs = [A.alloc([128, NP_], BF16) for _ in range(2)]
        Rg = A.alloc([64, 4096], BF16)
        Qr = [A.alloc([64, NP_], BF16) for _ in range(2)]
        Qrt = [A.alloc([64, 512], F32) for _ in range(2)]
        wq = [A.alloc([128, DC, 128], BF16) for _ in range(2)]
        wuq = [A.alloc([128, 4, 256], BF16) for _ in range(2)]
        self.wq, self.wuq = wq, wuq
        self.b_wq, self.b_wuq = P.bufs(2, "wq2"), P.bufs(2, "wuq2")
        b_Kg, b_Vg, b_Rg = P.bufs(2, "Kg"), P.bufs(2, "Vg"), P.buf("Rg")
        b_Qs, b_Qr, b_Qrt = P.bufs(2, "Qs"), P.bufs(2, "Qr"), P.buf("Qrt2")
        A.region(self.alt_cur, self.TOP)
        cosf = A.alloc([64, NP_], F32)
        sinf = A.alloc([64, NP_], F32)
        b_tabf = P.buf("tabf2")
        P.dma("sp", cosf[:], i["cos_fm"][:, 0:NP_], "t0", writes=[b_tabf])
        P.dma("sp", sinf[:], i["sin_fm"][:, 0:NP_], "t1", writes=[b_tabf])

        def load_kv(slot, row0, h):
            pairs, rds = [], []
            for r in range(4):
                ea, ej = self.exo(r, row0 + h * 128, 128)
                pairs.append((Kg[slot][:, r * 1024:(r + 1) * 1024], ea))
                rds.append(self.b_exo[ej])
            P.dma_batch("sp", pairs, f"kg{slot}", reads=rds, writes=[b_Kg[slot]])
            vrow = R_V if row0 == R_KT else R_VM
            pairs, rds = [], []
            for r in range(4):
                for hf in range(2):
                    ea, ej = self.exo(r, vrow + hf * 512, 512)
                    pairs.append((Vg[slot][:, r * 8 + hf * 4:r * 8 + hf * 4 + 4, :],
                                  ea[:, h * 128:(h + 1) * 128].rearrange("(m p) d -> p m d", p=128)))
                    rds.append(self.b_exo[ej])
            P.dma_batch("sp", pairs, f"vg{slot}", reads=rds, writes=[b_Vg[slot]])

        def key_tiles(M, slot, mla):
            tl = []
            for kt in range(16 * M + 15, -1, -1):
                r, m = kt % 4, kt // 4
                col = r * 1024 + m * 128
                mk = mask[:, kt - 16 * M, :] if kt >= 16 * M else None
                d = dict(nk=128, mask=mk, reads=[b_Kg[slot], b_Vg[slot]] + ([b_Rg] if mla else []))
                if kt >= 16 * M:
                    d["c_lo"] = 128 * max(0, (kt - 16 * M - 3 + 3) // 4)
                if mla:
                    d["nt"] = [Kg[slot][:, col:col + 128]]
                    d["rt"] = Rg[:, col:col + 128]
                    d["vm"] = [Vg[slot][:, r * 8 + m, :]]
                else:
                    d["kt"] = [Kg[slot][:, col:col + 128]]
                    d["v"] = [Vg[slot][:, r * 8 + m, :]]
                tl.append(d)
            return tl

        self.wcount = 0
        pcount = 0
        P.dma("pool", mask[:], i["mb_sb"], "mb2", writes=[self.b_mask])
        wslots = [self.load_wq(0), self.load_wq(1)]
        self.issue_cc([4, 5, 6, 7, 8])

        def qproj_sb(h):
            slot, s = h % 2, wslots[h]
            for half in range(2):
                bank = 6 + half
                ps = self.PS(bank)
                for c in range(DC):
                    self.mm(ps[:, :], wq[s][:, c, :], self.actb[:, c, half * 512:(half + 1) * 512], c == 0,
                            c == DC - 1, [self.b_wq[s], self.b_act[c]], [self.bk[bank]])
                self.act(Qs[slot][:, half * 512:(half + 1) * 512], ps[:, :], AF.Identity, [self.bk[bank]],
                         [b_Qs[slot]], scale=SB_SCALE)
            if h + 2 < H:
                wslots.append(self.load_wq(h + 2))

        load_kv(0, R_KT, 0)
        qproj_sb(0)
        for h in range(H):
            slot = h % 2
            for M in range(2):
                groups = [(0, 512, Qs[slot][:, M * 512:(M + 1) * 512], b_Qs[slot])]
                obank = 3 + pcount % 2
                pcount += 1

                def evac(O, ob, h=h, M=M):
                    self.act(self.oT_sb[:, h, M * 512:(M + 1) * 512], O[:, :], AF.Identity, [self.bk[ob]],
                             [self.b_oT_sb])
                self.sb_problem(w, 512, groups, key_tiles(M, slot, False), obank, evac)
                if M == 0 and h + 1 < H:
                    load_kv((h + 1) % 2, R_KT, h + 1)
                    qproj_sb(h + 1)
        for i4 in range(4):
            P.dma("pool", mask[:, 4 * i4:4 * i4 + 4, i4 * 128:(i4 + 1) * 128],
                  i["mb_mla"][:, 4 * i4:4 * i4 + 4, i4 * 128:(i4 + 1) * 128], "mb2", writes=[self.b_mask])
        for r in range(4):
            ea, ej = self.exo(r, R_RT, 64)
            P.dma("sp", Rg[:, r * 1024:(r + 1) * 1024], ea, "rg", reads=[self.b_exo[ej]],
                  writes=[b_Rg])
        mslots = [self.load_wq(0), self.load_wq(1)]

        def qproj_mla(h):
            slot, s = h % 2, mslots[h]
            for half in range(2):
                tsl = slice(half * 512, (half + 1) * 512)
                bank = 6 + half
                ps = self.PS(bank)
                for kc in range(4):
                    self.mm(ps[:, :], wuq[s][:, kc, 0:128], self.cqnT[:, kc, tsl], kc == 0, kc == 3,
                            [self.b_wuq[s], self.b_cqnT], [self.bk[bank]])
                self.act(Qs[slot][:, tsl], ps[:, :], AF.Identity, [self.bk[bank]], [b_Qs[slot]], scale=MLA_SCALE)
            for half in range(2):
                tsl = slice(half * 512, (half + 1) * 512)
                for part, bank in ((0, 6), (1, 7)):
                    ps = self.PS(bank)
                    for kc in range(4):
                        self.mm(ps[:64, :], wuq[s][:, kc, 128 + part * 64:192 + part * 64], self.cqnT[:, kc, tsl],
                                kc == 0, kc == 3, [self.b_wuq[s], self.b_cqnT], [self.bk[bank]])
                self.tt(Qrt[0][:, :], self.PS(6)[:64, :], cosf[:, tsl], ALU.mult, [self.bk[6], b_tabf], [b_Qrt])
                self.tt(Qrt[1][:, :], self.PS(7)[:64, :], sinf[:, tsl], ALU.mult, [self.bk[7], b_tabf], [b_Qrt])
                self.tt(Qrt[0][:, :], Qrt[0][:, :], Qrt[1][:, :], ALU.add, [b_Qrt], [b_Qrt])
                self.act(Qr[slot][:, tsl], Qrt[0][:, :], AF.Identity, [b_Qrt], [b_Qr[slot]], scale=MLA_SCALE)
            if h + 2 < H:
                mslots.append(self.load_wq(h + 2))

        load_kv(0, R_NT, 0)
        qproj_mla(0)
        for h in range(H):
            slot = h % 2
            for M in range(2):
                groups = [(0, 512, Qs[slot][:, M * 512:(M + 1) * 512], Qr[slot][:, M * 512:(M + 1) * 512],
                           b_Qs[slot])]
                obank = 3 + pcount % 2
                dbank = 5
                pcount += 1

                def evac(O, ob, rec, b_rec, h=h, M=M):
                    self.tt(self.oT_mla[:, h, M * 512:(M + 1) * 512], O[:, :], rec[:, :], ALU.mult,
                            [self.bk[ob], b_rec], [self.b_oT_mla])
                tl = key_tiles(M, slot, True)
                for t in tl:
                    t["reads"] = t["reads"] + [b_Qr[slot]]
                self.mla_problem(w, 512, groups, tl, obank, dbank, evac)
                if M == 0 and h + 1 < H:
                    load_kv((h + 1) % 2, R_NT, h + 1)
                    qproj_mla(h + 1)
        P.barrier()

    def phase_merge(self):
        P, A, i = self.P, self.A, self.i
        w_in = i["w_in"]
        A.region(self.R3 + 34000, self.R3 + 34000 + 34000)
        mg = A.alloc([128, DC, NT], BF16)
        b_mg = P.bufs(DC, "mg")
        A.region(self.R2, self.R3)
        wg = [A.alloc([128, DC, 256], BF16) for _ in range(2)]
        wb = [A.alloc([128, 16, 128], BF16) for _ in range(2)]
        gs = A.alloc([128, NT], F32)
        gm = A.alloc([128, NT], F32)
        t1 = A.alloc([128, NT], F32)
        t2 = A.alloc([128, NT], F32)
        b_wg, b_wb = P.bufs(2, "wg"), P.bufs(2, "wb")
        b_gs, b_gm, b_t1, b_t2 = P.bufs(3, "gs"), P.bufs(3, "gm"), P.bufs(3, "t1"), P.bufs(3, "t2")

        def load(ci):
            s = ci % 2
            P.dma("pool", wg[s][:, :, 0:128],
                  w_in[:, 4160 + ci * 128:4160 + (ci + 1) * 128].rearrange("(c p) n -> p c n", p=128),
                  f"wga{s}", writes=[b_wg[s]])
            P.dma("pool", wg[s][:, :, 128:256],
                  w_in[:, 6208 + ci * 128:6208 + (ci + 1) * 128].rearrange("(c p) n -> p c n", p=128),
                  f"wgb{s}", writes=[b_wg[s]])
            P.dma("pool", wb[s][:, 0:8, :],
                  i["w_br_sb"][:, ci * 128:(ci + 1) * 128].rearrange("(c p) n -> p c n", p=128),
                  f"wba{s}", writes=[b_wb[s]])
            P.dma("pool", wb[s][:, 8:16, :],
                  i["w_br_mla"][:, ci * 128:(ci + 1) * 128].rearrange("(c p) n -> p c n", p=128),
                  f"wbb{s}", writes=[b_wb[s]])

        load(0)
        load(1)
        for ci in range(DC):
            s = ci % 2
            for (b0, off, gt, b_g, bidx) in ((0, 0, gs, b_gs, ci), (3, 128, gm, b_gm, 16 + ci)):
                for c in range(DC):
                    for (t0, n, bk) in TT:
                        self.mm(self.PS(b0 + bk)[:, :n], wg[s][:, c, off:off + 128], self.actb[:, c, t0:t0 + n],
                                c == 0, c == DC - 1, [b_wg[s], self.b_act[c]], [self.bk[b0 + bk]])
                for (t0, n, bk) in TT:
                    self.act(gt[:, t0:t0 + n], self.PS(b0 + bk)[:, :n], AF.Sigmoid, [self.bk[b0 + bk], self.bconst],
                             [b_g[bk]], bias=self.bg[:, bidx:bidx + 1])
            for (b0, r0, oT, b_oT) in ((0, 0, self.oT_sb, self.b_oT_sb), (3, 8, self.oT_mla, self.b_oT_mla)):
                for h in range(H):
                    for (t0, n, bk) in TT:
                        self.mm(self.PS(b0 + bk)[:, :n], wb[s][:, r0 + h, :], oT[:, h, t0:t0 + n], h == 0, h == H - 1,
                                [b_wb[s], b_oT], [self.bk[b0 + bk]])
            for (t0, n, bk) in TT:
                self.tt(t1[:, t0:t0 + n], gs[:, t0:t0 + n], self.PS(bk)[:, :n], ALU.mult,
                        [b_gs[bk], self.bk[bk]], [b_t1[bk]])
                self.tt(t2[:, t0:t0 + n], gm[:, t0:t0 + n], self.PS(3 + bk)[:, :n], ALU.mult,
                        [b_gm[bk], self.bk[3 + bk]], [b_t2[bk]])
                self.tt(mg[:, ci, t0:t0 + n], t1[:, t0:t0 + n], t2[:, t0:t0 + n], ALU.add,
                        [b_t1[bk], b_t2[bk]], [b_mg[ci]])
            if ci + 2 < DC:
                load(ci + 2)
        P.barrier()
        P.dma("sp", self.resid[:], self.spill.ap(), "spill", writes=self.b_res)
        A.region(self.R3, self.R3 + 34000)
        wo = [A.alloc([128, DC, 512], BF16) for _ in range(2)]
        b_wo = P.bufs(2, "wo")

        def load_wo(q):
            s = q % 2
            P.dma("pool", wo[s][:], i["w_o"][:, q * 512:(q + 1) * 512].rearrange("(c p) n -> p c n", p=128),
                  f"wo{s}", writes=[b_wo[s]])
        load_wo(0)
        load_wo(1)
        for q in range(4):
            s = q % 2
            for j in range(4):
                ci = q * 4 + j
                b0 = 0 if ci % 2 == 0 else 3
                for c in range(DC):
                    for (t0, n, bk) in TT:
                        self.mm(self.PS(b0 + bk)[:, :n], wo[s][:, c, j * 128:(j + 1) * 128], mg[:, c, t0:t0 + n],
                                c == 0, c == DC - 1, [b_wo[s], b_mg[c]], [self.bk[b0 + bk]])
                for (t0, n, bk) in TT:
                    self.tt(self.resid[:, ci, t0:t0 + n], self.resid[:, ci, t0:t0 + n], self.PS(b0 + bk)[:, :n],
                            ALU.add, [self.b_res[ci], self.bk[b0 + bk]], [self.b_res[ci]])
            if q + 2 < 4:
                load_wo(q + 2)
        P.barrier()

    def phase_out(self):
        P, A, o = self.P, self.A, self.o
        A.region(self.R3, self.TOP)
        ys = [A.alloc([128, D], F32) for _ in range(2)]
        b_ys = P.bufs(2, "ys")
        k = 0
        for ti, (t0, m) in enumerate(TM):
            s = ti % 2
            for q in range(4):
                b = 4 + k % 4
                k += 1
                ps = self.PS(b)
                for j in range(4):
                    c = q * 4 + j
                    self.tr(ps[:m, j * 128:(j + 1) * 128], self.resid[:, c, t0:t0 + m], self.ident[:, :],
                            [self.b_res[c], self.bconst], [self.bk[b]])
                if k % 2 == 0:
                    self.act(ys[s][:m, q * 512:(q + 1) * 512], ps[:m, :], AF.Identity, [self.bk[b]], [b_ys[s]])
                else:
                    self.cp(ys[s][:m, q * 512:(q + 1) * 512], ps[:m, :], [self.bk[b]], [b_ys[s]])
            P.dma("sp", o["y"][t0:t0 + m, :], ys[s][:m, :], f"y{s}", reads=[b_ys[s]])

    def dump_act(self):
        dbg = self.nc.dram_tensor("dbg", [128, DC, NT], F32, kind="ExternalOutput").ap()
        self.P.dma("sp", dbg, self.resid[:], "dbg", reads=self.b_res)


def build_nc(stop_after=None, **kw):
    b = Builder(stop_after=stop_after, **kw)
    b.nc._used_inputs = list(b.i.keys())
    return b.nc


def host_inputs(inp):
    f = np.float32
    xp = np.asarray(inp["x_prompt"], f)
    xs = np.asarray(inp["x_sample"], f)
    common = {
        "w1a": np.ascontiguousarray(inp["ffn1_w_in"][0]), "w2a": np.ascontiguousarray(inp["ffn1_w_out"][0]),
        "w1b": np.ascontiguousarray(inp["ffn2_w_in"][0]), "w2b": np.ascontiguousarray(inp["ffn2_w_out"][0]),
        "w_in": np.ascontiguousarray(inp["w_in"][0]), "w_uq": np.ascontiguousarray(inp["w_uq"][0]),
        "w_ukv": np.ascontiguousarray(inp["w_ukv"][0]), "w_br_sb": np.ascontiguousarray(inp["w_br_sb"][0]),
        "w_br_mla": np.ascontiguousarray(inp["w_br_mla"][0]), "w_o": np.ascontiguousarray(inp["w_o"][0]),
    }
    lnp = np.concatenate([np.asarray(inp[k][0], f).reshape(16, 128).T for k in
                          ("ln1_g", "ln1_b", "ln2_g", "ln2_b", "ln3_g", "ln3_b")], axis=1)
    common["lnp"] = np.ascontiguousarray(lnp)
    common["bgate"] = np.ascontiguousarray(np.asarray(inp["b_gate"][0], f).reshape(32, 128).T)
    common["gcq"] = np.ascontiguousarray(np.broadcast_to(np.asarray(inp["g_cq"][0], f), (128, 512)))
    common["gckv"] = np.ascontiguousarray(np.broadcast_to(np.asarray(inp["g_ckv"][0], f), (128, 512)))
    common["ident"] = np.eye(128, dtype=f)
    cm = np.zeros((128, 4, 128), f)
    jj, ss = np.meshgrid(np.arange(128), np.arange(128), indexing="ij")
    cm[:, 0, :] = -(jj >= ss).astype(f)
    cm[:, 1, :] = -1.0
    cm[:, 2, :] = 1.0
    common["cmats"] = cm
    kk, tq = np.meshgrid(np.arange(16), np.arange(16), indexing="ij")
    common["mb_new"] = np.ascontiguousarray(np.tile(np.where(kk < tq, 0.0, NEG).astype(f), (1, 8)))
    maps = []
    inv_freq = (10000.0 ** (-np.arange(0, 64, 2, dtype=np.float32) / 64)).astype(f)
    for c in range(8):
        b, r = c // 4, c % 4
        tiles = [4 * m + r for m in range(8)]
        rows = np.concatenate([np.arange(g * 128, (g + 1) * 128) for g in tiles])
        x = np.concatenate([xp[b, rows], xs[2 * c], xs[2 * c + 1]], 0)
        pos = np.concatenate([rows, PAST + np.arange(16), PAST + np.arange(16)]).astype(f)
        ang = pos[:, None] * inv_freq[None, :]
        cos, sin = np.cos(ang).astype(f), np.sin(ang).astype(f)
        cpad = np.zeros((9 * 128, 32), f)
        spad = np.zeros((9 * 128, 32), f)
        cpad[:NT] = cos
        spad[:NT] = sin
        m = dict(common)
        m["x"] = np.ascontiguousarray(x)
        m["cos_tm"] = np.ascontiguousarray(cpad.reshape(9, 128, 32).transpose(1, 0, 2))
        m["sin_tm"] = np.ascontiguousarray(spad.reshape(9, 128, 32).transpose(1, 0, 2))
        m["cos_fm"] = np.ascontiguousarray(np.concatenate([cos.T, cos.T], 0))
        m["sin_fm"] = np.ascontiguousarray(np.concatenate([-sin.T, sin.T], 0))
        k_, q_ = np.meshgrid(np.arange(128), np.arange(128), indexing="ij")
        msb = np.zeros((128, 16, 512), f)
        mml = np.zeros((128, 16, 512), f)
        for o in range(16):
            for i4 in range(4):
                qt = 4 * i4 + r
                sl = slice(i4 * 128, (i4 + 1) * 128)
                if o > qt:
                    msb[:, o, sl] = NEG
                    mml[:, o, sl] = NEG
                elif o == qt:
                    msb[:, o, sl] = np.where(k_ < q_, 0.0, NEG)
                    mml[:, o, sl] = np.where((k_ // 64) <= (q_ // 64), 0.0, NEG)
        m["mb_sb"] = msb
        m["mb_mla"] = mml
        m["c_k"] = np.ascontiguousarray(np.asarray(inp["cache_sb_k"][0, 2 * c:2 * c + 2], f).reshape(2, PAST, 1024))
        m["c_v"] = np.ascontiguousarray(np.asarray(inp["cache_sb_v"][0, 2 * c:2 * c + 2], f).reshape(2, PAST, 1024))
        m["c_ckv"] = np.ascontiguousarray(np.asarray(inp["cache_mla_ckv"][0, 2 * c:2 * c + 2], f))
        m["c_kr"] = np.ascontiguousarray(np.asarray(inp["cache_mla_krope"][0, 2 * c:2 * c + 2], f))
        maps.append(m)
    return maps


_NC = None


def kernel(**inputs):
    global _NC
    maps = host_inputs(inputs)
    if _NC is None:
        _NC = build_nc()
    used = _NC._used_inputs
    maps = [{k: m[k] for k in used} for m in maps]
    res = run_bass_kernel_spmd(_NC, maps, core_ids=list(range(8)))
    return assemble(res.results)


def assemble(results):
    f = np.float32
    y_p = np.zeros((2, 4096, D), f)
    y_s = np.zeros((16, 16, D), f)
    nk_p = np.zeros((1, 2, 4096, 8, 128), f)
    nv_p = np.zeros((1, 2, 4096, 8, 128), f)
    nc_p = np.zeros((1, 2, 4096, 512), f)
    nr_p = np.zeros((1, 2, 4096, 64), f)
    nk_s = np.zeros((1, 16, 16, 8, 128), f)
    nv_s = np.zeros((1, 16, 16, 8, 128), f)
    nc_s = np.zeros((1, 16, 16, 512), f)
    nr_s = np.zeros((1, 16, 16, 64), f)
    for c in range(8):
        b, r = c // 4, c % 4
        R = results[c]
        for m in range(8):
            g = 4 * m + r
            dst = slice(g * 128, (g + 1) * 128)
            src = slice(m * 128, (m + 1) * 128)
            y_p[b, dst] = R["y"][src]
            nk_p[0, b, dst] = R["nk"][src].reshape(128, 8, 128)
            nv_p[0, b, dst] = R["nv"][src].reshape(128, 8, 128)
            nc_p[0, b, dst] = R["nckv"][src]
            nr_p[0, b, dst] = R["nkr"][src]
        for s in range(2):
            src = slice(1024 + 16 * s, 1024 + 16 * s + 16)
            y_s[2 * c + s] = R["y"][src]
            nk_s[0, 2 * c + s] = R["nk"][src].reshape(16, 8, 128)
            nv_s[0, 2 * c + s] = R["nv"][src].reshape(16, 8, 128)
            nc_s[0, 2 * c + s] = R["nckv"][src]
            nr_s[0, 2 * c + s] = R["nkr"][src]
    return (y_p, y_s, nk_p, nv_p, nc_p, nr_p, nk_s, nv_s, nc_s, nr_s)
```

```python
import numpy as np
import concourse.bass as bass
import concourse.mybir as mybir
from concourse.bass_utils import run_bass_kernel_spmd

F32 = mybir.dt.float32
BF16 = mybir.dt.bfloat16
AF = mybir.ActivationFunctionType
ALU = mybir.AluOpType

SEM_LIM = 30000
NEG = -30000.0

D = 2048
DC = 16
FF = 5504
FC = 43
NP_ = 1024
NS = 32
NT = NP_ + NS
TT = [(0, 352, 0), (352, 352, 1), (704, 352, 2)]
TM = [(i * 128, 128) for i in range(8)] + [(1024, 32)]
H = 8
PAST = 1024
ALPHA = 2.0 ** 0.25
SB_SCALE = 128.0 ** -0.5
MLA_SCALE = 192.0 ** -0.5
LN_EPS = 1e-5
RMS_EPS = 1e-6
EX_ROWS = 4160
R_KT, R_V, R_NT, R_VM, R_RT = 0, 1024, 2048, 3072, 4096


class Buf:
    __slots__ = ("name", "lastw", "readers", "excl")

    def __init__(self, name, excl=False):
        self.name = name
        self.lastw = None
        self.readers = {}
        self.excl = excl


class Op:
    __slots__ = ("eng", "fn", "deps", "signal", "count", "key", "seq", "is_dma", "inc")

    def __init__(self, eng, fn, key, seq, is_dma, inc=None):
        self.eng = eng
        self.fn = fn
        self.key = key
        self.seq = seq
        self.is_dma = is_dma
        self.deps = {}
        self.signal = is_dma
        self.count = 0
        self.inc = inc if inc is not None else (16 if is_dma else 1)


class Prog:
    ENGS = ("pe", "act", "dve", "pool", "sp")

    def __init__(self, nc):
        self.nc = nc
        self.ops = {e: [] for e in self.ENGS}
        self.latest = {}
        self.pending = {e: {} for e in self.ENGS}
        self.seq = 0
        self.dma_counts = {}
        self.dma_inc = {}
        self.nbuf = 0

    def buf(self, name=None, excl=False):
        self.nbuf += 1
        return Buf(name or f"b{self.nbuf}", excl)

    def bufs(self, n, name="b"):
        return [self.buf(f"{name}{i}") for i in range(n)]

    def _add(self, eng, fn, reads, writes, semkey=None, inc=None, touch=()):
        is_dma = semkey is not None
        key = ("dma", semkey) if is_dma else eng
        self.seq += 1
        o = Op(eng, fn, key, self.seq, is_dma, inc)
        if is_dma:
            self.dma_inc[semkey] = o.inc
        deps = {}

        def add(d):
            if d is None:
                return
            if d.key == "pe" and eng == "pe" and not is_dma:
                return
            cur = deps.get(d.key)
            if cur is None or cur.seq < d.seq:
                deps[d.key] = d

        for b in reads:
            add(b.lastw)
            if b.excl:
                for r in b.readers.values():
                    if r.key != key:
                        add(r)
        for b in writes:
            add(b.lastw)
            for r in b.readers.values():
                add(r)
        for d in self.pending[eng].values():
            add(d)
        self.pending[eng] = {}
        o.deps = deps
        for b in reads:
            cur = b.readers.get(key)
            if cur is None or cur.seq < o.seq:
                b.readers[key] = o
        for b in writes:
            b.lastw = o
            b.readers = {}
        for b in touch:
            b.lastw = o
            b.readers = {}
        if is_dma:
            c = self.dma_counts.get(semkey, 0) + 1
            self.dma_counts[semkey] = c
            o.count = c
        self.ops[eng].append(o)
        self.latest[key] = o
        return o

    def op(self, eng, fn, reads=(), writes=()):
        return self._add(eng, fn, reads, writes)

    def dma(self, queue, out, in_, semkey, reads=(), writes=(), **kw):
        def fn(e):
            return e.dma_start(out=out, in_=in_, **kw)
        return self._add(queue, fn, reads, writes, semkey=semkey)

    def dma_batch(self, queue, pairs, semkey, reads=(), writes=()):
        n = len(pairs)
        for j, (out, in_) in enumerate(pairs):
            def fn(e, out=out, in_=in_):
                return e.dma_start(out=out, in_=in_)
            if n == 1:
                self._add(queue, fn, reads, writes, semkey=semkey)
            elif j == 0:
                self._add(queue, fn, reads, writes, semkey=semkey)
            elif j == n - 1:
                self._add(queue, fn, (), (), semkey=semkey, touch=writes)
            else:
                self._add(queue, fn, (), (), semkey=semkey)

    def barrier(self):
        snap = {k: v for k, v in self.latest.items()
                if not (isinstance(k, tuple) and str(k[1]).startswith("cc"))}
        for e in self.ENGS:
            self.pending[e] = dict(snap)

    def emit(self):
        nc = self.nc
        for e in self.ENGS:
            for o in self.ops[e]:
                for d in o.deps.values():
                    d.signal = True
        tot = {}
        for e in self.ENGS:
            c = 0
            for o in self.ops[e]:
                if not o.is_dma and o.signal:
                    c += 1
                    o.count = c
            tot[e] = c
        sems = {}

        def nsem(units):
            return max(1, (units + SEM_LIM - 1) // SEM_LIM)

        for e in self.ENGS:
            sems[e] = [nc.alloc_semaphore(f"s_{e}_{i}") for i in range(nsem(tot[e]))]
        for k, c in self.dma_counts.items():
            sems[("dma", k)] = [nc.alloc_semaphore(f"sd_{k}_{i}") for i in range(nsem(c * self.dma_inc[k]))]

        def target(o):
            units = o.count * o.inc
            idx = (units - 1) // SEM_LIM
            val = (units - 1) % SEM_LIM + 1
            return sems[o.key][idx], idx, val

        handles = {"pe": "tensor", "act": "scalar", "dve": "vector", "pool": "gpsimd", "sp": "sync"}
        final_waits = [o for k, o in self.latest.items() if o.is_dma]
        with nc.Block() as block:
            for e in self.ENGS:
                ops = self.ops[e]
                extra = final_waits if e == "sp" else []

                def body(eng, ops=ops, extra=extra):
                    waited = {}

                    def wait_for(d):
                        sem, idx, val = target(d)
                        wk = (d.key, idx)
                        if waited.get(wk, 0) < val:
                            eng.wait_ge(sem, val)
                            waited[wk] = val

                    for o in ops:
                        for d in o.deps.values():
                            wait_for(d)
                        ins = o.fn(eng)
                        if o.signal:
                            sem, idx, val = target(o)
                            ins.then_inc(sem, o.inc)
                    for d in extra:
                        wait_for(d)

                getattr(block, handles[e])(body)


class Arena:
    def __init__(self, nc):
        self.nc = nc
        b0 = nc.sbuf_base
        n = (nc.sbuf_top - b0 - 3072) // 4
        self.slab = nc.alloc_sbuf_tensor("slab", [128, n], F32)
        self.base = (b0 + 63) // 64 * 64
        self.top = (b0 + n * 4) // 64 * 64
        self.cur = self.base
        self.lim = self.top
        self.n = 0

    def size(self, shape, dtype):
        per = 4 if dtype == F32 else 2
        for s in shape[1:]:
            per *= s
        return (per + 63) // 64 * 64

    def alloc(self, shape, dtype):
        per = self.size(shape, dtype)
        off = self.cur
        self.cur += per
        assert self.cur <= self.lim, f"SBUF overflow {self.cur} > {self.lim}"
        self.n += 1
        return self.nc.alloc_sbuf_tensor_at(f"sb{self.n}", list(shape), dtype, offset=off)

    def region(self, lo, hi):
        self.cur = (lo + 63) // 64 * 64
        self.lim = hi


class Builder:
    IN_SHAPES = {
        "x": [NT, D], "w1a": [D, 2 * FF], "w2a": [FF, D], "w1b": [D, 2 * FF], "w2b": [FF, D],
        "w_in": [D, 8256], "w_uq": [512, 1536], "w_ukv": [512, 2048], "w_br_sb": [1024, D],
        "w_br_mla": [1024, D], "w_o": [D, D], "lnp": [128, 96], "bgate": [128, 32], "gcq": [128, 512],
        "gckv": [128, 512], "cos_tm": [128, 9, 32], "sin_tm": [128, 9, 32], "cos_fm": [64, NT],
        "sin_fm": [64, NT], "ident": [128, 128], "cmats": [128, 4, 128], "mb_sb": [128, 16, 512],
        "mb_mla": [128, 16, 512], "mb_new": [16, 128], "c_k": [2, PAST, 1024], "c_v": [2, PAST, 1024],
        "c_ckv": [2, PAST, 512], "c_kr": [2, PAST, 64],
    }

    class _Lazy(dict):
        def __init__(self, b):
            super().__init__()
            self.b = b

        def __missing__(self, k):
            v = self.b.din(k, Builder.IN_SHAPES[k])
            self[k] = v
            return v

    def __init__(self, stop_after=None):
        self.stop_after = stop_after
        nc = bass.Bass("TRN2", target_bir_lowering=False)
        self.nc = nc
        self.P = Prog(nc)
        self.A = Arena(nc)
        self.i = Builder._Lazy(self)
        self.o = {}
        self.build()
        self.P.emit()

    def din(self, name, shape, dtype=F32):
        return self.nc.dram_tensor(name, list(shape), dtype, kind="ExternalInput").ap()

    def dout(self, name, shape, dtype=F32):
        return self.nc.dram_tensor(name, list(shape), dtype, kind="ExternalOutput").ap()

    def mm(self, out, lhsT, rhs, start, stop, reads, writes):
        self.P.op("pe", lambda e: e.matmul(out, lhsT, rhs, start=start, stop=stop, skip_group_check=True),
                  reads, writes)

    def tr(self, out, in_, ident, reads, writes):
        self.P.op("pe", lambda e: e.transpose(out, in_, ident), reads, writes)

    def act(self, out, in_, func, reads, writes, **kw):
        self.P.op("act", lambda e: e.activation(out, in_, func, **kw), reads, writes)

    def tt(self, out, in0, in1, op, reads, writes, eng="dve"):
        self.P.op(eng, lambda e: e.tensor_tensor(out, in0, in1, op), reads, writes)

    def ts(self, out, in0, s1, s2, op0, op1, reads, writes, eng="dve"):
        if op1 is None:
            self.P.op(eng, lambda e: e.tensor_scalar(out, in0, s1, None, op0), reads, writes)
        else:
            self.P.op(eng, lambda e: e.tensor_scalar(out, in0, s1, s2, op0, op1), reads, writes)

    def stt(self, out, in0, scalar, in1, op0, op1, reads, writes):
        self.P.op("dve", lambda e: e.scalar_tensor_tensor(out, in0, scalar, in1, op0, op1), reads, writes)

    def cp(self, out, in_, reads, writes, eng="dve"):
        self.P.op(eng, lambda e: e.tensor_copy(out, in_), reads, writes)

    def memset(self, ap, val, writes, eng="dve"):
        self.P.op(eng, lambda e: e.memset(ap, val), [], writes)

    def PS(self, b):
        return self.psum[:, b * 512:(b + 1) * 512]

    def exi(self, row0, nrows):
        j = row0 // 512
        a = row0 - j * 512
        return self.ex_in[j].ap()[a:a + nrows, :], j

    def exo(self, r, row0, nrows):
        j = row0 // 512
        a = r * self.ex_rows[j] + row0 - j * 512
        return self.ex_out[j].ap()[a:a + nrows, :], j

    def build(self):
        nc, P, A, i = self.nc, self.P, self.A, self.i
        st = self.stop_after
        self.psum = nc.alloc_psum_tensor("ps", [128, 4096], F32)
        self.bk = [P.buf(f"bank{b}", True) for b in range(8)]

        self.ident = A.alloc([128, 128], F32)
        self.identb = A.alloc([128, 128], BF16)
        self.cm = A.alloc([128, 4, 128], BF16)
        self.lnp = A.alloc([128, 96], F32)
        self.lnpa = A.alloc([128, 96], F32)
        self.bg = A.alloc([128, 32], F32)
        self.epsc = A.alloc([128, 4], F32)
        self.ones32 = A.alloc([128, 128], F32)
        self.bconst = P.buf("consts")
        P.dma("sp", self.ident[:], i["ident"], "c0", writes=[self.bconst])
        P.dma("pool", self.cm[:], i["cmats"], "c1", writes=[self.bconst])
        P.dma("pool", self.identb[:], i["ident"], "c4", writes=[self.bconst])
        P.dma("sp", self.lnp[:], i["lnp"], "c2", writes=[self.bconst])
        P.dma("sp", self.bg[:], i["bgate"], "c3", writes=[self.bconst])
        self.ts(self.lnpa[:], self.lnp[:], ALPHA, None, ALU.mult, None, [self.bconst], [self.bconst])
        self.memset(self.epsc[:, 0:1], LN_EPS, [self.bconst])
        self.memset(self.epsc[:, 1:2], RMS_EPS, [self.bconst])
        self.memset(self.epsc[:, 2:3], 1.0, [self.bconst])
        self.memset(self.ones32[:, :], 1.0, [self.bconst])
        self.actb = A.alloc([128, DC, NT], BF16)
        self.b_act = P.bufs(DC, "act")
        self.R2 = A.cur
        self.resid = A.alloc([128, DC, NT], F32)
        self.b_res = P.bufs(DC, "res")
        self.R3 = A.cur
        self.TOP = A.top
        assert self.TOP - self.R3 >= 86000, (self.TOP, self.R3)

        if st is None:
            o = self.o
            o["y"] = self.dout("y", [NT, D])
            o["nk"] = self.dout("nk", [NT, 1024])
            o["nv"] = self.dout("nv", [NT, 1024])
            o["nckv"] = self.dout("nckv", [NT, 512])
            o["nkr"] = self.dout("nkr", [NT, 64])
            self.ex_rows = [512] * 8 + [64]
            self.ex_in = [nc.dram_tensor(f"ex_in{j}", [n, NP_], BF16) for j, n in enumerate(self.ex_rows)]
            self.ex_out = [nc.dram_tensor(f"ex_out{j}", [4 * n, NP_], BF16) for j, n in enumerate(self.ex_rows)]
            self.b_exc = P.bufs(9, "exc")
            self.b_exo = P.bufs(9, "exo")
            self.spill = nc.dram_tensor("spill", [128, DC, NT], F32)

        self.ffn_prefetch(i["w1a"], i["w2a"])
        self.phase_load_x()
        if st == "x":
            return self.dump_act()
        self.phase_ffn()
        if st == "ffn1":
            return self.dump_act()
        if st is None:
            self.proj_prefetch()
        self.phase_ln(0, final=False)
        if st == "ln1":
            return self.dump_act()
        P.dma("sp", self.spill.ap(), self.resid[:], "spill", reads=self.b_res)
        P.barrier()
        self.phase_proj()
        self.phase_sample_attn()
        self.phase_prompt_attn()
        self.phase_merge()
        self.ffn_prefetch(i["w1b"], i["w2b"])
        self.phase_ln(1, final=False)
        self.phase_ffn()
        self.phase_ln(2, final=True)
        self.phase_out()

    def phase_load_x(self):
        P, A, i = self.P, self.A, self.i
        A.region(self.TOP - 19072, self.TOP)
        xs = [A.alloc([128, D], F32) for _ in range(2)]
        bx = P.bufs(2, "xs")
        k = 0
        for ti, (t0, m) in enumerate(TM):
            s = ti % 2
            P.dma("sp", xs[s][:m, :], i["x"][t0:t0 + m, :], f"x{s}", writes=[bx[s]])
            for q in range(4):
                b = 4 + k % 4
                k += 1
                ps = self.PS(b)
                for j in range(4):
                    c = q * 4 + j
                    self.tr(ps[:, j * 128:j * 128 + m], xs[s][:m, c * 128:(c + 1) * 128], self.ident[:m, :m],
                            [bx[s], self.bconst], [self.bk[b]])
                src = ps.rearrange("p (j t) -> p j t", j=4)[:, :, :m]
                self.act(self.resid[:, q * 4:q * 4 + 4, t0:t0 + m], src, AF.Identity, [self.bk[b]],
                         self.b_res[q * 4:q * 4 + 4], scale=ALPHA)
                self.cp(self.actb[:, q * 4:q * 4 + 4, t0:t0 + m], src, [self.bk[b]], self.b_act[q * 4:q * 4 + 4])
        P.barrier()

    def ffn_prefetch(self, w1, w2):
        P, A = self.P, self.A
        A.region(self.R3, self.TOP - 19072)
        NPAIR, NG = 22, 11
        w1g = [A.alloc([128, DC, 256], BF16) for _ in range(2)]
        w1u = [A.alloc([128, DC, 256], BF16) for _ in range(2)]
        self.ffn_free_lo = A.cur
        aT = [A.alloc([128, 4, NT], BF16) for _ in range(2)]
        sg = A.alloc([128, NT], BF16)
        self.ffn_free_hi = A.cur
        w2s = [A.alloc([128, 4, D], BF16) for _ in range(2)]
        b_w1, b_aT, b_w2 = P.bufs(2, "w1"), P.bufs(2, "aT"), P.bufs(2, "w2")
        b_sg = P.bufs(3, "sg")

        def load_w1(p):
            s = p % 2
            w = (2 if p < 21 else 1) * 128
            c0 = p * 256
            P.dma("pool", w1g[s][:, :, :w], w1[:, c0:c0 + w].rearrange("(c p) n -> p c n", p=128),
                  f"w1g{s}", writes=[b_w1[s]])
            P.dma("pool", w1u[s][:, :, :w], w1[:, FF + c0:FF + c0 + w].rearrange("(c p) n -> p c n", p=128),
                  f"w1u{s}", writes=[b_w1[s]])

        def load_w2(g):
            s = g % 2
            nch = 4 if g < 10 else 3
            r0 = g * 512
            P.dma("pool", w2s[s][:, :nch, :], w2[r0:r0 + nch * 128, :].rearrange("(j p) n -> p j n", p=128),
                  f"w2{s}", writes=[b_w2[s]])

        load_w1(0)
        load_w1(1)
        load_w2(0)
        load_w2(1)
        self.ffn_state = (w1g, w1u, aT, w2s, sg, b_w1, b_aT, b_w2, b_sg, load_w1, load_w2)

    def phase_ffn(self):
        P = self.P
        NPAIR, NG = 22, 11
        (w1g, w1u, aT, w2s, sg, b_w1, b_aT, b_w2, b_sg, load_w1, load_w2) = self.ffn_state
        for g in range(NG):
            gs = g % 2
            nch_g = 4 if g < 10 else 3
            for pp in range(2):
                p = g * 2 + pp
                if p >= NPAIR:
                    continue
                s = p % 2
                nch = 2 if p < 21 else 1
                for jj in range(nch):
                    ja = pp * 2 + jj
                    for (b0, wt) in ((0, w1g[s]), (3, w1u[s])):
                        for c in range(DC):
                            for (t0, n, bk) in TT:
                                self.mm(self.PS(b0 + bk)[:, :n], wt[:, c, jj * 128:(jj + 1) * 128],
                                        self.actb[:, c, t0:t0 + n], c == 0, c == DC - 1,
                                        [b_w1[s], self.b_act[c]], [self.bk[b0 + bk]])
                    for (t0, n, bk) in TT:
                        self.act(sg[:, t0:t0 + n], self.PS(bk)[:, :n], AF.Silu, [self.bk[bk]], [b_sg[bk]])
                    for (t0, n, bk) in TT:
                        self.tt(aT[gs][:, ja, t0:t0 + n], sg[:, t0:t0 + n], self.PS(3 + bk)[:, :n],
                                ALU.mult, [b_sg[bk], self.bk[3 + bk]], [b_aT[gs]])
                if p + 2 < NPAIR:
                    load_w1(p + 2)
            for ci in range(DC):
                b0 = 0 if ci % 2 == 0 else 3
                for jj in range(nch_g):
                    for (t0, n, bk) in TT:
                        self.mm(self.PS(b0 + bk)[:, :n], w2s[gs][:, jj, ci * 128:(ci + 1) * 128],
                                aT[gs][:, jj, t0:t0 + n], jj == 0, jj == nch_g - 1,
                                [b_w2[gs], b_aT[gs]], [self.bk[b0 + bk]])
                for (t0, n, bk) in TT:
                    self.stt(self.resid[:, ci, t0:t0 + n], self.PS(b0 + bk)[:, :n], 0.5,
                             self.resid[:, ci, t0:t0 + n], ALU.mult, ALU.add,
                             [self.bk[b0 + bk], self.b_res[ci]], [self.b_res[ci]])
            if g + 2 < NG:
                load_w2(g + 2)
        P.barrier()

    def phase_ln(self, idx, final):
        P, A = self.P, self.A
        A.region(self.ffn_free_lo, self.ffn_free_hi)
        xb = [A.alloc([128, NT], BF16) for _ in range(2)]
        xq = [A.alloc([128, NT], BF16) for _ in range(2)]
        mt = A.alloc([128, NT], F32)
        rs = A.alloc([128, NT], F32)
        A.region(self.TOP - 19072, self.TOP)
        tmp = [A.alloc([128, NT], F32) for _ in range(2)]
        b_xb, b_xq = P.bufs(2, "xb"), P.bufs(2, "xq")
        b_mt, b_rs = P.buf("mt"), P.buf("rs")
        b_tmp = P.bufs(2, "tmp")
        ones = self.cm[:, 2, :]
        for c in range(DC):
            s = c % 2
            self.act(xb[s][:], self.resid[:, c, :], AF.Copy, [self.b_res[c]], [b_xb[s]])
            self.tt(xq[s][:], self.resid[:, c, :], self.resid[:, c, :], ALU.mult, [self.b_res[c]], [b_xq[s]])
            for (t0, n, bk) in TT:
                self.mm(self.PS(bk)[:, :n], ones, xb[s][:, t0:t0 + n], c == 0, c == DC - 1,
                        [b_xb[s], self.bconst], [self.bk[bk]])
            for (t0, n, bk) in TT:
                self.mm(self.PS(3 + bk)[:, :n], ones, xq[s][:, t0:t0 + n], c == 0, c == DC - 1,
                        [b_xq[s], self.bconst], [self.bk[3 + bk]])
        for (t0, n, bk) in TT:
            sl = slice(t0, t0 + n)
            pa = self.PS(bk)[:, :n]
            pb = self.PS(3 + bk)[:, :n]
            self.ts(mt[:, sl], pa, 1.0 / D, None, ALU.mult, None, [self.bk[bk]], [b_mt])
            self.tt(tmp[0][:, sl], mt[:, sl], mt[:, sl], ALU.mult, [b_mt], [b_tmp[0]])
            self.stt(rs[:, sl], pb, 1.0 / D, tmp[0][:, sl], ALU.mult, ALU.subtract,
                     [self.bk[3 + bk], b_tmp[0]], [b_rs])
            self.act(rs[:, sl], rs[:, sl], AF.Ln, [b_rs, self.bconst], [b_rs], bias=self.epsc[:, 0:1])
            self.act(rs[:, sl], rs[:, sl], AF.Exp, [b_rs], [b_rs], scale=-0.5)
            self.stt(mt[:, sl], mt[:, sl], -1.0, rs[:, sl], ALU.mult, ALU.mult, [b_mt, b_rs], [b_mt])
        g = self.lnp[:, idx * 32:idx * 32 + 16]
        b = self.lnp[:, idx * 32 + 16:idx * 32 + 32]
        ga = self.lnpa[:, idx * 32:idx * 32 + 16]
        ba = self.lnpa[:, idx * 32 + 16:idx * 32 + 32]
        for c in range(DC):
            s = c % 2
            self.tt(tmp[s][:], self.resid[:, c, :], rs[:], ALU.mult, [self.b_res[c], b_rs], [b_tmp[s]])
            self.tt(tmp[s][:], tmp[s][:], mt[:], ALU.add, [b_tmp[s], b_mt], [b_tmp[s]])
            self.act(self.actb[:, c, :], tmp[s][:], AF.Identity, [b_tmp[s], self.bconst], [self.b_act[c]],
                     scale=g[:, c:c + 1], bias=b[:, c:c + 1])
            sc, bi = (g, b) if final else (ga, ba)
            self.act(self.resid[:, c, :], tmp[s][:], AF.Identity, [b_tmp[s], self.bconst], [self.b_res[c]],
                     scale=sc[:, c:c + 1], bias=bi[:, c:c + 1])
        P.barrier()

    PROJ_BLOCKS = [("k", 1024, 512, 0), ("k", 1536, 512, 1), ("v", 2048, 512, 0), ("v", 2560, 512, 1),
                   ("cq", 3072, 512, 0), ("ckv", 3584, 512, 0), ("kr", 4096, 64, 0)]

    def proj_prefetch(self):
        P, A, i = self.P, self.A, self.i
        A.region(self.R3, self.R3 + 32768)
        self.wblk = [A.alloc([128, DC, 512], BF16) for _ in range(2)]
        self.b_wblk = P.bufs(2, "wblk")
        self.load_blk(0)
        self.load_blk(1)

    def load_blk(self, bi):
        kind, c0, nc_, _ = self.PROJ_BLOCKS[bi]
        s = bi % 2
        self.P.dma("pool", self.wblk[s][:, :, :nc_],
                   self.i["w_in"][:, c0:c0 + nc_].rearrange("(c p) n -> p c n", p=128),
                   f"wblk{s}", writes=[self.b_wblk[s]])

    def phase_proj(self):
        P, A, i, o = self.P, self.A, self.i, self.o
        w_in = i["w_in"]
        A.region(self.TOP - 34000, self.TOP)
        self.cqnT = A.alloc([128, 4, NT], BF16)
        self.KTs = A.alloc([128, 8, NS], BF16)
        self.NTs = A.alloc([128, 8, NS], BF16)
        self.krT = A.alloc([64, NT], BF16)
        self.Vs_new = A.alloc([32, 1024], BF16)
        self.VMs_new = A.alloc([32, 1024], BF16)
        self.wukv = A.alloc([128, 4, 2048], BF16)
        self.b_cqnT, self.b_KTs, self.b_NTs = P.buf("cqnT"), P.buf("KTs"), P.buf("NTs")
        self.b_krT, self.b_Vsn, self.b_VMsn, self.b_wukv = P.buf("krT"), P.buf("Vsn"), P.buf("VMsn"), P.buf("wukv")
        P.dma("pool", self.wukv[:], i["w_ukv"].rearrange("(c p) n -> p c n", p=128), "wukv", writes=[self.b_wukv])
        A.region(self.R2, self.TOP - 34000)
        wblk = self.wblk
        stg32 = [A.alloc([128, 512], F32) for _ in range(2)]
        stgb = [A.alloc([128, 512], BF16) for _ in range(2)]
        nrm = [A.alloc([128, 512], F32) for _ in range(2)]
        junk = A.alloc([128, 512], F32)
        ckvnT = A.alloc([128, 4, NT], BF16)
        KTl = [A.alloc([128, NT], BF16) for _ in range(2)]
        gcq = A.alloc([128, 512], F32)
        gckv = A.alloc([128, 512], F32)
        cos = A.alloc([128, 9, 32], F32)
        sin = A.alloc([128, 9, 32], F32)
        kro = [A.alloc([128, 64], F32) for _ in range(2)]
        rt = [A.alloc([128, 32], F32) for _ in range(4)]
        ssq = A.alloc([128, 4], F32)
        b_wblk = self.b_wblk
        b_stg32, b_stgb, b_nrm = P.bufs(2, "stg32"), P.bufs(2, "stgb"), P.bufs(2, "nrm")
        b_junk, b_ckvnT, b_KTl = P.buf("junk"), P.buf("ckvnT"), P.bufs(2, "KTl")
        b_tab, b_kro, b_rt, b_ssq = P.buf("tab"), P.bufs(2, "kro"), P.buf("rt"), P.buf("ssq")
        P.dma("sp", gcq[:], i["gcq"], "t0", writes=[b_tab])
        P.dma("sp", gckv[:], i["gckv"], "t1", writes=[b_tab])
        P.dma("sp", cos[:], i["cos_tm"], "t2", writes=[b_tab])
        P.dma("sp", sin[:], i["sin_tm"], "t3", writes=[b_tab])
        blocks = self.PROJ_BLOCKS
        cnt = {"s32": 0, "sb": 0, "nrm": 0, "bank": 0, "ktl": 0, "kro": 0}

        load_blk = self.load_blk

        def rmsnorm(ps, m, bkb, gtile):
            self.act(junk[:m, :], ps[:m, :], AF.Square, [bkb], [b_junk])
            self.P.op("dve", lambda e: e.reduce_sum(ssq[:m, 0:1], junk[:m, :], mybir.AxisListType.X),
                      [b_junk], [b_ssq])
            self.act(ssq[:m, 1:2], ssq[:m, 0:1], AF.Ln, [b_ssq, self.bconst], [b_ssq], scale=1.0 / 512,
                     bias=self.epsc[:m, 1:2])
            self.act(ssq[:m, 2:3], ssq[:m, 1:2], AF.Exp, [b_ssq], [b_ssq], scale=-0.5)
            s = cnt["nrm"] % 2
            cnt["nrm"] += 1
            self.stt(nrm[s][:m, :], ps[:m, :], ssq[:m, 2:3], gtile[:m, :], ALU.mult, ALU.mult,
                     [bkb, b_ssq, b_tab], [b_nrm[s]])
            return s

        def transposes_to(dst, b_dst, src, b_src, m, t0, nchunk, bank):
            ps = self.PS(bank)
            for k in range(nchunk):
                self.tr(ps[:, k * 128:k * 128 + m], src[:m, k * 128:(k + 1) * 128], self.ident[:m, :m],
                        [b_src, self.bconst], [self.bk[bank]])
            srcv = ps.rearrange("p (j t) -> p j t", j=4)[:, :nchunk, :m]
            self.cp(dst[:, 0:nchunk, t0:t0 + m], srcv, [self.bk[bank]], [b_dst])

        for bi, (kind, c0, nc_, half) in enumerate(blocks):
            s = bi % 2
            for ti, (t0, m) in enumerate(TM):
                bank = 6 + cnt["bank"] % 2
                cnt["bank"] += 1
                ps = self.PS(bank)
                bkb = self.bk[bank]
                for c in range(DC):
                    self.mm(ps[:m, :nc_], self.actb[:, c, t0:t0 + m], wblk[s][:, c, :nc_], c == 0, c == DC - 1,
                            [self.b_act[c], b_wblk[s]], [bkb])
                if kind in ("k", "v"):
                    s2 = cnt["s32"] % 2
                    cnt["s32"] += 1
                    self.act(stg32[s2][:m, :], ps[:m, :], AF.Identity, [bkb], [b_stg32[s2]])
                    dst = o["nk"] if kind == "k" else o["nv"]
                    P.dma("sp", dst[t0:t0 + m, half * 512:(half + 1) * 512], stg32[s2][:m, :], f"o32_{s2}",
                          reads=[b_stg32[s2]])
                    if kind == "v":
                        if m == 128:
                            s3 = cnt["sb"] % 2
                            cnt["sb"] += 1
                            self.cp(stgb[s3][:m, :], ps[:m, :], [bkb], [b_stgb[s3]])
                            ea, ej = self.exi(R_V + t0, m)
                            P.dma("sp", ea[:, half * 512:(half + 1) * 512], stgb[s3][:m, :],
                                  f"ob_{s3}", reads=[b_stgb[s3], self.b_exc[ej]])
                        else:
                            self.cp(self.Vs_new[:m, half * 512:(half + 1) * 512], ps[:m, :], [bkb], [self.b_Vsn])
                elif kind == "cq":
                    sn = rmsnorm(ps, m, bkb, gcq)
                    bank2 = 6 + cnt["bank"] % 2
                    cnt["bank"] += 1
                    transposes_to(self.cqnT, self.b_cqnT, nrm[sn], b_nrm[sn], m, t0, 4, bank2)
                elif kind == "ckv":
                    sn = rmsnorm(ps, m, bkb, gckv)
                    P.dma("sp", o["nckv"][t0:t0 + m, :], nrm[sn][:m, :], f"on_{sn}", reads=[b_nrm[sn]])
                    bank2 = 6 + cnt["bank"] % 2
                    cnt["bank"] += 1
                    transposes_to(ckvnT, b_ckvnT, nrm[sn], b_nrm[sn], m, t0, 4, bank2)
                else:
                    sk = cnt["kro"] % 2
                    cnt["kro"] += 1
                    x1, x2 = ps[:m, 0:32], ps[:m, 32:64]
                    cs, sn_ = cos[:m, ti, :], sin[:m, ti, :]
                    self.tt(rt[0][:m, :], x1, cs, ALU.mult, [bkb, b_tab], [b_rt])
                    self.tt(rt[1][:m, :], x2, sn_, ALU.mult, [bkb, b_tab], [b_rt])
                    self.tt(rt[2][:m, :], x2, cs, ALU.mult, [bkb, b_tab], [b_rt])
                    self.tt(rt[3][:m, :], x1, sn_, ALU.mult, [bkb, b_tab], [b_rt])
                    self.tt(kro[sk][:m, 0:32], rt[0][:m, :], rt[1][:m, :], ALU.subtract, [b_rt], [b_kro[sk]])
                    self.tt(kro[sk][:m, 32:64], rt[2][:m, :], rt[3][:m, :], ALU.add, [b_rt], [b_kro[sk]])
                    P.dma("sp", o["nkr"][t0:t0 + m, :], kro[sk][:m, :], f"okr_{sk}", reads=[b_kro[sk]])
                    bank2 = 6 + cnt["bank"] % 2
                    cnt["bank"] += 1
                    ps2 = self.PS(bank2)
                    self.tr(ps2[:64, :m], kro[sk][:m, 0:64], self.ident[:m, :m], [b_kro[sk], self.bconst],
                            [self.bk[bank2]])
                    self.cp(self.krT[:, t0:t0 + m], ps2[:64, :m], [self.bk[bank2]], [self.b_krT])
            if kind == "k":
                for hh in range(4):
                    h = half * 4 + hh
                    b0 = 0 if hh % 2 == 0 else 3
                    sl_ = cnt["ktl"] % 2
                    cnt["ktl"] += 1
                    for c in range(DC):
                        for (t0, n, bk) in TT:
                            self.mm(self.PS(b0 + bk)[:, :n], wblk[s][:, c, hh * 128:(hh + 1) * 128],
                                    self.actb[:, c, t0:t0 + n], c == 0, c == DC - 1,
                                    [b_wblk[s], self.b_act[c]], [self.bk[b0 + bk]])
                    for (t0, n, bk) in TT:
                        self.act(KTl[sl_][:, t0:t0 + n], self.PS(b0 + bk)[:, :n], AF.Identity,
                                 [self.bk[b0 + bk]], [b_KTl[sl_]])
                    self.cp(self.KTs[:, h, :], KTl[sl_][:, NP_:NT], [b_KTl[sl_]], [self.b_KTs])
                    ea, ej = self.exi(R_KT + h * 128, 128)
                    P.dma("sp", ea, KTl[sl_][:, 0:NP_], f"okt_{sl_}", reads=[b_KTl[sl_], self.b_exc[ej]])
            if bi + 2 < len(blocks):
                load_blk(bi + 2)
            if bi == 3:
                self.issue_cc([0, 1, 2, 3])
        ea, ej = self.exi(R_RT, 64)
        P.dma("sp", ea, self.krT[:, 0:NP_], "okrt", reads=[self.b_krT, self.b_exc[ej]])
        for h in range(H):
            b0 = 0 if h % 2 == 0 else 3
            sl_ = cnt["ktl"] % 2
            cnt["ktl"] += 1
            for kc in range(4):
                for (t0, n, bk) in TT:
                    self.mm(self.PS(b0 + bk)[:, :n], self.wukv[:, kc, h * 256:h * 256 + 128],
                            ckvnT[:, kc, t0:t0 + n], kc == 0, kc == 3, [self.b_wukv, b_ckvnT],
                            [self.bk[b0 + bk]])
            for (t0, n, bk) in TT:
                self.act(KTl[sl_][:, t0:t0 + n], self.PS(b0 + bk)[:, :n], AF.Identity, [self.bk[b0 + bk]],
                         [b_KTl[sl_]])
            self.cp(self.NTs[:, h, :], KTl[sl_][:, NP_:NT], [b_KTl[sl_]], [self.b_NTs])
            ea, ej = self.exi(R_NT + h * 128, 128)
            P.dma("sp", ea, KTl[sl_][:, 0:NP_], f"okt_{sl_}", reads=[b_KTl[sl_], self.b_exc[ej]])
        for ti, (t0, m) in enumerate(TM):
            for half in range(2):
                bank = 6 + cnt["bank"] % 2
                cnt["bank"] += 1
                ps = self.PS(bank)
                for hh in range(4):
                    hc = (half * 4 + hh) * 256 + 128
                    for kc in range(4):
                        self.mm(ps[:m, hh * 128:(hh + 1) * 128], ckvnT[:, kc, t0:t0 + m],
                                self.wukv[:, kc, hc:hc + 128], kc == 0, kc == 3,
                                [b_ckvnT, self.b_wukv], [self.bk[bank]])
                if m == 128:
                    s3 = cnt["sb"] % 2
                    cnt["sb"] += 1
                    self.cp(stgb[s3][:m, :], ps[:m, :], [self.bk[bank]], [b_stgb[s3]])
                    ea, ej = self.exi(R_VM + t0, m)
                    P.dma("sp", ea[:, half * 512:(half + 1) * 512], stgb[s3][:m, :],
                          f"ob_{s3}", reads=[b_stgb[s3], self.b_exc[ej]])
                else:
                    self.cp(self.VMs_new[:m, half * 512:(half + 1) * 512], ps[:m, :], [self.bk[bank]],
                            [self.b_VMsn])
        P.barrier()

    def issue_cc(self, js):
        P = self.P
        for j in js:
            def fn(e, j=j):
                return e.collective_compute("AllGather", ALU.bypass, replica_groups=[[0, 1, 2, 3], [4, 5, 6, 7]],
                                            ins=[self.ex_in[j].ap().opt()], outs=[self.ex_out[j].ap().opt()],
                                            dma_qos="P2")
            P._add("pool", fn, [], [self.b_exc[j], self.b_exo[j]], semkey=f"cc{j}", inc=1)

    def attn_setup(self, alt=None):
        P, A = self.P, self.A
        w = {}
        w["e32"] = [A.alloc([128, 512], F32) for _ in range(2)]
        w["sp"] = [A.alloc([128, 512], BF16) for _ in range(3)]
        w["S"] = [A.alloc([128, 512], BF16) for _ in range(3)]
        w["A"] = [A.alloc([128, 512], BF16) for _ in range(3)]
        if alt is not None:
            save = (A.cur, A.lim)
            A.region(*alt)
        w["rec"] = A.alloc([128, 512], F32)
        w["acc"] = [A.alloc([128, 512], F32) for _ in range(2)]
        if alt is not None:
            self.alt_cur = A.cur
            A.cur, A.lim = save
        w["b_acc"] = P.bufs(2, "acc")
        w["b_e32"] = P.bufs(2, "e32")
        for k in ("sp", "S", "A"):
            w["b_" + k] = P.bufs(3, k)
        w["b_rec"] = P.buf("rec")
        w["cnt"] = 0
        return w

    def sb_problem(self, w, N, groups, tiles, obank, evac):
        negtri, negones = self.cm[:, 0, :], self.cm[:, 1, :]
        S, bS = w["S"], w["b_S"]
        for j in range(3):
            self.memset(S[j][:, :N], 0.0, [bS[j]])
        O = self.PS(obank)
        nt = len(tiles)
        base = w["cnt"]
        w["cnt"] += nt
        wbank = lambda k: (0, 1, 2, 5)[(base + k) % 4]

        def stA(k):
            t = tiles[k]
            nk, wb = t["nk"], wbank(k)
            cl = t.get("c_lo", 0)
            W = self.PS(wb)
            for gi, (c0, ncg, q, bq) in enumerate(groups):
                lo = max(c0, cl)
                self.mm(W[:nk, lo:c0 + ncg], t["kt"][gi], q[:, lo - c0:], gi == 0, False, t["reads"] + [bq],
                        [self.bk[wb]])
            if t["mask"] is not None:
                self.mm(W[:nk, cl:N], self.identb[:nk, :nk], t["mask"][:, cl:N], False, False,
                        [self.bconst, self.b_mask], [self.bk[wb]])
            s2 = (base + k) % 2
            self.act(w["e32"][s2][:nk, cl:N], W[:nk, cl:N], AF.Exp, [self.bk[wb]], [w["b_e32"][s2]])

        def stA2(k):
            t = tiles[k]
            nk, cl = t["nk"], t.get("c_lo", 0)
            s2, s3 = (base + k) % 2, (base + k) % 3
            self.act(w["sp"][s3][:nk, cl:N], w["e32"][s2][:nk, cl:N], AF.Ln, [w["b_e32"][s2], self.bconst],
                     [w["b_sp"][s3]], bias=self.epsc[:nk, 2:3])

        def stB(k):
            t = tiles[k]
            nk, wb = t["nk"], wbank(k)
            cl = t.get("c_lo", 0)
            W = self.PS(wb)
            s3 = (base + k) % 3
            first = k == 0
            self.mm(W[:nk, cl:N], negtri[:nk, :nk], w["sp"][s3][:nk, cl:N], False, first,
                    [self.bconst, w["b_sp"][s3]], [self.bk[wb]])
            if not first:
                self.mm(W[:nk, cl:N], negones[:, :nk], S[k % 3][:, cl:N], False, True,
                        [self.bconst, bS[k % 3]], [self.bk[wb]])
            self.act(w["A"][s3][:nk, cl:N], W[:nk, cl:N], AF.Exp, [self.bk[wb]], [w["b_A"][s3]])
            if k < nt - 1:
                nx = (k + 1) % 3
                if nk < 128:
                    self.tt(S[nx][:nk, cl:N], S[k % 3][:nk, cl:N], w["sp"][s3][:nk, cl:N], ALU.add,
                            [bS[k % 3], w["b_sp"][s3]], [bS[nx]])
                else:
                    self.tt(S[nx][:, cl:N], S[k % 3][:, cl:N], w["sp"][s3][:, cl:N], ALU.add,
                            [bS[k % 3], w["b_sp"][s3]], [bS[nx]])

        def stC(k):
            t = tiles[k]
            nk = t["nk"]
            s3 = (base + k) % 3
            cl = t.get("c_lo", 0)
            for gi, (c0, ncg, q, bq) in enumerate(groups):
                lo = max(c0, cl)
                self.mm(O[:, lo:c0 + ncg], t["v"][gi], w["A"][s3][:nk, lo:c0 + ncg], k == 0 and gi == 0,
                        k == nt - 1, t["reads"] + [w["b_A"][s3]], [self.bk[obank]])

        for it in range(nt + 3):
            if it < nt:
                stA(it)
            if 1 <= it <= nt:
                stA2(it - 1)
            if 2 <= it <= nt + 1:
                stB(it - 2)
            if 3 <= it:
                stC(it - 3)
        evac(O, obank)

    def mla_problem(self, w, N, groups, tiles, obank, dbank, evac):
        ones = self.cm[:, 2, :]
        O, Dn = self.PS(obank), self.PS(dbank)
        nt = len(tiles)
        base = w["cnt"]
        w["cnt"] += nt

        def stA(k):
            t = tiles[k]
            nk, zb = t["nk"], (base + k) % 3
            Z = self.PS(zb)
            s3 = (base + k) % 3
            cl = t.get("c_lo", 0)
            for gi, (c0, ncg, qn, qr, bq) in enumerate(groups):
                lo = max(c0, cl)
                self.mm(Z[:nk, lo:c0 + ncg], t["nt"][gi], qn[:, lo - c0:], gi == 0, False, t["reads"] + [bq],
                        [self.bk[zb]])
                self.mm(Z[:nk, lo:c0 + ncg], t["rt"], qr[:, lo - c0:], False,
                        t["mask"] is None and gi == len(groups) - 1, t["reads"] + [bq], [self.bk[zb]])
            if t["mask"] is not None:
                self.mm(Z[:nk, cl:N], self.identb[:nk, :nk], t["mask"][:, cl:N], False, True,
                        [self.bconst, self.b_mask], [self.bk[zb]])
            self.act(w["A"][s3][:nk, cl:N], Z[:nk, cl:N], AF.Exp, [self.bk[zb]], [w["b_A"][s3]])

        def stC(k):
            t = tiles[k]
            nk = t["nk"]
            s3 = (base + k) % 3
            first, last = k == 0, k == nt - 1
            cl = t.get("c_lo", 0)
            for gi, (c0, ncg, qn, qr, bq) in enumerate(groups):
                lo = max(c0, cl)
                self.mm(O[:, lo:c0 + ncg], t["vm"][gi], w["A"][s3][:nk, lo:c0 + ncg], first and gi == 0, last,
                        t["reads"] + [w["b_A"][s3]], [self.bk[obank]])
            a2 = k % 2
            self.tt(w["acc"][a2][:nk, cl:N], w["acc"][a2][:nk, cl:N], w["A"][s3][:nk, cl:N], ALU.add,
                    [w["b_acc"][a2], w["b_A"][s3]], [w["b_acc"][a2]])

        self.memset(w["acc"][0][:, :N], 0.0, [w["b_acc"][0]])
        self.memset(w["acc"][1][:, :N], 0.0, [w["b_acc"][1]])
        for it in range(nt + 1):
            if it < nt:
                stA(it)
            if it >= 1:
                stC(it - 1)
        self.mm(Dn[:, :N], self.ones32[:, :], w["acc"][0][:, :N], True, False, [self.bconst, w["b_acc"][0]],
                [self.bk[dbank]])
        self.mm(Dn[:, :N], self.ones32[:, :], w["acc"][1][:, :N], False, True, [self.bconst, w["b_acc"][1]],
                [self.bk[dbank]])
        self.P.op("dve", lambda e: e.reciprocal(w["rec"][:, :N], Dn[:, :N]), [self.bk[dbank]], [w["b_rec"]])
        evac(O, obank, w["rec"], w["b_rec"])

    def phase_sample_attn(self):
        P, A, i = self.P, self.A, self.i
        w_in = i["w_in"]
        A.region(self.R3, self.R3 + 34000)
        self.oT_sb = A.alloc([128, H, NT], BF16)
        self.oT_mla = A.alloc([128, H, NT], BF16)
        self.b_oT_sb, self.b_oT_mla = P.buf("oTsb"), P.buf("oTmla")
        A.region(self.R2, self.R3)
        w = self.attn_setup()
        self.aw = w
        Qss = A.alloc([128, H, NS], BF16)
        Qns = A.alloc([128, H, NS], BF16)
        Qrs = A.alloc([64, H, NS], BF16)
        Qrt = [A.alloc([64, NS], F32) for _ in range(2)]
        cosf = A.alloc([64, NS], F32)
        sinf = A.alloc([64, NS], F32)
        self.cosf, self.sinf = cosf, sinf
        self.b_tabf = P.buf("tabf")
        P.dma("sp", cosf[:], i["cos_fm"][:, NP_:NT], "t0", writes=[self.b_tabf])
        P.dma("sp", sinf[:], i["sin_fm"][:, NP_:NT], "t1", writes=[self.b_tabf])
        mbn = A.alloc([16, 128], BF16)
        mbn32 = A.alloc([16, 128], F32)
        b_mbn32 = P.buf("mbn32")
        self.b_mask = P.buf("mask")
        P.dma("sp", mbn32[:], i["mb_new"], "mb", writes=[b_mbn32])
        self.cp(mbn[:], mbn32[:], [b_mbn32], [self.b_mask])
        wq = None
        wuq = [A.alloc([128, 4, 256], BF16) for _ in range(2)]

        self.wq, self.wuq = wq, wuq
        self.b_wq, self.b_wuq = P.bufs(2, "wq"), P.bufs(2, "wuq")
        b_Qs = P.buf("Qs_s")
        b_Qrt = P.buf("Qrt")
        Vn1 = A.alloc([16, 1024], BF16)
        VMn1 = A.alloc([16, 1024], BF16)
        b_Vn1 = P.buf("Vn1")
        P.dma("sp", Vn1[:], self.Vs_new[16:32, :], "vn1", reads=[self.b_Vsn], writes=[b_Vn1])
        P.dma("sp", VMn1[:], self.VMs_new[16:32, :], "vmn1", reads=[self.b_VMsn], writes=[b_Vn1])
        self.wcount = 0
        wq2 = [A.alloc([128, DC, 256], BF16) for _ in range(2)]
        b_wq2 = P.bufs(2, "wq2")
        uqv = i["w_uq"].rearrange("(c p) n -> p c n", p=128)

        def load_pair(hp):
            sl2 = hp % 2
            P.dma("pool", wq2[sl2][:], w_in[:, hp * 256:(hp + 1) * 256].rearrange("(c p) n -> p c n", p=128),
                  f"wq2{sl2}", writes=[b_wq2[sl2]])

        def load_wuq(h):
            sl2 = h % 2
            b0 = h * 192
            P.dma("pool", wuq[sl2][:, :, 0:192], uqv[:, :, b0:b0 + 192], f"wuqa{sl2}", writes=[self.b_wuq[sl2]])
            P.dma("pool", wuq[sl2][:, :, 192:224], uqv[:, :, b0 + 160:b0 + 192], f"wuqb{sl2}",
                  writes=[self.b_wuq[sl2]])
            P.dma("pool", wuq[sl2][:, :, 224:256], uqv[:, :, b0 + 128:b0 + 160], f"wuqc{sl2}",
                  writes=[self.b_wuq[sl2]])
            return sl2

        load_pair(0)
        load_pair(1)
        for h in range(H):
            s = load_wuq(h)
            sp2, off = (h // 2) % 2, (h % 2) * 128
            bA, bB = (6, 7) if h % 2 == 0 else (3, 4)
            ps = self.PS(bA)
            for c in range(DC):
                self.mm(ps[:, :NS], wq2[sp2][:, c, off:off + 128], self.actb[:, c, NP_:NT], c == 0, c == DC - 1,
                        [b_wq2[sp2], self.b_act[c]], [self.bk[bA]])
            if h % 2 == 1 and h // 2 + 2 < 4:
                load_pair(h // 2 + 2)
            self.act(Qss[:, h, :], ps[:, :NS], AF.Identity, [self.bk[bA]], [b_Qs], scale=SB_SCALE)
            ps = self.PS(bB)
            for kc in range(4):
                self.mm(ps[:, :NS], wuq[s][:, kc, 0:128], self.cqnT[:, kc, NP_:NT], kc == 0, kc == 3,
                        [self.b_wuq[s], self.b_cqnT], [self.bk[bB]])
            self.act(Qns[:, h, :], ps[:, :NS], AF.Identity, [self.bk[bB]], [b_Qs], scale=MLA_SCALE)
            ps = self.PS(bA)
            for kc in range(4):
                self.mm(ps[:64, :NS], wuq[s][:, kc, 128:192], self.cqnT[:, kc, NP_:NT], kc == 0, kc == 3,
                        [self.b_wuq[s], self.b_cqnT], [self.bk[bA]])
            for kc in range(4):
                self.mm(ps[:64, 64:64 + NS], wuq[s][:, kc, 192:256], self.cqnT[:, kc, NP_:NT], kc == 0, kc == 3,
                        [self.b_wuq[s], self.b_cqnT], [self.bk[bA]])
            self.tt(Qrt[0][:, :], ps[:64, :NS], cosf[:, :], ALU.mult, [self.bk[bA], self.b_tabf], [b_Qrt])
            self.tt(Qrt[1][:, :], ps[:64, 64:64 + NS], sinf[:, :], ALU.mult, [self.bk[bA], self.b_tabf], [b_Qrt])
            self.tt(Qrt[0][:, :], Qrt[0][:, :], Qrt[1][:, :], ALU.add, [b_Qrt], [b_Qrt])
            self.act(Qrs[:, h, :], Qrt[0][:, :], AF.Identity, [b_Qrt], [b_Qs], scale=MLA_SCALE)
        r2cur = A.cur
        A.region(self.R3 + 34000, self.TOP - 34000)
        KTc = A.alloc([128, H, PAST], BF16)
        Vc = A.alloc([128, 8, 1024], BF16)
        A.region(r2cur, self.R3)
        ckvTc = A.alloc([128, 4, PAST], BF16)
        krTc = A.alloc([64, PAST], BF16)
        kst = [A.alloc([128, 1024], F32) for _ in range(2)]
        krst = A.alloc([128, 8, 64], F32)
        b_KTc, b_Vc, b_ckvTc, b_krTc = P.buf("KTc"), P.buf("Vc"), P.buf("ckvTc"), P.buf("krTc")
        b_kst, b_krst = P.bufs(2, "kst"), P.buf("krst")
        kcnt = 0
        for sq in range(2):
            qsl = slice(sq * 16, sq * 16 + 16)
            for kt in range(8):
                s = kcnt % 2
                kcnt += 1
                P.dma("sp", kst[s][:, :], i["c_k"][sq, kt * 128:(kt + 1) * 128, :], f"kst{s}", writes=[b_kst[s]])
                for hg in range(2):
                    bank = 6 + hg
                    ps = self.PS(bank)
                    for j in range(4):
                        h = hg * 4 + j
                        self.tr(ps[:, j * 128:(j + 1) * 128], kst[s][:, h * 128:(h + 1) * 128], self.ident[:, :],
                                [b_kst[s], self.bconst], [self.bk[bank]])
                    self.cp(KTc[:, hg * 4:hg * 4 + 4, kt * 128:(kt + 1) * 128],
                            ps.rearrange("p (j t) -> p j t", j=4), [self.bk[bank]], [b_KTc])
            P.dma("pool", Vc[:], i["c_v"][sq].rearrange("(k p) c -> p k c", p=128), "vc", writes=[b_Vc])
            groups = [(h * 16, 16, Qss[:, h, qsl], b_Qs) for h in range(H)]
            vnew = self.Vs_new if sq == 0 else Vn1
            b_vnew = self.b_Vsn if sq == 0 else b_Vn1
            tiles = [dict(nk=16, kt=[self.KTs[:, h, qsl] for h in range(H)],
                          v=[vnew[:16, h * 128:(h + 1) * 128] for h in range(H)], mask=mbn[:, :],
                          reads=[self.b_KTs, b_vnew])]
            for kt in range(7, -1, -1):
                tiles.append(dict(nk=128, kt=[KTc[:, h, kt * 128:(kt + 1) * 128] for h in range(H)],
                                  v=[Vc[:, kt, h * 128:(h + 1) * 128] for h in range(H)], mask=None,
                                  reads=[b_KTc, b_Vc]))
            c0 = NP_ + sq * 16

            def evac_sb(O, obank, c0=c0):
                self.cp(self.oT_sb[:, :, c0:c0 + 16], O[:, :128].rearrange("p (h q) -> p h q", q=16),
                        [self.bk[obank]], [self.b_oT_sb])
            self.sb_problem(w, 128, groups, tiles, 3, evac_sb)
            NTc, VMc = KTc, Vc
            for kt in range(8):
                s = kcnt % 2
                kcnt += 1
                P.dma("sp", kst[s][:, :512], i["c_ckv"][sq, kt * 128:(kt + 1) * 128, :], f"kst{s}",
                      writes=[b_kst[s]])
                bank = 6 + kt % 2
                ps = self.PS(bank)
                for j in range(4):
                    self.tr(ps[:, j * 128:(j + 1) * 128], kst[s][:, j * 128:(j + 1) * 128], self.ident[:, :],
                            [b_kst[s], self.bconst], [self.bk[bank]])
                self.cp(ckvTc[:, :, kt * 128:(kt + 1) * 128], ps.rearrange("p (j t) -> p j t", j=4),
                        [self.bk[bank]], [b_ckvTc])
            P.dma("sp", krst[:], i["c_kr"][sq].rearrange("(k p) c -> p k c", p=128), "krst", writes=[b_krst])
            for kg in range(2):
                bank = 6 + kg
                ps = self.PS(bank)
                for j in range(4):
                    kt = kg * 4 + j
                    self.tr(ps[:64, j * 128:(j + 1) * 128], krst[:, kt, :], self.ident[:, :],
                            [b_krst, self.bconst], [self.bk[bank]])
                self.cp(krTc[:, kg * 512:(kg + 1) * 512], ps[:64, :], [self.bk[bank]], [b_krTc])
            for h in range(H):
                for half in range(2):
                    bank = 6 + (h * 2 + half) % 2
                    ps = self.PS(bank)
                    for kc in range(4):
                        self.mm(ps[:, :], self.wukv[:, kc, h * 256:h * 256 + 128],
                                ckvTc[:, kc, half * 512:(half + 1) * 512], kc == 0, kc == 3,
                                [self.b_wukv, b_ckvTc], [self.bk[bank]])
                    self.act(NTc[:, h, half * 512:(half + 1) * 512], ps[:, :], AF.Identity, [self.bk[bank]],
                             [b_KTc])
            for kt in range(8):
                for half in range(2):
                    bank = 6 + (kt * 2 + half) % 2
                    ps = self.PS(bank)
                    for hh in range(4):
                        hc = (half * 4 + hh) * 256 + 128
                        for kc in range(4):
                            self.mm(ps[:, hh * 128:(hh + 1) * 128], ckvTc[:, kc, kt * 128:(kt + 1) * 128],
                                    self.wukv[:, kc, hc:hc + 128], kc == 0, kc == 3,
                                    [b_ckvTc, self.b_wukv], [self.bk[bank]])
                    self.cp(VMc[:, kt, half * 512:(half + 1) * 512], ps[:, :], [self.bk[bank]], [b_Vc])
            groups = [(h * 16, 16, Qns[:, h, qsl], Qrs[:, h, qsl], b_Qs) for h in range(H)]
            vmnew = self.VMs_new if sq == 0 else VMn1
            b_vmnew = self.b_VMsn if sq == 0 else b_Vn1
            tiles = [dict(nk=16, nt=[self.NTs[:, h, qsl] for h in range(H)], rt=self.krT[:, c0:c0 + 16],
                          vm=[vmnew[:16, h * 128:(h + 1) * 128] for h in range(H)], mask=None,
                          reads=[self.b_NTs, self.b_krT, b_vmnew])]
            for kt in range(7, -1, -1):
                tiles.append(dict(nk=128, nt=[NTc[:, h, kt * 128:(kt + 1) * 128] for h in range(H)],
                                  rt=krTc[:, kt * 128:(kt + 1) * 128],
                                  vm=[VMc[:, kt, h * 128:(h + 1) * 128] for h in range(H)], mask=None,
                                  reads=[b_KTc, b_krTc, b_Vc]))

            def evac_mla(O, obank, rec, b_rec, c0=c0):
                self.tt(self.oT_mla[:, :, c0:c0 + 16], O[:, :128].rearrange("p (h q) -> p h q", q=16),
                        rec[:, :128].rearrange("p (h q) -> p h q", q=16), ALU.mult,
                        [self.bk[obank], b_rec], [self.b_oT_mla])
            self.mla_problem(w, 128, groups, tiles, 3, 5, evac_mla)
        P.barrier()

    def load_wq(self, h):
        P, i = self.P, self.i
        s = self.wcount % 2
        self.wcount += 1
        P.dma("pool", self.wq[s][:], i["w_in"][:, h * 128:(h + 1) * 128].rearrange("(c p) n -> p c n", p=128),
              f"wq{s}", writes=[self.b_wq[s]])
        uq = i["w_uq"].rearrange("(c p) n -> p c n", p=128)
        b0 = h * 192
        P.dma("pool", self.wuq[s][:, :, 0:192], uq[:, :, b0:b0 + 192], f"wuqa{s}", writes=[self.b_wuq[s]])
        P.dma("pool", self.wuq[s][:, :, 192:224], uq[:, :, b0 + 160:b0 + 192], f"wuqb{s}", writes=[self.b_wuq[s]])
        P.dma("pool", self.wuq[s][:, :, 224:256], uq[:, :, b0 + 128:b0 + 160], f"wuqc{s}", writes=[self.b_wuq[s]])
        return s

    def phase_prompt_attn(self):
        P, A, i = self.P, self.A, self.i
        A.region(self.R2, self.R3)
        w = self.attn_setup(alt=(self.TOP - 34000 + 8448, self.TOP))
        mask = A.alloc([128, 16, 512], BF16)
        self.b_mask = P.buf("mask2")
        Kg = [A.alloc([128, 4096], BF16) for _ in range(2)]
        Vg = [A.alloc([128, 32, 128], BF16) for _ in range(2)]
        A.region(self.R3 + 34000, self.TOP - 34000)
        Qs = [A.alloc([128, NP_], BF16) for _ in range(2)]
        Rg = A.alloc([64, 4096], BF16)
        Qr = [A.alloc([64, NP_], BF16) for _ in range(2)]
        Qrt = [A.alloc([64, 512], F32) for _ in range(2)]
        wq = [A.alloc([128, DC, 128], BF16) for _ in range(2)]
        wuq = [A.alloc([128, 4, 256], BF16) for _ in range(2)]
        self.wq, self.wuq = wq, wuq
        self.b_wq, self.b_wuq = P.bufs(2, "wq2"), P.bufs(2, "wuq2")
        b_Kg, b_Vg, b_Rg = P.bufs(2, "Kg"), P.bufs(2, "Vg"), P.buf("Rg")
        b_Qs, b_Qr, b_Qrt = P.bufs(2, "Qs"), P.bufs(2, "Qr"), P.buf("Qrt2")
        A.region(self.alt_cur, self.TOP)
        cosf = A.alloc([64, NP_], F32)
        sinf = A.alloc([64, NP_], F32)
        b_tabf = P.buf("tabf2")
        P.dma("sp", cosf[:], i["cos_fm"][:, 0:NP_], "t0", writes=[b_tabf])
        P.dma("sp", sinf[:], i["sin_fm"][:, 0:NP_], "t1", writes=[b_tabf])

        def load_kv(slot, row0, h):
            pairs, rds = [], []
            for r in range(4):
                ea, ej = self.exo(r, row0 + h * 128, 128)
                pairs.append((Kg[slot][:, r * 1024:(r + 1) * 1024], ea))
                rds.append(self.b_exo[ej])
            P.dma_batch("sp", pairs, f"kg{slot}", reads=rds, writes=[b_Kg[slot]])
            vrow = R_V if row0 == R_KT else R_VM
            pairs, rds = [], []
            for r in range(4):
                for hf in range(2):
                    ea, ej = self.exo(r, vrow + hf * 512, 512)
                    pairs.append((Vg[slot][:, r * 8 + hf * 4:r * 8 + hf * 4 + 4, :],
                                  ea[:, h * 128:(h + 1) * 128].rearrange("(m p) d -> p m d", p=128)))
                    rds.append(self.b_exo[ej])
            P.dma_batch("sp", pairs, f"vg{slot}", reads=rds, writes=[b_Vg[slot]])

        def key_tiles(M, slot, mla):
            tl = []
            for kt in range(16 * M + 15, -1, -1):
                r, m = kt % 4, kt // 4
                col = r * 1024 + m * 128
                mk = mask[:, kt - 16 * M, :] if kt >= 16 * M else None
                d = dict(nk=128, mask=mk, reads=[b_Kg[slot], b_Vg[slot]] + ([b_Rg] if mla else []))
                if kt >= 16 * M:
                    d["c_lo"] = 128 * max(0, (kt - 16 * M - 3 + 3) // 4)
                if mla:
                    d["nt"] = [Kg[slot][:, col:col + 128]]
                    d["rt"] = Rg[:, col:col + 128]
                    d["vm"] = [Vg[slot][:, r * 8 + m, :]]
                else:
                    d["kt"] = [Kg[slot][:, col:col + 128]]
                    d["v"] = [Vg[slot][:, r * 8 + m, :]]
                tl.append(d)
            return tl

        self.wcount = 0
        pcount = 0
        P.dma("pool", mask[:], i["mb_sb"], "mb2", writes=[self.b_mask])
        wslots = [self.load_wq(0), self.load_wq(1)]
        self.issue_cc([4, 5, 6, 7, 8])

        def qproj_sb(h):
            slot, s = h % 2, wslots[h]
            for half in range(2):
                bank = 6 + half
                ps = self.PS(bank)
                for c in range(DC):
                    self.mm(ps[:, :], wq[s][:, c, :], self.actb[:, c, half * 512:(half + 1) * 512], c == 0,
                            c == DC - 1, [self.b_wq[s], self.b_act[c]], [self.bk[bank]])
                self.act(Qs[slot][:, half * 512:(half + 1) * 512], ps[:, :], AF.Identity, [self.bk[bank]],
                         [b_Qs[slot]], scale=SB_SCALE)
            if h + 2 < H:
                wslots.append(self.load_wq(h + 2))

        load_kv(0, R_KT, 0)
        qproj_sb(0)
        for h in range(H):
            slot = h % 2
            for M in range(2):
                groups = [(0, 512, Qs[slot][:, M * 512:(M + 1) * 512], b_Qs[slot])]
                obank = 3 + pcount % 2
                pcount += 1

                def evac(O, ob, h=h, M=M):
                    self.act(self.oT_sb[:, h, M * 512:(M + 1) * 512], O[:, :], AF.Identity, [self.bk[ob]],
                             [self.b_oT_sb])
                self.sb_problem(w, 512, groups, key_tiles(M, slot, False), obank, evac)
                if M == 0 and h + 1 < H:
                    load_kv((h + 1) % 2, R_KT, h + 1)
                    qproj_sb(h + 1)
        P.dma("pool", mask[:], i["mb_mla"], "mb2", writes=[self.b_mask])
        for r in range(4):
            ea, ej = self.exo(r, R_RT, 64)
            P.dma("sp", Rg[:, r * 1024:(r + 1) * 1024], ea, "rg", reads=[self.b_exo[ej]],
                  writes=[b_Rg])
        mslots = [self.load_wq(0), self.load_wq(1)]

        def qproj_mla(h):
            slot, s = h % 2, mslots[h]
            for half in range(2):
                tsl = slice(half * 512, (half + 1) * 512)
                bank = 6 + half
                ps = self.PS(bank)
                for kc in range(4):
                    self.mm(ps[:, :], wuq[s][:, kc, 0:128], self.cqnT[:, kc, tsl], kc == 0, kc == 3,
                            [self.b_wuq[s], self.b_cqnT], [self.bk[bank]])
                self.act(Qs[slot][:, tsl], ps[:, :], AF.Identity, [self.bk[bank]], [b_Qs[slot]], scale=MLA_SCALE)
            for half in range(2):
                tsl = slice(half * 512, (half + 1) * 512)
                for part, bank in ((0, 6), (1, 7)):
                    ps = self.PS(bank)
                    for kc in range(4):
                        self.mm(ps[:64, :], wuq[s][:, kc, 128 + part * 64:192 + part * 64], self.cqnT[:, kc, tsl],
                                kc == 0, kc == 3, [self.b_wuq[s], self.b_cqnT], [self.bk[bank]])
                self.tt(Qrt[0][:, :], self.PS(6)[:64, :], cosf[:, tsl], ALU.mult, [self.bk[6], b_tabf], [b_Qrt])
                self.tt(Qrt[1][:, :], self.PS(7)[:64, :], sinf[:, tsl], ALU.mult, [self.bk[7], b_tabf], [b_Qrt])
                self.tt(Qrt[0][:, :], Qrt[0][:, :], Qrt[1][:, :], ALU.add, [b_Qrt], [b_Qrt])
                self.act(Qr[slot][:, tsl], Qrt[0][:, :], AF.Identity, [b_Qrt], [b_Qr[slot]], scale=MLA_SCALE)
            if h + 2 < H:
                mslots.append(self.load_wq(h + 2))

        load_kv(0, R_NT, 0)
        qproj_mla(0)
        for h in range(H):
            slot = h % 2
            for M in range(2):
                groups = [(0, 512, Qs[slot][:, M * 512:(M + 1) * 512], Qr[slot][:, M * 512:(M + 1) * 512],
                           b_Qs[slot])]
                obank = 3 + pcount % 2
                dbank = 5
                pcount += 1

                def evac(O, ob, rec, b_rec, h=h, M=M):
                    self.tt(self.oT_mla[:, h, M * 512:(M + 1) * 512], O[:, :], rec[:, :], ALU.mult,
                            [self.bk[ob], b_rec], [self.b_oT_mla])
                tl = key_tiles(M, slot, True)
                for t in tl:
                    t["reads"] = t["reads"] + [b_Qr[slot]]
                self.mla_problem(w, 512, groups, tl, obank, dbank, evac)
                if M == 0 and h + 1 < H:
                    load_kv((h + 1) % 2, R_NT, h + 1)
                    qproj_mla(h + 1)
        P.barrier()

    def phase_merge(self):
        P, A, i = self.P, self.A, self.i
        w_in = i["w_in"]
        A.region(self.R3 + 34000, self.R3 + 34000 + 34000)
        mg = A.alloc([128, DC, NT], BF16)
        b_mg = P.bufs(DC, "mg")
        A.region(self.R2, self.R3)
        wg = [A.alloc([128, DC, 256], BF16) for _ in range(2)]
        wb = [A.alloc([128, 16, 128], BF16) for _ in range(2)]
        gs = A.alloc([128, NT], F32)
        gm = A.alloc([128, NT], F32)
        t1 = A.alloc([128, NT], F32)
        t2 = A.alloc([128, NT], F32)
        b_wg, b_wb = P.bufs(2, "wg"), P.bufs(2, "wb")
        b_gs, b_gm, b_t1, b_t2 = P.bufs(3, "gs"), P.bufs(3, "gm"), P.bufs(3, "t1"), P.bufs(3, "t2")

        def load(ci):
            s = ci % 2
            P.dma("pool", wg[s][:, :, 0:128],
                  w_in[:, 4160 + ci * 128:4160 + (ci + 1) * 128].rearrange("(c p) n -> p c n", p=128),
                  f"wga{s}", writes=[b_wg[s]])
            P.dma("pool", wg[s][:, :, 128:256],
                  w_in[:, 6208 + ci * 128:6208 + (ci + 1) * 128].rearrange("(c p) n -> p c n", p=128),
                  f"wgb{s}", writes=[b_wg[s]])
            P.dma("pool", wb[s][:, 0:8, :],
                  i["w_br_sb"][:, ci * 128:(ci + 1) * 128].rearrange("(c p) n -> p c n", p=128),
                  f"wba{s}", writes=[b_wb[s]])
            P.dma("pool", wb[s][:, 8:16, :],
                  i["w_br_mla"][:, ci * 128:(ci + 1) * 128].rearrange("(c p) n -> p c n", p=128),
                  f"wbb{s}", writes=[b_wb[s]])

        load(0)
        load(1)
        for ci in range(DC):
            s = ci % 2
            for (b0, off, gt, b_g, bidx) in ((0, 0, gs, b_gs, ci), (3, 128, gm, b_gm, 16 + ci)):
                for c in range(DC):
                    for (t0, n, bk) in TT:
                        self.mm(self.PS(b0 + bk)[:, :n], wg[s][:, c, off:off + 128], self.actb[:, c, t0:t0 + n],
                                c == 0, c == DC - 1, [b_wg[s], self.b_act[c]], [self.bk[b0 + bk]])
                for (t0, n, bk) in TT:
                    self.act(gt[:, t0:t0 + n], self.PS(b0 + bk)[:, :n], AF.Sigmoid, [self.bk[b0 + bk], self.bconst],
                             [b_g[bk]], bias=self.bg[:, bidx:bidx + 1])
            for (b0, r0, oT, b_oT) in ((0, 0, self.oT_sb, self.b_oT_sb), (3, 8, self.oT_mla, self.b_oT_mla)):
                for h in range(H):
                    for (t0, n, bk) in TT:
                        self.mm(self.PS(b0 + bk)[:, :n], wb[s][:, r0 + h, :], oT[:, h, t0:t0 + n], h == 0, h == H - 1,
                                [b_wb[s], b_oT], [self.bk[b0 + bk]])
            for (t0, n, bk) in TT:
                self.tt(t1[:, t0:t0 + n], gs[:, t0:t0 + n], self.PS(bk)[:, :n], ALU.mult,
                        [b_gs[bk], self.bk[bk]], [b_t1[bk]])
                self.tt(t2[:, t0:t0 + n], gm[:, t0:t0 + n], self.PS(3 + bk)[:, :n], ALU.mult,
                        [b_gm[bk], self.bk[3 + bk]], [b_t2[bk]])
                self.tt(mg[:, ci, t0:t0 + n], t1[:, t0:t0 + n], t2[:, t0:t0 + n], ALU.add,
                        [b_t1[bk], b_t2[bk]], [b_mg[ci]])
            if ci + 2 < DC:
                load(ci + 2)
        P.barrier()
        P.dma("sp", self.resid[:], self.spill.ap(), "spill", writes=self.b_res)
        A.region(self.R3, self.R3 + 34000)
        wo = [A.alloc([128, DC, 512], BF16) for _ in range(2)]
        b_wo = P.bufs(2, "wo")

        def load_wo(q):
            s = q % 2
            P.dma("pool", wo[s][:], i["w_o"][:, q * 512:(q + 1) * 512].rearrange("(c p) n -> p c n", p=128),
                  f"wo{s}", writes=[b_wo[s]])
        load_wo(0)
        load_wo(1)
        for q in range(4):
            s = q % 2
            for j in range(4):
                ci = q * 4 + j
                b0 = 0 if ci % 2 == 0 else 3
                for c in range(DC):
                    for (t0, n, bk) in TT:
                        self.mm(self.PS(b0 + bk)[:, :n], wo[s][:, c, j * 128:(j + 1) * 128], mg[:, c, t0:t0 + n],
                                c == 0, c == DC - 1, [b_wo[s], b_mg[c]], [self.bk[b0 + bk]])
                for (t0, n, bk) in TT:
                    self.tt(self.resid[:, ci, t0:t0 + n], self.resid[:, ci, t0:t0 + n], self.PS(b0 + bk)[:, :n],
                            ALU.add, [self.b_res[ci], self.bk[b0 + bk]], [self.b_res[ci]])
            if q + 2 < 4:
                load_wo(q + 2)
        P.barrier()

    def phase_out(self):
        P, A, o = self.P, self.A, self.o
        A.region(self.R3, self.TOP)
        ys = [A.alloc([128, D], F32) for _ in range(2)]
        b_ys = P.bufs(2, "ys")
        k = 0
        for ti, (t0, m) in enumerate(TM):
            s = ti % 2
            for q in range(4):
                b = 4 + k % 4
                k += 1
                ps = self.PS(b)
                for j in range(4):
                    c = q * 4 + j
                    self.tr(ps[:m, j * 128:(j + 1) * 128], self.resid[:, c, t0:t0 + m], self.ident[:, :],
                            [self.b_res[c], self.bconst], [self.bk[b]])
                if k % 2 == 0:
                    self.act(ys[s][:m, q * 512:(q + 1) * 512], ps[:m, :], AF.Identity, [self.bk[b]], [b_ys[s]])
                else:
                    self.cp(ys[s][:m, q * 512:(q + 1) * 512], ps[:m, :], [self.bk[b]], [b_ys[s]])
            P.dma("sp", o["y"][t0:t0 + m, :], ys[s][:m, :], f"y{s}", reads=[b_ys[s]])

    def dump_act(self):
        dbg = self.nc.dram_tensor("dbg", [128, DC, NT], F32, kind="ExternalOutput").ap()
        self.P.dma("sp", dbg, self.resid[:], "dbg", reads=self.b_res)


def build_nc(stop_after=None, **kw):
    b = Builder(stop_after=stop_after, **kw)
    b.nc._used_inputs = list(b.i.keys())
    return b.nc


def host_inputs(inp):
    f = np.float32
    xp = np.asarray(inp["x_prompt"], f)
    xs = np.asarray(inp["x_sample"], f)
    common = {
        "w1a": np.ascontiguousarray(inp["ffn1_w_in"][0]), "w2a": np.ascontiguousarray(inp["ffn1_w_out"][0]),
        "w1b": np.ascontiguousarray(inp["ffn2_w_in"][0]), "w2b": np.ascontiguousarray(inp["ffn2_w_out"][0]),
        "w_in": np.ascontiguousarray(inp["w_in"][0]), "w_uq": np.ascontiguousarray(inp["w_uq"][0]),
        "w_ukv": np.ascontiguousarray(inp["w_ukv"][0]), "w_br_sb": np.ascontiguousarray(inp["w_br_sb"][0]),
        "w_br_mla": np.ascontiguousarray(inp["w_br_mla"][0]), "w_o": np.ascontiguousarray(inp["w_o"][0]),
    }
    lnp = np.concatenate([np.asarray(inp[k][0], f).reshape(16, 128).T for k in
                          ("ln1_g", "ln1_b", "ln2_g", "ln2_b", "ln3_g", "ln3_b")], axis=1)
    common["lnp"] = np.ascontiguousarray(lnp)
    common["bgate"] = np.ascontiguousarray(np.asarray(inp["b_gate"][0], f).reshape(32, 128).T)
    common["gcq"] = np.ascontiguousarray(np.broadcast_to(np.asarray(inp["g_cq"][0], f), (128, 512)))
    common["gckv"] = np.ascontiguousarray(np.broadcast_to(np.asarray(inp["g_ckv"][0], f), (128, 512)))
    common["ident"] = np.eye(128, dtype=f)
    cm = np.zeros((128, 4, 128), f)
    jj, ss = np.meshgrid(np.arange(128), np.arange(128), indexing="ij")
    cm[:, 0, :] = -(jj >= ss).astype(f)
    cm[:, 1, :] = -1.0
    cm[:, 2, :] = 1.0
    common["cmats"] = cm
    kk, tq = np.meshgrid(np.arange(16), np.arange(16), indexing="ij")
    common["mb_new"] = np.ascontiguousarray(np.tile(np.where(kk < tq, 0.0, NEG).astype(f), (1, 8)))
    maps = []
    inv_freq = (10000.0 ** (-np.arange(0, 64, 2, dtype=np.float32) / 64)).astype(f)
    for c in range(8):
        b, r = c // 4, c % 4
        tiles = [4 * m + r for m in range(8)]
        rows = np.concatenate([np.arange(g * 128, (g + 1) * 128) for g in tiles])
        x = np.concatenate([xp[b, rows], xs[2 * c], xs[2 * c + 1]], 0)
        pos = np.concatenate([rows, PAST + np.arange(16), PAST + np.arange(16)]).astype(f)
        ang = pos[:, None] * inv_freq[None, :]
        cos, sin = np.cos(ang).astype(f), np.sin(ang).astype(f)
        cpad = np.zeros((9 * 128, 32), f)
        spad = np.zeros((9 * 128, 32), f)
        cpad[:NT] = cos
        spad[:NT] = sin
        m = dict(common)
        m["x"] = np.ascontiguousarray(x)
        m["cos_tm"] = np.ascontiguousarray(cpad.reshape(9, 128, 32).transpose(1, 0, 2))
        m["sin_tm"] = np.ascontiguousarray(spad.reshape(9, 128, 32).transpose(1, 0, 2))
        m["cos_fm"] = np.ascontiguousarray(np.concatenate([cos.T, cos.T], 0))
        m["sin_fm"] = np.ascontiguousarray(np.concatenate([-sin.T, sin.T], 0))
        k_, q_ = np.meshgrid(np.arange(128), np.arange(128), indexing="ij")
        msb = np.zeros((128, 16, 512), f)
        mml = np.zeros((128, 16, 512), f)
        for o in range(16):
            for i4 in range(4):
                qt = 4 * i4 + r
                sl = slice(i4 * 128, (i4 + 1) * 128)
                if o > qt:
                    msb[:, o, sl] = NEG
                    mml[:, o, sl] = NEG
                elif o == qt:
                    msb[:, o, sl] = np.where(k_ < q_, 0.0, NEG)
                    mml[:, o, sl] = np.where((k_ // 64) <= (q_ // 64), 0.0, NEG)
        m["mb_sb"] = msb
        m["mb_mla"] = mml
        m["c_k"] = np.ascontiguousarray(np.asarray(inp["cache_sb_k"][0, 2 * c:2 * c + 2], f).reshape(2, PAST, 1024))
        m["c_v"] = np.ascontiguousarray(np.asarray(inp["cache_sb_v"][0, 2 * c:2 * c + 2], f).reshape(2, PAST, 1024))
        m["c_ckv"] = np.ascontiguousarray(np.asarray(inp["cache_mla_ckv"][0, 2 * c:2 * c + 2], f))
        m["c_kr"] = np.ascontiguousarray(np.asarray(inp["cache_mla_krope"][0, 2 * c:2 * c + 2], f))
        maps.append(m)
    return maps


_NC = None


def kernel(**inputs):
    global _NC
    maps = host_inputs(inputs)
    if _NC is None:
        _NC = build_nc()
    used = _NC._used_inputs
    maps = [{k: m[k] for k in used} for m in maps]
    res = run_bass_kernel_spmd(_NC, maps, core_ids=list(range(8)))
    return assemble(res.results)


def assemble(results):
    f = np.float32
    y_p = np.zeros((2, 4096, D), f)
    y_s = np.zeros((16, 16, D), f)
    nk_p = np.zeros((1, 2, 4096, 8, 128), f)
    nv_p = np.zeros((1, 2, 4096, 8, 128), f)
    nc_p = np.zeros((1, 2, 4096, 512), f)
    nr_p = np.zeros((1, 2, 4096, 64), f)
    nk_s = np.zeros((1, 16, 16, 8, 128), f)
    nv_s = np.zeros((1, 16, 16, 8, 128), f)
    nc_s = np.zeros((1, 16, 16, 512), f)
    nr_s = np.zeros((1, 16, 16, 64), f)
    for c in range(8):
        b, r = c // 4, c % 4
        R = results[c]
        for m in range(8):
            g = 4 * m + r
            dst = slice(g * 128, (g + 1) * 128)
            src = slice(m * 128, (m + 1) * 128)
            y_p[b, dst] = R["y"][src]
            nk_p[0, b, dst] = R["nk"][src].reshape(128, 8, 128)
            nv_p[0, b, dst] = R["nv"][src].reshape(128, 8, 128)
            nc_p[0, b, dst] = R["nckv"][src]
            nr_p[0, b, dst] = R["nkr"][src]
        for s in range(2):
            src = slice(1024 + 16 * s, 1024 + 16 * s + 16)
            y_s[2 * c + s] = R["y"][src]
            nk_s[0, 2 * c + s] = R["nk"][src].reshape(16, 8, 128)
            nv_s[0, 2 * c + s] = R["nv"][src].reshape(16, 8, 128)
            nc_s[0, 2 * c + s] = R["nckv"][src]
            nr_s[0, 2 * c + s] = R["nkr"][src]
    return (y_p, y_s, nk_p, nv_p, nc_p, nr_p, nk_s, nv_s, nc_s, nr_s)
```

```python
import numpy as np
import concourse.bass as bass
import concourse.mybir as mybir
from concourse.bass_utils import run_bass_kernel_spmd

F32 = mybir.dt.float32
BF16 = mybir.dt.bfloat16
AF = mybir.ActivationFunctionType
ALU = mybir.AluOpType

SEM_LIM = 30000
NEG = -30000.0

D = 2048
DC = 16
FF = 5504
FC = 43
NP_ = 1024
NS = 32
NT = NP_ + NS
TT = [(0, 352, 0), (352, 352, 1), (704, 352, 2)]
TM = [(i * 128, 128) for i in range(8)] + [(1024, 32)]
H = 8
PAST = 1024
ALPHA = 2.0 ** 0.25
SB_SCALE = 128.0 ** -0.5
MLA_SCALE = 192.0 ** -0.5
LN_EPS = 1e-5
RMS_EPS = 1e-6
EX_ROWS = 4160
R_KT, R_V, R_NT, R_VM, R_RT = 0, 1024, 2048, 3072, 4096


class Buf:
    __slots__ = ("name", "lastw", "readers", "excl")

    def __init__(self, name, excl=False):
        self.name = name
        self.lastw = None
        self.readers = {}
        self.excl = excl


class Op:
    __slots__ = ("eng", "fn", "deps", "signal", "count", "key", "seq", "is_dma", "inc")

    def __init__(self, eng, fn, key, seq, is_dma, inc=None):
        self.eng = eng
        self.fn = fn
        self.key = key
        self.seq = seq
        self.is_dma = is_dma
        self.deps = {}
        self.signal = is_dma
        self.count = 0
        self.inc = inc if inc is not None else (16 if is_dma else 1)


class Prog:
    ENGS = ("pe", "act", "dve", "pool", "sp")

    def __init__(self, nc):
        self.nc = nc
        self.ops = {e: [] for e in self.ENGS}
        self.latest = {}
        self.pending = {e: {} for e in self.ENGS}
        self.seq = 0
        self.dma_counts = {}
        self.dma_inc = {}
        self.nbuf = 0

    def buf(self, name=None, excl=False):
        self.nbuf += 1
        return Buf(name or f"b{self.nbuf}", excl)

    def bufs(self, n, name="b"):
        return [self.buf(f"{name}{i}") for i in range(n)]

    def _add(self, eng, fn, reads, writes, semkey=None, inc=None, touch=()):
        is_dma = semkey is not None
        key = ("dma", semkey) if is_dma else eng
        self.seq += 1
        o = Op(eng, fn, key, self.seq, is_dma, inc)
        if is_dma:
            self.dma_inc[semkey] = o.inc
        deps = {}

        def add(d):
            if d is None:
                return
            if d.key == "pe" and eng == "pe" and not is_dma:
                return
            cur = deps.get(d.key)
            if cur is None or cur.seq < d.seq:
                deps[d.key] = d

        for b in reads:
            add(b.lastw)
            if b.excl:
                for r in b.readers.values():
                    if r.key != key:
                        add(r)
        for b in writes:
            add(b.lastw)
            for r in b.readers.values():
                add(r)
        for d in self.pending[eng].values():
            add(d)
        self.pending[eng] = {}
        o.deps = deps
        for b in reads:
            cur = b.readers.get(key)
            if cur is None or cur.seq < o.seq:
                b.readers[key] = o
        for b in writes:
            b.lastw = o
            b.readers = {}
        for b in touch:
            b.lastw = o
            b.readers = {}
        if is_dma:
            c = self.dma_counts.get(semkey, 0) + 1
            self.dma_counts[semkey] = c
            o.count = c
        self.ops[eng].append(o)
        self.latest[key] = o
        return o

    def op(self, eng, fn, reads=(), writes=()):
        return self._add(eng, fn, reads, writes)

    def dma(self, queue, out, in_, semkey, reads=(), writes=(), **kw):
        def fn(e):
            return e.dma_start(out=out, in_=in_, **kw)
        return self._add(queue, fn, reads, writes, semkey=semkey)

    def dma_batch(self, queue, pairs, semkey, reads=(), writes=()):
        n = len(pairs)
        for j, (out, in_) in enumerate(pairs):
            def fn(e, out=out, in_=in_):
                return e.dma_start(out=out, in_=in_)
            if n == 1:
                self._add(queue, fn, reads, writes, semkey=semkey)
            elif j == 0:
                self._add(queue, fn, reads, writes, semkey=semkey)
            elif j == n - 1:
                self._add(queue, fn, (), (), semkey=semkey, touch=writes)
            else:
                self._add(queue, fn, (), (), semkey=semkey)

    def barrier(self):
        snap = {k: v for k, v in self.latest.items()
                if not (isinstance(k, tuple) and str(k[1]).startswith("cc"))}
        for e in self.ENGS:
            self.pending[e] = dict(snap)

    def emit(self):
        nc = self.nc
        for e in self.ENGS:
            for o in self.ops[e]:
                for d in o.deps.values():
                    d.signal = True
        tot = {}
        for e in self.ENGS:
            c = 0
            for o in self.ops[e]:
                if not o.is_dma and o.signal:
                    c += 1
                    o.count = c
            tot[e] = c
        sems = {}

        def nsem(units):
            return max(1, (units + SEM_LIM - 1) // SEM_LIM)

        for e in self.ENGS:
            sems[e] = [nc.alloc_semaphore(f"s_{e}_{i}") for i in range(nsem(tot[e]))]
        for k, c in self.dma_counts.items():
            sems[("dma", k)] = [nc.alloc_semaphore(f"sd_{k}_{i}") for i in range(nsem(c * self.dma_inc[k]))]

        def target(o):
            units = o.count * o.inc
            idx = (units - 1) // SEM_LIM
            val = (units - 1) % SEM_LIM + 1
            return sems[o.key][idx], idx, val

        handles = {"pe": "tensor", "act": "scalar", "dve": "vector", "pool": "gpsimd", "sp": "sync"}
        final_waits = [o for k, o in self.latest.items() if o.is_dma]
        with nc.Block() as block:
            for e in self.ENGS:
                ops = self.ops[e]
                extra = final_waits if e == "sp" else []

                def body(eng, ops=ops, extra=extra):
                    waited = {}

                    def wait_for(d):
                        sem, idx, val = target(d)
                        wk = (d.key, idx)
                        if waited.get(wk, 0) < val:
                            eng.wait_ge(sem, val)
                            waited[wk] = val

                    for o in ops:
                        for d in o.deps.values():
                            wait_for(d)
                        ins = o.fn(eng)
                        if o.signal:
                            sem, idx, val = target(o)
                            ins.then_inc(sem, o.inc)
                    for d in extra:
                        wait_for(d)

                getattr(block, handles[e])(body)


class Arena:
    def __init__(self, nc):
        self.nc = nc
        b0 = nc.sbuf_base
        n = (nc.sbuf_top - b0 - 3072) // 4
        self.slab = nc.alloc_sbuf_tensor("slab", [128, n], F32)
        self.base = (b0 + 63) // 64 * 64
        self.top = (b0 + n * 4) // 64 * 64
        self.cur = self.base
        self.lim = self.top
        self.n = 0

    def size(self, shape, dtype):
        per = 4 if dtype == F32 else 2
        for s in shape[1:]:
            per *= s
        return (per + 63) // 64 * 64

    def alloc(self, shape, dtype):
        per = self.size(shape, dtype)
        off = self.cur
        self.cur += per
        assert self.cur <= self.lim, f"SBUF overflow {self.cur} > {self.lim}"
        self.n += 1
        return self.nc.alloc_sbuf_tensor_at(f"sb{self.n}", list(shape), dtype, offset=off)

    def region(self, lo, hi):
        self.cur = (lo + 63) // 64 * 64
        self.lim = hi


class Builder:
    IN_SHAPES = {
        "x": [NT, D], "w1a": [D, 2 * FF], "w2a": [FF, D], "w1b": [D, 2 * FF], "w2b": [FF, D],
        "w_in": [D, 8256], "w_uq": [512, 1536], "w_ukv": [512, 2048], "w_br_sb": [1024, D],
        "w_br_mla": [1024, D], "w_o": [D, D], "lnp": [128, 96], "bgate": [128, 32], "gcq": [128, 512],
        "gckv": [128, 512], "cos_tm": [128, 9, 32], "sin_tm": [128, 9, 32], "cos_fm": [64, NT],
        "sin_fm": [64, NT], "ident": [128, 128], "cmats": [128, 4, 128], "mb_sb": [128, 16, 512],
        "mb_mla": [128, 16, 512], "mb_new": [16, 128], "c_k": [2, PAST, 1024], "c_v": [2, PAST, 1024],
        "c_ckv": [2, PAST, 512], "c_kr": [2, PAST, 64],
    }

    class _Lazy(dict):
        def __init__(self, b):
            super().__init__()
            self.b = b

        def __missing__(self, k):
            v = self.b.din(k, Builder.IN_SHAPES[k])
            self[k] = v
            return v

    def __init__(self, stop_after=None):
        self.stop_after = stop_after
        nc = bass.Bass("TRN2", target_bir_lowering=False)
        self.nc = nc
        self.P = Prog(nc)
        self.A = Arena(nc)
        self.i = Builder._Lazy(self)
        self.o = {}
        self.build()
        self.P.emit()

    def din(self, name, shape, dtype=F32):
        return self.nc.dram_tensor(name, list(shape), dtype, kind="ExternalInput").ap()

    def dout(self, name, shape, dtype=F32):
        return self.nc.dram_tensor(name, list(shape), dtype, kind="ExternalOutput").ap()

    def mm(self, out, lhsT, rhs, start, stop, reads, writes):
        self.P.op("pe", lambda e: e.matmul(out, lhsT, rhs, start=start, stop=stop, skip_group_check=True),
                  reads, writes)

    def tr(self, out, in_, ident, reads, writes):
        self.P.op("pe", lambda e: e.transpose(out, in_, ident), reads, writes)

    def act(self, out, in_, func, reads, writes, **kw):
        self.P.op("act", lambda e: e.activation(out, in_, func, **kw), reads, writes)

    def tt(self, out, in0, in1, op, reads, writes, eng="dve"):
        self.P.op(eng, lambda e: e.tensor_tensor(out, in0, in1, op), reads, writes)

    def ts(self, out, in0, s1, s2, op0, op1, reads, writes, eng="dve"):
        if op1 is None:
            self.P.op(eng, lambda e: e.tensor_scalar(out, in0, s1, None, op0), reads, writes)
        else:
            self.P.op(eng, lambda e: e.tensor_scalar(out, in0, s1, s2, op0, op1), reads, writes)

    def stt(self, out, in0, scalar, in1, op0, op1, reads, writes):
        self.P.op("dve", lambda e: e.scalar_tensor_tensor(out, in0, scalar, in1, op0, op1), reads, writes)

    def cp(self, out, in_, reads, writes, eng="dve"):
        self.P.op(eng, lambda e: e.tensor_copy(out, in_), reads, writes)

    def memset(self, ap, val, writes, eng="dve"):
        self.P.op(eng, lambda e: e.memset(ap, val), [], writes)

    def PS(self, b):
        return self.psum[:, b * 512:(b + 1) * 512]

    def exi(self, row0, nrows):
        j = row0 // 512
        a = row0 - j * 512
        return self.ex_in[j].ap()[a:a + nrows, :], j

    def exo(self, r, row0, nrows):
        j = row0 // 512
        a = r * self.ex_rows[j] + row0 - j * 512
        return self.ex_out[j].ap()[a:a + nrows, :], j

    def build(self):
        nc, P, A, i = self.nc, self.P, self.A, self.i
        st = self.stop_after
        self.psum = nc.alloc_psum_tensor("ps", [128, 4096], F32)
        self.bk = [P.buf(f"bank{b}", True) for b in range(8)]

        self.ident = A.alloc([128, 128], F32)
        self.identb = A.alloc([128, 128], BF16)
        self.cm = A.alloc([128, 4, 128], BF16)
        self.lnp = A.alloc([128, 96], F32)
        self.lnpa = A.alloc([128, 96], F32)
        self.bg = A.alloc([128, 32], F32)
        self.epsc = A.alloc([128, 4], F32)
        self.ones32 = A.alloc([128, 128], F32)
        self.bconst = P.buf("consts")
        P.dma("sp", self.ident[:], i["ident"], "c0", writes=[self.bconst])
        P.dma("pool", self.cm[:], i["cmats"], "c1", writes=[self.bconst])
        P.dma("pool", self.identb[:], i["ident"], "c4", writes=[self.bconst])
        P.dma("sp", self.lnp[:], i["lnp"], "c2", writes=[self.bconst])
        P.dma("sp", self.bg[:], i["bgate"], "c3", writes=[self.bconst])
        self.ts(self.lnpa[:], self.lnp[:], ALPHA, None, ALU.mult, None, [self.bconst], [self.bconst])
        self.memset(self.epsc[:, 0:1], LN_EPS, [self.bconst])
        self.memset(self.epsc[:, 1:2], RMS_EPS, [self.bconst])
        self.memset(self.epsc[:, 2:3], 1.0, [self.bconst])
        self.memset(self.ones32[:, :], 1.0, [self.bconst])
        self.actb = A.alloc([128, DC, NT], BF16)
        self.b_act = P.bufs(DC, "act")
        self.R2 = A.cur
        self.resid = A.alloc([128, DC, NT], F32)
        self.b_res = P.bufs(DC, "res")
        self.R3 = A.cur
        self.TOP = A.top
        assert self.TOP - self.R3 >= 86000, (self.TOP, self.R3)

        if st is None:
            o = self.o
            o["y"] = self.dout("y", [NT, D])
            o["nk"] = self.dout("nk", [NT, 1024])
            o["nv"] = self.dout("nv", [NT, 1024])
            o["nckv"] = self.dout("nckv", [NT, 512])
            o["nkr"] = self.dout("nkr", [NT, 64])
            self.ex_rows = [512] * 8 + [64]
            self.ex_in = [nc.dram_tensor(f"ex_in{j}", [n, NP_], BF16) for j, n in enumerate(self.ex_rows)]
            self.ex_out = [nc.dram_tensor(f"ex_out{j}", [4 * n, NP_], BF16) for j, n in enumerate(self.ex_rows)]
            self.b_exc = P.bufs(9, "exc")
            self.b_exo = P.bufs(9, "exo")
            self.spill = nc.dram_tensor("spill", [128, DC, NT], F32)

        self.ffn_prefetch(i["w1a"], i["w2a"])
        self.phase_load_x()
        if st == "x":
            return self.dump_act()
        self.phase_ffn()
        if st == "ffn1":
            return self.dump_act()
        if st is None:
            self.proj_prefetch()
        self.phase_ln(0, final=False)
        if st == "ln1":
            return self.dump_act()
        P.dma("sp", self.spill.ap(), self.resid[:], "spill", reads=self.b_res)
        P.barrier()
        self.phase_proj()
        self.phase_sample_attn()
        self.phase_prompt_attn()
        self.phase_merge()
        self.ffn_prefetch(i["w1b"], i["w2b"])
        self.phase_ln(1, final=False)
        self.phase_ffn()
        self.phase_ln(2, final=True)
        self.phase_out()

    def phase_load_x(self):
        P, A, i = self.P, self.A, self.i
        A.region(self.TOP - 19072, self.TOP)
        xs = [A.alloc([128, D], F32) for _ in range(2)]
        bx = P.bufs(2, "xs")
        k = 0
        for ti, (t0, m) in enumerate(TM):
            s = ti % 2
            P.dma("sp", xs[s][:m, :], i["x"][t0:t0 + m, :], f"x{s}", writes=[bx[s]])
            for q in range(4):
                b = 4 + k % 4
                k += 1
                ps = self.PS(b)
                for j in range(4):
                    c = q * 4 + j
                    self.tr(ps[:, j * 128:j * 128 + m], xs[s][:m, c * 128:(c + 1) * 128], self.ident[:m, :m],
                            [bx[s], self.bconst], [self.bk[b]])
                src = ps.rearrange("p (j t) -> p j t", j=4)[:, :, :m]
                self.act(self.resid[:, q * 4:q * 4 + 4, t0:t0 + m], src, AF.Identity, [self.bk[b]],
                         self.b_res[q * 4:q * 4 + 4], scale=ALPHA)
                self.cp(self.actb[:, q * 4:q * 4 + 4, t0:t0 + m], src, [self.bk[b]], self.b_act[q * 4:q * 4 + 4])
        P.barrier()

    def ffn_prefetch(self, w1, w2):
        P, A = self.P, self.A
        A.region(self.R3, self.TOP - 19072)
        NPAIR, NG = 22, 11
        w1g = [A.alloc([128, DC, 256], BF16) for _ in range(2)]
        w1u = [A.alloc([128, DC, 256], BF16) for _ in range(2)]
        self.ffn_free_lo = A.cur
        aT = [A.alloc([128, 4, NT], BF16) for _ in range(2)]
        sg = A.alloc([128, NT], BF16)
        self.ffn_free_hi = A.cur
        w2s = [A.alloc([128, 4, D], BF16) for _ in range(2)]
        b_w1, b_aT, b_w2 = P.bufs(2, "w1"), P.bufs(2, "aT"), P.bufs(2, "w2")
        b_sg = P.bufs(3, "sg")

        def load_w1(p):
            s = p % 2
            w = (2 if p < 21 else 1) * 128
            c0 = p * 256
            P.dma("pool", w1g[s][:, :, :w], w1[:, c0:c0 + w].rearrange("(c p) n -> p c n", p=128),
                  f"w1g{s}", writes=[b_w1[s]])
            P.dma("pool", w1u[s][:, :, :w], w1[:, FF + c0:FF + c0 + w].rearrange("(c p) n -> p c n", p=128),
                  f"w1u{s}", writes=[b_w1[s]])

        def load_w2(g):
            s = g % 2
            nch = 4 if g < 10 else 3
            r0 = g * 512
            P.dma("pool", w2s[s][:, :nch, :], w2[r0:r0 + nch * 128, :].rearrange("(j p) n -> p j n", p=128),
                  f"w2{s}", writes=[b_w2[s]])

        load_w1(0)
        load_w1(1)
        load_w2(0)
        load_w2(1)
        self.ffn_state = (w1g, w1u, aT, w2s, sg, b_w1, b_aT, b_w2, b_sg, load_w1, load_w2)

    def phase_ffn(self):
        P = self.P
        NPAIR, NG = 22, 11
        (w1g, w1u, aT, w2s, sg, b_w1, b_aT, b_w2, b_sg, load_w1, load_w2) = self.ffn_state
        for g in range(NG):
            gs = g % 2
            nch_g = 4 if g < 10 else 3
            for pp in range(2):
                p = g * 2 + pp
                if p >= NPAIR:
                    continue
                s = p % 2
                nch = 2 if p < 21 else 1
                for jj in range(nch):
                    ja = pp * 2 + jj
                    for (b0, wt) in ((0, w1g[s]), (3, w1u[s])):
                        for c in range(DC):
                            for (t0, n, bk) in TT:
                                self.mm(self.PS(b0 + bk)[:, :n], wt[:, c, jj * 128:(jj + 1) * 128],
                                        self.actb[:, c, t0:t0 + n], c == 0, c == DC - 1,
                                        [b_w1[s], self.b_act[c]], [self.bk[b0 + bk]])
                    for (t0, n, bk) in TT:
                        self.act(sg[:, t0:t0 + n], self.PS(bk)[:, :n], AF.Silu, [self.bk[bk]], [b_sg[bk]])
                    for (t0, n, bk) in TT:
                        self.tt(aT[gs][:, ja, t0:t0 + n], sg[:, t0:t0 + n], self.PS(3 + bk)[:, :n],
                                ALU.mult, [b_sg[bk], self.bk[3 + bk]], [b_aT[gs]])
                if p + 2 < NPAIR:
                    load_w1(p + 2)
            for ci in range(DC):
                b0 = 0 if ci % 2 == 0 else 3
                for jj in range(nch_g):
                    for (t0, n, bk) in TT:
                        self.mm(self.PS(b0 + bk)[:, :n], w2s[gs][:, jj, ci * 128:(ci + 1) * 128],
                                aT[gs][:, jj, t0:t0 + n], jj == 0, jj == nch_g - 1,
                                [b_w2[gs], b_aT[gs]], [self.bk[b0 + bk]])
                for (t0, n, bk) in TT:
                    self.stt(self.resid[:, ci, t0:t0 + n], self.PS(b0 + bk)[:, :n], 0.5,
                             self.resid[:, ci, t0:t0 + n], ALU.mult, ALU.add,
                             [self.bk[b0 + bk], self.b_res[ci]], [self.b_res[ci]])
            if g + 2 < NG:
                load_w2(g + 2)
        P.barrier()

    def phase_ln(self, idx, final):
        P, A = self.P, self.A
        A.region(self.ffn_free_lo, self.ffn_free_hi)
        xb = [A.alloc([128, NT], BF16) for _ in range(2)]
        xq = [A.alloc([128, NT], BF16) for _ in range(2)]
        mt = A.alloc([128, NT], F32)
        rs = A.alloc([128, NT], F32)
        A.region(self.TOP - 19072, self.TOP)
        tmp = [A.alloc([128, NT], F32) for _ in range(2)]
        b_xb, b_xq = P.bufs(2, "xb"), P.bufs(2, "xq")
        b_mt, b_rs = P.buf("mt"), P.buf("rs")
        b_tmp = P.bufs(2, "tmp")
        ones = self.cm[:, 2, :]
        for c in range(DC):
            s = c % 2
            self.act(xb[s][:], self.resid[:, c, :], AF.Copy, [self.b_res[c]], [b_xb[s]])
            self.tt(xq[s][:], self.resid[:, c, :], self.resid[:, c, :], ALU.mult, [self.b_res[c]], [b_xq[s]])
            for (t0, n, bk) in TT:
                self.mm(self.PS(bk)[:, :n], ones, xb[s][:, t0:t0 + n], c == 0, c == DC - 1,
                        [b_xb[s], self.bconst], [self.bk[bk]])
            for (t0, n, bk) in TT:
                self.mm(self.PS(3 + bk)[:, :n], ones, xq[s][:, t0:t0 + n], c == 0, c == DC - 1,
                        [b_xq[s], self.bconst], [self.bk[3 + bk]])
        for (t0, n, bk) in TT:
            sl = slice(t0, t0 + n)
            pa = self.PS(bk)[:, :n]
            pb = self.PS(3 + bk)[:, :n]
            self.ts(mt[:, sl], pa, 1.0 / D, None, ALU.mult, None, [self.bk[bk]], [b_mt])
            self.tt(tmp[0][:, sl], mt[:, sl], mt[:, sl], ALU.mult, [b_mt], [b_tmp[0]])
            self.stt(rs[:, sl], pb, 1.0 / D, tmp[0][:, sl], ALU.mult, ALU.subtract,
                     [self.bk[3 + bk], b_tmp[0]], [b_rs])
            self.act(rs[:, sl], rs[:, sl], AF.Ln, [b_rs, self.bconst], [b_rs], bias=self.epsc[:, 0:1])
            self.act(rs[:, sl], rs[:, sl], AF.Exp, [b_rs], [b_rs], scale=-0.5)
            self.stt(mt[:, sl], mt[:, sl], -1.0, rs[:, sl], ALU.mult, ALU.mult, [b_mt, b_rs], [b_mt])
        g = self.lnp[:, idx * 32:idx * 32 + 16]
        b = self.lnp[:, idx * 32 + 16:idx * 32 + 32]
        ga = self.lnpa[:, idx * 32:idx * 32 + 16]
        ba = self.lnpa[:, idx * 32 + 16:idx * 32 + 32]
        for c in range(DC):
            s = c % 2
            self.tt(tmp[s][:], self.resid[:, c, :], rs[:], ALU.mult, [self.b_res[c], b_rs], [b_tmp[s]])
            self.tt(tmp[s][:], tmp[s][:], mt[:], ALU.add, [b_tmp[s], b_mt], [b_tmp[s]])
            self.act(self.actb[:, c, :], tmp[s][:], AF.Identity, [b_tmp[s], self.bconst], [self.b_act[c]],
                     scale=g[:, c:c + 1], bias=b[:, c:c + 1])
            sc, bi = (g, b) if final else (ga, ba)
            self.act(self.resid[:, c, :], tmp[s][:], AF.Identity, [b_tmp[s], self.bconst], [self.b_res[c]],
                     scale=sc[:, c:c + 1], bias=bi[:, c:c + 1])
        P.barrier()

    PROJ_BLOCKS = [("k", 1024, 512, 0), ("k", 1536, 512, 1), ("v", 2048, 512, 0), ("v", 2560, 512, 1),
                   ("cq", 3072, 512, 0), ("ckv", 3584, 512, 0), ("kr", 4096, 64, 0)]

    def proj_prefetch(self):
        P, A, i = self.P, self.A, self.i
        A.region(self.R3, self.R3 + 32768)
        self.wblk = [A.alloc([128, DC, 512], BF16) for _ in range(2)]
        self.b_wblk = P.bufs(2, "wblk")
        self.load_blk(0)
        self.load_blk(1)

    def load_blk(self, bi):
        kind, c0, nc_, _ = self.PROJ_BLOCKS[bi]
        s = bi % 2
        self.P.dma("pool", self.wblk[s][:, :, :nc_],
                   self.i["w_in"][:, c0:c0 + nc_].rearrange("(c p) n -> p c n", p=128),
                   f"wblk{s}", writes=[self.b_wblk[s]])

    def phase_proj(self):
        P, A, i, o = self.P, self.A, self.i, self.o
        w_in = i["w_in"]
        A.region(self.TOP - 34000, self.TOP)
        self.cqnT = A.alloc([128, 4, NT], BF16)
        self.KTs = A.alloc([128, 8, NS], BF16)
        self.NTs = A.alloc([128, 8, NS], BF16)
        self.krT = A.alloc([64, NT], BF16)
        self.Vs_new = A.alloc([32, 1024], BF16)
        self.VMs_new = A.alloc([32, 1024], BF16)
        self.wukv = A.alloc([128, 4, 2048], BF16)
        self.b_cqnT, self.b_KTs, self.b_NTs = P.buf("cqnT"), P.buf("KTs"), P.buf("NTs")
        self.b_krT, self.b_Vsn, self.b_VMsn, self.b_wukv = P.buf("krT"), P.buf("Vsn"), P.buf("VMsn"), P.buf("wukv")
        P.dma("pool", self.wukv[:], i["w_ukv"].rearrange("(c p) n -> p c n", p=128), "wukv", writes=[self.b_wukv])
        A.region(self.R2, self.TOP - 34000)
        wblk = self.wblk
        stg32 = [A.alloc([128, 512], F32) for _ in range(2)]
        stgb = [A.alloc([128, 512], BF16) for _ in range(2)]
        nrm = [A.alloc([128, 512], F32) for _ in range(2)]
        junk = A.alloc([128, 512], F32)
        ckvnT = A.alloc([128, 4, NT], BF16)
        KTl = [A.alloc([128, NT], BF16) for _ in range(2)]
        gcq = A.alloc([128, 512], F32)
        gckv = A.alloc([128, 512], F32)
        cos = A.alloc([128, 9, 32], F32)
        sin = A.alloc([128, 9, 32], F32)
        kro = [A.alloc([128, 64], F32) for _ in range(2)]
        rt = [A.alloc([128, 32], F32) for _ in range(4)]
        ssq = A.alloc([128, 4], F32)
        b_wblk = self.b_wblk
        b_stg32, b_stgb, b_nrm = P.bufs(2, "stg32"), P.bufs(2, "stgb"), P.bufs(2, "nrm")
        b_junk, b_ckvnT, b_KTl = P.buf("junk"), P.buf("ckvnT"), P.bufs(2, "KTl")
        b_tab, b_kro, b_rt, b_ssq = P.buf("tab"), P.bufs(2, "kro"), P.buf("rt"), P.buf("ssq")
        P.dma("sp", gcq[:], i["gcq"], "t0", writes=[b_tab])
        P.dma("sp", gckv[:], i["gckv"], "t1", writes=[b_tab])
        P.dma("sp", cos[:], i["cos_tm"], "t2", writes=[b_tab])
        P.dma("sp", sin[:], i["sin_tm"], "t3", writes=[b_tab])
        blocks = self.PROJ_BLOCKS
        cnt = {"s32": 0, "sb": 0, "nrm": 0, "bank": 0, "ktl": 0, "kro": 0}

        load_blk = self.load_blk

        def rmsnorm(ps, m, bkb, gtile):
            self.act(junk[:m, :], ps[:m, :], AF.Square, [bkb], [b_junk])
            self.P.op("dve", lambda e: e.reduce_sum(ssq[:m, 0:1], junk[:m, :], mybir.AxisListType.X),
                      [b_junk], [b_ssq])
            self.act(ssq[:m, 1:2], ssq[:m, 0:1], AF.Ln, [b_ssq, self.bconst], [b_ssq], scale=1.0 / 512,
                     bias=self.epsc[:m, 1:2])
            self.act(ssq[:m, 2:3], ssq[:m, 1:2], AF.Exp, [b_ssq], [b_ssq], scale=-0.5)
            s = cnt["nrm"] % 2
            cnt["nrm"] += 1
            self.stt(nrm[s][:m, :], ps[:m, :], ssq[:m, 2:3], gtile[:m, :], ALU.mult, ALU.mult,
                     [bkb, b_ssq, b_tab], [b_nrm[s]])
            return s

        def transposes_to(dst, b_dst, src, b_src, m, t0, nchunk, bank):
            ps = self.PS(bank)
            for k in range(nchunk):
                self.tr(ps[:, k * 128:k * 128 + m], src[:m, k * 128:(k + 1) * 128], self.ident[:m, :m],
                        [b_src, self.bconst], [self.bk[bank]])
            srcv = ps.rearrange("p (j t) -> p j t", j=4)[:, :nchunk, :m]
            self.cp(dst[:, 0:nchunk, t0:t0 + m], srcv, [self.bk[bank]], [b_dst])

        for bi, (kind, c0, nc_, half) in enumerate(blocks):
            s = bi % 2
            for ti, (t0, m) in enumerate(TM):
                bank = 6 + cnt["bank"] % 2
                cnt["bank"] += 1
                ps = self.PS(bank)
                bkb = self.bk[bank]
                for c in range(DC):
                    self.mm(ps[:m, :nc_], self.actb[:, c, t0:t0 + m], wblk[s][:, c, :nc_], c == 0, c == DC - 1,
                            [self.b_act[c], b_wblk[s]], [bkb])
                if kind in ("k", "v"):
                    s2 = cnt["s32"] % 2
                    cnt["s32"] += 1
                    self.act(stg32[s2][:m, :], ps[:m, :], AF.Identity, [bkb], [b_stg32[s2]])
                    dst = o["nk"] if kind == "k" else o["nv"]
                    P.dma("sp", dst[t0:t0 + m, half * 512:(half + 1) * 512], stg32[s2][:m, :], f"o32_{s2}",
                          reads=[b_stg32[s2]])
                    if kind == "v":
                        if m == 128:
                            s3 = cnt["sb"] % 2
                            cnt["sb"] += 1
                            self.cp(stgb[s3][:m, :], ps[:m, :], [bkb], [b_stgb[s3]])
                            ea, ej = self.exi(R_V + t0, m)
                            P.dma("sp", ea[:, half * 512:(half + 1) * 512], stgb[s3][:m, :],
                                  f"ob_{s3}", reads=[b_stgb[s3], self.b_exc[ej]])
                        else:
                            self.cp(self.Vs_new[:m, half * 512:(half + 1) * 512], ps[:m, :], [bkb], [self.b_Vsn])
                elif kind == "cq":
                    sn = rmsnorm(ps, m, bkb, gcq)
                    bank2 = 6 + cnt["bank"] % 2
                    cnt["bank"] += 1
                    transposes_to(self.cqnT, self.b_cqnT, nrm[sn], b_nrm[sn], m, t0, 4, bank2)
                elif kind == "ckv":
                    sn = rmsnorm(ps, m, bkb, gckv)
                    P.dma("sp", o["nckv"][t0:t0 + m, :], nrm[sn][:m, :], f"on_{sn}", reads=[b_nrm[sn]])
                    bank2 = 6 + cnt["bank"] % 2
                    cnt["bank"] += 1
                    transposes_to(ckvnT, b_ckvnT, nrm[sn], b_nrm[sn], m, t0, 4, bank2)
                else:
                    sk = cnt["kro"] % 2
                    cnt["kro"] += 1
                    x1, x2 = ps[:m, 0:32], ps[:m, 32:64]
                    cs, sn_ = cos[:m, ti, :], sin[:m, ti, :]
                    self.tt(rt[0][:m, :], x1, cs, ALU.mult, [bkb, b_tab], [b_rt])
                    self.tt(rt[1][:m, :], x2, sn_, ALU.mult, [bkb, b_tab], [b_rt])
                    self.tt(rt[2][:m, :], x2, cs, ALU.mult, [bkb, b_tab], [b_rt])
                    self.tt(rt[3][:m, :], x1, sn_, ALU.mult, [bkb, b_tab], [b_rt])
                    self.tt(kro[sk][:m, 0:32], rt[0][:m, :], rt[1][:m, :], ALU.subtract, [b_rt], [b_kro[sk]])
                    self.tt(kro[sk][:m, 32:64], rt[2][:m, :], rt[3][:m, :], ALU.add, [b_rt], [b_kro[sk]])
                    P.dma("sp", o["nkr"][t0:t0 + m, :], kro[sk][:m, :], f"okr_{sk}", reads=[b_kro[sk]])
                    bank2 = 6 + cnt["bank"] % 2
                    cnt["bank"] += 1
                    ps2 = self.PS(bank2)
                    self.tr(ps2[:64, :m], kro[sk][:m, 0:64], self.ident[:m, :m], [b_kro[sk], self.bconst],
                            [self.bk[bank2]])
                    self.cp(self.krT[:, t0:t0 + m], ps2[:64, :m], [self.bk[bank2]], [self.b_krT])
            if kind == "k":
                for hh in range(4):
                    h = half * 4 + hh
                    b0 = 0 if hh % 2 == 0 else 3
                    sl_ = cnt["ktl"] % 2
                    cnt["ktl"] += 1
                    for c in range(DC):
                        for (t0, n, bk) in TT:
                            self.mm(self.PS(b0 + bk)[:, :n], wblk[s][:, c, hh * 128:(hh + 1) * 128],
                                    self.actb[:, c, t0:t0 + n], c == 0, c == DC - 1,
                                    [b_wblk[s], self.b_act[c]], [self.bk[b0 + bk]])
                    for (t0, n, bk) in TT:
                        self.act(KTl[sl_][:, t0:t0 + n], self.PS(b0 + bk)[:, :n], AF.Identity,
                                 [self.bk[b0 + bk]], [b_KTl[sl_]])
                    self.cp(self.KTs[:, h, :], KTl[sl_][:, NP_:NT], [b_KTl[sl_]], [self.b_KTs])
                    ea, ej = self.exi(R_KT + h * 128, 128)
                    P.dma("sp", ea, KTl[sl_][:, 0:NP_], f"okt_{sl_}", reads=[b_KTl[sl_], self.b_exc[ej]])
            if bi + 2 < len(blocks):
                load_blk(bi + 2)
            if bi == 3:
                self.issue_cc([0, 1, 2, 3])
        ea, ej = self.exi(R_RT, 64)
        P.dma("sp", ea, self.krT[:, 0:NP_], "okrt", reads=[self.b_krT, self.b_exc[ej]])
        for h in range(H):
            b0 = 0 if h % 2 == 0 else 3
            sl_ = cnt["ktl"] % 2
            cnt["ktl"] += 1
            for kc in range(4):
                for (t0, n, bk) in TT:
                    self.mm(self.PS(b0 + bk)[:, :n], self.wukv[:, kc, h * 256:h * 256 + 128],
                            ckvnT[:, kc, t0:t0 + n], kc == 0, kc == 3, [self.b_wukv, b_ckvnT],
                            [self.bk[b0 + bk]])
            for (t0, n, bk) in TT:
                self.act(KTl[sl_][:, t0:t0 + n], self.PS(b0 + bk)[:, :n], AF.Identity, [self.bk[b0 + bk]],
                         [b_KTl[sl_]])
            self.cp(self.NTs[:, h, :], KTl[sl_][:, NP_:NT], [b_KTl[sl_]], [self.b_NTs])
            ea, ej = self.exi(R_NT + h * 128, 128)
            P.dma("sp", ea, KTl[sl_][:, 0:NP_], f"okt_{sl_}", reads=[b_KTl[sl_], self.b_exc[ej]])
        for ti, (t0, m) in enumerate(TM):
            for half in range(2):
                bank = 6 + cnt["bank"] % 2
                cnt["bank"] += 1
                ps = self.PS(bank)
                for hh in range(4):
                    hc = (half * 4 + hh) * 256 + 128
                    for kc in range(4):
                        self.mm(ps[:m, hh * 128:(hh + 1) * 128], ckvnT[:, kc, t0:t0 + m],
                                self.wukv[:, kc, hc:hc + 128], kc == 0, kc == 3,
                                [b_ckvnT, self.b_wukv], [self.bk[bank]])
                if m == 128:
                    s3 = cnt["sb"] % 2
                    cnt["sb"] += 1
                    self.cp(stgb[s3][:m, :], ps[:m, :], [self.bk[bank]], [b_stgb[s3]])
                    ea, ej = self.exi(R_VM + t0, m)
                    P.dma("sp", ea[:, half * 512:(half + 1) * 512], stgb[s3][:m, :],
                          f"ob_{s3}", reads=[b_stgb[s3], self.b_exc[ej]])
                else:
                    self.cp(self.VMs_new[:m, half * 512:(half + 1) * 512], ps[:m, :], [self.bk[bank]],
                            [self.b_VMsn])
        P.barrier()

    def issue_cc(self, js):
        P = self.P
        for j in js:
            def fn(e, j=j):
                return e.collective_compute("AllGather", ALU.bypass, replica_groups=[[0, 1, 2, 3], [4, 5, 6, 7]],
                                            ins=[self.ex_in[j].ap().opt()], outs=[self.ex_out[j].ap().opt()],
                                            dma_qos="P2")
            P._add("pool", fn, [], [self.b_exc[j], self.b_exo[j]], semkey=f"cc{j}", inc=1)

    def attn_setup(self, alt=None):
        P, A = self.P, self.A
        w = {}
        w["e32"] = [A.alloc([128, 512], F32) for _ in range(2)]
        w["sp"] = [A.alloc([128, 512], BF16) for _ in range(3)]
        w["S"] = [A.alloc([128, 512], BF16) for _ in range(3)]
        w["A"] = [A.alloc([128, 512], BF16) for _ in range(3)]
        if alt is not None:
            save = (A.cur, A.lim)
            A.region(*alt)
        w["rec"] = A.alloc([128, 512], F32)
        w["acc"] = [A.alloc([128, 512], F32) for _ in range(2)]
        if alt is not None:
            self.alt_cur = A.cur
            A.cur, A.lim = save
        w["b_acc"] = P.bufs(2, "acc")
        w["b_e32"] = P.bufs(2, "e32")
        for k in ("sp", "S", "A"):
            w["b_" + k] = P.bufs(3, k)
        w["b_rec"] = P.buf("rec")
        w["cnt"] = 0
        return w

    def sb_problem(self, w, N, groups, tiles, obank, evac):
        negtri, negones = self.cm[:, 0, :], self.cm[:, 1, :]
        S, bS = w["S"], w["b_S"]
        for j in range(3):
            self.memset(S[j][:, :N], 0.0, [bS[j]])
        O = self.PS(obank)
        nt = len(tiles)
        base = w["cnt"]
        w["cnt"] += nt
        wbank = lambda k: (0, 1, 2, 5)[(base + k) % 4]

        def stA(k):
            t = tiles[k]
            nk, wb = t["nk"], wbank(k)
            cl = t.get("c_lo", 0)
            W = self.PS(wb)
            for gi, (c0, ncg, q, bq) in enumerate(groups):
                lo = max(c0, cl)
                self.mm(W[:nk, lo:c0 + ncg], t["kt"][gi], q[:, lo - c0:], gi == 0, False, t["reads"] + [bq],
                        [self.bk[wb]])
            if t["mask"] is not None:
                self.mm(W[:nk, cl:N], self.identb[:nk, :nk], t["mask"][:, cl:N], False, False,
                        [self.bconst, self.b_mask], [self.bk[wb]])
            s2 = (base + k) % 2
            self.act(w["e32"][s2][:nk, cl:N], W[:nk, cl:N], AF.Exp, [self.bk[wb]], [w["b_e32"][s2]])

        def stA2(k):
            t = tiles[k]
            nk, cl = t["nk"], t.get("c_lo", 0)
            s2, s3 = (base + k) % 2, (base + k) % 3
            self.act(w["sp"][s3][:nk, cl:N], w["e32"][s2][:nk, cl:N], AF.Ln, [w["b_e32"][s2], self.bconst],
                     [w["b_sp"][s3]], bias=self.epsc[:nk, 2:3])

        def stB(k):
            t = tiles[k]
            nk, wb = t["nk"], wbank(k)
            cl = t.get("c_lo", 0)
            W = self.PS(wb)
            s3 = (base + k) % 3
            first = k == 0
            self.mm(W[:nk, cl:N], negtri[:nk, :nk], w["sp"][s3][:nk, cl:N], False, first,
                    [self.bconst, w["b_sp"][s3]], [self.bk[wb]])
            if not first:
                self.mm(W[:nk, cl:N], negones[:, :nk], S[k % 3][:, cl:N], False, True,
                        [self.bconst, bS[k % 3]], [self.bk[wb]])
            self.act(w["A"][s3][:nk, cl:N], W[:nk, cl:N], AF.Exp, [self.bk[wb]], [w["b_A"][s3]])
            if k < nt - 1:
                nx = (k + 1) % 3
                if nk < 128:
                    self.tt(S[nx][:nk, cl:N], S[k % 3][:nk, cl:N], w["sp"][s3][:nk, cl:N], ALU.add,
                            [bS[k % 3], w["b_sp"][s3]], [bS[nx]])
                else:
                    self.tt(S[nx][:, cl:N], S[k % 3][:, cl:N], w["sp"][s3][:, cl:N], ALU.add,
                            [bS[k % 3], w["b_sp"][s3]], [bS[nx]])

        def stC(k):
            t = tiles[k]
            nk = t["nk"]
            s3 = (base + k) % 3
            cl = t.get("c_lo", 0)
            for gi, (c0, ncg, q, bq) in enumerate(groups):
                lo = max(c0, cl)
                self.mm(O[:, lo:c0 + ncg], t["v"][gi], w["A"][s3][:nk, lo:c0 + ncg], k == 0 and gi == 0,
                        k == nt - 1, t["reads"] + [w["b_A"][s3]], [self.bk[obank]])

        for it in range(nt + 3):
            if it < nt:
                stA(it)
            if 1 <= it <= nt:
                stA2(it - 1)
            if 2 <= it <= nt + 1:
                stB(it - 2)
            if 3 <= it:
                stC(it - 3)
        evac(O, obank)

    def mla_problem(self, w, N, groups, tiles, obank, dbank, evac):
        ones = self.cm[:, 2, :]
        O, Dn = self.PS(obank), self.PS(dbank)
        nt = len(tiles)
        base = w["cnt"]
        w["cnt"] += nt

        def stA(k):
            t = tiles[k]
            nk, zb = t["nk"], (base + k) % 3
            Z = self.PS(zb)
            s3 = (base + k) % 3
            cl = t.get("c_lo", 0)
            for gi, (c0, ncg, qn, qr, bq) in enumerate(groups):
                lo = max(c0, cl)
                self.mm(Z[:nk, lo:c0 + ncg], t["nt"][gi], qn[:, lo - c0:], gi == 0, False, t["reads"] + [bq],
                        [self.bk[zb]])
                self.mm(Z[:nk, lo:c0 + ncg], t["rt"], qr[:, lo - c0:], False,
                        t["mask"] is None and gi == len(groups) - 1, t["reads"] + [bq], [self.bk[zb]])
            if t["mask"] is not None:
                self.mm(Z[:nk, cl:N], self.identb[:nk, :nk], t["mask"][:, cl:N], False, True,
                        [self.bconst, self.b_mask], [self.bk[zb]])
            self.act(w["A"][s3][:nk, cl:N], Z[:nk, cl:N], AF.Exp, [self.bk[zb]], [w["b_A"][s3]])

        def stC(k):
            t = tiles[k]
            nk = t["nk"]
            s3 = (base + k) % 3
            first, last = k == 0, k == nt - 1
            cl = t.get("c_lo", 0)
            for gi, (c0, ncg, qn, qr, bq) in enumerate(groups):
                lo = max(c0, cl)
                self.mm(O[:, lo:c0 + ncg], t["vm"][gi], w["A"][s3][:nk, lo:c0 + ncg], first and gi == 0, last,
                        t["reads"] + [w["b_A"][s3]], [self.bk[obank]])
            a2 = k % 2
            self.tt(w["acc"][a2][:nk, cl:N], w["acc"][a2][:nk, cl:N], w["A"][s3][:nk, cl:N], ALU.add,
                    [w["b_acc"][a2], w["b_A"][s3]], [w["b_acc"][a2]])

        self.memset(w["acc"][0][:, :N], 0.0, [w["b_acc"][0]])
        self.memset(w["acc"][1][:, :N], 0.0, [w["b_acc"][1]])
        for it in range(nt + 1):
            if it < nt:
                stA(it)
            if it >= 1:
                stC(it - 1)
        self.mm(Dn[:, :N], self.ones32[:, :], w["acc"][0][:, :N], True, False, [self.bconst, w["b_acc"][0]],
                [self.bk[dbank]])
        self.mm(Dn[:, :N], self.ones32[:, :], w["acc"][1][:, :N], False, True, [self.bconst, w["b_acc"][1]],
                [self.bk[dbank]])
        self.P.op("dve", lambda e: e.reciprocal(w["rec"][:, :N], Dn[:, :N]), [self.bk[dbank]], [w["b_rec"]])
        evac(O, obank, w["rec"], w["b_rec"])

    def phase_sample_attn(self):
        P, A, i = self.P, self.A, self.i
        w_in = i["w_in"]
        A.region(self.R3, self.R3 + 34000)
        self.oT_sb = A.alloc([128, H, NT], BF16)
        self.oT_mla = A.alloc([128, H, NT], BF16)
        self.b_oT_sb, self.b_oT_mla = P.buf("oTsb"), P.buf("oTmla")
        A.region(self.R2, self.R3)
        w = self.attn_setup()
        self.aw = w
        Qss = A.alloc([128, H, NS], BF16)
        Qns = A.alloc([128, H, NS], BF16)
        Qrs = A.alloc([64, H, NS], BF16)
        Qrt = [A.alloc([64, NS], F32) for _ in range(2)]
        cosf = A.alloc([64, NS], F32)
        sinf = A.alloc([64, NS], F32)
        self.cosf, self.sinf = cosf, sinf
        self.b_tabf = P.buf("tabf")
        P.dma("sp", cosf[:], i["cos_fm"][:, NP_:NT], "t0", writes=[self.b_tabf])
        P.dma("sp", sinf[:], i["sin_fm"][:, NP_:NT], "t1", writes=[self.b_tabf])
        mbn = A.alloc([16, 128], BF16)
        mbn32 = A.alloc([16, 128], F32)
        b_mbn32 = P.buf("mbn32")
        self.b_mask = P.buf("mask")
        P.dma("sp", mbn32[:], i["mb_new"], "mb", writes=[b_mbn32])
        self.cp(mbn[:], mbn32[:], [b_mbn32], [self.b_mask])
        wq = None
        wuq = [A.alloc([128, 4, 256], BF16) for _ in range(2)]

        self.wq, self.wuq = wq, wuq
        self.b_wq, self.b_wuq = P.bufs(2, "wq"), P.bufs(2, "wuq")
        b_Qs = P.buf("Qs_s")
        b_Qrt = P.buf("Qrt")
        Vn1 = A.alloc([16, 1024], BF16)
        VMn1 = A.alloc([16, 1024], BF16)
        b_Vn1 = P.buf("Vn1")
        P.dma("sp", Vn1[:], self.Vs_new[16:32, :], "vn1", reads=[self.b_Vsn], writes=[b_Vn1])
        P.dma("sp", VMn1[:], self.VMs_new[16:32, :], "vmn1", reads=[self.b_VMsn], writes=[b_Vn1])
        self.wcount = 0
        wq2 = [A.alloc([128, DC, 256], BF16) for _ in range(2)]
        b_wq2 = P.bufs(2, "wq2")
        uqv = i["w_uq"].rearrange("(c p) n -> p c n", p=128)

        def load_pair(hp):
            sl2 = hp % 2
            P.dma("pool", wq2[sl2][:], w_in[:, hp * 256:(hp + 1) * 256].rearrange("(c p) n -> p c n", p=128),
                  f"wq2{sl2}", writes=[b_wq2[sl2]])

        def load_wuq(h):
            sl2 = h % 2
            b0 = h * 192
            P.dma("pool", wuq[sl2][:, :, 0:192], uqv[:, :, b0:b0 + 192], f"wuqa{sl2}", writes=[self.b_wuq[sl2]])
            P.dma("pool", wuq[sl2][:, :, 192:224], uqv[:, :, b0 + 160:b0 + 192], f"wuqb{sl2}",
                  writes=[self.b_wuq[sl2]])
            P.dma("pool", wuq[sl2][:, :, 224:256], uqv[:, :, b0 + 128:b0 + 160], f"wuqc{sl2}",
                  writes=[self.b_wuq[sl2]])
            return sl2

        load_pair(0)
        load_pair(1)
        for h in range(H):
            s = load_wuq(h)
            sp2, off = (h // 2) % 2, (h % 2) * 128
            bA, bB = (6, 7) if h % 2 == 0 else (3, 4)
            ps = self.PS(bA)
            for c in range(DC):
                self.mm(ps[:, :NS], wq2[sp2][:, c, off:off + 128], self.actb[:, c, NP_:NT], c == 0, c == DC - 1,
                        [b_wq2[sp2], self.b_act[c]], [self.bk[bA]])
            if h % 2 == 1 and h // 2 + 2 < 4:
                load_pair(h // 2 + 2)
            self.act(Qss[:, h, :], ps[:, :NS], AF.Identity, [self.bk[bA]], [b_Qs], scale=SB_SCALE)
            ps = self.PS(bB)
            for kc in range(4):
                self.mm(ps[:, :NS], wuq[s][:, kc, 0:128], self.cqnT[:, kc, NP_:NT], kc == 0, kc == 3,
                        [self.b_wuq[s], self.b_cqnT], [self.bk[bB]])
            self.act(Qns[:, h, :], ps[:, :NS], AF.Identity, [self.bk[bB]], [b_Qs], scale=MLA_SCALE)
            ps = self.PS(bA)
            for kc in range(4):
                self.mm(ps[:64, :NS], wuq[s][:, kc, 128:192], self.cqnT[:, kc, NP_:NT], kc == 0, kc == 3,
                        [self.b_wuq[s], self.b_cqnT], [self.bk[bA]])
            for kc in range(4):
                self.mm(ps[:64, 64:64 + NS], wuq[s][:, kc, 192:256], self.cqnT[:, kc, NP_:NT], kc == 0, kc == 3,
                        [self.b_wuq[s], self.b_cqnT], [self.bk[bA]])
            self.tt(Qrt[0][:, :], ps[:64, :NS], cosf[:, :], ALU.mult, [self.bk[bA], self.b_tabf], [b_Qrt])
            self.tt(Qrt[1][:, :], ps[:64, 64:64 + NS], sinf[:, :], ALU.mult, [self.bk[bA], self.b_tabf], [b_Qrt])
            self.tt(Qrt[0][:, :], Qrt[0][:, :], Qrt[1][:, :], ALU.add, [b_Qrt], [b_Qrt])
            self.act(Qrs[:, h, :], Qrt[0][:, :], AF.Identity, [b_Qrt], [b_Qs], scale=MLA_SCALE)
        r2cur = A.cur
        A.region(self.R3 + 34000, self.TOP - 34000)
        KTc = A.alloc([128, H, PAST], BF16)
        Vc = A.alloc([128, 8, 1024], BF16)
        A.region(r2cur, self.R3)
        ckvTc = A.alloc([128, 4, PAST], BF16)
        krTc = A.alloc([64, PAST], BF16)
        kst = [A.alloc([128, 1024], F32) for _ in range(2)]
        krst = A.alloc([128, 8, 64], F32)
        b_KTc, b_Vc, b_ckvTc, b_krTc = P.buf("KTc"), P.buf("Vc"), P.buf("ckvTc"), P.buf("krTc")
        b_kst, b_krst = P.bufs(2, "kst"), P.buf("krst")
        kcnt = 0
        for sq in range(2):
            qsl = slice(sq * 16, sq * 16 + 16)
            for kt in range(8):
                s = kcnt % 2
                kcnt += 1
                P.dma("sp", kst[s][:, :], i["c_k"][sq, kt * 128:(kt + 1) * 128, :], f"kst{s}", writes=[b_kst[s]])
                for hg in range(2):
                    bank = 6 + hg
                    ps = self.PS(bank)
                    for j in range(4):
                        h = hg * 4 + j
                        self.tr(ps[:, j * 128:(j + 1) * 128], kst[s][:, h * 128:(h + 1) * 128], self.ident[:, :],
                                [b_kst[s], self.bconst], [self.bk[bank]])
                    self.cp(KTc[:, hg * 4:hg * 4 + 4, kt * 128:(kt + 1) * 128],
                            ps.rearrange("p (j t) -> p j t", j=4), [self.bk[bank]], [b_KTc])
            P.dma("pool", Vc[:], i["c_v"][sq].rearrange("(k p) c -> p k c", p=128), "vc", writes=[b_Vc])
            groups = [(h * 16, 16, Qss[:, h, qsl], b_Qs) for h in range(H)]
            vnew = self.Vs_new if sq == 0 else Vn1
            b_vnew = self.b_Vsn if sq == 0 else b_Vn1
            tiles = [dict(nk=16, kt=[self.KTs[:, h, qsl] for h in range(H)],
                          v=[vnew[:16, h * 128:(h + 1) * 128] for h in range(H)], mask=mbn[:, :],
                          reads=[self.b_KTs, b_vnew])]
            for kt in range(7, -1, -1):
                tiles.append(dict(nk=128, kt=[KTc[:, h, kt * 128:(kt + 1) * 128] for h in range(H)],
                                  v=[Vc[:, kt, h * 128:(h + 1) * 128] for h in range(H)], mask=None,
                                  reads=[b_KTc, b_Vc]))
            c0 = NP_ + sq * 16

            def evac_sb(O, obank, c0=c0):
                self.cp(self.oT_sb[:, :, c0:c0 + 16], O[:, :128].rearrange("p (h q) -> p h q", q=16),
                        [self.bk[obank]], [self.b_oT_sb])
            self.sb_problem(w, 128, groups, tiles, 3, evac_sb)
            NTc, VMc = KTc, Vc
            for kt in range(8):
                s = kcnt % 2
                kcnt += 1
                P.dma("sp", kst[s][:, :512], i["c_ckv"][sq, kt * 128:(kt + 1) * 128, :], f"kst{s}",
                      writes=[b_kst[s]])
                bank = 6 + kt % 2
                ps = self.PS(bank)
                for j in range(4):
                    self.tr(ps[:, j * 128:(j + 1) * 128], kst[s][:, j * 128:(j + 1) * 128], self.ident[:, :],
                            [b_kst[s], self.bconst], [self.bk[bank]])
                self.cp(ckvTc[:, :, kt * 128:(kt + 1) * 128], ps.rearrange("p (j t) -> p j t", j=4),
                        [self.bk[bank]], [b_ckvTc])
            P.dma("sp", krst[:], i["c_kr"][sq].rearrange("(k p) c -> p k c", p=128), "krst", writes=[b_krst])
            for kg in range(2):
                bank = 6 + kg
                ps = self.PS(bank)
                for j in range(4):
                    kt = kg * 4 + j
                    self.tr(ps[:64, j * 128:(j + 1) * 128], krst[:, kt, :], self.ident[:, :],
                            [b_krst, self.bconst], [self.bk[bank]])
                self.cp(krTc[:, kg * 512:(kg + 1) * 512], ps[:64, :], [self.bk[bank]], [b_krTc])
            for h in range(H):
                for half in range(2):
                    bank = 6 + (h * 2 + half) % 2
                    ps = self.PS(bank)
                    for kc in range(4):
                        self.mm(ps[:, :], self.wukv[:, kc, h * 256:h * 256 + 128],
                                ckvTc[:, kc, half * 512:(half + 1) * 512], kc == 0, kc == 3,
                                [self.b_wukv, b_ckvTc], [self.bk[bank]])
                    self.act(NTc[:, h, half * 512:(half + 1) * 512], ps[:, :], AF.Identity, [self.bk[bank]],
                             [b_KTc])
            for kt in range(8):
                for half in range(2):
                    bank = 6 + (kt * 2 + half) % 2
                    ps = self.PS(bank)
                    for hh in range(4):
                        hc = (half * 4 + hh) * 256 + 128
                        for kc in range(4):
                            self.mm(ps[:, hh * 128:(hh + 1) * 128], ckvTc[:, kc, kt * 128:(kt + 1) * 128],
                                    self.wukv[:, kc, hc:hc + 128], kc == 0, kc == 3,
                                    [b_ckvTc, self.b_wukv], [self.bk[bank]])
                    self.cp(VMc[:, kt, half * 512:(half + 1) * 512], ps[:, :], [self.bk[bank]], [b_Vc])
            groups = [(h * 16, 16, Qns[:, h, qsl], Qrs[:, h, qsl], b_Qs) for h in range(H)]
            vmnew = self.VMs_new if sq == 0 else VMn1
            b_vmnew = self.b_VMsn if sq == 0 else b_Vn1
            tiles = [dict(nk=16, nt=[self.NTs[:, h, qsl] for h in range(H)], rt=self.krT[:, c0:c0 + 16],
                          vm=[vmnew[:16, h * 128:(h + 1) * 128] for h in range(H)], mask=None,
                          reads=[self.b_NTs, self.b_krT, b_vmnew])]
            for kt in range(7, -1, -1):
                tiles.append(dict(nk=128, nt=[NTc[:, h, kt * 128:(kt + 1) * 128] for h in range(H)],
                                  rt=krTc[:, kt * 128:(kt + 1) * 128],
                                  vm=[VMc[:, kt, h * 128:(h + 1) * 128] for h in range(H)], mask=None,
                                  reads=[b_KTc, b_krTc, b_Vc]))

            def evac_mla(O, obank, rec, b_rec, c0=c0):
                self.tt(self.oT_mla[:, :, c0:c0 + 16], O[:, :128].rearrange("p (h q) -> p h q", q=16),
                        rec[:, :128].rearrange("p (h q) -> p h q", q=16), ALU.mult,
                        [self.bk[obank], b_rec], [self.b_oT_mla])
            self.mla_problem(w, 128, groups, tiles, 3, 5, evac_mla)
        P.barrier()

    def load_wq(self, h):
        P, i = self.P, self.i
        s = self.wcount % 2
        self.wcount += 1
        P.dma("pool", self.wq[s][:], i["w_in"][:, h * 128:(h + 1) * 128].rearrange("(c p) n -> p c n", p=128),
              f"wq{s}", writes=[self.b_wq[s]])
        uq = i["w_uq"].rearrange("(c p) n -> p c n", p=128)
        b0 = h * 192
        P.dma("pool", self.wuq[s][:, :, 0:192], uq[:, :, b0:b0 + 192], f"wuqa{s}", writes=[self.b_wuq[s]])
        P.dma("pool", self.wuq[s][:, :, 192:224], uq[:, :, b0 + 160:b0 + 192], f"wuqb{s}", writes=[self.b_wuq[s]])
        P.dma("pool", self.wuq[s][:, :, 224:256], uq[:, :, b0 + 128:b0 + 160], f"wuqc{s}", writes=[self.b_wuq[s]])
        return s

    def phase_prompt_attn(self):
        P, A, i = self.P, self.A, self.i
        A.region(self.R2, self.R3)
        w = self.attn_setup(alt=(self.TOP - 34000 + 8448, self.TOP))
        mask = A.alloc([128, 16, 512], BF16)
        self.b_mask = P.buf("mask2")
        Kg = [A.alloc([128, 4096], BF16) for _ in range(2)]
        Vg = [A.alloc([128, 32, 128], BF16) for _ in range(2)]
        A.region(self.R3 + 34000, self.TOP - 34000)
        Qs = [A.alloc([128, NP_], BF16) for _ in range(2)]
        Rg = A.alloc([64, 4096], BF16)
        Qr = [A.alloc([64, NP_], BF16) for _ in range(2)]
        Qrt = [A.alloc([64, 512], F32) for _ in range(2)]
        wq = [A.alloc([128, DC, 128], BF16) for _ in range(2)]
        wuq = [A.alloc([128, 4, 256], BF16) for _ in range(2)]
        self.wq, self.wuq = wq, wuq
        self.b_wq, self.b_wuq = P.bufs(2, "wq2"), P.bufs(2, "wuq2")
        b_Kg, b_Vg, b_Rg = P.bufs(2, "Kg"), P.bufs(2, "Vg"), P.buf("Rg")
        b_Qs, b_Qr, b_Qrt = P.bufs(2, "Qs"), P.bufs(2, "Qr"), P.buf("Qrt2")
        A.region(self.alt_cur, self.TOP)
        cosf = A.alloc([64, NP_], F32)
        sinf = A.alloc([64, NP_], F32)
        b_tabf = P.buf("tabf2")
        P.dma("sp", cosf[:], i["cos_fm"][:, 0:NP_], "t0", writes=[b_tabf])
        P.dma("sp", sinf[:], i["sin_fm"][:, 0:NP_], "t1", writes=[b_tabf])

        def load_kv(slot, row0, h):
            pairs, rds = [], []
            for r in range(4):
                ea, ej = self.exo(r, row0 + h * 128, 128)
                pairs.append((Kg[slot][:, r * 1024:(r + 1) * 1024], ea))
                rds.append(self.b_exo[ej])
            P.dma_batch("sp", pairs, f"kg{slot}", reads=rds, writes=[b_Kg[slot]])
            vrow = R_V if row0 == R_KT else R_VM
            pairs, rds = [], []
            for r in range(4):
                for hf in range(2):
                    ea, ej = self.exo(r, vrow + hf * 512, 512)
                    pairs.append((Vg[slot][:, r * 8 + hf * 4:r * 8 + hf * 4 + 4, :],
                                  ea[:, h * 128:(h + 1) * 128].rearrange("(m p) d -> p m d", p=128)))
                    rds.append(self.b_exo[ej])
            P.dma_batch("sp", pairs, f"vg{slot}", reads=rds, writes=[b_Vg[slot]])

        def key_tiles(M, slot, mla):
            tl = []
            for kt in range(16 * M + 15, -1, -1):
                r, m = kt % 4, kt // 4
                col = r * 1024 + m * 128
                mk = mask[:, kt - 16 * M, :] if kt >= 16 * M else None
                d = dict(nk=128, mask=mk, reads=[b_Kg[slot], b_Vg[slot]] + ([b_Rg] if mla else []))
                if kt >= 16 * M:
                    d["c_lo"] = 128 * max(0, (kt - 16 * M - 3 + 3) // 4)
                if mla:
                    d["nt"] = [Kg[slot][:, col:col + 128]]
                    d["rt"] = Rg[:, col:col + 128]
                    d["vm"] = [Vg[slot][:, r * 8 + m, :]]
                else:
                    d["kt"] = [Kg[slot][:, col:col + 128]]
                    d["v"] = [Vg[slot][:, r * 8 + m, :]]
                tl.append(d)
            return tl

        self.wcount = 0
        pcount = 0
        P.dma("pool", mask[:], i["mb_sb"], "mb2", writes=[self.b_mask])
        wslots = [self.load_wq(0), self.load_wq(1)]
        self.issue_cc([4, 5, 6, 7, 8])

        def qproj_sb(h):
            slot, s = h % 2, wslots[h]
            for half in range(2):
                bank = 6 + half
                ps = self.PS(bank)
                for c in range(DC):
                    self.mm(ps[:, :], wq[s][:, c, :], self.actb[:, c, half * 512:(half + 1) * 512], c == 0,
                            c == DC - 1, [self.b_wq[s], self.b_act[c]], [self.bk[bank]])
                self.act(Qs[slot][:, half * 512:(half + 1) * 512], ps[:, :], AF.Identity, [self.bk[bank]],
                         [b_Qs[slot]], scale=SB_SCALE)
            if h + 2 < H:
                wslots.append(self.load_wq(h + 2))

        load_kv(0, R_KT, 0)
        qproj_sb(0)
        for h in range(H):
            slot = h % 2
            for M in range(2):
                groups = [(0, 512, Qs[slot][:, M * 512:(M + 1) * 512], b_Qs[slot])]
                obank = 3 + pcount % 2
                pcount += 1

                def evac(O, ob, h=h, M=M):
                    self.act(self.oT_sb[:, h, M * 512:(M + 1) * 512], O[:, :], AF.Identity, [self.bk[ob]],
                             [self.b_oT_sb])
                self.sb_problem(w, 512, groups, key_tiles(M, slot, False), obank, evac)
                if M == 0 and h + 1 < H:
                    load_kv((h + 1) % 2, R_KT, h + 1)
                    qproj_sb(h + 1)
        for i4 in range(4):
            P.dma("pool", mask[:, 4 * i4:4 * i4 + 4, i4 * 128:(i4 + 1) * 128],
                  i["mb_mla"][:, 4 * i4:4 * i4 + 4, i4 * 128:(i4 + 1) * 128], "mb2", writes=[self.b_mask])
        for r in range(4):
            ea, ej = self.exo(r, R_RT, 64)
            P.dma("sp", Rg[:, r * 1024:(r + 1) * 1024], ea, "rg", reads=[self.b_exo[ej]],
                  writes=[b_Rg])
        mslots = [self.load_wq(0), self.load_wq(1)]

        def qproj_mla(h):
            slot, s = h % 2, mslots[h]
            for half in range(2):
                tsl = slice(half * 512, (half + 1) * 512)
                bank = 6 + half
                ps = self.PS(bank)
                for kc in range(4):
                    self.mm(ps[:, :], wuq[s][:, kc, 0:128], self.cqnT[:, kc, tsl], kc == 0, kc == 3,
                            [self.b_wuq[s], self.b_cqnT], [self.bk[bank]])
                self.act(Qs[slot][:, tsl], ps[:, :], AF.Identity, [self.bk[bank]], [b_Qs[slot]], scale=MLA_SCALE)
            for half in range(2):
                tsl = slice(half * 512, (half + 1) * 512)
                for part, bank in ((0, 6), (1, 7)):
                    ps = self.PS(bank)
                    for kc in range(4):
                        self.mm(ps[:64, :], wuq[s][:, kc, 128 + part * 64:192 + part * 64], self.cqnT[:, kc, tsl],
                                kc == 0, kc == 3, [self.b_wuq[s], self.b_cqnT], [self.bk[bank]])
                self.tt(Qrt[0][:, :], self.PS(6)[:64, :], cosf[:, tsl], ALU.mult, [self.bk[6], b_tabf], [b_Qrt])
                self.tt(Qrt[1][:, :], self.PS(7)[:64, :], sinf[:, tsl], ALU.mult, [self.bk[7], b_tabf], [b_Qrt])
                self.tt(Qrt[0][:, :], Qrt[0][:, :], Qrt[1][:, :], ALU.add, [b_Qrt], [b_Qrt])
                self.act(Qr[slot][:, tsl], Qrt[0][:, :], AF.Identity, [b_Qrt], [b_Qr[slot]], scale=MLA_SCALE)
            if h + 2 < H:
                mslots.append(self.load_wq(h + 2))

        load_kv(0, R_NT, 0)
        qproj_mla(0)
        for h in range(H):
            slot = h % 2
            for M in range(2):
                groups = [(0, 512, Qs[slot][:, M * 512:(M + 1) * 512], Qr[slot][:, M * 512:(M + 1) * 512],
                           b_Qs[slot])]
                obank = 3 + pcount % 2
                dbank = 5
                pcount += 1

                def evac(O, ob, rec, b_rec, h=h, M=M):
                    self.tt(self.oT_mla[:, h, M * 512:(M + 1) * 512], O[:, :], rec[:, :], ALU.mult,
                            [self.bk[ob], b_rec], [self.b_oT_mla])
                tl = key_tiles(M, slot, True)
                for t in tl:
                    t["reads"] = t["reads"] + [b_Qr[slot]]
                self.mla_problem(w, 512, groups, tl, obank, dbank, evac)
                if M == 0 and h + 1 < H:
                    load_kv((h + 1) % 2, R_NT, h + 1)
                    qproj_mla(h + 1)
        P.barrier()

    def phase_merge(self):
        P, A, i = self.P, self.A, self.i
        w_in = i["w_in"]
        A.region(self.R3 + 34000, self.R3 + 34000 + 34000)
        mg = A.alloc([128, DC, NT], BF16)
        b_mg = P.bufs(DC, "mg")
        A.region(self.R2, self.R3)
        wg = [A.alloc([128, DC, 256], BF16) for _ in range(2)]
        wb = [A.alloc([128, 16, 128], BF16) for _ in range(2)]
        gs = A.alloc([128, NT], F32)
        gm = A.alloc([128, NT], F32)
        t1 = A.alloc([128, NT], F32)
        t2 = A.alloc([128, NT], F32)
        b_wg, b_wb = P.bufs(2, "wg"), P.bufs(2, "wb")
        b_gs, b_gm, b_t1, b_t2 = P.bufs(3, "gs"), P.bufs(3, "gm"), P.bufs(3, "t1"), P.bufs(3, "t2")

        def load(ci):
            s = ci % 2
            P.dma("pool", wg[s][:, :, 0:128],
                  w_in[:, 4160 + ci * 128:4160 + (ci + 1) * 128].rearrange("(c p) n -> p c n", p=128),
                  f"wga{s}", writes=[b_wg[s]])
            P.dma("pool", wg[s][:, :, 128:256],
                  w_in[:, 6208 + ci * 128:6208 + (ci + 1) * 128].rearrange("(c p) n -> p c n", p=128),
                  f"wgb{s}", writes=[b_wg[s]])
            P.dma("pool", wb[s][:, 0:8, :],
                  i["w_br_sb"][:, ci * 128:(ci + 1) * 128].rearrange("(c p) n -> p c n", p=128),
                  f"wba{s}", writes=[b_wb[s]])
            P.dma("pool", wb[s][:, 8:16, :],
                  i["w_br_mla"][:, ci * 128:(ci + 1) * 128].rearrange("(c p) n -> p c n", p=128),
                  f"wbb{s}", writes=[b_wb[s]])

        load(0)
        load(1)
        for ci in range(DC):
            s = ci % 2
            for (b0, off, gt, b_g, bidx) in ((0, 0, gs, b_gs, ci), (3, 128, gm, b_gm, 16 + ci)):
                for c in range(DC):
                    for (t0, n, bk) in TT:
                        self.mm(self.PS(b0 + bk)[:, :n], wg[s][:, c, off:off + 128], self.actb[:, c, t0:t0 + n],
                                c == 0, c == DC - 1, [b_wg[s], self.b_act[c]], [self.bk[b0 + bk]])
                for (t0, n, bk) in TT:
                    self.act(gt[:, t0:t0 + n], self.PS(b0 + bk)[:, :n], AF.Sigmoid, [self.bk[b0 + bk], self.bconst],
                             [b_g[bk]], bias=self.bg[:, bidx:bidx + 1])
            for (b0, r0, oT, b_oT) in ((0, 0, self.oT_sb, self.b_oT_sb), (3, 8, self.oT_mla, self.b_oT_mla)):
                for h in range(H):
                    for (t0, n, bk) in TT:
                        self.mm(self.PS(b0 + bk)[:, :n], wb[s][:, r0 + h, :], oT[:, h, t0:t0 + n], h == 0, h == H - 1,
                                [b_wb[s], b_oT], [self.bk[b0 + bk]])
            for (t0, n, bk) in TT:
                self.tt(t1[:, t0:t0 + n], gs[:, t0:t0 + n], self.PS(bk)[:, :n], ALU.mult,
                        [b_gs[bk], self.bk[bk]], [b_t1[bk]])
                self.tt(t2[:, t0:t0 + n], gm[:, t0:t0 + n], self.PS(3 + bk)[:, :n], ALU.mult,
                        [b_gm[bk], self.bk[3 + bk]], [b_t2[bk]])
                self.tt(mg[:, ci, t0:t0 + n], t1[:, t0:t0 + n], t2[:, t0:t0 + n], ALU.add,
                        [b_t1[bk], b_t2[bk]], [b_mg[ci]])
            if ci + 2 < DC:
                load(ci + 2)
        P.barrier()
        P.dma("sp", self.resid[:], self.spill.ap(), "spill", writes=self.b_res)
        A.region(self.R3, self.R3 + 34000)
        wo = [A.alloc([128, DC, 512], BF16) for _ in range(2)]
        b_wo = P.bufs(2, "wo")

        def load_wo(q):
            s = q % 2
            P.dma("pool", wo[s][:], i["w_o"][:, q * 512:(q + 1) * 512].rearrange("(c p) n -> p c n", p=128),
                  f"wo{s}", writes=[b_wo[s]])
        load_wo(0)
        load_wo(1)
        for q in range(4):
            s = q % 2
            for j in range(4):
                ci = q * 4 + j
                b0 = 0 if ci % 2 == 0 else 3
                for c in range(DC):
                    for (t0, n, bk) in TT:
                        self.mm(self.PS(b0 + bk)[:, :n], wo[s][:, c, j * 128:(j + 1) * 128], mg[:, c, t0:t0 + n],
                                c == 0, c == DC - 1, [b_wo[s], b_mg[c]], [self.bk[b0 + bk]])
                for (t0, n, bk) in TT:
                    self.tt(self.resid[:, ci, t0:t0 + n], self.resid[:, ci, t0:t0 + n], self.PS(b0 + bk)[:, :n],
                            ALU.add, [self.b_res[ci], self.bk[b0 + bk]], [self.b_res[ci]])
            if q + 2 < 4:
                load_wo(q + 2)
        P.barrier()

    def phase_out(self):
        P, A, o = self.P, self.A, self.o
        A.region(self.R3, self.TOP)
        ys = [A.alloc([128, D], F32) for _ in range(2)]
        b_ys = P.bufs(2, "ys")
        k = 0
        for ti, (t0, m) in enumerate(TM):
            s = ti % 2
            for q in range(4):
                b = 4 + k % 4
                k += 1
                ps = self.PS(b)
                for j in range(4):
                    c = q * 4 + j
                    self.tr(ps[:m, j * 128:(j + 1) * 128], self.resid[:, c, t0:t0 + m], self.ident[:, :],
                            [self.b_res[c], self.bconst], [self.bk[b]])
                if k % 2 == 0:
                    self.act(ys[s][:m, q * 512:(q + 1) * 512], ps[:m, :], AF.Identity, [self.bk[b]], [b_ys[s]])
                else:
                    self.cp(ys[s][:m, q * 512:(q + 1) * 512], ps[:m, :], [self.bk[b]], [b_ys[s]])
            P.dma("sp", o["y"][t0:t0 + m, :], ys[s][:m, :], f"y{s}", reads=[b_ys[s]])

    def dump_act(self):
        dbg = self.nc.dram_tensor("dbg", [128, DC, NT], F32, kind="ExternalOutput").ap()
        self.P.dma("sp", dbg, self.resid[:], "dbg", reads=self.b_res)


def build_nc(stop_after=None, **kw):
    b = Builder(stop_after=stop_after, **kw)
    b.nc._used_inputs = list(b.i.keys())
    return b.nc


def host_inputs(inp):
    f = np.float32
    xp = np.asarray(inp["x_prompt"], f)
    xs = np.asarray(inp["x_sample"], f)
    common = {
        "w1a": np.ascontiguousarray(inp["ffn1_w_in"][0]), "w2a": np.ascontiguousarray(inp["ffn1_w_out"][0]),
        "w1b": np.ascontiguousarray(inp["ffn2_w_in"][0]), "w2b": np.ascontiguousarray(inp["ffn2_w_out"][0]),
        "w_in": np.ascontiguousarray(inp["w_in"][0]), "w_uq": np.ascontiguousarray(inp["w_uq"][0]),
        "w_ukv": np.ascontiguousarray(inp["w_ukv"][0]), "w_br_sb": np.ascontiguousarray(inp["w_br_sb"][0]),
        "w_br_mla": np.ascontiguousarray(inp["w_br_mla"][0]), "w_o": np.ascontiguousarray(inp["w_o"][0]),
    }
    lnp = np.concatenate([np.asarray(inp[k][0], f).reshape(16, 128).T for k in
                          ("ln1_g", "ln1_b", "ln2_g", "ln2_b", "ln3_g", "ln3_b")], axis=1)
    common["lnp"] = np.ascontiguousarray(lnp)
    common["bgate"] = np.ascontiguousarray(np.asarray(inp["b_gate"][0], f).reshape(32, 128).T)
    common["gcq"] = np.ascontiguousarray(np.broadcast_to(np.asarray(inp["g_cq"][0], f), (128, 512)))
    common["gckv"] = np.ascontiguousarray(np.broadcast_to(np.asarray(inp["g_ckv"][0], f), (128, 512)))
    common["ident"] = np.eye(128, dtype=f)
    cm = np.zeros((128, 4, 128), f)
    jj, ss = np.meshgrid(np.arange(128), np.arange(128), indexing="ij")
    cm[:, 0, :] = -(jj >= ss).astype(f)
    cm[:, 1, :] = -1.0
    cm[:, 2, :] = 1.0
    common["cmats"] = cm
    kk, tq = np.meshgrid(np.arange(16), np.arange(16), indexing="ij")
    common["mb_new"] = np.ascontiguousarray(np.tile(np.where(kk < tq, 0.0, NEG).astype(f), (1, 8)))
    maps = []
    inv_freq = (10000.0 ** (-np.arange(0, 64, 2, dtype=np.float32) / 64)).astype(f)
    for c in range(8):
        b, r = c // 4, c % 4
        tiles = [4 * m + r for m in range(8)]
        rows = np.concatenate([np.arange(g * 128, (g + 1) * 128) for g in tiles])
        x = np.concatenate([xp[b, rows], xs[2 * c], xs[2 * c + 1]], 0)
        pos = np.concatenate([rows, PAST + np.arange(16), PAST + np.arange(16)]).astype(f)
        ang = pos[:, None] * inv_freq[None, :]
        cos, sin = np.cos(ang).astype(f), np.sin(ang).astype(f)
        cpad = np.zeros((9 * 128, 32), f)
        spad = np.zeros((9 * 128, 32), f)
        cpad[:NT] = cos
        spad[:NT] = sin
        m = dict(common)
        m["x"] = np.ascontiguousarray(x)
        m["cos_tm"] = np.ascontiguousarray(cpad.reshape(9, 128, 32).transpose(1, 0, 2))
        m["sin_tm"] = np.ascontiguousarray(spad.reshape(9, 128, 32).transpose(1, 0, 2))
        m["cos_fm"] = np.ascontiguousarray(np.concatenate([cos.T, cos.T], 0))
        m["sin_fm"] = np.ascontiguousarray(np.concatenate([-sin.T, sin.T], 0))
        k_, q_ = np.meshgrid(np.arange(128), np.arange(128), indexing="ij")
        msb = np.zeros((128, 16, 512), f)
        mml = np.zeros((128, 16, 512), f)
        for o in range(16):
            for i4 in range(4):
                qt = 4 * i4 + r
                sl = slice(i4 * 128, (i4 + 1) * 128)
                if o > qt:
                    msb[:, o, sl] = NEG
                    mml[:, o, sl] = NEG
                elif o == qt:
                    msb[:, o, sl] = np.where(k_ < q_, 0.0, NEG)
                    mml[:, o, sl] = np.where((k_ // 64) <= (q_ // 64), 0.0, NEG)
        m["mb_sb"] = msb
        m["mb_mla"] = mml
        m["c_k"] = np.ascontiguousarray(np.asarray(inp["cache_sb_k"][0, 2 * c:2 * c + 2], f).reshape(2, PAST, 1024))
        m["c_v"] = np.ascontiguousarray(np.asarray(inp["cache_sb_v"][0, 2 * c:2 * c + 2], f).reshape(2, PAST, 1024))
        m["c_ckv"] = np.ascontiguousarray(np.asarray(inp["cache_mla_ckv"][0, 2 * c:2 * c + 2], f))
        m["c_kr"] = np.ascontiguousarray(np.asarray(inp["cache_mla_krope"][0, 2 * c:2 * c + 2], f))
        maps.append(m)
    return maps


_NC = None


def kernel(**inputs):
    global _NC
    maps = host_inputs(inputs)
    if _NC is None:
        _NC = build_nc()
    used = _NC._used_inputs
    maps = [{k: m[k] for k in used} for m in maps]
    res = run_bass_kernel_spmd(_NC, maps, core_ids=list(range(8)))
    return assemble(res.results)


def assemble(results):
    f = np.float32
    y_p = np.zeros((2, 4096, D), f)
    y_s = np.zeros((16, 16, D), f)
    nk_p = np.zeros((1, 2, 4096, 8, 128), f)
    nv_p = np.zeros((1, 2, 4096, 8, 128), f)
    nc_p = np.zeros((1, 2, 4096, 512), f)
    nr_p = np.zeros((1, 2, 4096, 64), f)
    nk_s = np.zeros((1, 16, 16, 8, 128), f)
    nv_s = np.zeros((1, 16, 16, 8, 128), f)
    nc_s = np.zeros((1, 16, 16, 512), f)
    nr_s = np.zeros((1, 16, 16, 64), f)
    for c in range(8):
        b, r = c // 4, c % 4
        R = results[c]
        for m in range(8):
            g = 4 * m + r
            dst = slice(g * 128, (g + 1) * 128)
            src = slice(m * 128, (m + 1) * 128)
            y_p[b, dst] = R["y"][src]
            nk_p[0, b, dst] = R["nk"][src].reshape(128, 8, 128)
            nv_p[0, b, dst] = R["nv"][src].reshape(128, 8, 128)
            nc_p[0, b, dst] = R["nckv"][src]
            nr_p[0, b, dst] = R["nkr"][src]
        for s in range(2):
            src = slice(1024 + 16 * s, 1024 + 16 * s + 16)
            y_s[2 * c + s] = R["y"][src]
            nk_s[0, 2 * c + s] = R["nk"][src].reshape(16, 8, 128)
            nv_s[0, 2 * c + s] = R["nv"][src].reshape(16, 8, 128)
            nc_s[0, 2 * c + s] = R["nckv"][src]
            nr_s[0, 2 * c + s] = R["nkr"][src]
    return (y_p, y_s, nk_p, nv_p, nc_p, nr_p, nk_s, nv_s, nc_s, nr_s)
```
